# Optimizing a Trainium2 kernel written in Bass

```python
import math
import jax, jax.numpy as jnp
from jax import lax
import numpy as np

D_MODEL = 1024
BATCH = 32
SEQ = 2048
DEPTH = 4
DEC_BATCH = 8
DEC_SEQ = 2048
PAST_LEN = 128

MIX = D_MODEL
GROUP_W = MIX // 4
HEAD_DIM = 64
POOL_WINDOWS = (2, 4, 8, 16)
POOL_GROUP = GROUP_W // len(POOL_WINDOWS)
ATTN_HEADS = GROUP_W // HEAD_DIM
DILATED_PATTERNS = ((128, 1), (512, 4), (2048, 16))
DIL_BLOCK = 64
SSM_CH = 16
SSM_GROUPS = GROUP_W // SSM_CH
SSM_STATE = 64
NA_HEADS = GROUP_W // HEAD_DIM
GRID_W = 64
NA_ROWS_MAX = 8
NA_COLS = 16
ROPE_THETA = 10000.0
EPS = 1e-6
N_BLOCKS = 12
PROJ_W = N_BLOCKS * GROUP_W
NEG = -1e30

kernel_name = 'hybrid_parallel_encoder'


def rmsnorm(x, g):
    xf = x.astype(jnp.float32)
    y = xf * lax.rsqrt(jnp.mean(xf * xf, axis=-1, keepdims=True) + EPS)
    return (y * g.astype(jnp.float32)).astype(x.dtype)


def rope(x):
    s = x.shape[1]
    inv = ROPE_THETA ** (-jnp.arange(0, HEAD_DIM, 2, dtype=jnp.float32) / HEAD_DIM)
    ang = jnp.arange(s, dtype=jnp.float32)[:, None] * inv[None, :]
    cos = jnp.cos(ang)[None, :, None, :]
    sin = jnp.sin(ang)[None, :, None, :]
    xf = x.astype(jnp.float32)
    x1, x2 = xf[..., :HEAD_DIM // 2], xf[..., HEAD_DIM // 2:]
    return jnp.concatenate([x1 * cos - x2 * sin, x2 * cos + x1 * sin], axis=-1).astype(x.dtype)


def pool_mixer(u, pool_w, pool_scale):
    s = u.shape[1]
    t = jnp.arange(s)
    outs = []
    for gi, w in enumerate(POOL_WINDOWS):
        ug = u[..., gi * POOL_GROUP:(gi + 1) * POOL_GROUP].astype(jnp.float32)
        cs = jnp.pad(jnp.cumsum(ug, axis=1), ((0, 0), (1, 0), (0, 0)))
        lo = jnp.clip(t - w // 2, 0, s)
        hi = jnp.clip(t + w // 2, 0, s)
        cnt = (hi - lo).astype(jnp.float32)[None, :, None]
        mean = (cs[:, hi] - cs[:, lo]) / cnt
        outs.append(jnp.einsum('bsc,cd->bsd', mean - ug, pool_w[gi].astype(jnp.float32)))
    return (jnp.concatenate(outs, axis=-1) * pool_scale.astype(jnp.float32)).astype(u.dtype)


def dilated_attention(q, k, v):
    bsz, s, h, e = q.shape
    f32 = jnp.float32
    outs, lses = [], []
    for window, d in DILATED_PATTERNS:
        r = window // (2 * d)
        n_sub = s // d
        nb = -(-n_sub // DIL_BLOCK)
        lp = nb * DIL_BLOCK

        def sub(a):
            return a.reshape(bsz, n_sub, d, h, e).transpose(0, 2, 1, 3, 4)

        qs = jnp.pad(sub(q), ((0, 0), (0, 0), (0, lp - n_sub), (0, 0), (0, 0)))
        qs = qs.reshape(bsz, d, nb, DIL_BLOCK, h, e)
        kpad = ((0, 0), (0, 0), (r, r + lp - n_sub), (0, 0), (0, 0))
        ks = jnp.pad(sub(k), kpad)
        vs = jnp.pad(sub(v), kpad)
        idx = jnp.arange(nb)[:, None] * DIL_BLOCK + jnp.arange(DIL_BLOCK + 2 * r)[None, :]
        kb = ks[:, :, idx]
        vb = vs[:, :, idx].astype(f32)
        sc = jnp.einsum('bdnqhe,bdnkhe->bdhnqk', qs, kb, preferred_element_type=f32) * (e ** -0.5)
        lq = jnp.arange(nb)[:, None] * DIL_BLOCK + jnp.arange(DIL_BLOCK)[None, :]
        lk = idx - r
        rel = lk[:, None, :] - lq[:, :, None]
        ok = (jnp.abs(rel) <= r) & (lk[:, None, :] >= 0) & (lk[:, None, :] < n_sub)
        sc = jnp.where(ok, sc, NEG)
        m = jnp.max(sc, axis=-1, keepdims=True)
        p = jnp.exp(sc - m)
        den = jnp.sum(p, axis=-1, keepdims=True)
        o = jnp.einsum('bdhnqk,bdnkhe->bdnqhe', p / den, vb)
        lse = (m + jnp.log(den))[..., 0]
        o = o.reshape(bsz, d, lp, h, e)[:, :, :n_sub].transpose(0, 2, 1, 3, 4).reshape(bsz, s, h, e)
        lse = lse.transpose(0, 1, 3, 4, 2).reshape(bsz, d, lp, h)[:, :, :n_sub]
        lse = lse.transpose(0, 2, 1, 3).reshape(bsz, s, h)
        outs.append(o)
        lses.append(lse)
    wts = jax.nn.softmax(jnp.stack(lses, axis=0), axis=0)
    out = jnp.sum(wts[..., None] * jnp.stack(outs, axis=0), axis=0)
    return out.astype(q.dtype)


def _complex_linear_combine(e1, e2):
    a1r, a1i, b1r, b1i = e1
    a2r, a2i, b2r, b2i = e2
    return (a2r * a1r - a2i * a1i,
            a2r * a1i + a2i * a1r,
            a2r * b1r - a2i * b1i + b2r,
            a2r * b1i + a2i * b1r + b2i)


def s5_mixer(u, a_re, a_im, log_dt, b_re, b_im, c_re, c_im, d_skip, glu_w, glu_b):
    f32 = jnp.float32
    bsz, s, _ = u.shape
    uf = u.astype(f32)
    ug = uf.reshape(bsz, s, SSM_GROUPS, SSM_CH)
    y = uf * d_skip.astype(f32)
    br = b_re.astype(f32)
    bi = b_im.astype(f32)
    for direction in range(2):
        lr = a_re[direction].astype(f32)
        li = a_im[direction].astype(f32)
        dt = jnp.exp(log_dt[direction].astype(f32))[:, None]
        mag = jnp.exp(lr * dt)
        abr = mag * jnp.cos(li * dt)
        abi = mag * jnp.sin(li * dt)
        den = lr * lr + li * li
        gr = ((abr - 1.0) * lr + abi * li) / den
        gi = (abi * lr - (abr - 1.0) * li) / den
        bbr = gr[..., None] * br - gi[..., None] * bi
        bbi = gr[..., None] * bi + gi[..., None] * br
        xr = jnp.einsum('bsgc,gpc->bsgp', ug, bbr)
        xi = jnp.einsum('bsgc,gpc->bsgp', ug, bbi)
        ar = jnp.broadcast_to(abr, xr.shape)
        ai = jnp.broadcast_to(abi, xi.shape)
        _, _, hr, hi = lax.associative_scan(_complex_linear_combine, (ar, ai, xr, xi),
                                            axis=1, reverse=(direction == 1))
        out = (jnp.einsum('bsgp,gcp->bsgc', hr, c_re[direction].astype(f32))
               - jnp.einsum('bsgp,gcp->bsgc', hi, c_im[direction].astype(f32)))
        y = y + out.reshape(bsz, s, GROUP_W)
    g = jax.nn.gelu(y)
    g = g * jax.nn.sigmoid(jnp.einsum('bsc,ce->bse', g, glu_w.astype(f32)) + glu_b.astype(f32))
    return g.astype(u.dtype)


def neighbourhood_attention(q, k, v, rpb):
    f32 = jnp.float32
    bsz, s, h, e = q.shape
    rows = s // GRID_W
    wr = min(NA_ROWS_MAX, rows)
    r = jnp.arange(rows)
    c = jnp.arange(GRID_W)
    rs = jnp.clip(r - wr // 2, 0, rows - wr)
    row_idx = rs[:, None] + jnp.arange(wr)[None, :]
    qg = q.reshape(bsz, rows, GRID_W, h, e)
    kr = k.reshape(bsz, rows, GRID_W, h, e)[:, row_idx]
    vr = v.reshape(bsz, rows, GRID_W, h, e)[:, row_idx].astype(f32)
    sc = jnp.einsum('brchd,briwhd->bhrciw', qg, kr, preferred_element_type=f32) * (e ** -0.5)
    cs = jnp.clip(c - NA_COLS // 2, 0, GRID_W - NA_COLS)
    col_ok = (c[None, :] >= cs[:, None]) & (c[None, :] < cs[:, None] + NA_COLS)
    drow = row_idx - r[:, None] + NA_ROWS_MAX - 1
    dcol = jnp.clip(c[None, :] - c[:, None] + NA_COLS - 1, 0, 2 * NA_COLS - 2)
    bias = rpb.astype(f32)[:, drow[:, None, :, None], dcol[None, :, None, :]]
    bias = jnp.where(col_ok[None, None, :, None, :], bias, NEG)
    sc = (sc + bias[None]).reshape(bsz, h, rows, GRID_W, wr * GRID_W)
    p = jax.nn.softmax(sc, axis=-1)
    o = jnp.einsum('bhrcn,brnhd->brchd', p, vr.reshape(bsz, rows, wr * GRID_W, h, e))
    return o.reshape(bsz, s, h, e).astype(q.dtype)


def mixer_layer(x, norm_g, w_in, w_out, pool_w, pool_scale, a_re, a_im, log_dt,
                b_re, b_im, c_re, c_im, ssm_d, glu_w, glu_b, na_rpb):
    bsz, s, _ = x.shape
    hdn = rmsnorm(x, norm_g)
    z = jnp.einsum('bsd,dp->bsp', hdn, w_in)
    (a_v, a_g, b_q, b_k, b_v, b_g, c_u, c_g, d_q, d_k, d_v, d_g) = jnp.split(z, N_BLOCKS, axis=-1)

    def heads(t):
        return t.reshape(bsz, s, ATTN_HEADS, HEAD_DIM)

    y_a = pool_mixer(a_v, pool_w, pool_scale) * jax.nn.silu(a_g)
    y_b = dilated_attention(rope(heads(b_q)), rope(heads(b_k)), heads(b_v)).reshape(bsz, s, GROUP_W)
    y_b = y_b * jax.nn.silu(b_g)
    y_c = s5_mixer(c_u, a_re, a_im, log_dt, b_re, b_im, c_re, c_im, ssm_d, glu_w, glu_b) * jax.nn.silu(c_g)
    y_d = neighbourhood_attention(heads(d_q), heads(d_k), heads(d_v), na_rpb).reshape(bsz, s, GROUP_W)
    y_d = y_d * jax.nn.silu(d_g)
    y = jnp.concatenate([y_a, y_b, y_c, y_d], axis=-1)
    return x + jnp.einsum('bsm,md->bsd', y, w_out)


def setup_inputs(seed: int = 0) -> dict:
    key = jax.random.key(seed)
    ks = jax.random.split(key, 19)
    nrm = jax.random.normal
    f32 = jnp.float32
    return {
        'x_prompt': nrm(ks[0], (BATCH, SEQ, D_MODEL), f32),
        'x_sample': nrm(ks[1], (DEC_BATCH, DEC_SEQ, D_MODEL), f32),
        'norm_g': 1.0 + 0.02 * nrm(ks[2], (DEPTH, D_MODEL), f32),
        'w_in': nrm(ks[3], (DEPTH, D_MODEL, PROJ_W), f32) * D_MODEL ** -0.5,
        'w_out': nrm(ks[4], (DEPTH, MIX, D_MODEL), f32) * (0.5 * MIX ** -0.5),
        'pool_w': nrm(ks[5], (DEPTH, len(POOL_WINDOWS), POOL_GROUP, POOL_GROUP), f32) * POOL_GROUP ** -0.5,
        'pool_scale': 1.0 + 0.02 * nrm(ks[6], (DEPTH, GROUP_W), f32),
        'ssm_a_re': -0.5 + 0.01 * nrm(ks[7], (DEPTH, 2, SSM_GROUPS, SSM_STATE), f32),
        'ssm_a_im': math.pi * jnp.arange(SSM_STATE, dtype=f32) + 0.01 * nrm(ks[8], (DEPTH, 2, SSM_GROUPS, SSM_STATE), f32),
        'ssm_log_dt': jax.random.uniform(ks[9], (DEPTH, 2, SSM_GROUPS), f32, math.log(1e-3), math.log(1e-1)),
        'ssm_b_re': nrm(ks[10], (DEPTH, SSM_GROUPS, SSM_STATE, SSM_CH), f32) * (2 * SSM_CH) ** -0.5,
        'ssm_b_im': nrm(ks[11], (DEPTH, SSM_GROUPS, SSM_STATE, SSM_CH), f32) * (2 * SSM_CH) ** -0.5,
        'ssm_c_re': nrm(ks[12], (DEPTH, 2, SSM_GROUPS, SSM_CH, SSM_STATE), f32) * SSM_STATE ** -0.5,
        'ssm_c_im': nrm(ks[13], (DEPTH, 2, SSM_GROUPS, SSM_CH, SSM_STATE), f32) * SSM_STATE ** -0.5,
        'ssm_d': nrm(ks[14], (DEPTH, GROUP_W), f32),
        'glu_w': nrm(ks[15], (DEPTH, GROUP_W, GROUP_W), f32) * GROUP_W ** -0.5,
        'glu_b': 0.01 * nrm(ks[16], (DEPTH, GROUP_W), f32),
        'na_rpb': 0.02 * nrm(ks[17], (DEPTH, NA_HEADS, 2 * NA_ROWS_MAX - 1, 2 * NA_COLS - 1), f32),
        'final_g': 1.0 + 0.02 * nrm(ks[18], (D_MODEL,), f32),
    }


def reference(x_prompt, x_sample, norm_g, w_in, w_out, pool_w, pool_scale, ssm_a_re, ssm_a_im,
              ssm_log_dt, ssm_b_re, ssm_b_im, ssm_c_re, ssm_c_im, ssm_d, glu_w, glu_b, na_rpb, final_g):
    def trunk(x):
        for l in range(DEPTH):
            x = mixer_layer(x, norm_g[l], w_in[l], w_out[l], pool_w[l], pool_scale[l],
                            ssm_a_re[l], ssm_a_im[l], ssm_log_dt[l], ssm_b_re[l], ssm_b_im[l],
                            ssm_c_re[l], ssm_c_im[l], ssm_d[l], glu_w[l], glu_b[l], na_rpb[l])
        return rmsnorm(x, final_g)

    y_prompt = trunk(x_prompt)
    y_sample = trunk(x_sample)
    return (y_prompt, y_sample)
```

```python
import math
from contextlib import ExitStack
import numpy as np
import ml_dtypes
import concourse.bass as bass
import concourse.mybir as mybir
from concourse.bass_utils import run_bass_kernel_spmd

F32 = mybir.dt.float32
BF16 = mybir.dt.bfloat16
I32 = mybir.dt.int32
AF = mybir.ActivationFunctionType
ALU = mybir.AluOpType
AX = mybir.AxisListType

S = 2048
D = 1024
NT = 16
EPS = 1e-6
NDMA = 24


class KB:
    def __init__(self):
        self.nc = bass.Bass("TRN2", target_bir_lowering=False)
        nc = self.nc
        self.es = ExitStack()
        self.eng = {"pe": nc.tensor, "act": nc.scalar, "dve": nc.vector, "pool": nc.gpsimd, "sp": nc.sync}
        self.sem = {}
        for e in ["pe", "act", "dve", "pool"]:
            self.sem[e] = self.es.enter_context(nc.semaphore("s_" + e))
        for i in range(NDMA):
            self.sem[("dma", i)] = self.es.enter_context(nc.semaphore("s_dma%d" % i))
        self.cnt = {e: 0 for e in ["pe", "act", "dve", "pool"]}
        self.seen = {e: {} for e in ["pe", "act", "dve", "pool", "sp"]}
        self.res = {}
        self.ndma = 0
        self.same_eng = {"act", "dve", "pool"}
        self.same_depth = 1000000
        self.clock = {}

    def sb(self, name, shape, dt):
        return self.es.enter_context(self.nc.sbuf_tensor(name, shape, dt))

    def ps(self, name, shape, dt):
        return self.es.enter_context(self.nc.psum_tensor(name, shape, dt))

    def dram(self, name, shape, dt, kind="Internal"):
        return self.nc.dram_tensor(name, shape, dt, kind=kind).ap()

    def _wait(self, e, key, val):
        if key == e:
            if e not in self.same_eng or val < self.cnt[e] - self.same_depth + 1:
                return
        if self.seen[e].get(key, 0) >= val:
            return
        self.eng[e].wait_ge(self.sem[key], val)
        self.seen[e][key] = val
        clk = self.clock.get((key, val))
        if clk:
            se = self.seen[e]
            for k2, v2 in clk.items():
                if se.get(k2, 0) < v2:
                    se[k2] = v2

    def _deps(self, e, reads, writes):
        for r in reads:
            st = self.res.get(r)
            if st:
                for k, v in st["w"].items():
                    self._wait(e, k, v)
        for w in writes:
            st = self.res.get(w)
            if st:
                for k, v in st["w"].items():
                    self._wait(e, k, v)
                for k, v in st["r"].items():
                    self._wait(e, k, v)

    def _mark(self, key, val, reads, writes):
        for r in reads:
            st = self.res.setdefault(r, {"w": {}, "r": {}})
            st["r"][key] = max(st["r"].get(key, 0), val)
        for w in writes:
            st = self.res.setdefault(w, {"w": {}, "r": {}})
            st["w"][key] = max(st["w"].get(key, 0), val)

    def op(self, e, fn, reads=(), writes=()):
        self._deps(e, reads, writes)
        inst = fn(self.eng[e])
        self.cnt[e] += 1
        inst.then_inc(self.sem[e], 1)
        snap = dict(self.seen[e])
        snap[e] = self.cnt[e]
        self.clock[(e, self.cnt[e])] = snap
        self._mark(e, self.cnt[e], reads, writes)

    def dma(self, out, in_, reads=(), writes=(), q="sp"):
        n = self.ndma
        self.ndma += 1
        i = n % NDMA
        key = ("dma", i)
        if n >= NDMA:
            self._wait(q, key, 16 * (n // NDMA))
        self._deps(q, reads, writes)
        self.eng[q].dma_start(out=out, in_=in_).then_inc(self.sem[key], 16)
        self.clock[(key, 16 * (n // NDMA + 1))] = dict(self.seen[q])
        self._mark(key, 16 * (n // NDMA + 1), reads, writes)

    def barrier(self, engines=("pe", "act", "dve", "pool")):
        for e in engines:
            for e2 in engines:
                if e2 != e and self.cnt[e2] > 0:
                    self._wait(e, e2, self.cnt[e2])

    def finish(self):
        for k in list(self.sem.keys()):
            if isinstance(k, tuple):
                i = k[1]
                uses = (self.ndma - 1 - i) // NDMA + 1 if self.ndma > i else 0
                if uses > 0:
                    self._wait("sp", k, 16 * uses)
            else:
                if self.cnt[k] > 0:
                    self._wait("sp", k, self.cnt[k])
        self.es.close()


def build(nseq=5, depth=4, mixers=("a", "b", "c", "d"), dbg=False):
    kb = KB()
    nc = kb.nc
    x_d = nc.dram_tensor("x", [nseq, S, D], F32, kind="ExternalInput").ap()
    y_d = nc.dram_tensor("y", [nseq, S, D], F32, kind="ExternalOutput").ap()
    w_in_d = nc.dram_tensor("w_in", [4, D, 3072], F32, kind="ExternalInput").ap()
    w_out_d = nc.dram_tensor("w_out", [4, D, D], F32, kind="ExternalInput").ap()
    ng_d = nc.dram_tensor("norm_g_t", [128, 4, 8], F32, kind="ExternalInput").ap()
    fg_d = nc.dram_tensor("final_g_b", [128, D], F32, kind="ExternalInput").ap()
    pwb_d = nc.dram_tensor("pool_w_blk", [128, 4, 2, 128], F32, kind="ExternalInput").ap()
    psc_d = nc.dram_tensor("pool_scale_t", [128, 4, 2], F32, kind="ExternalInput").ap()
    pcn_d = nc.dram_tensor("pool_const", [128, 2, 17], F32, kind="ExternalInput").ap()
    dbg_d = nc.dram_tensor("dbg", [128, 8, S], BF16, kind="ExternalOutput").ap() if dbg else None
    rope_d = nc.dram_tensor("rope_t", [128, 2, NT, 32], F32, kind="ExternalInput").ap()
    band_d = nc.dram_tensor("band", [128, 256], BF16, kind="ExternalInput").ap()
    rpb_t = nc.dram_tensor("rpbpad", [4, 4, 15, 128], F32, kind="ExternalInput")
    nam_d = nc.dram_tensor("na_mask", [128, 2368], BF16, kind="ExternalInput").ap()
    et_d = kb.dram("et", [4, 4, 128, 2368], BF16)
    lam_d = nc.dram_tensor("ssm_lam", [4, 128, 16, 3], F32, kind="ExternalInput").ap()
    bp_d = nc.dram_tensor("ssm_bp", [4, 128, 16, 16, 2], F32, kind="ExternalInput").ap()
    cp_d = nc.dram_tensor("ssm_cp", [4, 128, 16, 16, 2], F32, kind="ExternalInput").ap()
    dt_d = nc.dram_tensor("ssm_dt", [16, 4, 16], F32, kind="ExternalInput").ap()
    glw_d = nc.dram_tensor("glu_w_t", [4, 128, 2, 256], F32, kind="ExternalInput").ap()
    glb_d = nc.dram_tensor("glu_b_t", [128, 4, 2], F32, kind="ExternalInput").ap()
    sblk_d = kb.dram("sblk", [4, 16, 128, 11, 128], BF16)
    etab_d = kb.dram("etab", [4, 4, 128, 4, 2, 128], F32)
    ktab_t = nc.dram_tensor("ktab", [4, 16, 31, 16, 16], F32, kind="Internal")
    ktab_d = ktab_t.ap()
    gwb_d = kb.dram("gwb", [4, 128, 2, 256], BF16)
    wib_d = kb.dram("wib", [4, 12, 128, 8, 256], BF16)
    wob_d = kb.dram("wob", [4, 4, 128, 8, 256], BF16)

    x_res = kb.sb("x_res", [128, NT, D], F32)
    hT = kb.sb("hT", [128, 8, S], BF16)
    yT = kb.sb("yT", [128, 8, S], BF16)
    ARENA = 56 * 1024
    arena = kb.sb("arena", [128, ARENA // 2], BF16)
    wb = [kb.sb("wb%d" % i, [128, 8, 256], BF16) for i in range(3)]
    ng = kb.sb("ng", [128, 4, 8], F32)
    ident = kb.sb("ident", [128, 128], BF16)
    identf = kb.sb("identf", [128, 128], F32)
    small = kb.sb("small", [128, 64], F32)
    pwb = kb.sb("pwb", [128, 4, 2, 128], BF16)
    psc = kb.sb("psc", [128, 4, 2], F32)
    pcn = kb.sb("pcn", [128, 2, 17], F32)
    ropet = kb.sb("ropet", [128, 2, NT, 32], F32)
    band = kb.sb("band_sb", [128, 256], BF16)
    rho_sb = kb.sb("rho_sb", [128, 4, 16], F32)
    glb = kb.sb("glb", [128, 4, 2], F32)
    pp = [kb.ps("pp%d" % i, [128, 512], F32) for i in range(8)]

    GR = 256

    def ak(off, nbytes):
        return [("ar", j) for j in range(off // GR, (off + nbytes - 1) // GR + 1)]

    dumped = set()

    def dump(name, src, reads):
        if not dbg or name in dumped:
            return
        dumped.add(name)
        shp = list(src.shape)
        dd = nc.dram_tensor("d_" + name, shp, src.dtype, kind="ExternalOutput").ap()
        kb.dma(dd, src, reads=reads, writes=["dump_" + name])

    def ky(c0, c1=None, h=None):
        c1 = c0 + 1 if c1 is None else c1
        hs_ = (0, 1) if h is None else (h,)
        return [("yT", c, hh) for c in range(c0, c1) for hh in hs_]

    def av(off, shape, dt):
        n = int(np.prod(shape))
        esz = 4 if dt in (F32, I32) else 2
        a = arena[:, off // 2: off // 2 + n * esz // 2]
        if dt != BF16:
            a = a.bitcast(dt)
        if len(shape) == 2:
            return a.rearrange("p (a b) -> p a b", a=shape[0])
        if len(shape) == 3:
            return a.rearrange("p (a b c) -> p a b c", a=shape[0], b=shape[1])
        return a

    pstate = {"n": 0}

    def next_bank(lo=0, hi=2):
        i = lo + pstate.setdefault((lo, hi), 0) % (hi - lo)
        pstate[(lo, hi)] += 1
        return i

    hs = [av(16384 + i * 2048, [D], BF16) for i in range(6)]
    khs = [ak(16384 + i * 2048, 2048) for i in range(6)]
    kb.dma(ng[:], ng_d, writes=["ng"])
    pwst = av(8192, [4 * 2 * 128], F32)
    kb.dma(pwst, pwb_d.rearrange("p a b c -> p (a b c)"), writes=ak(8192, 4096))
    kb.op("dve", lambda v: v.tensor_copy(out=pwb[:].rearrange("p a b c -> p (a b c)"), in_=pwst), reads=ak(8192, 4096), writes=["pwb"])
    kb.dma(psc[:], psc_d, writes=["psc"])
    kb.op("dve", lambda v: v.tensor_scalar(out=psc[:], in0=psc[:], scalar1=0.5, scalar2=None, op0=ALU.mult), reads=["psc"], writes=["psc"])
    kb.dma(pcn[:], pcn_d, writes=["pcn"])
    kb.dma(ropet[:], rope_d, writes=["ropet"])
    kb.dma(band[:], band_d, writes=["band"])
    io = av(0, [128], I32)
    kb.op("pool", lambda g: g.iota(io, [[1, 128]], base=0, channel_multiplier=-1), writes=ak(0, 512))
    kb.op("dve", lambda v: v.tensor_single_scalar(out=identf[:], in_=io, scalar=0, op=ALU.is_equal),
          reads=ak(0, 512), writes=["identf"])
    kb.op("dve", lambda v: v.tensor_copy(out=ident[:], in_=identf[:]), reads=["identf"], writes=["ident"])

    for l in range(depth):
        for c in range(8):
            k = (l * 8 + c) % 2
            st = av(k * 12288, [3072], F32)
            sb_ = av(24576 + k * 6144, [3072], BF16)
            kst, ksb = ak(k * 12288, 12288), ak(24576 + k * 6144, 6144)
            kb.dma(st, w_in_d[l, c * 128:(c + 1) * 128, :], writes=kst)
            kb.op("pool" if k else "dve",
                  lambda v, st=st, sb_=sb_, l=l, c=c: v.tensor_scalar(out=sb_, in0=st, scalar1=ng[:, l, c:c + 1],
                                                                        scalar2=None, op0=ALU.mult),
                  reads=kst + ["ng"], writes=ksb)
            kb.dma(wib_d[l, :, :, c, :].rearrange("b p n -> p b n"), sb_.rearrange("p (b n) -> p b n", b=12),
                   reads=ksb, writes=["wib"])
        for c in range(8):
            k = (l * 8 + c) % 2
            st = av(36864 + k * 4096, [1024], F32)
            sb_ = av(36864 + 8192 + k * 2048, [1024], BF16)
            kst, ksb = ak(36864 + k * 4096, 4096), ak(36864 + 8192 + k * 2048, 2048)
            kb.dma(st, w_out_d[l, c * 128:(c + 1) * 128, :], writes=kst)
            kb.op("act", lambda a, st=st, sb_=sb_: a.copy(out=sb_, in_=st), reads=kst, writes=ksb)
            kb.dma(wob_d[l, :, :, c, :].rearrange("h p n -> p h n"), sb_.rearrange("p (h n) -> p h n", h=4),
                   reads=ksb, writes=["wob"])

    if "d" in mixers:
        nmask = av(0, [2368], BF16)
        kb.dma(nmask, nam_d, writes=ak(0, 4736))
        negm = av(33152, [2368], F32)
        knegm = ak(33152, 9472)
        kb.op("dve", lambda v: v.tensor_scalar(out=negm, in0=nmask, scalar1=-1.0, scalar2=240000.0, op0=ALU.add, op1=ALU.mult),
              reads=ak(0, 4736), writes=knegm)
        for l in range(depth):
            for h in range(4):
                k = (l * 4 + h) % 2
                o_st = 4736 + k * 14208
                stg = av(o_st, [2368], F32)
                eo = av(o_st + 9472, [2368], BF16)
                kst, keo = ak(o_st, 9472), ak(o_st + 9472, 4736)
                base = (l * 4 + h) * 15 * 128
                for krl in range(2):
                    ps_ = slice(krl * 64, krl * 64 + 64)
                    src = bass.AP(rpb_t, base + (4 - krl) * 128, [[1, 64], [128, 9], [1, 64]])
                    kb.dma(stg[ps_, 0:576].rearrange("p (r c) -> p r c", r=9), src, writes=kst)
                    for sr in range(7):
                        r = sr if sr < 4 else 25 + sr
                        if sr < 4:
                            i0 = 1 - krl + r
                            dst = stg[ps_, 576:1600].rearrange("p (u s c) -> p u s c", u=4, s=4)[:, :, sr, :]
                        else:
                            i0 = r - 23 - krl
                            dst = stg[ps_, 1600:2368].rearrange("p (u s c) -> p u s c", u=4, s=3)[:, :, sr - 4, :]
                        src = bass.AP(rpb_t, base + i0 * 128, [[1, 64], [256, 4], [1, 64]])
                        kb.dma(dst, src, writes=kst)
                st3 = stg.rearrange("p (b c) -> p b c", c=64)
                kb.op("dve", lambda v, st3=st3: v.scalar_tensor_tensor(
                    out=st3, in0=st3, scalar=8.0, in1=nmask.rearrange("p (b c) -> p b c", c=64),
                    op0=ALU.mult, op1=ALU.mult), reads=kst + ak(0, 4736), writes=kst)
                kb.op("dve", lambda v, st3=st3, eo=eo: v.tensor_tensor(
                    out=eo.rearrange("p (b c) -> p b c", c=64), in0=st3[:, :, ::-1], in1=negm.rearrange("p (b c) -> p b c", c=64)[:, :, ::-1],
                    op=ALU.add), reads=kst + knegm, writes=keo)
                kb.dma(et_d[l, h], eo, reads=keo, writes=["et"])

    TWO_PI = 2.0 * math.pi

    def s5_precompute(l):
        st = {"o": 0}

        def A(shape, dt=F32):
            n = int(np.prod(shape))
            nb = n * (4 if dt in (F32, I32) else 2)
            off = st["o"]
            st["o"] = off + (nb + 63) // 64 * 64
            v = arena[:, off // 2: off // 2 + nb // 2]
            if dt != BF16:
                v = v.bitcast(dt)
            if len(shape) == 2:
                v = v.rearrange("p (a b) -> p a b", a=shape[0])
            elif len(shape) == 3:
                v = v.rearrange("p (a b c) -> p a b c", a=shape[0], b=shape[1])
            return v, ak(off, nb)

        def tt_(e, out, a, b, op, rd, wr):
            kb.op(e, lambda v: v.tensor_tensor(out=out, in0=a, in1=b, op=op), reads=rd, writes=wr)

        def ts_(e, out, a, s1, s2, op0, op1, rd, wr):
            kb.op(e, lambda v: v.tensor_scalar(out=out, in0=a, scalar1=s1, scalar2=s2, op0=op0, op1=op1) if s2 is not None
                  else v.tensor_scalar(out=out, in0=a, scalar1=s1, scalar2=None, op0=op0), reads=rd, writes=wr)

        def frac_(T, kT, TI, kTI, TF, kTF):
            MAGIC = 12582912.0
            ts_("dve", TF, T, MAGIC, None, ALU.add, None, kT, kTF)
            ts_("dve", TF, TF, -MAGIC, None, ALU.add, None, kTF, kTF)
            tt_("dve", T, T, TF, ALU.subtract, kT + kTF, kT)
            kb.op("dve", lambda v: v.tensor_single_scalar(out=TF, in_=T, scalar=0.5, op=ALU.is_gt), reads=kT, writes=kTF)
            tt_("dve", T, T, TF, ALU.subtract, kT + kTF, kT)
            kb.op("dve", lambda v: v.tensor_single_scalar(out=TF, in_=T, scalar=-0.5, op=ALU.is_lt), reads=kT, writes=kTF)
            tt_("dve", T, T, TF, ALU.add, kT + kTF, kT)

        def sincos_(T, kT, SN, kSN, CS, kCS, TI, kTI, TF, kTF):
            frac_(T, kT, TI, kTI, TF, kTF)
            kb.op("act", lambda a: a.activation(out=SN, in_=T, func=AF.Sin, scale=6.28318), reads=kT, writes=kSN)
            ts_("dve", T, T, 0.25, None, ALU.add, None, kT, kT)
            kb.op("dve", lambda v: v.tensor_single_scalar(out=TF, in_=T, scalar=0.5, op=ALU.is_gt), reads=kT, writes=kTF)
            tt_("dve", T, T, TF, ALU.subtract, kT + kTF, kT)
            kb.op("act", lambda a: a.activation(out=CS, in_=T, func=AF.Sin, scale=6.28318), reads=kT, writes=kCS)

        lam, klam = A([16, 3])
        Bp, kBp = A([16, 16, 2])
        Cp, kCp = A([16, 16, 2])
        Dt, kDt = A([16])
        kb.dma(lam, lam_d[l], writes=klam)
        kb.dma(Bp, bp_d[l], writes=kBp)
        kb.dma(Cp, cp_d[l], writes=kCp)
        kb.dma(Dt[0:16, :], dt_d[:, l, :], writes=kDt)
        BBr, kBBr = A([16, 16])
        BBi, kBBi = A([16, 16])
        nBBi, knBBi = A([16, 16])
        WP, kWP = A([2, 16, 16])
        WC, kWC = A([2, 16, 16])
        WK, kWK = A([2, 16, 31])
        mark = st["o"]
        dtt, kdt = A([16])
        xr, kxr = A([16])
        tht, ktht = A([16])
        NNi, kNNi = A([128], I32)
        NN, kNN = A([128])
        TI, kTI = A([512], I32)
        TF, kTF = A([512])
        ARG, kARG = A([16, 17])
        MAG, kMAG = A([16, 17])
        SN, kSN = A([16, 17])
        CS, kCS = A([16, 17])
        WR, kWR = A([16, 17])
        WI, kWI = A([16, 17])
        kb.op("act", lambda a: a.activation(out=dtt, in_=lam[:, :, 2], func=AF.Exp), reads=klam, writes=kdt)
        tt_("dve", xr, lam[:, :, 0], dtt, ALU.mult, klam + kdt, kxr)
        tt_("dve", tht, lam[:, :, 1], dtt, ALU.mult, klam + kdt, ktht)
        ts_("dve", tht, tht, 1.0 / TWO_PI, None, ALU.mult, None, ktht, ktht)
        frac_(tht, ktht, TI[:, 0:16], kTI, TF[:, 0:16], kTF)
        kb.op("pool", lambda g: g.iota(NNi, [[1, 128]], base=0, channel_multiplier=0), writes=kNNi)
        kb.op("dve", lambda v: v.tensor_copy(out=NN, in_=NNi), reads=kNNi, writes=kNN)
        nb17 = NN[:, 0:17].unsqueeze(1).broadcast_to([128, 16, 17])
        tt_("dve", ARG, tht.unsqueeze(2).broadcast_to([128, 16, 17]), nb17, ALU.mult, ktht + kNN, kARG)
        tt_("dve", MAG, xr.unsqueeze(2).broadcast_to([128, 16, 17]), nb17, ALU.mult, kxr + kNN, kMAG)
        kb.op("act", lambda a: a.activation(out=MAG, in_=MAG, func=AF.Exp), reads=kMAG, writes=kMAG)
        f2 = lambda t: t.rearrange("p a b -> p (a b)")
        sincos_(f2(ARG), kARG, f2(SN), kSN, f2(CS), kCS, TI[:, 0:272], kTI, TF[:, 0:272], kTF)
        tt_("dve", WR, MAG, CS, ALU.mult, kMAG + kCS, kWR)
        tt_("dve", WI, MAG, SN, ALU.mult, kMAG + kSN, kWI)
        if l == 0:
            dump("WR", WR, kWR)
            dump("WI", WI, kWI)
            dump("dtt", dtt, kdt)
            dump("xr", xr, kxr)
            dump("tht", tht, ktht)
            dump("NN", NN, kNN)
            dump("MAG", MAG, kMAG)
            dump("SN", SN, kSN)
            dump("CS", CS, kCS)
            dump("lam", lam, klam)
        kb.op("act", lambda a: a.copy(out=rho_sb[:, l, :], in_=MAG[:, :, 16]), reads=kMAG, writes=["rho"])
        den, kden = A([16])
        t1, kt1 = A([16])
        t2, kt2 = A([16])
        gr, kgr = A([16])
        gi, kgi = A([16])
        lr_, li_ = lam[:, :, 0], lam[:, :, 1]
        tt_("dve", den, lr_, lr_, ALU.mult, klam, kden)
        tt_("dve", t1, li_, li_, ALU.mult, klam, kt1)
        tt_("dve", den, den, t1, ALU.add, kden + kt1, kden)
        kb.op("dve", lambda v: v.reciprocal(out=den, in_=den), reads=kden, writes=kden)
        ts_("dve", t1, WR[:, :, 1], -1.0, None, ALU.add, None, kWR, kt1)
        tt_("dve", gr, t1, lr_, ALU.mult, kt1 + klam, kgr)
        tt_("dve", t2, WI[:, :, 1], li_, ALU.mult, kWI + klam, kt2)
        tt_("dve", gr, gr, t2, ALU.add, kgr + kt2, kgr)
        tt_("dve", gr, gr, den, ALU.mult, kgr + kden, kgr)
        tt_("dve", gi, WI[:, :, 1], lr_, ALU.mult, kWI + klam, kgi)
        tt_("dve", t2, t1, li_, ALU.mult, kt1 + klam, kt2)
        tt_("dve", gi, gi, t2, ALU.subtract, kgi + kt2, kgi)
        tt_("dve", gi, gi, den, ALU.mult, kgi + kden, kgi)
        u1, ku1 = A([16, 16])
        grb = gr.unsqueeze(2).broadcast_to([128, 16, 16])
        gib = gi.unsqueeze(2).broadcast_to([128, 16, 16])
        Br_, Bi_ = Bp[:, :, :, 0], Bp[:, :, :, 1]
        tt_("dve", BBr, grb, Br_, ALU.mult, kgr + kBp, kBBr)
        tt_("dve", u1, gib, Bi_, ALU.mult, kgi + kBp, ku1)
        tt_("dve", BBr, BBr, u1, ALU.subtract, kBBr + ku1, kBBr)
        tt_("dve", BBi, grb, Bi_, ALU.mult, kgr + kBp, kBBi)
        tt_("dve", u1, gib, Br_, ALU.mult, kgi + kBp, ku1)
        tt_("dve", BBi, BBi, u1, ALU.add, kBBi + ku1, kBBi)
        ts_("dve", nBBi, BBi, -1.0, None, ALU.mult, None, kBBi, knBBi)
        kb.op("pool", lambda g: g.memset(WK, 0.0), writes=kWK)
        F_, B_ = slice(0, 64), slice(64, 128)
        for ri, W_, kW_ in ((0, WR, kWR), (1, WI, kWI)):
            cp = lambda dst, src, wk: kb.op("act", lambda a: a.copy(out=dst, in_=src), reads=kW_, writes=wk)
            cp(WP[F_, ri, :, 0:8], W_[F_, :, 8:16], kWP)
            cp(WP[F_, ri, :, 8:16], W_[F_, :, 0:8], kWP)
            cp(WP[B_, ri, :, 0:8], W_[B_, :, 7::-1], kWP)
            cp(WP[B_, ri, :, 8:16], W_[B_, :, 15:7:-1], kWP)
            cp(WC[F_, ri, :, :], W_[F_, :, 1:17], kWC)
            cp(WC[B_, ri, :, :], W_[B_, :, 16:0:-1], kWC)
            cp(WK[F_, ri, :, 15:31], W_[F_, :, 0:16], kWK)
            cp(WK[B_, ri, :, 0:16], W_[B_, :, 15::-1], kWK)
        pht, kpht = A([16])
        ts_("dve", pht, tht, 16.0, None, ALU.mult, None, ktht, kpht)
        frac_(pht, kpht, TI[:, 0:16], kTI, TF[:, 0:16], kTF)
        EA, kEA = A([4, 128])
        ES, kES = A([4, 2, 128])
        for q in range(4):
            tt_("dve", EA, pht[:, 4 * q:4 * q + 4].unsqueeze(2).broadcast_to([128, 4, 128]),
                NN.unsqueeze(1).broadcast_to([128, 4, 128]), ALU.mult, kpht + kNN, kEA)
            sincos_(f2(EA), kEA, ES[:, :, 1, :], kES, ES[:, :, 0, :], kES, TI[:, 0:512], kTI, TF[:, 0:512], kTF)
            kb.dma(etab_d[l, q], ES, reads=kES, writes=["etab"])
            if l == 0 and q == 0:
                dump("ES0", ES, kES)
        gst, kgst = A([2, 256])
        gbf, kgbf = A([2, 256], BF16)
        kb.dma(gst, glw_d[l], writes=kgst)
        kb.op("act", lambda a: a.copy(out=gbf, in_=gst), reads=kgst, writes=kgbf)
        kb.dma(gwb_d[l], gbf, reads=kgbf, writes=["gwb"])
        st["o"] = mark
        bufs = []
        for par in range(2):
            d_ = {}
            d_["PPr"], d_["kPPr"] = A([2, 128])
            d_["PPi"], d_["kPPi"] = A([2, 128])
            d_["q1"], d_["kq1"] = A([2, 128])
            d_["q2"], d_["kq2"] = A([2, 128])
            d_["Rr"], d_["kRr"] = A([31, 16])
            d_["Ri"], d_["kRi"] = A([31, 16])
            d_["r1"], d_["kr1"] = A([31, 16])
            d_["r2"], d_["kr2"] = A([31, 16])
            d_["Ks"], d_["kKs"] = A([496])
            d_["Ms"], d_["kMs"] = A([3, 128])
            d_["blk"], d_["kblk"] = A([11, 128], BF16)
            bufs.append(d_)
        assert st["o"] <= ARENA, st["o"]
        for g in range(16):
            d_ = bufs[g % 2]
            PPr, PPi, q1, q2 = d_["PPr"], d_["PPi"], d_["q1"], d_["q2"]
            kPPr, kPPi, kq1, kq2 = d_["kPPr"], d_["kPPi"], d_["kq1"], d_["kq2"]
            blk, kblk = d_["blk"], d_["kblk"]
            v4 = lambda t: t.rearrange("p s (j c) -> p s j c", j=8)
            wpr = WP[:, 0, g, :].rearrange("p (s j) -> p s j", s=2).unsqueeze(3).broadcast_to([128, 2, 8, 16])
            wpi = WP[:, 1, g, :].rearrange("p (s j) -> p s j", s=2).unsqueeze(3).broadcast_to([128, 2, 8, 16])
            bbr = BBr[:, g, :].unsqueeze(1).unsqueeze(1).broadcast_to([128, 2, 8, 16])
            bbi = BBi[:, g, :].unsqueeze(1).unsqueeze(1).broadcast_to([128, 2, 8, 16])
            e1, e2 = ("dve", "pool") if g % 2 == 0 else ("pool", "dve")
            tt_(e1, v4(q1), wpr, bbr, ALU.mult, kWP + kBBr, kq1)
            tt_(e2, v4(q2), wpi, bbi, ALU.mult, kWP + kBBi, kq2)
            tt_(e1, PPr, q1, q2, ALU.subtract, kq1 + kq2, kPPr)
            tt_(e2, v4(q1), wpr, bbi, ALU.mult, kWP + kBBi, kq1)
            tt_(e1, v4(q2), wpi, bbr, ALU.mult, kWP + kBBr, kq2)
            tt_(e2, PPi, q1, q2, ALU.add, kq1 + kq2, kPPi)
            bt = next_bank(6, 8)
            for j, (src, ksrc) in enumerate(((PPr[:, 0, :], kPPr), (PPr[:, 1, :], kPPr), (PPi[:, 0, :], kPPi), (PPi[:, 1, :], kPPi))):
                kb.op("pe", lambda pe, j=j, src=src: pe.transpose(pp[bt][:, j * 128:(j + 1) * 128], src, identf[:]),
                      reads=ksrc + ["identf"], writes=[("pp", bt)])
            kb.op("act", lambda a: a.copy(out=blk[:, 3:7, :], in_=pp[bt][:, :].rearrange("p (a b) -> p a b", a=4)),
                  reads=[("pp", bt)], writes=kblk)
            c4 = lambda t: t.rearrange("p s (i c) -> p (s i) c", i=8)
            wcr = WC[:, 0, g, :].unsqueeze(2).broadcast_to([128, 16, 16])
            wci = WC[:, 1, g, :].unsqueeze(2).broadcast_to([128, 16, 16])
            cr = Cp[:, g, :, 0].unsqueeze(1).broadcast_to([128, 16, 16])
            ci = Cp[:, g, :, 1].unsqueeze(1).broadcast_to([128, 16, 16])
            tt_(e1, c4(q1), cr, wcr, ALU.mult, kCp + kWC, kq1)
            tt_(e2, c4(q2), ci, wci, ALU.mult, kCp + kWC, kq2)
            tt_(e1, blk[:, 7:10:2, :], q1, q2, ALU.subtract, kq1 + kq2, kblk)
            tt_(e2, c4(q1), cr, wci, ALU.mult, kCp + kWC, kq1)
            tt_(e1, c4(q2), ci, wcr, ALU.mult, kCp + kWC, kq2)
            kb.op("dve", lambda v: v.scalar_tensor_tensor(out=blk[:, 8:11:2, :], in0=q1, scalar=-1.0, in1=q2,
                                                           op0=ALU.mult, op1=ALU.subtract),
                  reads=kq1 + kq2, writes=kblk)
            Rr, Ri, r1, r2 = d_["Rr"], d_["Ri"], d_["r1"], d_["r2"]
            kRr, kRi, kr1, kr2 = d_["kRr"], d_["kRi"], d_["kr1"], d_["kr2"]
            wkr = WK[:, 0, g, :].unsqueeze(2).broadcast_to([128, 31, 16])
            wki = WK[:, 1, g, :].unsqueeze(2).broadcast_to([128, 31, 16])
            cr3 = Cp[:, g, :, 0].unsqueeze(1).broadcast_to([128, 31, 16])
            ci3 = Cp[:, g, :, 1].unsqueeze(1).broadcast_to([128, 31, 16])
            tt_(e1, r1, cr3, wkr, ALU.mult, kCp + kWK, kr1)
            tt_(e2, r2, ci3, wki, ALU.mult, kCp + kWK, kr2)
            tt_(e1, Rr, r1, r2, ALU.subtract, kr1 + kr2, kRr)
            tt_(e2, r1, cr3, wki, ALU.mult, kCp + kWK, kr1)
            tt_(e1, r2, ci3, wkr, ALU.mult, kCp + kWK, kr2)
            tt_(e2, Ri, r1, r2, ALU.add, kr1 + kr2, kRi)
            bk = next_bank(4, 6)
            kb.op("pe", lambda pe: pe.matmul(pp[bk][0:16, 0:496], BBr[:, g, :], Rr.rearrange("p a b -> p (a b)"),
                                             start=True, stop=False), reads=kBBr + kRr, writes=[("pp", bk)])
            kb.op("pe", lambda pe: pe.matmul(pp[bk][0:16, 0:496], nBBi[:, g, :], Ri.rearrange("p a b -> p (a b)"),
                                             start=False, stop=True), reads=knBBi + kRi, writes=[("pp", bk)])
            Ks, kKs = d_["Ks"], d_["kKs"]
            kb.op("act", lambda a: a.copy(out=Ks[0:16, :], in_=pp[bk][0:16, 0:496]), reads=[("pp", bk)], writes=kKs)
            kb.op("dve", lambda v: v.scalar_tensor_tensor(out=Ks[0:16, 240:256], in0=identf[0:16, 0:16],
                                                           scalar=Dt[0:16, g:g + 1], in1=Ks[0:16, 240:256],
                                                           op0=ALU.mult, op1=ALU.add),
                  reads=kKs + kDt + ["identf"], writes=kKs)
            kb.dma(ktab_d[l, g].rearrange("i c o -> c i o"), Ks[0:16, :].rearrange("p (i o) -> p i o", i=31),
                   reads=kKs, writes=[("ktab", l, g)])
            Ms, kMs = d_["Ms"], d_["kMs"]
            kbase = (l * 16 + g) * 31 * 256
            for bi_, boff in enumerate((8, 16, 0)):
                src = bass.AP(ktab_t, kbase + boff * 256, [[16, 128], [256, 8], [1, 16]])
                kb.dma(Ms[:, bi_, :].rearrange("p (i o) -> p i o", i=8), src, reads=[("ktab", l, g)], writes=kMs)
            kb.op("act", lambda a: a.copy(out=blk[:, 0:3, :], in_=Ms), reads=kMs, writes=kblk)
            kb.dma(sblk_d[l, g], blk, reads=kblk, writes=["sblk"])
            if l == 0 and g == 0:
                dump("blk0", blk, kblk)
                dump("Ks0", Ks[0:16, :], kKs)
                dump("BBr", BBr, kBBr)
                dump("BBi", BBi, kBBi)
                dump("WP", WP, kWP)
                dump("WC", WC, kWC)
                dump("WK", WK, kWK)
                dump("PPr", PPr, kPPr)

    if "c" in mixers:
        kb.dma(glb[:], glb_d, writes=["glb"])
        kb.op("dve", lambda v: v.tensor_scalar(out=glb[:], in0=glb[:], scalar1=0.5, scalar2=None, op0=ALU.mult),
              reads=["glb"], writes=["glb"])
        for l in range(depth):
            s5_precompute(l)

    wstate = {"n": 0}

    def load_wblock(l, b):
        k = wstate["n"] % 3
        wstate["n"] += 1
        kb.dma(wb[k][:], wib_d[l, b], reads=["wib"], writes=[("wb", k)])
        return k

    def load_woblock(l, h):
        k = wstate["n"] % 3
        wstate["n"] += 1
        kb.dma(wb[k][:], wob_d[l, h], reads=["wob"], writes=[("wb", k)])
        return k


    def proj_fm(k, cb, evac, tqs=range(4), M=128, moff=0):
        for tq in tqs:
            bi = next_bank(0, 2)
            for c in range(8):
                kb.op("pe", lambda pe, c=c, bi=bi, tq=tq: pe.matmul(
                    pp[bi][0:M, :], wb[k][:, c, cb * 128 + moff: cb * 128 + moff + M], hT[:, c, tq * 512:(tq + 1) * 512],
                    start=(c == 0), stop=(c == 7)),
                    reads=[("wb", k), "hT"], writes=[("pp", bi)])
            evac(tq, bi)

    def proj_tm(k, tok_ap_fn, ntiles, evac, ncols=256):
        for i in range(ntiles):
            bi = next_bank(0, 2)
            for c in range(8):
                kb.op("pe", lambda pe, c=c, bi=bi, i=i: pe.matmul(
                    pp[bi][:, 0:ncols], tok_ap_fn(c, i), wb[k][:, c, 0:ncols], start=(c == 0), stop=(c == 7)),
                    reads=[("wb", k), "hT"], writes=[("pp", bi)])
            evac(i, bi)

    def rms_all():
        for i in range(NT):
            junk = hs[4 + i % 2]
            kb.op("act", lambda a, i=i, junk=junk: a.activation(out=junk, in_=x_res[:, i, :], func=AF.Square,
                                                                accum_out=small[:, 16 + i:17 + i]),
                  reads=[("x", i)], writes=khs[4 + i % 2] + ["ss"])
        kb.op("dve", lambda v: v.tensor_scalar(out=small[:, 32:48], in0=small[:, 16:32],
                                               scalar1=1.0 / D, scalar2=EPS, op0=ALU.mult, op1=ALU.add),
              reads=["ss"], writes=["ms"])
        kb.op("pool", lambda g: g.tensor_tensor(out=small[:, 0:16], in0=small[:, 32:48],
                                                in1=small[:, 48:49].broadcast_to([128, 16]), op=ALU.pow),
              reads=["ms", "mhalf"], writes=["rstd"])

    kb.op("dve", lambda v: v.memset(small[:, 48:49], -0.5), writes=["mhalf"])
    nlh = small[:, 49:50]
    kb.op("dve", lambda v: v.memset(nlh, -math.log(2.0)), writes=["nlh"])

    W = S + 32

    def mixer_a(l):
        kU, kA, kB, kg2, kDm = ak(0, 8320), ak(8320, 8320), ak(16640, 8320), ak(24960, 4096), ak(29056, 4096)
        ktt = [ak(33152 + j * 2048, 2048) for j in range(2)]
        U = av(0, [W], F32)
        A = av(8320, [W], F32)
        B = av(16640, [W], F32)
        g2 = av(24960, [S], BF16)
        Dm = av(29056, [S], BF16)
        tt = [av(33152 + j * 2048, [512], F32) for j in range(2)]
        for cb in range(2):
            kv = load_wblock(l, 0)
            kg = load_wblock(l, 1)
            kb.op("pool", lambda g: g.memset(U[:, 0:16], 0.0), writes=kU)
            kb.op("pool", lambda g: g.memset(U[:, 16 + S:W], 0.0), writes=kU)

            def ev_u(tq, bi):
                kb.op("act", lambda a: a.copy(out=U[:, 16 + tq * 512:16 + (tq + 1) * 512], in_=pp[bi][:, :]),
                      reads=[("pp", bi)], writes=kU)
            proj_fm(kv, cb, ev_u)
            kb.op("pool", lambda g: g.tensor_tensor(out=A[:, 1:W], in0=U[:, 0:W - 1], in1=U[:, 1:W], op=ALU.add),
                  reads=kU, writes=kA)
            kb.op("pool", lambda g: g.tensor_tensor(out=B[:, 2:W - 1], in0=A[:, 1:W - 2], in1=A[:, 3:W], op=ALU.add),
                  reads=kA, writes=kB)
            if cb == 1:
                kb.op("pool", lambda g: g.tensor_tensor(out=A[:, 4:W - 3], in0=B[:, 2:W - 5], in1=B[:, 6:W - 1], op=ALU.add),
                      reads=kB, writes=kA)
                kb.op("pool", lambda g: g.tensor_tensor(out=B[64:128, 8:W - 7], in0=A[64:128, 4:W - 11],
                                                        in1=A[64:128, 12:W - 3], op=ALU.add),
                      reads=kA, writes=kB)
            for (buf, nm, p0) in ((A, kA, 0), (B, kB, 64)):
                sl = slice(p0, p0 + 64)
                kb.op("dve", lambda v, buf=buf, sl=sl: v.tensor_tensor(
                    out=buf[sl, 16:24], in0=buf[sl, 16:24], in1=pcn[sl, cb, 0:8], op=ALU.mult),
                    reads=nm + ["pcn"], writes=nm)
                kb.op("dve", lambda v, buf=buf, sl=sl: v.tensor_tensor(
                    out=buf[sl, 8 + S:16 + S], in0=buf[sl, 8 + S:16 + S], in1=pcn[sl, cb, 8:16], op=ALU.mult),
                    reads=nm + ["pcn"], writes=nm)
                kb.op("dve", lambda v, buf=buf, sl=sl: v.scalar_tensor_tensor(
                    out=Dm[sl, :], in0=buf[sl, 16:16 + S], scalar=pcn[sl, cb, 16:17], in1=U[sl, 16:16 + S],
                    op0=ALU.mult, op1=ALU.subtract),
                    reads=nm + kU + ["pcn"], writes=kDm)

            def ev_g(tq, bi):
                t = tt[tq % 2]
                kb.op("act", lambda a: a.activation(out=t, in_=pp[bi][:, :], func=AF.Tanh, scale=0.5),
                      reads=[("pp", bi)], writes=ktt[tq % 2])
                kb.op("dve", lambda v: v.scalar_tensor_tensor(
                    out=g2[:, tq * 512:(tq + 1) * 512], in0=t, scalar=1.0, in1=pp[bi][:, :], op0=ALU.add, op1=ALU.mult),
                    reads=ktt[tq % 2] + [("pp", bi)], writes=kg2)
            proj_fm(kg, cb, ev_g)
            for tq in range(4):
                bi = next_bank(0, 2)
                kb.op("pe", lambda pe: pe.matmul(pp[bi][:, :], pwb[:, l, cb, :], Dm[:, tq * 512:(tq + 1) * 512],
                                                 start=True, stop=True),
                      reads=["pwb"] + kDm, writes=[("pp", bi)])
                kb.op("dve", lambda v: v.scalar_tensor_tensor(
                    out=yT[:, cb, tq * 512:(tq + 1) * 512], in0=pp[bi][:, :], scalar=psc[:, l, cb:cb + 1],
                    in1=g2[:, tq * 512:(tq + 1) * 512], op0=ALU.mult, op1=ALU.mult),
                    reads=[("pp", bi), "psc"] + kg2, writes=ky(cb))

    def run_pipeline(tasks, la):
        n = len(tasks)
        for i in range(n + la):
            if i < n:
                t = tasks[i]
                t["slot"] = i
                if t.get("pre"):
                    t["pre"]()
                t["s1"]()
            if i >= la:
                t = tasks[i - la]
                t["s2"]()
                if t.get("post"):
                    t["post"]()

    def run_pipeline_b(tasks, bs):
        n = len(tasks)
        nb = (n + bs - 1) // bs
        for b in range(nb + 1):
            if b < nb:
                for i in range(b * bs, min(n, (b + 1) * bs)):
                    t = tasks[i]
                    t["slot"] = i
                    if t.get("pre"):
                        t["pre"]()
                for i in range(b * bs, min(n, (b + 1) * bs)):
                    tasks[i]["s1a"]()
                for i in range(b * bs, min(n, (b + 1) * bs)):
                    tasks[i]["s1b"]()
            if b >= 1:
                for i in range((b - 1) * bs, min(n, b * bs)):
                    t = tasks[i]
                    t["s2"]()
                    if t.get("post"):
                        t["post"]()

    def gate_fm(k, kg2, g2, ktt, tt):
        for cb in range(2):
            def ev_g(tq, bi):
                t = tt[tq % 2]
                kb.op("act", lambda a: a.activation(out=t, in_=pp[bi][:, :], func=AF.Tanh, scale=0.5),
                      reads=[("pp", bi)], writes=ktt[tq % 2])
                kb.op("dve", lambda v: v.scalar_tensor_tensor(
                    out=g2[:, cb, tq * 512:(tq + 1) * 512], in0=t, scalar=1.0, in1=pp[bi][:, :], op0=ALU.add, op1=ALU.mult),
                    reads=ktt[tq % 2] + [("pp", bi)], writes=kg2)
            proj_fm(k, cb, ev_g)

    def psl(p0, n, d, n_sub):
        st = (p0 % n_sub) * d + p0 // n_sub
        return slice(st, st + (n - 1) * d + 1, d)

    def attn_finalize(h, acc, kacc, g2, kg2, rc, krc, tmp, ktmp, chunk0):
        nr = slice((h % 2) * 64, (h % 2) * 64 + 64)
        dr = slice(((h + 1) % 2) * 64, ((h + 1) % 2) * 64 + 64)
        for tq in range(8):
            ts_ = slice(tq * 256, (tq + 1) * 256)
            kb.op("dve", lambda v: v.reciprocal(out=rc[tq % 2][nr, :], in_=acc[dr, ts_]),
                  reads=kacc, writes=krc[tq % 2])
            kb.op("dve", lambda v: v.scalar_tensor_tensor(out=tmp[tq % 2][nr, :], in0=acc[nr, ts_], scalar=0.5,
                                                           in1=rc[tq % 2][nr, :], op0=ALU.mult, op1=ALU.mult),
                  reads=kacc + krc[tq % 2], writes=ktmp[tq % 2])
            kb.op("pool", lambda g: g.tensor_tensor(out=yT[nr, chunk0 + h // 2, ts_], in0=tmp[tq % 2][nr, :],
                                                    in1=g2[nr, h // 2, ts_], op=ALU.mult),
                  reads=ktmp[tq % 2] + kg2, writes=ky(chunk0 + h // 2, h=h % 2))

    def gate_apply(l, blk, chunk0, tt, ktt, gq, kgq):
        k = load_wblock(l, blk)
        for cb in range(2):
            def ev_g(tq, bi):
                t = tt[tq % 2]
                g_ = gq[tq % 2]
                ts_ = slice(tq * 512, (tq + 1) * 512)
                kb.op("act", lambda a: a.activation(out=t, in_=pp[bi][:, :], func=AF.Tanh, scale=0.5),
                      reads=[("pp", bi)], writes=ktt[tq % 2])
                kb.op("dve", lambda v: v.scalar_tensor_tensor(out=g_, in0=t, scalar=1.0, in1=pp[bi][:, :],
                                                               op0=ALU.add, op1=ALU.mult),
                      reads=ktt[tq % 2] + [("pp", bi)], writes=kgq[tq % 2])
                kb.op("pool", lambda g: g.tensor_tensor(out=yT[:, chunk0 + cb, ts_], in0=g_, in1=yT[:, chunk0 + cb, ts_],
                                                        op=ALU.mult),
                      reads=kgq[tq % 2] + ky(chunk0 + cb), writes=ky(chunk0 + cb))
            proj_fm(k, cb, ev_g)

    def mixer_b(l):
        o_q, o_kz, o_v, o_acc, o_pt, o_rc = 0, 8192, 24576, 32768, 49152, 51200
        qT = av(o_q, [2, S], BF16)
        kqT = ak(o_q, 8192)
        kTz = [av(o_kz + h * 4096, [S], BF16) for h in range(4)]
        kkTz = [ak(o_kz + h * 4096, 4096) for h in range(4)]
        Va = [av(o_v + j * 4096, [16, 128], BF16) for j in range(2)]
        kVa = [ak(o_v + j * 4096, 4096) for j in range(2)]
        accs = [av(o_acc + j * 8192, [S], F32) for j in range(2)]
        kaccs = [ak(o_acc + j * 8192, 8192) for j in range(2)]
        pt = [av(o_pt + j * 512, [256], BF16) for j in range(4)]
        kpt = [ak(o_pt + j * 512, 512) for j in range(4)]
        rcq = [av(o_rc + j * 1024, [256], F32) for j in range(2)]
        krcq = [ak(o_rc + j * 1024, 1024) for j in range(2)]
        for h in range(4):
            oh = slice(((h + 1) % 2) * 64, ((h + 1) % 2) * 64 + 64)
            kb.op("pool", lambda g, h=h, oh=oh: g.memset(kTz[h][oh, :], 0.0), writes=kkTz[h])
        rts = [[av(o_acc + sl * 2560 + j * 512, [128], F32) for j in range(4)] for sl in range(3)]
        krts = [[ak(o_acc + sl * 2560 + j * 512, 512) for j in range(4)] for sl in range(3)]
        qrs = [av(o_acc + sl * 2560 + 2048, [256], BF16) for sl in range(3)]
        kqrs = [ak(o_acc + sl * 2560 + 2048, 512) for sl in range(3)]
        rtasks = []
        for blk in (2, 3):
            k = load_wblock(l, blk)
            for i in range(NT):
                t = {"pre": None, "post": None}

                def s1(t=t, i=i, k=k):
                    sl = t["slot"] % 3
                    bi = (0, 1, 2)[sl]
                    for c in range(8):
                        kb.op("pe", lambda pe, c=c: pe.matmul(pp[bi][:, 0:256], hT[:, c, i * 128:(i + 1) * 128],
                                                              wb[k][:, c, 0:256], start=(c == 0), stop=(c == 7)),
                              reads=[("wb", k), "hT"], writes=[("pp", bi)])
                    z4 = pp[bi][:, 0:256].rearrange("p (h t f) -> p h t f", h=4, t=2)
                    x1, x2 = z4[:, :, 0, :], z4[:, :, 1, :]
                    cs = ropet[:, 0, i, :].unsqueeze(1).broadcast_to([128, 4, 32])
                    sn = ropet[:, 1, i, :].unsqueeze(1).broadcast_to([128, 4, 32])
                    r4 = [r.rearrange("p (h f) -> p h f", h=4) for r in rts[sl]]
                    q4 = qrs[sl].rearrange("p (h t f) -> p h t f", h=4, t=2)
                    for j, (a_, b_) in enumerate(((x1, cs), (x2, sn), (x2, cs), (x1, sn))):
                        kb.op("dve", lambda v, a_=a_, b_=b_, j=j: v.tensor_tensor(out=r4[j], in0=a_, in1=b_, op=ALU.mult),
                              reads=[("pp", bi), "ropet"], writes=krts[sl][j])
                    kb.op("pool", lambda g: g.tensor_tensor(out=q4[:, :, 0, :], in0=r4[0], in1=r4[1], op=ALU.subtract),
                          reads=krts[sl][0] + krts[sl][1], writes=kqrs[sl])
                    kb.op("pool", lambda g: g.tensor_tensor(out=q4[:, :, 1, :], in0=r4[2], in1=r4[3], op=ALU.add),
                          reads=krts[sl][2] + krts[sl][3], writes=kqrs[sl])

                def s2(t=t, i=i, blk=blk):
                    sl = t["slot"] % 3
                    b2 = next_bank(6, 8)
                    ptb = pp[b2][:].bitcast(BF16)
                    for pr in range(2):
                        kb.op("pe", lambda pe, pr=pr: pe.transpose(ptb[:, pr * 128:(pr + 1) * 128],
                                                                   qrs[sl][:, pr * 128:(pr + 1) * 128], ident[:]),
                              reads=kqrs[sl] + ["ident"], writes=[("pp", b2)])
                    if blk == 2:
                        kb.op("act", lambda a: a.copy(out=qT[:, :, i * 128:(i + 1) * 128],
                                                      in_=ptb[:, 0:256].rearrange("p (c n) -> p c n", c=2)),
                              reads=[("pp", b2)], writes=kqT)
                    else:
                        for h in range(4):
                            hp = slice((h % 2) * 64, (h % 2) * 64 + 64)
                            pr = h // 2
                            kb.op("act" if h % 2 == 0 else "dve",
                                  lambda e, h=h, hp=hp, pr=pr: (e.copy if h % 2 == 0 else e.tensor_copy)(
                                      out=kTz[h][hp, i * 128:(i + 1) * 128], in_=ptb[hp, pr * 128:(pr + 1) * 128]),
                                  reads=[("pp", b2)], writes=kkTz[h])
                t["s1"], t["s2"] = s1, s2
                rtasks.append(t)
        run_pipeline(rtasks, 2)
        kv = load_wblock(l, 4)
        for cb in range(2):
            def ev_vt(tq, bi):
                kb.op("act", lambda a: a.copy(out=yT[:, 2 + cb, tq * 512:(tq + 1) * 512], in_=pp[bi][:, :]),
                      reads=[("pp", bi)], writes=ky(2 + cb))
            proj_fm(kv, cb, ev_vt)
        tasks = []
        nva = 0
        for h in range(4):
            acc, kacc = accs[h % 2], kaccs[h % 2]
            voff = 0 if h % 2 == 0 else 64
            for pi, (d, n_sub) in enumerate(((1, 2048), (4, 512), (16, 128))):
                V, kV = Va[nva % 2], kVa[nva % 2]
                nva += 1

                def pre_v(V=V, kV=kV, h=h, voff=voff, d=d, n_sub=n_sub, pi=pi):
                    if pi < 2:
                        kb.op("pool", lambda g: g.memset(V[:, :, 64 - voff:128 - voff], 1.0), writes=kV)
                    for half in range(2):
                        b2 = next_bank(0, 2)
                        ptb = pp[b2][:].bitcast(BF16)
                        for q in range(8):
                            j = half * 8 + q
                            kb.op("pe", lambda pe, q=q, j=j: pe.transpose(
                                ptb[:, q * 128:(q + 1) * 128], yT[:, 2 + h // 2, psl(128 * j, 128, d, n_sub)], ident[:, :]),
                                reads=ky(2 + h // 2) + ["ident"], writes=[("pp", b2)])
                        kb.op("act", lambda a: a.copy(
                            out=V[:, half * 8:half * 8 + 8, voff:voff + 64],
                            in_=ptb.rearrange("p (q e) -> p q e", q=8)[:, :, voff:voff + 64]),
                            reads=[("pp", b2)], writes=kV)
                first_of_pattern = True
                for qb in range(4):
                    ob = next_bank(4, 6)
                    js = []
                    for j in range(max(0, 4 * qb - 1), min(16, 4 * qb + 5)):
                        slo = (128 * j // n_sub) * n_sub
                        qlo = max(128 * j - 64, slo, 512 * qb)
                        qhi = min(128 * j + 192, slo + n_sub, 512 * qb + 512)
                        if qlo < qhi:
                            js.append((j, qlo, qhi))
                    for ji, (j, qlo, qhi) in enumerate(js):
                        t = {}
                        t["pre"] = pre_v if first_of_pattern else None
                        first_of_pattern = False

                        def s1(t=t, j=j, qlo=qlo, qhi=qhi, h=h, d=d, n_sub=n_sub):
                            n = qhi - qlo
                            ns = t["slot"] % 4
                            sbk = (2, 3, 6, 7)[ns]
                            p_, kp_ = pt[ns], kpt[ns]
                            mo = qlo - (128 * j - 64)
                            kb.op("pe", lambda pe: pe.matmul(pp[sbk][:, 0:n], kTz[h][:, psl(128 * j, 128, d, n_sub)],
                                                             qT[:, h // 2, psl(qlo, n, d, n_sub)], start=True, stop=False),
                                  reads=kkTz[h] + kqT, writes=[("pp", sbk)])
                            kb.op("pe", lambda pe: pe.matmul(pp[sbk][:, 0:n], ident[:, :], band[:, mo:mo + n],
                                                             start=False, stop=True),
                                  reads=["ident", "band"], writes=[("pp", sbk)])
                            kb.op("act", lambda a: a.activation(out=p_[:, 0:n], in_=pp[sbk][:, 0:n], func=AF.Exp, scale=0.125),
                                  reads=[("pp", sbk)], writes=kp_)

                        def s2(t=t, j=j, qlo=qlo, qhi=qhi, qb=qb, ob=ob, V=V, kV=kV, first=(ji == 0)):
                            n = qhi - qlo
                            ns = t["slot"] % 4
                            p_, kp_ = pt[ns], kpt[ns]
                            kb.op("pe", lambda pe: pe.matmul(pp[ob][:, qlo - 512 * qb:qhi - 512 * qb], V[:, j, :], p_[:, 0:n],
                                                             start=first, stop=False, skip_group_check=True),
                                  reads=kV + kp_, writes=[("pp", ob)])
                        t["s1"], t["s2"], t["post"] = s1, s2, None
                        if ji == len(js) - 1:
                            def post(qb=qb, ob=ob, d=d, pi=pi, acc=acc, kacc=kacc, h=h):
                                if d == 1:
                                    dst = acc[:, 512 * qb:512 * qb + 512]
                                    src = pp[ob][:, :]
                                elif d == 4:
                                    dst = acc.rearrange("p (l x) -> p x l", x=4)[:, qb, :]
                                    src = pp[ob][:, :]
                                else:
                                    dst = acc.rearrange("p (l x) -> p x l", x=16)[:, 4 * qb:4 * qb + 4, :]
                                    src = pp[ob][:, :].rearrange("p (r l) -> p r l", r=4)
                                if pi == 0:
                                    kb.op("dve", lambda v: v.tensor_copy(out=dst, in_=src), reads=[("pp", ob)], writes=kacc)
                                else:
                                    kb.op("dve", lambda v: v.tensor_tensor(out=dst, in0=src, in1=dst, op=ALU.add),
                                          reads=[("pp", ob)] + kacc, writes=kacc)
                                if pi == 2 and qb == 3:
                                    nr = slice((h % 2) * 64, (h % 2) * 64 + 64)
                                    dr = slice(((h + 1) % 2) * 64, ((h + 1) % 2) * 64 + 64)
                                    for tq in range(8):
                                        ts_ = slice(tq * 256, (tq + 1) * 256)
                                        rc, krc = rcq[tq % 2], krcq[tq % 2]
                                        kb.op("act", lambda a: a.activation(out=rc[nr, :], in_=acc[dr, ts_], func=AF.Ln),
                                              reads=kacc, writes=krc)
                                        kb.op("act", lambda a: a.activation(out=rc[nr, :], in_=rc[nr, :], func=AF.Exp, scale=-1.0,
                                                                            bias=nlh[nr, :]),
                                              reads=krc + ["nlh"], writes=krc)
                                        kb.op("pool", lambda g: g.tensor_tensor(out=yT[nr, 2 + h // 2, ts_], in0=acc[nr, ts_],
                                                                                in1=rc[nr, :], op=ALU.mult),
                                              reads=kacc + krc, writes=ky(2 + h // 2, h=h % 2))
                            t["post"] = post
                        tasks.append(t)
        run_pipeline(tasks, 3)
        tt = [av(o_acc + j * 2048, [512], F32) for j in range(2)]
        ktt = [ak(o_acc + j * 2048, 2048) for j in range(2)]
        gq = [av(o_acc + 4096 + j * 2048, [512], F32) for j in range(2)]
        kgq = [ak(o_acc + 4096 + j * 2048, 2048) for j in range(2)]
        gate_apply(l, 5, 2, tt, ktt, gq, kgq)

    def na_rows(kt):
        rows = []
        for r in range(32):
            rs = min(max(r - 4, 0), 24)
            if any(rs <= 2 * kt + krl < rs + 8 for krl in range(2)):
                rows.append(r)
        return rows[0], rows[-1]

    def mixer_d(l):
        o_q, o_kz, o_v, o_e, o_tt, o_pt = 0, 8192, 24576, 32768, 42240, 46336
        qT = av(o_q, [2, S], BF16)
        kqT = ak(o_q, 8192)
        kTz = [av(o_kz + h * 4096, [S], BF16) for h in range(4)]
        kkTz = [ak(o_kz + h * 4096, 4096) for h in range(4)]
        Va = [av(o_v + j * 4096, [16, 128], BF16) for j in range(2)]
        kVa = [ak(o_v + j * 4096, 4096) for j in range(2)]
        Eb = [av(o_e + j * 4736, [2368], BF16) for j in range(2)]
        kEb = [ak(o_e + j * 4736, 4736) for j in range(2)]
        tt = [av(o_tt + j * 2048, [512], F32) for j in range(2)]
        ktt = [ak(o_tt + j * 2048, 2048) for j in range(2)]
        pt = [av(o_pt + j * 1024, [512], BF16) for j in range(4)]
        kpt = [ak(o_pt + j * 1024, 1024) for j in range(4)]
        for h in range(4):
            oh = slice(((h + 1) % 2) * 64, ((h + 1) % 2) * 64 + 64)
            kb.op("pool", lambda g, h=h, oh=oh: g.memset(kTz[h][oh, :], 0.0), writes=kkTz[h])
        k = load_wblock(l, 8)
        for cb in range(2):
            def ev_q(tq, bi):
                kb.op("act", lambda a: a.copy(out=qT[:, cb, tq * 512:(tq + 1) * 512], in_=pp[bi][:, :]),
                      reads=[("pp", bi)], writes=kqT)
            proj_fm(k, cb, ev_q)
        k = load_wblock(l, 9)
        for cb in range(2):
            def ev_k(tq, bi):
                for hh in range(2):
                    h = 2 * cb + hh
                    hp = slice(hh * 64, hh * 64 + 64)
                    kb.op("act" if hh == 0 else "dve",
                          lambda e, h=h, hp=hp, hh=hh: (e.copy if hh == 0 else e.tensor_copy)(
                              out=kTz[h][hp, tq * 512:(tq + 1) * 512], in_=pp[bi][hp, :]),
                          reads=[("pp", bi)], writes=kkTz[h])
            proj_fm(k, cb, ev_k)
        kv = load_wblock(l, 10)
        for cb in range(2):
            def ev_vt(tq, bi):
                kb.op("act", lambda a: a.copy(out=yT[:, 6 + cb, tq * 512:(tq + 1) * 512], in_=pp[bi][:, :]),
                      reads=[("pp", bi)], writes=ky(6 + cb))
            proj_fm(kv, cb, ev_vt)
        tasks = []
        for h in range(4):
            nr = slice((h % 2) * 64, (h % 2) * 64 + 64)
            dr = slice(((h + 1) % 2) * 64, ((h + 1) % 2) * 64 + 64)
            voff = 0 if h % 2 == 0 else 64
            V, kV = Va[h % 2], kVa[h % 2]
            E, kE = Eb[h % 2], kEb[h % 2]

            def pre_h(h=h, voff=voff, V=V, kV=kV, E=E, kE=kE):
                kb.dma(E, et_d[l, h], reads=["et"], writes=kE)
                if h < 2:
                    kb.op("pool", lambda g: g.memset(V[:, :, 64 - voff:128 - voff], 1.0), writes=kV)
                for half in range(2):
                    b2 = next_bank(0, 2)
                    ptb = pp[b2][:].bitcast(BF16)
                    for q in range(8):
                        j = half * 8 + q
                        kb.op("pe", lambda pe, q=q, j=j: pe.transpose(
                            ptb[:, q * 128:(q + 1) * 128], yT[:, 6 + h // 2, 128 * j:128 * j + 128], ident[:, :]),
                            reads=ky(6 + h // 2) + ["ident"], writes=[("pp", b2)])
                    kb.op("act", lambda a: a.copy(out=V[:, half * 8:half * 8 + 8, voff:voff + 64],
                                                  in_=ptb.rearrange("p (q e) -> p q e", q=8)[:, :, voff:voff + 64]),
                          reads=[("pp", b2)], writes=kV)
            first_of_head = True
            for qb in range(4):
                ob = next_bank(4, 6)
                kts = []
                for kt in range(16):
                    ra, rb = na_rows(kt)
                    ra, rb = max(ra, 8 * qb), min(rb, 8 * qb + 7)
                    if ra <= rb:
                        kts.append((kt, ra, rb))
                for ki, (kt, ra, rb) in enumerate(kts):
                    t = {"pre": pre_h if first_of_head else None, "post": None}
                    first_of_head = False

                    def s1(t=t, kt=kt, ra=ra, rb=rb, h=h, E=E, kE=kE):
                        n = 64 * (rb - ra + 1)
                        ns = t["slot"]
                        sbk = (2, 3, 6, 7)[ns % 4]
                        p_, kp_ = pt[ns % 4], kpt[ns % 4]
                        kb.op("pe", lambda pe: pe.matmul(pp[sbk][:, 0:n], kTz[h][:, 128 * kt:128 * kt + 128],
                                                         qT[:, h // 2, 64 * ra:64 * (rb + 1)], start=True, stop=True,
                                                         skip_group_check=True),
                              reads=kkTz[h] + kqT, writes=[("pp", sbk)])
                        segs = []
                        if ra <= 3:
                            r1 = min(rb, 3)
                            segs.append((ra, r1, 576 + (3 - kt) * 256 + ra * 64))
                        if max(ra, 4) <= min(rb, 28):
                            r0, r1 = max(ra, 4), min(rb, 28)
                            segs.append((r0, r1, (r0 - 2 * kt + 3) * 64))
                        if rb >= 29:
                            r0 = max(ra, 29)
                            segs.append((r0, rb, 1600 + (15 - kt) * 192 + (r0 - 29) * 64))
                        for si, (r0, r1, eoff) in enumerate(segs):
                            c0, c1 = 64 * (r0 - ra), 64 * (r1 - ra + 1)
                            kb.op("pe", lambda pe, c0=c0, c1=c1, eoff=eoff, si=si: pe.matmul(
                                pp[sbk][:, c0:c1], ident[:, :], E[:, eoff:eoff + c1 - c0], start=False,
                                stop=True, skip_group_check=True),
                                reads=["ident"] + kE, writes=[("pp", sbk)])
                        kb.op("act", lambda a: a.activation(out=p_[:, 0:n], in_=pp[sbk][:, 0:n], func=AF.Exp, scale=0.125),
                              reads=[("pp", sbk)], writes=kp_)

                    def s2(t=t, kt=kt, ra=ra, rb=rb, qb=qb, ob=ob, V=V, kV=kV, first=(ki == 0)):
                        n = 64 * (rb - ra + 1)
                        ns = t["slot"]
                        p_, kp_ = pt[ns % 4], kpt[ns % 4]
                        kb.op("pe", lambda pe: pe.matmul(pp[ob][:, 64 * ra - 512 * qb:64 * (rb + 1) - 512 * qb], V[:, kt, :],
                                                         p_[:, 0:n], start=first, stop=False, skip_group_check=True),
                              reads=kV + kp_, writes=[("pp", ob)])
                    t["s1"], t["s2"] = s1, s2
                    if ki == len(kts) - 1:
                        def post(qb=qb, ob=ob, h=h, nr=nr, dr=dr):
                            ts_ = slice(qb * 512, (qb + 1) * 512)
                            rc, krc = tt[qb % 2], ktt[qb % 2]
                            kb.op("act", lambda a: a.activation(out=rc[nr, :], in_=pp[ob][dr, :], func=AF.Ln),
                                  reads=[("pp", ob)], writes=krc)
                            kb.op("act", lambda a: a.activation(out=rc[nr, :], in_=rc[nr, :], func=AF.Exp, scale=-1.0,
                                                                bias=nlh[nr, :]),
                                  reads=krc + ["nlh"], writes=krc)
                            kb.op("dve", lambda v: v.tensor_tensor(out=yT[nr, 6 + h // 2, ts_], in0=pp[ob][nr, :],
                                                                   in1=rc[nr, :], op=ALU.mult),
                                  reads=[("pp", ob)] + krc, writes=ky(6 + h // 2, h=h % 2))
                        t["post"] = post
                    tasks.append(t)
        run_pipeline(tasks, 3)
        ttg = [av(o_v + j * 2048, [512], F32) for j in range(2)]
        kttg = [ak(o_v + j * 2048, 2048) for j in range(2)]
        gq = [av(o_v + 4096 + j * 2048, [512], F32) for j in range(2)]
        kgq = [ak(o_v + 4096 + j * 2048, 2048) for j in range(2)]
        gate_apply(l, 11, 6, ttg, kttg, gq, kgq)

    GC1 = math.sqrt(2.0 / math.pi)
    GC2 = 0.044715

    def mixer_c(l):
        o_u, o_G, o_gT, o_g2, o_V, o_P, o_MC, o_et, o_w, o_sin = 0, 8192, 16384, 24576, 32768, 34816, 36864, 40448, 44544, 54784
        ucm = [av(o_u + t * 4096, [16, 8, 16], BF16) for t in range(2)]
        kucm = [ak(o_u + t * 4096, 4096) for t in range(2)]
        Gcm = av(o_G, [16, 256], BF16)
        kG = ak(o_G, 8192)
        gT = av(o_gT, [2, S], BF16)
        kgT = ak(o_gT, 8192)
        g2c = av(o_g2, [2, S], BF16)
        kg2c = ak(o_g2, 8192)
        Vs = [av(o_V + j * 512, [2, 128], BF16) for j in range(4)]
        kVs = [ak(o_V + j * 512, 512) for j in range(4)]
        Pb = [av(o_P + j * 1024, [4, 128], BF16) for j in range(2)]
        kPb = [ak(o_P + j * 1024, 1024) for j in range(2)]
        MC = [av(o_MC + j * 1792, [7, 128], BF16) for j in range(2)]
        kMC = [ak(o_MC + j * 1792, 1792) for j in range(2)]
        et = av(o_et, [4, 2, 128], F32)
        ket = ak(o_et, 4096)
        wk_ = [av(o_w + j * 2048, [4, 128], F32) for j in range(5)]
        kwk = [ak(o_w + j * 2048, 2048) for j in range(5)]
        A_, B_, T1, T2, T3 = wk_
        kA, kB_, kT1, kT2, kT3 = kwk
        sre = av(o_sin, [4, 128], BF16)
        sim = av(o_sin + 1024, [4, 128], BF16)
        ksre, ksim = ak(o_sin, 1024), ak(o_sin + 1024, 1024)
        F_, Bh = slice(0, 64), slice(64, 128)
        kcu = load_wblock(l, 6)
        for j in range(8):
            for mt in range(2):
                bi = next_bank(0, 2)
                for c in range(8):
                    kb.op("pe", lambda pe, c=c: pe.matmul(
                        pp[bi][:, 0:256], hT[:, c, slice(1024 * mt + j, 1024 * mt + j + 8 * 127 + 1, 8)],
                        wb[kcu][:, c, :], start=(c == 0), stop=(c == 7)),
                        reads=[("wb", kcu), "hT"], writes=[("pp", bi)])
                kb.op("act", lambda a: a.copy(out=ucm[mt][:, :, 7 - j, :],
                                              in_=pp[bi][:, 0:256].rearrange("p (g c) -> p g c", g=16)),
                      reads=[("pp", bi)], writes=kucm[mt])
        kb.op("pool", lambda g: g.memset(sre[F_, :, 0:1], 0.0), writes=ksre)
        kb.op("pool", lambda g: g.memset(sre[Bh, :, 127:128], 0.0), writes=ksre)
        kb.op("pool", lambda g: g.memset(sim[F_, :, 0:1], 0.0), writes=ksim)
        kb.op("pool", lambda g: g.memset(sim[Bh, :, 127:128], 0.0), writes=ksim)
        for gb in range(4):
            kb.dma(et, etab_d[l, gb], reads=["etab"], writes=ket)
            s0r, s0i = next_bank(2, 4), None
            s0i = next_bank(2, 4)
            for gi in range(4):
                g = 4 * gb + gi
                V, kV = Vs[gi], kVs[gi]
                P_, kP_ = Pb[g % 2], kPb[g % 2]
                kb.dma(P_, sblk_d[l, g, :, 3:7, :], reads=["sblk"], writes=kP_)
                bt = next_bank(6, 8)
                ptb = pp[bt][:].bitcast(BF16)
                for mt in range(2):
                    kb.op("pe", lambda pe, mt=mt: pe.transpose(
                        ptb[:, mt * 128:(mt + 1) * 128], ucm[mt][:, g, :, :].rearrange("p j c -> p (j c)"), ident[:]),
                        reads=kucm[mt] + ["ident"], writes=[("pp", bt)])
                kb.op("act", lambda a: a.copy(out=V.rearrange("p s (t q) -> p t q s", t=2),
                                              in_=ptb[:, 0:256].rearrange("p (t q s) -> p t q s", t=2, s=2)),
                      reads=[("pp", bt)], writes=kV)
                for (bank, b0) in ((s0r, 0), (s0i, 2)):
                    for sub in range(2):
                        kb.op("pe", lambda pe, sub=sub, bank=bank, b0=b0: pe.matmul(
                            pp[bank][:, gi * 128:(gi + 1) * 128], P_[:, b0 + sub, :], V[:, sub, :],
                            start=(sub == 0), stop=(sub == 1), skip_group_check=True),
                            reads=kP_ + kV, writes=[("pp", bank)])
            for (bank, dst, kd) in ((s0r, A_, kA), (s0i, B_, kB_)):
                src = pp[bank][:, :].rearrange("p (g m) -> p g m", g=4)
                kb.op("act", lambda a, src=src, dst=dst: a.copy(out=dst[F_], in_=src[F_]), reads=[("pp", bank)], writes=kd)
                kb.op("act", lambda a, src=src, dst=dst: a.copy(out=dst[Bh], in_=src[Bh, :, ::-1]), reads=[("pp", bank)], writes=kd)
            cs_, sn_ = et[:, :, 0, :], et[:, :, 1, :]
            if gb == 0:
                dump("ucm0", ucm[0], kucm[0])
                dump("V0", Vs[0], kVs[0])
                dump("S0r", A_, kA)
                dump("S0i", B_, kB_)
            tt2 = lambda e, o, a, b, op, rd, wr: kb.op(e, lambda v: v.tensor_tensor(out=o, in0=a, in1=b, op=op), reads=rd, writes=wr)
            tt2("dve", T1, A_, cs_, ALU.mult, kA + ket, kT1)
            tt2("pool", T2, B_, sn_, ALU.mult, kB_ + ket, kT2)
            tt2("dve", T1, T1, T2, ALU.add, kT1 + kT2, kT1)
            tt2("pool", T3, B_, cs_, ALU.mult, kB_ + ket, kT3)
            tt2("dve", T2, A_, sn_, ALU.mult, kA + ket, kT2)
            tt2("pool", T3, T3, T2, ALU.subtract, kT3 + kT2, kT3)
            for gi in range(4):
                g = 4 * gb + gi
                rb_ = rho_sb[:, l, g:g + 1].broadcast_to([128, 128])
                kb.op("dve", lambda v, gi=gi, rb_=rb_: v.tensor_tensor_scan(
                    out=A_[:, gi, :], data0=rb_, data1=T1[:, gi, :], initial=0.0, op0=ALU.mult, op1=ALU.add),
                    reads=kT1 + ["rho"], writes=kA)
                kb.op("dve", lambda v, gi=gi, rb_=rb_: v.tensor_tensor_scan(
                    out=B_[:, gi, :], data0=rb_, data1=T3[:, gi, :], initial=0.0, op0=ALU.mult, op1=ALU.add),
                    reads=kT3 + ["rho"], writes=kB_)
            tt2("dve", T1, A_, cs_, ALU.mult, kA + ket, kT1)
            tt2("pool", T2, B_, sn_, ALU.mult, kB_ + ket, kT2)
            tt2("dve", T1, T1, T2, ALU.subtract, kT1 + kT2, kT1)
            tt2("pool", T3, B_, cs_, ALU.mult, kB_ + ket, kT3)
            tt2("dve", T2, A_, sn_, ALU.mult, kA + ket, kT2)
            tt2("pool", T3, T3, T2, ALU.add, kT3 + kT2, kT3)
            for (src, ksrc, dst, kd) in ((T1, kT1, sre, ksre), (T3, kT3, sim, ksim)):
                kb.op("act", lambda a, src=src, dst=dst: a.copy(out=dst[F_, :, 1:128], in_=src[F_, :, 0:127]),
                      reads=ksrc, writes=kd)
                kb.op("act", lambda a, src=src, dst=dst: a.copy(out=dst[Bh, :, 0:127], in_=src[Bh, :, 126::-1]),
                      reads=ksrc, writes=kd)
            if gb == 0:
                dump("sre", sre, ksre)
                dump("sim", sim, ksim)
                dump("sdr", T1, kT1)
            for gi in range(4):
                g = 4 * gb + gi
                V, kV = Vs[gi], kVs[gi]
                M_, kM_ = MC[g % 2], kMC[g % 2]
                kb.dma(M_[:, 0:3, :], sblk_d[l, g, :, 0:3, :], reads=["sblk"], writes=kM_)
                kb.dma(M_[:, 3:7, :], sblk_d[l, g, :, 7:11, :], reads=["sblk"], writes=kM_)
                yb = next_bank(4, 6)
                plan = ((0, [(0, V[:, 0, :], kV), (2, V[:, 1, :], kV), (3, sre[:, gi, :], ksre), (4, sim[:, gi, :], ksim)]),
                        (1, [(0, V[:, 1, :], kV), (1, V[:, 0, :], kV), (5, sre[:, gi, :], ksre), (6, sim[:, gi, :], ksim)]))
                for so, terms in plan:
                    for ti, (bidx, rhs, krhs) in enumerate(terms):
                        kb.op("pe", lambda pe, so=so, ti=ti, bidx=bidx, rhs=rhs: pe.matmul(
                            pp[yb][:, so * 128:(so + 1) * 128], M_[:, bidx, :], rhs, start=(ti == 0), stop=(ti == 3),
                            skip_group_check=True), reads=kM_ + krhs, writes=[("pp", yb)])
                Ysb = T2.rearrange("p a b -> p (a b)")[:, 0:256]
                sq = T2.rearrange("p a b -> p (a b)")[:, 256:512]
                kb.op("act", lambda a: a.copy(out=Ysb, in_=pp[yb][:, 0:256]), reads=[("pp", yb)], writes=kT2)
                tb = next_bank(6, 8)
                for so in range(2):
                    kb.op("pe", lambda pe, so=so: pe.transpose(pp[tb][:, so * 128:(so + 1) * 128],
                                                               Ysb[:, so * 128:(so + 1) * 128], identf[:]),
                          reads=kT2 + ["identf"], writes=[("pp", tb)])
                yy = pp[tb][:, 0:256]
                u_ = T3.rearrange("p a b -> p (a b)")[:, 0:256]
                th_ = T3.rearrange("p a b -> p (a b)")[:, 256:512]
                kb.op("act", lambda a: a.activation(out=sq, in_=yy, func=AF.Square), reads=[("pp", tb)], writes=kT2)
                kb.op("pool", lambda g_: g_.tensor_scalar(out=sq, in0=sq, scalar1=GC2, scalar2=1.0, op0=ALU.mult, op1=ALU.add),
                      reads=kT2, writes=kT2)
                kb.op("dve", lambda v: v.tensor_tensor(out=u_, in0=sq, in1=yy, op=ALU.mult), reads=kT2 + [("pp", tb)], writes=kT3)
                kb.op("act", lambda a: a.activation(out=th_, in_=u_, func=AF.Tanh, scale=GC1), reads=kT3, writes=kT3)
                kb.op("dve", lambda v: v.scalar_tensor_tensor(
                    out=Gcm[:, :, g * 16:(g + 1) * 16], in0=th_.rearrange("p (s c) -> p s c", c=16), scalar=1.0,
                    in1=yy.rearrange("p (s c) -> p s c", c=16), op0=ALU.add, op1=ALU.mult),
                    reads=kT3 + [("pp", tb)], writes=kG)
        dump("Gcm", Gcm, kG)
        for chc in range(2):
            for half in range(2):
                bt = next_bank(6, 8)
                ptb = pp[bt][:].bitcast(BF16)
                for q in range(8):
                    si = half * 8 + q
                    kb.op("pe", lambda pe, q=q, si=si: pe.transpose(ptb[:, q * 128:(q + 1) * 128],
                                                                    Gcm[:, si, chc * 128:(chc + 1) * 128], ident[:]),
                          reads=kG + ["ident"], writes=[("pp", bt)])
                kb.op("act", lambda a: a.copy(
                    out=gT[:, chc, :].rearrange("p (m s) -> p s m", s=16)[:, half * 8:half * 8 + 8, :],
                    in_=ptb.rearrange("p (q m) -> p q m", q=8)), reads=[("pp", bt)], writes=kgT)
        dump("gT", gT, kgT)
        k = load_wblock(l, 7)
        ttg = [av(o_w + j * 2048, [512], F32) for j in range(2)]
        gate_fm(k, kg2c, g2c, kwk[0:2], ttg)
        gwl = av(o_et, [2, 256], BF16)
        kgwl = ak(o_et, 1024)
        kb.dma(gwl, gwb_d[l], reads=["gwb"], writes=kgwl)
        t2s = [av(o_w + (2 + j) * 2048, [512], F32) for j in range(2)]
        n_ = 0
        for ec in range(2):
            for tq in range(4):
                ts_ = slice(tq * 512, (tq + 1) * 512)
                bi = next_bank(0, 2)
                for cc in range(2):
                    kb.op("pe", lambda pe, cc=cc: pe.matmul(pp[bi][:, :], gwl[:, cc, ec * 128:(ec + 1) * 128], gT[:, cc, ts_],
                                                            start=(cc == 0), stop=(cc == 1)),
                          reads=kgwl + kgT, writes=[("pp", bi)])
                th2, kth2 = ttg[n_ % 2], kwk[n_ % 2]
                t_, kt_ = t2s[n_ % 2], kwk[2 + n_ % 2]
                n_ += 1
                kb.op("act", lambda a: a.activation(out=th2, in_=pp[bi][:, :], func=AF.Tanh, scale=0.25,
                                                    bias=glb[:, l, ec:ec + 1]),
                      reads=[("pp", bi), "glb"], writes=kth2)
                kb.op("dve", lambda v: v.scalar_tensor_tensor(out=t_, in0=th2, scalar=1.0, in1=gT[:, ec, ts_],
                                                               op0=ALU.add, op1=ALU.mult),
                      reads=kth2 + kgT, writes=kt_)
                kb.op("dve", lambda v: v.scalar_tensor_tensor(out=yT[:, 4 + ec, ts_], in0=t_, scalar=0.125,
                                                               in1=g2c[:, ec, ts_], op0=ALU.mult, op1=ALU.mult),
                      reads=kt_ + kg2c, writes=ky(4 + ec))

    kb.same_depth = 2
    for s in range(nseq):
        for i in range(NT):
            kb.dma(x_res[:, i, :], x_d[s, i * 128:(i + 1) * 128, :], writes=[("x", i)])
        for l in range(depth):
            rms_all()
            for i in range(NT + 2):
                if i < NT:
                    kb.op("act", lambda a, i=i: a.activation(out=hs[i % 4], in_=x_res[:, i, :], func=AF.Copy,
                                                             scale=small[:, i:i + 1]),
                          reads=[("x", i), "rstd"], writes=khs[i % 4])
                    bi = 4 + i % 4
                    pt = pp[bi][:].bitcast(BF16)
                    for c in range(8):
                        kb.op("pe", lambda pe, c=c, i=i, pt=pt: pe.transpose(
                            pt[:, c * 128:(c + 1) * 128], hs[i % 4][:, c * 128:(c + 1) * 128], ident[:]),
                            reads=khs[i % 4] + ["ident"], writes=[("pp", bi)])
                if i >= 2:
                    j = i - 2
                    bj = 4 + j % 4
                    ptj = pp[bj][:].bitcast(BF16)
                    if j % 2 == 0:
                        kb.op("act", lambda a, j=j, ptj=ptj: a.copy(
                            out=hT[:, :, j * 128:(j + 1) * 128], in_=ptj.rearrange("p (c n) -> p c n", c=8)),
                            reads=[("pp", bj)], writes=["hT"])
                    else:
                        kb.op("dve", lambda v, j=j, ptj=ptj: v.tensor_copy(
                            out=hT[:, :, j * 128:(j + 1) * 128], in_=ptj.rearrange("p (c n) -> p c n", c=8)),
                            reads=[("pp", bj)], writes=["hT"])
            if "a" in mixers:
                mixer_a(l)
            if "b" in mixers:
                mixer_b(l)
            if "c" in mixers:
                mixer_c(l)
            if "d" in mixers:
                mixer_d(l)
            for mi, m in enumerate("abcd"):
                if m not in mixers:
                    kb.op("pool", lambda g, mi=mi: g.memset(yT[:, 2 * mi:2 * mi + 2, :], 0.0), writes=ky(2 * mi, 2 * mi + 2))
            if dbg and s == 0 and l == 0:
                kb.dma(dbg_d, yT[:], reads=ky(0, 8), writes=["dbgout"])
            for h in range(4):
                k = load_woblock(l, h)
                for i in range(NT):
                    bi = next_bank(0, 2)
                    for c in range(8):
                        kb.op("pe", lambda pe, c=c, i=i, bi=bi, k=k: pe.matmul(
                            pp[bi][:, 0:256], yT[:, c, i * 128:(i + 1) * 128], wb[k][:, c, :],
                            start=(c == 0), stop=(c == 7)),
                            reads=[("wb", k)] + ky(c), writes=[("pp", bi)])
                    kb.op("dve", lambda v, i=i, bi=bi, h=h: v.tensor_tensor(
                        out=x_res[:, i, h * 256:(h + 1) * 256], in0=pp[bi][:, 0:256],
                        in1=x_res[:, i, h * 256:(h + 1) * 256], op=ALU.add),
                        reads=[("pp", bi)], writes=[("x", i)])
        fg = av(8192, [D], F32)
        kb.dma(fg, fg_d, writes=ak(8192, 4096))
        rms_all()
        for i in range(NT):
            oo = 28672 + (i % 4) * 4096
            ot = av(oo, [1024], F32)
            kb.op("dve", lambda v, i=i, ot=ot: v.scalar_tensor_tensor(
                out=ot, in0=x_res[:, i, :], scalar=small[:, i:i + 1], in1=fg, op0=ALU.mult, op1=ALU.mult),
                reads=[("x", i), "rstd"] + ak(8192, 4096), writes=ak(oo, 4096))
            kb.dma(y_d[s, i * 128:(i + 1) * 128, :], ot, reads=ak(oo, 4096), writes=["y"])
    kb.finish()
    return nc


def host_prep(inputs):
    f = np.float32
    ng = np.ascontiguousarray(np.asarray(inputs["norm_g"], f).reshape(4, 8, 128).transpose(2, 0, 1))
    fgb = np.ascontiguousarray(np.broadcast_to(np.asarray(inputs["final_g"], f)[None, :], (128, D)))
    pw = np.asarray(inputs["pool_w"], f)
    pwb = np.zeros((128, 4, 2, 128), f)
    for g in range(4):
        cb, h = g // 2, g % 2
        pwb[h * 64:(h + 1) * 64, :, cb, h * 64:(h + 1) * 64] = pw[:, g].transpose(1, 0, 2)
    psc = np.ascontiguousarray(np.asarray(inputs["pool_scale"], f).reshape(4, 2, 128).transpose(2, 0, 1))
    pcn = np.zeros((128, 2, 17), f)
    for g, w in enumerate((2, 4, 8, 16)):
        cb, h = g // 2, g % 2
        t = np.arange(S)
        cnt = np.minimum(t + w // 2, S) - np.maximum(t - w // 2, 0)
        pcn[h * 64:(h + 1) * 64, cb, 0:8] = (w / cnt[0:8])[None, :]
        pcn[h * 64:(h + 1) * 64, cb, 8:16] = (w / cnt[S - 8:S])[None, :]
        pcn[h * 64:(h + 1) * 64, cb, 16] = 1.0 / w
    inv = 10000.0 ** (-np.arange(0, 64, 2, dtype=np.float32) / 64)
    ang = np.arange(S, dtype=np.float32)[:, None] * inv[None, :]
    rope = np.stack([np.cos(ang), np.sin(ang)], 0).astype(f)
    rope_t = np.ascontiguousarray(rope.reshape(2, NT, 128, 32).transpose(2, 0, 1, 3))
    kk = np.arange(128)[:, None]
    cc = np.arange(256)[None, :]
    band = np.where(((cc - kk) >= 0) & ((cc - kk) <= 128), 0.0, -240000.0).astype(ml_dtypes.bfloat16)
    rpb = np.asarray(inputs["na_rpb"], f)
    rpbpad = np.zeros((4, 4, 15, 128), f)
    rpbpad[:, :, :, 48:79] = rpb[:, :, ::-1, :]
    kc = np.arange(64)
    c = 63 - np.arange(64)
    cs = np.clip(c - 8, 0, 48)
    colok = ((kc[:, None] >= cs[None, :]) & (kc[:, None] < cs[None, :] + 16)).astype(f)
    nam = np.zeros((128, 2368), f)
    for krl in range(2):
        for ri in range(9):
            dlt = krl - ri + 3
            if -4 <= dlt <= 3:
                nam[krl * 64:(krl + 1) * 64, ri * 64:(ri + 1) * 64] = colok
        for blk in range(28):
            nam[krl * 64:(krl + 1) * 64, 576 + blk * 64:576 + (blk + 1) * 64] = colok
    are = np.asarray(inputs["ssm_a_re"], f)
    aim = np.asarray(inputs["ssm_a_im"], f)
    ldt = np.asarray(inputs["ssm_log_dt"], f)
    lam = np.stack([are.transpose(0, 1, 3, 2), aim.transpose(0, 1, 3, 2),
                    np.broadcast_to(ldt[:, :, None, :], (4, 2, 64, 16))], axis=-1)
    lam = np.ascontiguousarray(lam.reshape(4, 128, 16, 3))
    bre = np.asarray(inputs["ssm_b_re"], f)
    bim = np.asarray(inputs["ssm_b_im"], f)
    bp1 = np.stack([bre.transpose(0, 2, 1, 3), bim.transpose(0, 2, 1, 3)], axis=-1)
    bp = np.ascontiguousarray(np.concatenate([bp1, bp1], axis=1))
    cre = np.asarray(inputs["ssm_c_re"], f)
    cim = np.asarray(inputs["ssm_c_im"], f)
    cp1 = np.stack([cre.transpose(0, 1, 4, 2, 3), cim.transpose(0, 1, 4, 2, 3)], axis=-1)
    cp = np.ascontiguousarray(cp1.reshape(4, 128, 16, 16, 2))
    sd = np.asarray(inputs["ssm_d"], f).reshape(4, 16, 16)
    sdt = np.ascontiguousarray(sd.transpose(2, 0, 1))
    glw = np.ascontiguousarray(np.asarray(inputs["glu_w"], f).reshape(4, 2, 128, 256).transpose(0, 2, 1, 3))
    glbt = np.ascontiguousarray(np.asarray(inputs["glu_b"], f).reshape(4, 2, 128).transpose(2, 0, 1))
    return {
        "ssm_lam": lam, "ssm_bp": bp, "ssm_cp": cp, "ssm_dt": sdt, "glu_w_t": glw, "glu_b_t": glbt,
        "rpbpad": rpbpad, "na_mask": nam.astype(ml_dtypes.bfloat16),
        "rope_t": rope_t, "band": band,
        "pool_w_blk": pwb, "pool_scale_t": psc, "pool_const": pcn,
        "w_in": np.ascontiguousarray(inputs["w_in"], dtype=f),
        "w_out": np.ascontiguousarray(inputs["w_out"], dtype=f),
        "norm_g_t": ng,
        "final_g_b": fgb,
    }


def kernel(**inputs):
    xp = np.asarray(inputs["x_prompt"], np.float32)
    xs = np.asarray(inputs["x_sample"], np.float32)
    shared = host_prep(inputs)
    nc = build()
    in_maps = []
    for c in range(8):
        xc = np.concatenate([xp[4 * c:4 * c + 4], xs[c:c + 1]], axis=0)
        m = dict(shared)
        m["x"] = np.ascontiguousarray(xc)
        in_maps.append(m)
    res = run_bass_kernel_spmd(nc, in_maps, core_ids=list(range(8)))
    yp = np.empty_like(xp)
    ys = np.empty_like(xs)
    for c in range(8):
        y = res.results[c]["y"]
        yp[4 * c:4 * c + 4] = y[0:4]
        ys[c] = y[4]
    return (yp, ys)
```

```python
import math
from contextlib import ExitStack
import numpy as np
import ml_dtypes
import concourse.bass as bass
import concourse.mybir as mybir
from concourse.bass_utils import run_bass_kernel_spmd

F32 = mybir.dt.float32
BF16 = mybir.dt.bfloat16
I32 = mybir.dt.int32
AF = mybir.ActivationFunctionType
ALU = mybir.AluOpType
AX = mybir.AxisListType

S = 2048
D = 1024
NT = 16
EPS = 1e-6
NDMA = 24


class KB:
    def __init__(self):
        self.nc = bass.Bass("TRN2", target_bir_lowering=False)
        nc = self.nc
        self.es = ExitStack()
        self.eng = {"pe": nc.tensor, "act": nc.scalar, "dve": nc.vector, "pool": nc.gpsimd, "sp": nc.sync}
        self.sem = {}
        for e in ["pe", "act", "dve", "pool"]:
            self.sem[e] = self.es.enter_context(nc.semaphore("s_" + e))
        for i in range(NDMA):
            self.sem[("dma", i)] = self.es.enter_context(nc.semaphore("s_dma%d" % i))
        self.cnt = {e: 0 for e in ["pe", "act", "dve", "pool"]}
        self.seen = {e: {} for e in ["pe", "act", "dve", "pool", "sp"]}
        self.res = {}
        self.ndma = 0
        self.same_eng = {"act", "dve", "pool"}
        self.same_depth = 1000000
        self.clock = {}

    def sb(self, name, shape, dt):
        return self.es.enter_context(self.nc.sbuf_tensor(name, shape, dt))

    def ps(self, name, shape, dt):
        return self.es.enter_context(self.nc.psum_tensor(name, shape, dt))

    def dram(self, name, shape, dt, kind="Internal"):
        return self.nc.dram_tensor(name, shape, dt, kind=kind).ap()

    def _wait(self, e, key, val):
        if key == e:
            if e not in self.same_eng or val < self.cnt[e] - self.same_depth + 1:
                return
        if self.seen[e].get(key, 0) >= val:
            return
        self.eng[e].wait_ge(self.sem[key], val)
        self.seen[e][key] = val
        clk = self.clock.get((key, val))
        if clk:
            se = self.seen[e]
            for k2, v2 in clk.items():
                if se.get(k2, 0) < v2:
                    se[k2] = v2

    def _deps(self, e, reads, writes):
        for r in reads:
            st = self.res.get(r)
            if st:
                for k, v in st["w"].items():
                    self._wait(e, k, v)
        for w in writes:
            st = self.res.get(w)
            if st:
                for k, v in st["w"].items():
                    self._wait(e, k, v)
                for k, v in st["r"].items():
                    self._wait(e, k, v)

    def _mark(self, key, val, reads, writes):
        for r in reads:
            st = self.res.setdefault(r, {"w": {}, "r": {}})
            st["r"][key] = max(st["r"].get(key, 0), val)
        for w in writes:
            st = self.res.setdefault(w, {"w": {}, "r": {}})
            st["w"][key] = max(st["w"].get(key, 0), val)

    def op(self, e, fn, reads=(), writes=()):
        self._deps(e, reads, writes)
        inst = fn(self.eng[e])
        self.cnt[e] += 1
        inst.then_inc(self.sem[e], 1)
        snap = dict(self.seen[e])
        snap[e] = self.cnt[e]
        self.clock[(e, self.cnt[e])] = snap
        self._mark(e, self.cnt[e], reads, writes)

    def dma(self, out, in_, reads=(), writes=(), q="sp"):
        n = self.ndma
        self.ndma += 1
        i = n % NDMA
        key = ("dma", i)
        if n >= NDMA:
            self._wait(q, key, 16 * (n // NDMA))
        self._deps(q, reads, writes)
        self.eng[q].dma_start(out=out, in_=in_).then_inc(self.sem[key], 16)
        self.clock[(key, 16 * (n // NDMA + 1))] = dict(self.seen[q])
        self._mark(key, 16 * (n // NDMA + 1), reads, writes)

    def barrier(self, engines=("pe", "act", "dve", "pool")):
        for e in engines:
            for e2 in engines:
                if e2 != e and self.cnt[e2] > 0:
                    self._wait(e, e2, self.cnt[e2])

    def finish(self):
        for k in list(self.sem.keys()):
            if isinstance(k, tuple):
                i = k[1]
                uses = (self.ndma - 1 - i) // NDMA + 1 if self.ndma > i else 0
                if uses > 0:
                    self._wait("sp", k, 16 * uses)
            else:
                if self.cnt[k] > 0:
                    self._wait("sp", k, self.cnt[k])
        self.es.close()


def build(nseq=5, depth=4, mixers=("a", "b", "c", "d"), dbg=False):
    kb = KB()
    nc = kb.nc
    x_d = nc.dram_tensor("x", [nseq, S, D], F32, kind="ExternalInput").ap()
    y_d = nc.dram_tensor("y", [nseq, S, D], F32, kind="ExternalOutput").ap()
    w_in_d = nc.dram_tensor("w_in", [4, D, 3072], F32, kind="ExternalInput").ap()
    w_out_d = nc.dram_tensor("w_out", [4, D, D], F32, kind="ExternalInput").ap()
    ng_d = nc.dram_tensor("norm_g_t", [128, 4, 8], F32, kind="ExternalInput").ap()
    fg_d = nc.dram_tensor("final_g_b", [128, D], F32, kind="ExternalInput").ap()
    pwb_d = nc.dram_tensor("pool_w_blk", [128, 4, 2, 128], F32, kind="ExternalInput").ap()
    psc_d = nc.dram_tensor("pool_scale_t", [128, 4, 2], F32, kind="ExternalInput").ap()
    pcn_d = nc.dram_tensor("pool_const", [128, 2, 17], F32, kind="ExternalInput").ap()
    dbg_d = nc.dram_tensor("dbg", [128, 8, S], BF16, kind="ExternalOutput").ap() if dbg else None
    rope_d = nc.dram_tensor("rope_t", [128, 2, NT, 32], F32, kind="ExternalInput").ap()
    band_d = nc.dram_tensor("band", [128, 256], BF16, kind="ExternalInput").ap()
    rpb_t = nc.dram_tensor("rpbpad", [4, 4, 15, 128], F32, kind="ExternalInput")
    nam_d = nc.dram_tensor("na_mask", [128, 2368], BF16, kind="ExternalInput").ap()
    et_d = kb.dram("et", [4, 4, 128, 2368], BF16)
    lam_d = nc.dram_tensor("ssm_lam", [4, 128, 16, 3], F32, kind="ExternalInput").ap()
    bp_d = nc.dram_tensor("ssm_bp", [4, 128, 16, 16, 2], F32, kind="ExternalInput").ap()
    cp_d = nc.dram_tensor("ssm_cp", [4, 128, 16, 16, 2], F32, kind="ExternalInput").ap()
    dt_d = nc.dram_tensor("ssm_dt", [16, 4, 16], F32, kind="ExternalInput").ap()
    glw_d = nc.dram_tensor("glu_w_t", [4, 128, 2, 256], F32, kind="ExternalInput").ap()
    glb_d = nc.dram_tensor("glu_b_t", [128, 4, 2], F32, kind="ExternalInput").ap()
    sblk_d = kb.dram("sblk", [4, 16, 128, 11, 128], BF16)
    etab_d = kb.dram("etab", [4, 4, 128, 4, 2, 128], F32)
    ktab_t = nc.dram_tensor("ktab", [4, 16, 31, 16, 16], F32, kind="Internal")
    ktab_d = ktab_t.ap()
    gwb_d = kb.dram("gwb", [4, 128, 2, 256], BF16)
    wib_d = kb.dram("wib", [4, 12, 128, 8, 256], BF16)
    wob_d = kb.dram("wob", [4, 4, 128, 8, 256], BF16)

    x_res = kb.sb("x_res", [128, NT, D], F32)
    hT = kb.sb("hT", [128, 8, S], BF16)
    yT = kb.sb("yT", [128, 8, S], BF16)
    ARENA = 56 * 1024
    arena = kb.sb("arena", [128, ARENA // 2], BF16)
    wb = [kb.sb("wb%d" % i, [128, 8, 256], BF16) for i in range(3)]
    ng = kb.sb("ng", [128, 4, 8], F32)
    ident = kb.sb("ident", [128, 128], BF16)
    identf = kb.sb("identf", [128, 128], F32)
    small = kb.sb("small", [128, 64], F32)
    pwb = kb.sb("pwb", [128, 4, 2, 128], BF16)
    psc = kb.sb("psc", [128, 4, 2], F32)
    pcn = kb.sb("pcn", [128, 2, 17], F32)
    ropet = kb.sb("ropet", [128, 2, NT, 32], F32)
    band = kb.sb("band_sb", [128, 256], BF16)
    rho_sb = kb.sb("rho_sb", [128, 4, 16], F32)
    glb = kb.sb("glb", [128, 4, 2], F32)
    pp = [kb.ps("pp%d" % i, [128, 512], F32) for i in range(8)]

    GR = 256

    def ak(off, nbytes):
        return [("ar", j) for j in range(off // GR, (off + nbytes - 1) // GR + 1)]

    dumped = set()

    def dump(name, src, reads):
        if not dbg or name in dumped:
            return
        dumped.add(name)
        shp = list(src.shape)
        dd = nc.dram_tensor("d_" + name, shp, src.dtype, kind="ExternalOutput").ap()
        kb.dma(dd, src, reads=reads, writes=["dump_" + name])

    def ky(c0, c1=None, h=None):
        c1 = c0 + 1 if c1 is None else c1
        hs_ = (0, 1) if h is None else (h,)
        return [("yT", c, hh) for c in range(c0, c1) for hh in hs_]

    def av(off, shape, dt):
        n = int(np.prod(shape))
        esz = 4 if dt in (F32, I32) else 2
        a = arena[:, off // 2: off // 2 + n * esz // 2]
        if dt != BF16:
            a = a.bitcast(dt)
        if len(shape) == 2:
            return a.rearrange("p (a b) -> p a b", a=shape[0])
        if len(shape) == 3:
            return a.rearrange("p (a b c) -> p a b c", a=shape[0], b=shape[1])
        return a

    pstate = {"n": 0}

    def next_bank(lo=0, hi=2):
        i = lo + pstate.setdefault((lo, hi), 0) % (hi - lo)
        pstate[(lo, hi)] += 1
        return i

    hs = [av(16384 + i * 2048, [D], BF16) for i in range(6)]
    khs = [ak(16384 + i * 2048, 2048) for i in range(6)]
    kb.dma(ng[:], ng_d, writes=["ng"])
    pwst = av(8192, [4 * 2 * 128], F32)
    kb.dma(pwst, pwb_d.rearrange("p a b c -> p (a b c)"), writes=ak(8192, 4096))
    kb.op("dve", lambda v: v.tensor_copy(out=pwb[:].rearrange("p a b c -> p (a b c)"), in_=pwst), reads=ak(8192, 4096), writes=["pwb"])
    kb.dma(psc[:], psc_d, writes=["psc"])
    kb.op("dve", lambda v: v.tensor_scalar(out=psc[:], in0=psc[:], scalar1=0.5, scalar2=None, op0=ALU.mult), reads=["psc"], writes=["psc"])
    kb.dma(pcn[:], pcn_d, writes=["pcn"])
    kb.dma(ropet[:], rope_d, writes=["ropet"])
    kb.dma(band[:], band_d, writes=["band"])
    io = av(0, [128], I32)
    kb.op("pool", lambda g: g.iota(io, [[1, 128]], base=0, channel_multiplier=-1), writes=ak(0, 512))
    kb.op("dve", lambda v: v.tensor_single_scalar(out=identf[:], in_=io, scalar=0, op=ALU.is_equal),
          reads=ak(0, 512), writes=["identf"])
    kb.op("dve", lambda v: v.tensor_copy(out=ident[:], in_=identf[:]), reads=["identf"], writes=["ident"])

    for l in range(depth):
        for c in range(8):
            k = (l * 8 + c) % 2
            st = av(k * 12288, [3072], F32)
            sb_ = av(24576 + k * 6144, [3072], BF16)
            kst, ksb = ak(k * 12288, 12288), ak(24576 + k * 6144, 6144)
            kb.dma(st, w_in_d[l, c * 128:(c + 1) * 128, :], writes=kst)
            kb.op("pool" if k else "dve",
                  lambda v, st=st, sb_=sb_, l=l, c=c: v.tensor_scalar(out=sb_, in0=st, scalar1=ng[:, l, c:c + 1],
                                                                        scalar2=None, op0=ALU.mult),
                  reads=kst + ["ng"], writes=ksb)
            kb.dma(wib_d[l, :, :, c, :].rearrange("b p n -> p b n"), sb_.rearrange("p (b n) -> p b n", b=12),
                   reads=ksb, writes=["wib"])
        for c in range(8):
            k = (l * 8 + c) % 2
            st = av(36864 + k * 4096, [1024], F32)
            sb_ = av(36864 + 8192 + k * 2048, [1024], BF16)
            kst, ksb = ak(36864 + k * 4096, 4096), ak(36864 + 8192 + k * 2048, 2048)
            kb.dma(st, w_out_d[l, c * 128:(c + 1) * 128, :], writes=kst)
            kb.op("act", lambda a, st=st, sb_=sb_: a.copy(out=sb_, in_=st), reads=kst, writes=ksb)
            kb.dma(wob_d[l, :, :, c, :].rearrange("h p n -> p h n"), sb_.rearrange("p (h n) -> p h n", h=4),
                   reads=ksb, writes=["wob"])

    if "d" in mixers:
        nmask = av(0, [2368], BF16)
        kb.dma(nmask, nam_d, writes=ak(0, 4736))
        negm = av(33152, [2368], F32)
        knegm = ak(33152, 9472)
        kb.op("dve", lambda v: v.tensor_scalar(out=negm, in0=nmask, scalar1=-1.0, scalar2=240000.0, op0=ALU.add, op1=ALU.mult),
              reads=ak(0, 4736), writes=knegm)
        for l in range(depth):
            for h in range(4):
                k = (l * 4 + h) % 2
                o_st = 4736 + k * 14208
                stg = av(o_st, [2368], F32)
                eo = av(o_st + 9472, [2368], BF16)
                kst, keo = ak(o_st, 9472), ak(o_st + 9472, 4736)
                base = (l * 4 + h) * 15 * 128
                for krl in range(2):
                    ps_ = slice(krl * 64, krl * 64 + 64)
                    src = bass.AP(rpb_t, base + (4 - krl) * 128, [[1, 64], [128, 9], [1, 64]])
                    kb.dma(stg[ps_, 0:576].rearrange("p (r c) -> p r c", r=9), src, writes=kst)
                    for sr in range(7):
                        r = sr if sr < 4 else 25 + sr
                        if sr < 4:
                            i0 = 1 - krl + r
                            dst = stg[ps_, 576:1600].rearrange("p (u s c) -> p u s c", u=4, s=4)[:, :, sr, :]
                        else:
                            i0 = r - 23 - krl
                            dst = stg[ps_, 1600:2368].rearrange("p (u s c) -> p u s c", u=4, s=3)[:, :, sr - 4, :]
                        src = bass.AP(rpb_t, base + i0 * 128, [[1, 64], [256, 4], [1, 64]])
                        kb.dma(dst, src, writes=kst)
                st3 = stg.rearrange("p (b c) -> p b c", c=64)
                kb.op("dve", lambda v, st3=st3: v.scalar_tensor_tensor(
                    out=st3, in0=st3, scalar=8.0, in1=nmask.rearrange("p (b c) -> p b c", c=64),
                    op0=ALU.mult, op1=ALU.mult), reads=kst + ak(0, 4736), writes=kst)
                kb.op("dve", lambda v, st3=st3, eo=eo: v.tensor_tensor(
                    out=eo.rearrange("p (b c) -> p b c", c=64), in0=st3[:, :, ::-1], in1=negm.rearrange("p (b c) -> p b c", c=64)[:, :, ::-1],
                    op=ALU.add), reads=kst + knegm, writes=keo)
                kb.dma(et_d[l, h], eo, reads=keo, writes=["et"])

    TWO_PI = 2.0 * math.pi

    def s5_precompute(l):
        st = {"o": 0}

        def A(shape, dt=F32):
            n = int(np.prod(shape))
            nb = n * (4 if dt in (F32, I32) else 2)
            off = st["o"]
            st["o"] = off + (nb + 63) // 64 * 64
            v = arena[:, off // 2: off // 2 + nb // 2]
            if dt != BF16:
                v = v.bitcast(dt)
            if len(shape) == 2:
                v = v.rearrange("p (a b) -> p a b", a=shape[0])
            elif len(shape) == 3:
                v = v.rearrange("p (a b c) -> p a b c", a=shape[0], b=shape[1])
            return v, ak(off, nb)

        def tt_(e, out, a, b, op, rd, wr):
            kb.op(e, lambda v: v.tensor_tensor(out=out, in0=a, in1=b, op=op), reads=rd, writes=wr)

        def ts_(e, out, a, s1, s2, op0, op1, rd, wr):
            kb.op(e, lambda v: v.tensor_scalar(out=out, in0=a, scalar1=s1, scalar2=s2, op0=op0, op1=op1) if s2 is not None
                  else v.tensor_scalar(out=out, in0=a, scalar1=s1, scalar2=None, op0=op0), reads=rd, writes=wr)

        def frac_(T, kT, TI, kTI, TF, kTF):
            MAGIC = 12582912.0
            ts_("dve", TF, T, MAGIC, None, ALU.add, None, kT, kTF)
            ts_("dve", TF, TF, -MAGIC, None, ALU.add, None, kTF, kTF)
            tt_("dve", T, T, TF, ALU.subtract, kT + kTF, kT)
            kb.op("dve", lambda v: v.tensor_single_scalar(out=TF, in_=T, scalar=0.5, op=ALU.is_gt), reads=kT, writes=kTF)
            tt_("dve", T, T, TF, ALU.subtract, kT + kTF, kT)
            kb.op("dve", lambda v: v.tensor_single_scalar(out=TF, in_=T, scalar=-0.5, op=ALU.is_lt), reads=kT, writes=kTF)
            tt_("dve", T, T, TF, ALU.add, kT + kTF, kT)

        def sincos_(T, kT, SN, kSN, CS, kCS, TI, kTI, TF, kTF):
            frac_(T, kT, TI, kTI, TF, kTF)
            kb.op("act", lambda a: a.activation(out=SN, in_=T, func=AF.Sin, scale=6.28318), reads=kT, writes=kSN)
            ts_("dve", T, T, 0.25, None, ALU.add, None, kT, kT)
            kb.op("dve", lambda v: v.tensor_single_scalar(out=TF, in_=T, scalar=0.5, op=ALU.is_gt), reads=kT, writes=kTF)
            tt_("dve", T, T, TF, ALU.subtract, kT + kTF, kT)
            kb.op("act", lambda a: a.activation(out=CS, in_=T, func=AF.Sin, scale=6.28318), reads=kT, writes=kCS)

        lam, klam = A([16, 3])
        Bp, kBp = A([16, 16, 2])
        Cp, kCp = A([16, 16, 2])
        Dt, kDt = A([16])
        kb.dma(lam, lam_d[l], writes=klam)
        kb.dma(Bp, bp_d[l], writes=kBp)
        kb.dma(Cp, cp_d[l], writes=kCp)
        kb.dma(Dt[0:16, :], dt_d[:, l, :], writes=kDt)
        BBr, kBBr = A([16, 16])
        BBi, kBBi = A([16, 16])
        nBBi, knBBi = A([16, 16])
        WP, kWP = A([2, 16, 16])
        WC, kWC = A([2, 16, 16])
        WK, kWK = A([2, 16, 31])
        mark = st["o"]
        dtt, kdt = A([16])
        xr, kxr = A([16])
        tht, ktht = A([16])
        NNi, kNNi = A([128], I32)
        NN, kNN = A([128])
        TI, kTI = A([512], I32)
        TF, kTF = A([512])
        ARG, kARG = A([16, 17])
        MAG, kMAG = A([16, 17])
        SN, kSN = A([16, 17])
        CS, kCS = A([16, 17])
        WR, kWR = A([16, 17])
        WI, kWI = A([16, 17])
        kb.op("act", lambda a: a.activation(out=dtt, in_=lam[:, :, 2], func=AF.Exp), reads=klam, writes=kdt)
        tt_("dve", xr, lam[:, :, 0], dtt, ALU.mult, klam + kdt, kxr)
        tt_("dve", tht, lam[:, :, 1], dtt, ALU.mult, klam + kdt, ktht)
        ts_("dve", tht, tht, 1.0 / TWO_PI, None, ALU.mult, None, ktht, ktht)
        frac_(tht, ktht, TI[:, 0:16], kTI, TF[:, 0:16], kTF)
        kb.op("pool", lambda g: g.iota(NNi, [[1, 128]], base=0, channel_multiplier=0), writes=kNNi)
        kb.op("dve", lambda v: v.tensor_copy(out=NN, in_=NNi), reads=kNNi, writes=kNN)
        nb17 = NN[:, 0:17].unsqueeze(1).broadcast_to([128, 16, 17])
        tt_("dve", ARG, tht.unsqueeze(2).broadcast_to([128, 16, 17]), nb17, ALU.mult, ktht + kNN, kARG)
        tt_("dve", MAG, xr.unsqueeze(2).broadcast_to([128, 16, 17]), nb17, ALU.mult, kxr + kNN, kMAG)
        kb.op("act", lambda a: a.activation(out=MAG, in_=MAG, func=AF.Exp), reads=kMAG, writes=kMAG)
        f2 = lambda t: t.rearrange("p a b -> p (a b)")
        sincos_(f2(ARG), kARG, f2(SN), kSN, f2(CS), kCS, TI[:, 0:272], kTI, TF[:, 0:272], kTF)
        tt_("dve", WR, MAG, CS, ALU.mult, kMAG + kCS, kWR)
        tt_("dve", WI, MAG, SN, ALU.mult, kMAG + kSN, kWI)
        if l == 0:
            dump("WR", WR, kWR)
            dump("WI", WI, kWI)
            dump("dtt", dtt, kdt)
            dump("xr", xr, kxr)
            dump("tht", tht, ktht)
            dump("NN", NN, kNN)
            dump("MAG", MAG, kMAG)
            dump("SN", SN, kSN)
            dump("CS", CS, kCS)
            dump("lam", lam, klam)
        kb.op("act", lambda a: a.copy(out=rho_sb[:, l, :], in_=MAG[:, :, 16]), reads=kMAG, writes=["rho"])
        den, kden = A([16])
        t1, kt1 = A([16])
        t2, kt2 = A([16])
        gr, kgr = A([16])
        gi, kgi = A([16])
        lr_, li_ = lam[:, :, 0], lam[:, :, 1]
        tt_("dve", den, lr_, lr_, ALU.mult, klam, kden)
        tt_("dve", t1, li_, li_, ALU.mult, klam, kt1)
        tt_("dve", den, den, t1, ALU.add, kden + kt1, kden)
        kb.op("dve", lambda v: v.reciprocal(out=den, in_=den), reads=kden, writes=kden)
        ts_("dve", t1, WR[:, :, 1], -1.0, None, ALU.add, None, kWR, kt1)
        tt_("dve", gr, t1, lr_, ALU.mult, kt1 + klam, kgr)
        tt_("dve", t2, WI[:, :, 1], li_, ALU.mult, kWI + klam, kt2)
        tt_("dve", gr, gr, t2, ALU.add, kgr + kt2, kgr)
        tt_("dve", gr, gr, den, ALU.mult, kgr + kden, kgr)
        tt_("dve", gi, WI[:, :, 1], lr_, ALU.mult, kWI + klam, kgi)
        tt_("dve", t2, t1, li_, ALU.mult, kt1 + klam, kt2)
        tt_("dve", gi, gi, t2, ALU.subtract, kgi + kt2, kgi)
        tt_("dve", gi, gi, den, ALU.mult, kgi + kden, kgi)
        u1, ku1 = A([16, 16])
        grb = gr.unsqueeze(2).broadcast_to([128, 16, 16])
        gib = gi.unsqueeze(2).broadcast_to([128, 16, 16])
        Br_, Bi_ = Bp[:, :, :, 0], Bp[:, :, :, 1]
        tt_("dve", BBr, grb, Br_, ALU.mult, kgr + kBp, kBBr)
        tt_("dve", u1, gib, Bi_, ALU.mult, kgi + kBp, ku1)
        tt_("dve", BBr, BBr, u1, ALU.subtract, kBBr + ku1, kBBr)
        tt_("dve", BBi, grb, Bi_, ALU.mult, kgr + kBp, kBBi)
        tt_("dve", u1, gib, Br_, ALU.mult, kgi + kBp, ku1)
        tt_("dve", BBi, BBi, u1, ALU.add, kBBi + ku1, kBBi)
        ts_("dve", nBBi, BBi, -1.0, None, ALU.mult, None, kBBi, knBBi)
        kb.op("pool", lambda g: g.memset(WK, 0.0), writes=kWK)
        F_, B_ = slice(0, 64), slice(64, 128)
        for ri, W_, kW_ in ((0, WR, kWR), (1, WI, kWI)):
            cp = lambda dst, src, wk: kb.op("act", lambda a: a.copy(out=dst, in_=src), reads=kW_, writes=wk)
            cp(WP[F_, ri, :, 0:8], W_[F_, :, 8:16], kWP)
            cp(WP[F_, ri, :, 8:16], W_[F_, :, 0:8], kWP)
            cp(WP[B_, ri, :, 0:8], W_[B_, :, 7::-1], kWP)
            cp(WP[B_, ri, :, 8:16], W_[B_, :, 15:7:-1], kWP)
            cp(WC[F_, ri, :, :], W_[F_, :, 1:17], kWC)
            cp(WC[B_, ri, :, :], W_[B_, :, 16:0:-1], kWC)
            cp(WK[F_, ri, :, 15:31], W_[F_, :, 0:16], kWK)
            cp(WK[B_, ri, :, 0:16], W_[B_, :, 15::-1], kWK)
        pht, kpht = A([16])
        ts_("dve", pht, tht, 16.0, None, ALU.mult, None, ktht, kpht)
        frac_(pht, kpht, TI[:, 0:16], kTI, TF[:, 0:16], kTF)
        EA, kEA = A([4, 128])
        ES, kES = A([4, 2, 128])
        for q in range(4):
            tt_("dve", EA, pht[:, 4 * q:4 * q + 4].unsqueeze(2).broadcast_to([128, 4, 128]),
                NN.unsqueeze(1).broadcast_to([128, 4, 128]), ALU.mult, kpht + kNN, kEA)
            sincos_(f2(EA), kEA, ES[:, :, 1, :], kES, ES[:, :, 0, :], kES, TI[:, 0:512], kTI, TF[:, 0:512], kTF)
            kb.dma(etab_d[l, q], ES, reads=kES, writes=["etab"])
            if l == 0 and q == 0:
                dump("ES0", ES, kES)
        gst, kgst = A([2, 256])
        gbf, kgbf = A([2, 256], BF16)
        kb.dma(gst, glw_d[l], writes=kgst)
        kb.op("act", lambda a: a.copy(out=gbf, in_=gst), reads=kgst, writes=kgbf)
        kb.dma(gwb_d[l], gbf, reads=kgbf, writes=["gwb"])
        st["o"] = mark
        bufs = []
        for par in range(2):
            d_ = {}
            d_["PPr"], d_["kPPr"] = A([2, 128])
            d_["PPi"], d_["kPPi"] = A([2, 128])
            d_["q1"], d_["kq1"] = A([2, 128])
            d_["q2"], d_["kq2"] = A([2, 128])
            d_["Rr"], d_["kRr"] = A([31, 16])
            d_["Ri"], d_["kRi"] = A([31, 16])
            d_["r1"], d_["kr1"] = A([31, 16])
            d_["r2"], d_["kr2"] = A([31, 16])
            d_["Ks"], d_["kKs"] = A([496])
            d_["Ms"], d_["kMs"] = A([3, 128])
            d_["blk"], d_["kblk"] = A([11, 128], BF16)
            bufs.append(d_)
        assert st["o"] <= ARENA, st["o"]
        for g in range(16):
            d_ = bufs[g % 2]
            PPr, PPi, q1, q2 = d_["PPr"], d_["PPi"], d_["q1"], d_["q2"]
            kPPr, kPPi, kq1, kq2 = d_["kPPr"], d_["kPPi"], d_["kq1"], d_["kq2"]
            blk, kblk = d_["blk"], d_["kblk"]
            v4 = lambda t: t.rearrange("p s (j c) -> p s j c", j=8)
            wpr = WP[:, 0, g, :].rearrange("p (s j) -> p s j", s=2).unsqueeze(3).broadcast_to([128, 2, 8, 16])
            wpi = WP[:, 1, g, :].rearrange("p (s j) -> p s j", s=2).unsqueeze(3).broadcast_to([128, 2, 8, 16])
            bbr = BBr[:, g, :].unsqueeze(1).unsqueeze(1).broadcast_to([128, 2, 8, 16])
            bbi = BBi[:, g, :].unsqueeze(1).unsqueeze(1).broadcast_to([128, 2, 8, 16])
            e1, e2 = ("dve", "pool") if g % 2 == 0 else ("pool", "dve")
            tt_(e1, v4(q1), wpr, bbr, ALU.mult, kWP + kBBr, kq1)
            tt_(e2, v4(q2), wpi, bbi, ALU.mult, kWP + kBBi, kq2)
            tt_(e1, PPr, q1, q2, ALU.subtract, kq1 + kq2, kPPr)
            tt_(e2, v4(q1), wpr, bbi, ALU.mult, kWP + kBBi, kq1)
            tt_(e1, v4(q2), wpi, bbr, ALU.mult, kWP + kBBr, kq2)
            tt_(e2, PPi, q1, q2, ALU.add, kq1 + kq2, kPPi)
            bt = next_bank(6, 8)
            for j, (src, ksrc) in enumerate(((PPr[:, 0, :], kPPr), (PPr[:, 1, :], kPPr), (PPi[:, 0, :], kPPi), (PPi[:, 1, :], kPPi))):
                kb.op("pe", lambda pe, j=j, src=src: pe.transpose(pp[bt][:, j * 128:(j + 1) * 128], src, identf[:]),
                      reads=ksrc + ["identf"], writes=[("pp", bt)])
            kb.op("act", lambda a: a.copy(out=blk[:, 3:7, :], in_=pp[bt][:, :].rearrange("p (a b) -> p a b", a=4)),
                  reads=[("pp", bt)], writes=kblk)
            c4 = lambda t: t.rearrange("p s (i c) -> p (s i) c", i=8)
            wcr = WC[:, 0, g, :].unsqueeze(2).broadcast_to([128, 16, 16])
            wci = WC[:, 1, g, :].unsqueeze(2).broadcast_to([128, 16, 16])
            cr = Cp[:, g, :, 0].unsqueeze(1).broadcast_to([128, 16, 16])
            ci = Cp[:, g, :, 1].unsqueeze(1).broadcast_to([128, 16, 16])
            tt_(e1, c4(q1), cr, wcr, ALU.mult, kCp + kWC, kq1)
            tt_(e2, c4(q2), ci, wci, ALU.mult, kCp + kWC, kq2)
            tt_(e1, blk[:, 7:10:2, :], q1, q2, ALU.subtract, kq1 + kq2, kblk)
            tt_(e2, c4(q1), cr, wci, ALU.mult, kCp + kWC, kq1)
            tt_(e1, c4(q2), ci, wcr, ALU.mult, kCp + kWC, kq2)
            kb.op("dve", lambda v: v.scalar_tensor_tensor(out=blk[:, 8:11:2, :], in0=q1, scalar=-1.0, in1=q2,
                                                           op0=ALU.mult, op1=ALU.subtract),
                  reads=kq1 + kq2, writes=kblk)
            Rr, Ri, r1, r2 = d_["Rr"], d_["Ri"], d_["r1"], d_["r2"]
            kRr, kRi, kr1, kr2 = d_["kRr"], d_["kRi"], d_["kr1"], d_["kr2"]
            wkr = WK[:, 0, g, :].unsqueeze(2).broadcast_to([128, 31, 16])
            wki = WK[:, 1, g, :].unsqueeze(2).broadcast_to([128, 31, 16])
            cr3 = Cp[:, g, :, 0].unsqueeze(1).broadcast_to([128, 31, 16])
            ci3 = Cp[:, g, :, 1].unsqueeze(1).broadcast_to([128, 31, 16])
            tt_(e1, r1, cr3, wkr, ALU.mult, kCp + kWK, kr1)
            tt_(e2, r2, ci3, wki, ALU.mult, kCp + kWK, kr2)
            tt_(e1, Rr, r1, r2, ALU.subtract, kr1 + kr2, kRr)
            tt_(e2, r1, cr3, wki, ALU.mult, kCp + kWK, kr1)
            tt_(e1, r2, ci3, wkr, ALU.mult, kCp + kWK, kr2)
            tt_(e2, Ri, r1, r2, ALU.add, kr1 + kr2, kRi)
            bk = next_bank(4, 6)
            kb.op("pe", lambda pe: pe.matmul(pp[bk][0:16, 0:496], BBr[:, g, :], Rr.rearrange("p a b -> p (a b)"),
                                             start=True, stop=False), reads=kBBr + kRr, writes=[("pp", bk)])
            kb.op("pe", lambda pe: pe.matmul(pp[bk][0:16, 0:496], nBBi[:, g, :], Ri.rearrange("p a b -> p (a b)"),
                                             start=False, stop=True), reads=knBBi + kRi, writes=[("pp", bk)])
            Ks, kKs = d_["Ks"], d_["kKs"]
            kb.op("act", lambda a: a.copy(out=Ks[0:16, :], in_=pp[bk][0:16, 0:496]), reads=[("pp", bk)], writes=kKs)
            kb.op("dve", lambda v: v.scalar_tensor_tensor(out=Ks[0:16, 240:256], in0=identf[0:16, 0:16],
                                                           scalar=Dt[0:16, g:g + 1], in1=Ks[0:16, 240:256],
                                                           op0=ALU.mult, op1=ALU.add),
                  reads=kKs + kDt + ["identf"], writes=kKs)
            kb.dma(ktab_d[l, g].rearrange("i c o -> c i o"), Ks[0:16, :].rearrange("p (i o) -> p i o", i=31),
                   reads=kKs, writes=[("ktab", l, g)])
            Ms, kMs = d_["Ms"], d_["kMs"]
            kbase = (l * 16 + g) * 31 * 256
            for bi_, boff in enumerate((8, 16, 0)):
                src = bass.AP(ktab_t, kbase + boff * 256, [[16, 128], [256, 8], [1, 16]])
                kb.dma(Ms[:, bi_, :].rearrange("p (i o) -> p i o", i=8), src, reads=[("ktab", l, g)], writes=kMs)
            kb.op("act", lambda a: a.copy(out=blk[:, 0:3, :], in_=Ms), reads=kMs, writes=kblk)
            kb.dma(sblk_d[l, g], blk, reads=kblk, writes=["sblk"])
            if l == 0 and g == 0:
                dump("blk0", blk, kblk)
                dump("Ks0", Ks[0:16, :], kKs)
                dump("BBr", BBr, kBBr)
                dump("BBi", BBi, kBBi)
                dump("WP", WP, kWP)
                dump("WC", WC, kWC)
                dump("WK", WK, kWK)
                dump("PPr", PPr, kPPr)

    if "c" in mixers:
        kb.dma(glb[:], glb_d, writes=["glb"])
        kb.op("dve", lambda v: v.tensor_scalar(out=glb[:], in0=glb[:], scalar1=0.5, scalar2=None, op0=ALU.mult),
              reads=["glb"], writes=["glb"])
        for l in range(depth):
            s5_precompute(l)

    wstate = {"n": 0}

    def load_wblock(l, b):
        k = wstate["n"] % 3
        wstate["n"] += 1
        kb.dma(wb[k][:], wib_d[l, b], reads=["wib"], writes=[("wb", k)])
        return k

    def load_woblock(l, h):
        k = wstate["n"] % 3
        wstate["n"] += 1
        kb.dma(wb[k][:], wob_d[l, h], reads=["wob"], writes=[("wb", k)])
        return k


    def proj_fm(k, cb, evac, tqs=range(4), M=128, moff=0):
        for tq in tqs:
            bi = next_bank(0, 2)
            for c in range(8):
                kb.op("pe", lambda pe, c=c, bi=bi, tq=tq: pe.matmul(
                    pp[bi][0:M, :], wb[k][:, c, cb * 128 + moff: cb * 128 + moff + M], hT[:, c, tq * 512:(tq + 1) * 512],
                    start=(c == 0), stop=(c == 7)),
                    reads=[("wb", k), "hT"], writes=[("pp", bi)])
            evac(tq, bi)

    def proj_tm(k, tok_ap_fn, ntiles, evac, ncols=256):
        for i in range(ntiles):
            bi = next_bank(0, 2)
            for c in range(8):
                kb.op("pe", lambda pe, c=c, bi=bi, i=i: pe.matmul(
                    pp[bi][:, 0:ncols], tok_ap_fn(c, i), wb[k][:, c, 0:ncols], start=(c == 0), stop=(c == 7)),
                    reads=[("wb", k), "hT"], writes=[("pp", bi)])
            evac(i, bi)

    def rms_all():
        for i in range(NT):
            junk = hs[4 + i % 2]
            kb.op("act", lambda a, i=i, junk=junk: a.activation(out=junk, in_=x_res[:, i, :], func=AF.Square,
                                                                accum_out=small[:, 16 + i:17 + i]),
                  reads=[("x", i)], writes=khs[4 + i % 2] + ["ss"])
        kb.op("dve", lambda v: v.tensor_scalar(out=small[:, 32:48], in0=small[:, 16:32],
                                               scalar1=1.0 / D, scalar2=EPS, op0=ALU.mult, op1=ALU.add),
              reads=["ss"], writes=["ms"])
        kb.op("pool", lambda g: g.tensor_tensor(out=small[:, 0:16], in0=small[:, 32:48],
                                                in1=small[:, 48:49].broadcast_to([128, 16]), op=ALU.pow),
              reads=["ms", "mhalf"], writes=["rstd"])

    kb.op("dve", lambda v: v.memset(small[:, 48:49], -0.5), writes=["mhalf"])
    nlh = small[:, 49:50]
    kb.op("dve", lambda v: v.memset(nlh, -math.log(2.0)), writes=["nlh"])

    W = S + 32

    def mixer_a(l):
        kU, kA, kB, kg2, kDm = ak(0, 8320), ak(8320, 8320), ak(16640, 8320), ak(24960, 4096), ak(29056, 4096)
        ktt = [ak(33152 + j * 2048, 2048) for j in range(2)]
        U = av(0, [W], F32)
        A = av(8320, [W], F32)
        B = av(16640, [W], F32)
        g2 = av(24960, [S], BF16)
        Dm = av(29056, [S], BF16)
        tt = [av(33152 + j * 2048, [512], F32) for j in range(2)]
        for cb in range(2):
            kv = load_wblock(l, 0)
            kg = load_wblock(l, 1)
            kb.op("pool", lambda g: g.memset(U[:, 0:16], 0.0), writes=kU)
            kb.op("pool", lambda g: g.memset(U[:, 16 + S:W], 0.0), writes=kU)

            def ev_u(tq, bi):
                kb.op("act", lambda a: a.copy(out=U[:, 16 + tq * 512:16 + (tq + 1) * 512], in_=pp[bi][:, :]),
                      reads=[("pp", bi)], writes=kU)
            proj_fm(kv, cb, ev_u)
            kb.op("pool", lambda g: g.tensor_tensor(out=A[:, 1:W], in0=U[:, 0:W - 1], in1=U[:, 1:W], op=ALU.add),
                  reads=kU, writes=kA)
            kb.op("pool", lambda g: g.tensor_tensor(out=B[:, 2:W - 1], in0=A[:, 1:W - 2], in1=A[:, 3:W], op=ALU.add),
                  reads=kA, writes=kB)
            if cb == 1:
                kb.op("pool", lambda g: g.tensor_tensor(out=A[:, 4:W - 3], in0=B[:, 2:W - 5], in1=B[:, 6:W - 1], op=ALU.add),
                      reads=kB, writes=kA)
                kb.op("pool", lambda g: g.tensor_tensor(out=B[64:128, 8:W - 7], in0=A[64:128, 4:W - 11],
                                                        in1=A[64:128, 12:W - 3], op=ALU.add),
                      reads=kA, writes=kB)
            for (buf, nm, p0) in ((A, kA, 0), (B, kB, 64)):
                sl = slice(p0, p0 + 64)
                kb.op("dve", lambda v, buf=buf, sl=sl: v.tensor_tensor(
                    out=buf[sl, 16:24], in0=buf[sl, 16:24], in1=pcn[sl, cb, 0:8], op=ALU.mult),
                    reads=nm + ["pcn"], writes=nm)
                kb.op("dve", lambda v, buf=buf, sl=sl: v.tensor_tensor(
                    out=buf[sl, 8 + S:16 + S], in0=buf[sl, 8 + S:16 + S], in1=pcn[sl, cb, 8:16], op=ALU.mult),
                    reads=nm + ["pcn"], writes=nm)
                kb.op("dve", lambda v, buf=buf, sl=sl: v.scalar_tensor_tensor(
                    out=Dm[sl, :], in0=buf[sl, 16:16 + S], scalar=pcn[sl, cb, 16:17], in1=U[sl, 16:16 + S],
                    op0=ALU.mult, op1=ALU.subtract),
                    reads=nm + kU + ["pcn"], writes=kDm)

            def ev_g(tq, bi):
                t = tt[tq % 2]
                kb.op("act", lambda a: a.activation(out=t, in_=pp[bi][:, :], func=AF.Tanh, scale=0.5),
                      reads=[("pp", bi)], writes=ktt[tq % 2])
                kb.op("dve", lambda v: v.scalar_tensor_tensor(
                    out=g2[:, tq * 512:(tq + 1) * 512], in0=t, scalar=1.0, in1=pp[bi][:, :], op0=ALU.add, op1=ALU.mult),
                    reads=ktt[tq % 2] + [("pp", bi)], writes=kg2)
            proj_fm(kg, cb, ev_g)
            for tq in range(4):
                bi = next_bank(0, 2)
                kb.op("pe", lambda pe: pe.matmul(pp[bi][:, :], pwb[:, l, cb, :], Dm[:, tq * 512:(tq + 1) * 512],
                                                 start=True, stop=True),
                      reads=["pwb"] + kDm, writes=[("pp", bi)])
                kb.op("dve", lambda v: v.scalar_tensor_tensor(
                    out=yT[:, cb, tq * 512:(tq + 1) * 512], in0=pp[bi][:, :], scalar=psc[:, l, cb:cb + 1],
                    in1=g2[:, tq * 512:(tq + 1) * 512], op0=ALU.mult, op1=ALU.mult),
                    reads=[("pp", bi), "psc"] + kg2, writes=ky(cb))

    def run_pipeline(tasks, la):
        n = len(tasks)
        for i in range(n + la):
            if i < n:
                t = tasks[i]
                t["slot"] = i
                if t.get("pre"):
                    t["pre"]()
                t["s1"]()
            if i >= la:
                t = tasks[i - la]
                t["s2"]()
                if t.get("post"):
                    t["post"]()

    def run_pipeline_b(tasks, bs):
        n = len(tasks)
        nb = (n + bs - 1) // bs
        for b in range(nb + 1):
            if b < nb:
                for i in range(b * bs, min(n, (b + 1) * bs)):
                    t = tasks[i]
                    t["slot"] = i
                    if t.get("pre"):
                        t["pre"]()
                for i in range(b * bs, min(n, (b + 1) * bs)):
                    tasks[i]["s1a"]()
                for i in range(b * bs, min(n, (b + 1) * bs)):
                    tasks[i]["s1b"]()
            if b >= 1:
                for i in range((b - 1) * bs, min(n, b * bs)):
                    t = tasks[i]
                    t["s2"]()
                    if t.get("post"):
                        t["post"]()

    def gate_fm(k, kg2, g2, ktt, tt):
        for cb in range(2):
            def ev_g(tq, bi):
                t = tt[tq % 2]
                kb.op("act", lambda a: a.activation(out=t, in_=pp[bi][:, :], func=AF.Tanh, scale=0.5),
                      reads=[("pp", bi)], writes=ktt[tq % 2])
                kb.op("dve", lambda v: v.scalar_tensor_tensor(
                    out=g2[:, cb, tq * 512:(tq + 1) * 512], in0=t, scalar=1.0, in1=pp[bi][:, :], op0=ALU.add, op1=ALU.mult),
                    reads=ktt[tq % 2] + [("pp", bi)], writes=kg2)
            proj_fm(k, cb, ev_g)

    def psl(p0, n, d, n_sub):
        st = (p0 % n_sub) * d + p0 // n_sub
        return slice(st, st + (n - 1) * d + 1, d)

    def attn_finalize(h, acc, kacc, g2, kg2, rc, krc, tmp, ktmp, chunk0):
        nr = slice((h % 2) * 64, (h % 2) * 64 + 64)
        dr = slice(((h + 1) % 2) * 64, ((h + 1) % 2) * 64 + 64)
        for tq in range(8):
            ts_ = slice(tq * 256, (tq + 1) * 256)
            kb.op("dve", lambda v: v.reciprocal(out=rc[tq % 2][nr, :], in_=acc[dr, ts_]),
                  reads=kacc, writes=krc[tq % 2])
            kb.op("dve", lambda v: v.scalar_tensor_tensor(out=tmp[tq % 2][nr, :], in0=acc[nr, ts_], scalar=0.5,
                                                           in1=rc[tq % 2][nr, :], op0=ALU.mult, op1=ALU.mult),
                  reads=kacc + krc[tq % 2], writes=ktmp[tq % 2])
            kb.op("pool", lambda g: g.tensor_tensor(out=yT[nr, chunk0 + h // 2, ts_], in0=tmp[tq % 2][nr, :],
                                                    in1=g2[nr, h // 2, ts_], op=ALU.mult),
                  reads=ktmp[tq % 2] + kg2, writes=ky(chunk0 + h // 2, h=h % 2))

    def gate_apply(l, blk, chunk0, tt, ktt, gq, kgq):
        k = load_wblock(l, blk)
        for cb in range(2):
            def ev_g(tq, bi):
                t = tt[tq % 2]
                g_ = gq[tq % 2]
                ts_ = slice(tq * 512, (tq + 1) * 512)
                kb.op("act", lambda a: a.activation(out=t, in_=pp[bi][:, :], func=AF.Tanh, scale=0.5),
                      reads=[("pp", bi)], writes=ktt[tq % 2])
                kb.op("dve", lambda v: v.scalar_tensor_tensor(out=g_, in0=t, scalar=1.0, in1=pp[bi][:, :],
                                                               op0=ALU.add, op1=ALU.mult),
                      reads=ktt[tq % 2] + [("pp", bi)], writes=kgq[tq % 2])
                kb.op("pool", lambda g: g.tensor_tensor(out=yT[:, chunk0 + cb, ts_], in0=g_, in1=yT[:, chunk0 + cb, ts_],
                                                        op=ALU.mult),
                      reads=kgq[tq % 2] + ky(chunk0 + cb), writes=ky(chunk0 + cb))
            proj_fm(k, cb, ev_g)

    def mixer_b(l):
        o_q, o_kz, o_v, o_acc, o_pt, o_rc = 0, 8192, 24576, 32768, 49152, 51200
        qT = av(o_q, [2, S], BF16)
        kqT = ak(o_q, 8192)
        kTz = [av(o_kz + h * 4096, [S], BF16) for h in range(4)]
        kkTz = [ak(o_kz + h * 4096, 4096) for h in range(4)]
        Va = [av(o_v + j * 4096, [16, 128], BF16) for j in range(2)]
        kVa = [ak(o_v + j * 4096, 4096) for j in range(2)]
        accs = [av(o_acc + j * 8192, [S], F32) for j in range(2)]
        kaccs = [ak(o_acc + j * 8192, 8192) for j in range(2)]
        pt = [av(o_pt + j * 512, [256], BF16) for j in range(4)]
        kpt = [ak(o_pt + j * 512, 512) for j in range(4)]
        rcq = [av(o_rc + j * 1024, [256], F32) for j in range(2)]
        krcq = [ak(o_rc + j * 1024, 1024) for j in range(2)]
        for h in range(4):
            oh = slice(((h + 1) % 2) * 64, ((h + 1) % 2) * 64 + 64)
            kb.op("pool", lambda g, h=h, oh=oh: g.memset(kTz[h][oh, :], 0.0), writes=kkTz[h])
        rts = [[av(o_acc + sl * 2560 + j * 512, [128], F32) for j in range(4)] for sl in range(3)]
        krts = [[ak(o_acc + sl * 2560 + j * 512, 512) for j in range(4)] for sl in range(3)]
        qrs = [av(o_acc + sl * 2560 + 2048, [256], BF16) for sl in range(3)]
        kqrs = [ak(o_acc + sl * 2560 + 2048, 512) for sl in range(3)]
        rtasks = []
        for blk in (2, 3):
            k = load_wblock(l, blk)
            for i in range(NT):
                t = {"pre": None, "post": None}

                def s1(t=t, i=i, k=k):
                    sl = t["slot"] % 3
                    bi = (0, 1, 2)[sl]
                    for c in range(8):
                        kb.op("pe", lambda pe, c=c: pe.matmul(pp[bi][:, 0:256], hT[:, c, i * 128:(i + 1) * 128],
                                                              wb[k][:, c, 0:256], start=(c == 0), stop=(c == 7)),
                              reads=[("wb", k), "hT"], writes=[("pp", bi)])
                    z4 = pp[bi][:, 0:256].rearrange("p (h t f) -> p h t f", h=4, t=2)
                    x1, x2 = z4[:, :, 0, :], z4[:, :, 1, :]
                    cs = ropet[:, 0, i, :].unsqueeze(1).broadcast_to([128, 4, 32])
                    sn = ropet[:, 1, i, :].unsqueeze(1).broadcast_to([128, 4, 32])
                    r4 = [r.rearrange("p (h f) -> p h f", h=4) for r in rts[sl]]
                    q4 = qrs[sl].rearrange("p (h t f) -> p h t f", h=4, t=2)
                    for j, (a_, b_) in enumerate(((x1, cs), (x2, sn), (x2, cs), (x1, sn))):
                        kb.op("dve", lambda v, a_=a_, b_=b_, j=j: v.tensor_tensor(out=r4[j], in0=a_, in1=b_, op=ALU.mult),
                              reads=[("pp", bi), "ropet"], writes=krts[sl][j])
                    kb.op("pool", lambda g: g.tensor_tensor(out=q4[:, :, 0, :], in0=r4[0], in1=r4[1], op=ALU.subtract),
                          reads=krts[sl][0] + krts[sl][1], writes=kqrs[sl])
                    kb.op("pool", lambda g: g.tensor_tensor(out=q4[:, :, 1, :], in0=r4[2], in1=r4[3], op=ALU.add),
                          reads=krts[sl][2] + krts[sl][3], writes=kqrs[sl])

                def s2(t=t, i=i, blk=blk):
                    sl = t["slot"] % 3
                    b2 = next_bank(6, 8)
                    ptb = pp[b2][:].bitcast(BF16)
                    for pr in range(2):
                        kb.op("pe", lambda pe, pr=pr: pe.transpose(ptb[:, pr * 128:(pr + 1) * 128],
                                                                   qrs[sl][:, pr * 128:(pr + 1) * 128], ident[:]),
                              reads=kqrs[sl] + ["ident"], writes=[("pp", b2)])
                    if blk == 2:
                        kb.op("act", lambda a: a.copy(out=qT[:, :, i * 128:(i + 1) * 128],
                                                      in_=ptb[:, 0:256].rearrange("p (c n) -> p c n", c=2)),
                              reads=[("pp", b2)], writes=kqT)
                    else:
                        for h in range(4):
                            hp = slice((h % 2) * 64, (h % 2) * 64 + 64)
                            pr = h // 2
                            kb.op("act" if h % 2 == 0 else "dve",
                                  lambda e, h=h, hp=hp, pr=pr: (e.copy if h % 2 == 0 else e.tensor_copy)(
                                      out=kTz[h][hp, i * 128:(i + 1) * 128], in_=ptb[hp, pr * 128:(pr + 1) * 128]),
                                  reads=[("pp", b2)], writes=kkTz[h])
                t["s1"], t["s2"] = s1, s2
                rtasks.append(t)
        run_pipeline(rtasks, 2)
        kv = load_wblock(l, 4)
        for cb in range(2):
            def ev_vt(tq, bi):
                kb.op("act", lambda a: a.copy(out=yT[:, 2 + cb, tq * 512:(tq + 1) * 512], in_=pp[bi][:, :]),
                      reads=[("pp", bi)], writes=ky(2 + cb))
            proj_fm(kv, cb, ev_vt)
        tasks = []
        nva = 0
        for h in range(4):
            acc, kacc = accs[h % 2], kaccs[h % 2]
            voff = 0 if h % 2 == 0 else 64
            for pi, (d, n_sub) in enumerate(((1, 2048), (4, 512), (16, 128))):
                V, kV = Va[nva % 2], kVa[nva % 2]
                nva += 1

                def pre_v(V=V, kV=kV, h=h, voff=voff, d=d, n_sub=n_sub, pi=pi):
                    if pi < 2:
                        kb.op("pool", lambda g: g.memset(V[:, :, 64 - voff:128 - voff], 1.0), writes=kV)
                    for half in range(2):
                        b2 = next_bank(0, 2)
                        ptb = pp[b2][:].bitcast(BF16)
                        for q in range(8):
                            j = half * 8 + q
                            kb.op("pe", lambda pe, q=q, j=j: pe.transpose(
                                ptb[:, q * 128:(q + 1) * 128], yT[:, 2 + h // 2, psl(128 * j, 128, d, n_sub)], ident[:, :]),
                                reads=ky(2 + h // 2) + ["ident"], writes=[("pp", b2)])
                        kb.op("act", lambda a: a.copy(
                            out=V[:, half * 8:half * 8 + 8, voff:voff + 64],
                            in_=ptb.rearrange("p (q e) -> p q e", q=8)[:, :, voff:voff + 64]),
                            reads=[("pp", b2)], writes=kV)
                first_of_pattern = True
                for qb in range(4):
                    ob = next_bank(4, 6)
                    js = []
                    for j in range(max(0, 4 * qb - 1), min(16, 4 * qb + 5)):
                        slo = (128 * j // n_sub) * n_sub
                        qlo = max(128 * j - 64, slo, 512 * qb)
                        qhi = min(128 * j + 192, slo + n_sub, 512 * qb + 512)
                        if qlo < qhi:
                            js.append((j, qlo, qhi))
                    for ji, (j, qlo, qhi) in enumerate(js):
                        t = {}
                        t["pre"] = pre_v if first_of_pattern else None
                        first_of_pattern = False

                        def s1(t=t, j=j, qlo=qlo, qhi=qhi, h=h, d=d, n_sub=n_sub):
                            n = qhi - qlo
                            ns = t["slot"] % 4
                            sbk = (2, 3, 6, 7)[ns]
                            p_, kp_ = pt[ns], kpt[ns]
                            mo = qlo - (128 * j - 64)
                            kb.op("pe", lambda pe: pe.matmul(pp[sbk][:, 0:n], kTz[h][:, psl(128 * j, 128, d, n_sub)],
                                                             qT[:, h // 2, psl(qlo, n, d, n_sub)], start=True, stop=False),
                                  reads=kkTz[h] + kqT, writes=[("pp", sbk)])
                            kb.op("pe", lambda pe: pe.matmul(pp[sbk][:, 0:n], ident[:, :], band[:, mo:mo + n],
                                                             start=False, stop=True),
                                  reads=["ident", "band"], writes=[("pp", sbk)])
                            kb.op("act", lambda a: a.activation(out=p_[:, 0:n], in_=pp[sbk][:, 0:n], func=AF.Exp, scale=0.125),
                                  reads=[("pp", sbk)], writes=kp_)

                        def s2(t=t, j=j, qlo=qlo, qhi=qhi, qb=qb, ob=ob, V=V, kV=kV, first=(ji == 0)):
                            n = qhi - qlo
                            ns = t["slot"] % 4
                            p_, kp_ = pt[ns], kpt[ns]
                            kb.op("pe", lambda pe: pe.matmul(pp[ob][:, qlo - 512 * qb:qhi - 512 * qb], V[:, j, :], p_[:, 0:n],
                                                             start=first, stop=False, skip_group_check=True),
                                  reads=kV + kp_, writes=[("pp", ob)])
                        t["s1"], t["s2"], t["post"] = s1, s2, None
                        if ji == len(js) - 1:
                            def post(qb=qb, ob=ob, d=d, pi=pi, acc=acc, kacc=kacc, h=h):
                                if d == 1:
                                    dst = acc[:, 512 * qb:512 * qb + 512]
                                    src = pp[ob][:, :]
                                elif d == 4:
                                    dst = acc.rearrange("p (l x) -> p x l", x=4)[:, qb, :]
                                    src = pp[ob][:, :]
                                else:
                                    dst = acc.rearrange("p (l x) -> p x l", x=16)[:, 4 * qb:4 * qb + 4, :]
                                    src = pp[ob][:, :].rearrange("p (r l) -> p r l", r=4)
                                if pi == 0:
                                    kb.op("dve", lambda v: v.tensor_copy(out=dst, in_=src), reads=[("pp", ob)], writes=kacc)
                                else:
                                    kb.op("dve", lambda v: v.tensor_tensor(out=dst, in0=src, in1=dst, op=ALU.add),
                                          reads=[("pp", ob)] + kacc, writes=kacc)
                                if pi == 2 and qb == 3:
                                    nr = slice((h % 2) * 64, (h % 2) * 64 + 64)
                                    dr = slice(((h + 1) % 2) * 64, ((h + 1) % 2) * 64 + 64)
                                    for tq in range(8):
                                        ts_ = slice(tq * 256, (tq + 1) * 256)
                                        rc, krc = rcq[tq % 2], krcq[tq % 2]
                                        kb.op("act", lambda a: a.activation(out=rc[nr, :], in_=acc[dr, ts_], func=AF.Ln),
                                              reads=kacc, writes=krc)
                                        kb.op("act", lambda a: a.activation(out=rc[nr, :], in_=rc[nr, :], func=AF.Exp, scale=-1.0,
                                                                            bias=nlh[nr, :]),
                                              reads=krc + ["nlh"], writes=krc)
                                        kb.op("pool", lambda g: g.tensor_tensor(out=yT[nr, 2 + h // 2, ts_], in0=acc[nr, ts_],
                                                                                in1=rc[nr, :], op=ALU.mult),
                                              reads=kacc + krc, writes=ky(2 + h // 2, h=h % 2))
                            t["post"] = post
                        tasks.append(t)
        run_pipeline(tasks, 3)
        tt = [av(o_acc + j * 2048, [512], F32) for j in range(2)]
        ktt = [ak(o_acc + j * 2048, 2048) for j in range(2)]
        gq = [av(o_acc + 4096 + j * 2048, [512], F32) for j in range(2)]
        kgq = [ak(o_acc + 4096 + j * 2048, 2048) for j in range(2)]
        gate_apply(l, 5, 2, tt, ktt, gq, kgq)

    def na_rows(kt):
        rows = []
        for r in range(32):
            rs = min(max(r - 4, 0), 24)
            if any(rs <= 2 * kt + krl < rs + 8 for krl in range(2)):
                rows.append(r)
        return rows[0], rows[-1]

    def mixer_d(l):
        o_q, o_kz, o_v, o_e, o_tt, o_pt = 0, 8192, 24576, 32768, 42240, 46336
        qT = av(o_q, [2, S], BF16)
        kqT = ak(o_q, 8192)
        kTz = [av(o_kz + h * 4096, [S], BF16) for h in range(4)]
        kkTz = [ak(o_kz + h * 4096, 4096) for h in range(4)]
        Va = [av(o_v + j * 4096, [16, 128], BF16) for j in range(2)]
        kVa = [ak(o_v + j * 4096, 4096) for j in range(2)]
        Eb = [av(o_e + j * 4736, [2368], BF16) for j in range(2)]
        kEb = [ak(o_e + j * 4736, 4736) for j in range(2)]
        tt = [av(o_tt + j * 2048, [512], F32) for j in range(2)]
        ktt = [ak(o_tt + j * 2048, 2048) for j in range(2)]
        pt = [av(o_pt + j * 1024, [512], BF16) for j in range(4)]
        kpt = [ak(o_pt + j * 1024, 1024) for j in range(4)]
        for h in range(4):
            oh = slice(((h + 1) % 2) * 64, ((h + 1) % 2) * 64 + 64)
            kb.op("pool", lambda g, h=h, oh=oh: g.memset(kTz[h][oh, :], 0.0), writes=kkTz[h])
        k = load_wblock(l, 8)
        for cb in range(2):
            def ev_q(tq, bi):
                kb.op("act", lambda a: a.copy(out=qT[:, cb, tq * 512:(tq + 1) * 512], in_=pp[bi][:, :]),
                      reads=[("pp", bi)], writes=kqT)
            proj_fm(k, cb, ev_q)
        k = load_wblock(l, 9)
        for cb in range(2):
            def ev_k(tq, bi):
                for hh in range(2):
                    h = 2 * cb + hh
                    hp = slice(hh * 64, hh * 64 + 64)
                    kb.op("act" if hh == 0 else "dve",
                          lambda e, h=h, hp=hp, hh=hh: (e.copy if hh == 0 else e.tensor_copy)(
                              out=kTz[h][hp, tq * 512:(tq + 1) * 512], in_=pp[bi][hp, :]),
                          reads=[("pp", bi)], writes=kkTz[h])
            proj_fm(k, cb, ev_k)
        kv = load_wblock(l, 10)
        for cb in range(2):
            def ev_vt(tq, bi):
                kb.op("act", lambda a: a.copy(out=yT[:, 6 + cb, tq * 512:(tq + 1) * 512], in_=pp[bi][:, :]),
                      reads=[("pp", bi)], writes=ky(6 + cb))
            proj_fm(kv, cb, ev_vt)
        tasks = []
        for h in range(4):
            nr = slice((h % 2) * 64, (h % 2) * 64 + 64)
            dr = slice(((h + 1) % 2) * 64, ((h + 1) % 2) * 64 + 64)
            voff = 0 if h % 2 == 0 else 64
            V, kV = Va[h % 2], kVa[h % 2]
            E, kE = Eb[h % 2], kEb[h % 2]

            def pre_h(h=h, voff=voff, V=V, kV=kV, E=E, kE=kE):
                kb.dma(E, et_d[l, h], reads=["et"], writes=kE)
                if h < 2:
                    kb.op("pool", lambda g: g.memset(V[:, :, 64 - voff:128 - voff], 1.0), writes=kV)
                for half in range(2):
                    b2 = next_bank(0, 2)
                    ptb = pp[b2][:].bitcast(BF16)
                    for q in range(8):
                        j = half * 8 + q
                        kb.op("pe", lambda pe, q=q, j=j: pe.transpose(
                            ptb[:, q * 128:(q + 1) * 128], yT[:, 6 + h // 2, 128 * j:128 * j + 128], ident[:, :]),
                            reads=ky(6 + h // 2) + ["ident"], writes=[("pp", b2)])
                    kb.op("act", lambda a: a.copy(out=V[:, half * 8:half * 8 + 8, voff:voff + 64],
                                                  in_=ptb.rearrange("p (q e) -> p q e", q=8)[:, :, voff:voff + 64]),
                          reads=[("pp", b2)], writes=kV)
            first_of_head = True
            for qb in range(4):
                ob = next_bank(4, 6)
                kts = []
                for kt in range(16):
                    ra, rb = na_rows(kt)
                    ra, rb = max(ra, 8 * qb), min(rb, 8 * qb + 7)
                    if ra <= rb:
                        kts.append((kt, ra, rb))
                for ki, (kt, ra, rb) in enumerate(kts):
                    t = {"pre": pre_h if first_of_head else None, "post": None}
                    first_of_head = False

                    def s1(t=t, kt=kt, ra=ra, rb=rb, h=h, E=E, kE=kE):
                        n = 64 * (rb - ra + 1)
                        ns = t["slot"]
                        sbk = (2, 3, 6, 7)[ns % 4]
                        p_, kp_ = pt[ns % 4], kpt[ns % 4]
                        kb.op("pe", lambda pe: pe.matmul(pp[sbk][:, 0:n], kTz[h][:, 128 * kt:128 * kt + 128],
                                                         qT[:, h // 2, 64 * ra:64 * (rb + 1)], start=True, stop=True,
                                                         skip_group_check=True),
                              reads=kkTz[h] + kqT, writes=[("pp", sbk)])
                        segs = []
                        if ra <= 3:
                            r1 = min(rb, 3)
                            segs.append((ra, r1, 576 + (3 - kt) * 256 + ra * 64))
                        if max(ra, 4) <= min(rb, 28):
                            r0, r1 = max(ra, 4), min(rb, 28)
                            segs.append((r0, r1, (r0 - 2 * kt + 3) * 64))
                        if rb >= 29:
                            r0 = max(ra, 29)
                            segs.append((r0, rb, 1600 + (15 - kt) * 192 + (r0 - 29) * 64))
                        for si, (r0, r1, eoff) in enumerate(segs):
                            c0, c1 = 64 * (r0 - ra), 64 * (r1 - ra + 1)
                            kb.op("pe", lambda pe, c0=c0, c1=c1, eoff=eoff, si=si: pe.matmul(
                                pp[sbk][:, c0:c1], ident[:, :], E[:, eoff:eoff + c1 - c0], start=False,
                                stop=True, skip_group_check=True),
                                reads=["ident"] + kE, writes=[("pp", sbk)])
                        kb.op("act", lambda a: a.activation(out=p_[:, 0:n], in_=pp[sbk][:, 0:n], func=AF.Exp, scale=0.125),
                              reads=[("pp", sbk)], writes=kp_)

                    def s2(t=t, kt=kt, ra=ra, rb=rb, qb=qb, ob=ob, V=V, kV=kV, first=(ki == 0)):
                        n = 64 * (rb - ra + 1)
                        ns = t["slot"]
                        p_, kp_ = pt[ns % 4], kpt[ns % 4]
                        kb.op("pe", lambda pe: pe.matmul(pp[ob][:, 64 * ra - 512 * qb:64 * (rb + 1) - 512 * qb], V[:, kt, :],
                                                         p_[:, 0:n], start=first, stop=False, skip_group_check=True),
                              reads=kV + kp_, writes=[("pp", ob)])
                    t["s1"], t["s2"] = s1, s2
                    if ki == len(kts) - 1:
                        def post(qb=qb, ob=ob, h=h, nr=nr, dr=dr):
                            ts_ = slice(qb * 512, (qb + 1) * 512)
                            rc, krc = tt[qb % 2], ktt[qb % 2]
                            kb.op("act", lambda a: a.activation(out=rc[nr, :], in_=pp[ob][dr, :], func=AF.Ln),
                                  reads=[("pp", ob)], writes=krc)
                            kb.op("act", lambda a: a.activation(out=rc[nr, :], in_=rc[nr, :], func=AF.Exp, scale=-1.0,
                                                                bias=nlh[nr, :]),
                                  reads=krc + ["nlh"], writes=krc)
                            kb.op("dve", lambda v: v.tensor_tensor(out=yT[nr, 6 + h // 2, ts_], in0=pp[ob][nr, :],
                                                                   in1=rc[nr, :], op=ALU.mult),
                                  reads=[("pp", ob)] + krc, writes=ky(6 + h // 2, h=h % 2))
                        t["post"] = post
                    tasks.append(t)
        run_pipeline(tasks, 3)
        ttg = [av(o_v + j * 2048, [512], F32) for j in range(2)]
        kttg = [ak(o_v + j * 2048, 2048) for j in range(2)]
        gq = [av(o_v + 4096 + j * 2048, [512], F32) for j in range(2)]
        kgq = [ak(o_v + 4096 + j * 2048, 2048) for j in range(2)]
        gate_apply(l, 11, 6, ttg, kttg, gq, kgq)

    GC1 = math.sqrt(2.0 / math.pi)
    GC2 = 0.044715

    def mixer_c(l):
        o_u, o_G, o_V, o_P, o_MC, o_et, o_w, o_sin, o_y = 0, 8192, 16384, 20480, 22528, 26112, 30208, 40448, 44544
        ucm = [av(o_u + t * 4096, [16, 8, 16], BF16) for t in range(2)]
        kucm = [ak(o_u + t * 4096, 4096) for t in range(2)]
        Gcm = av(o_G, [16, 256], BF16)
        kG = ak(o_G, 8192)
        gT = av(o_u, [2, S], BF16)
        kgT = ak(o_u, 8192)
        Vs = [av(o_V + j * 512, [256], BF16) for j in range(8)]
        kVs = [ak(o_V + j * 512, 512) for j in range(8)]
        Pb = [av(o_P + j * 1024, [4, 128], BF16) for j in range(2)]
        kPb = [ak(o_P + j * 1024, 1024) for j in range(2)]
        MC = [av(o_MC + j * 1792, [7, 128], BF16) for j in range(2)]
        kMC = [ak(o_MC + j * 1792, 1792) for j in range(2)]
        et = av(o_et, [4, 2, 128], F32)
        ket = ak(o_et, 4096)
        wk_ = [av(o_w + j * 2048, [4, 128], F32) for j in range(5)]
        kwk = [ak(o_w + j * 2048, 2048) for j in range(5)]
        A_, B_, T1, T2, T3 = wk_
        kA, kB_, kT1, kT2, kT3 = kwk
        sres = [av(o_sin + j * 2048, [4, 128], BF16) for j in range(2)]
        sims = [av(o_sin + j * 2048 + 1024, [4, 128], BF16) for j in range(2)]
        ksres = [ak(o_sin + j * 2048, 1024) for j in range(2)]
        ksims = [ak(o_sin + j * 2048 + 1024, 1024) for j in range(2)]
        ytmp = [[av(o_y + s_ * 4096 + j * 1024, [256], F32) for j in range(4)] for s_ in range(2)]
        kytmp = [[ak(o_y + s_ * 4096 + j * 1024, 1024) for j in range(4)] for s_ in range(2)]
        F_, Bh = slice(0, 64), slice(64, 128)
        tt2 = lambda e, o, a, b, op, rd, wr: kb.op(e, lambda v: v.tensor_tensor(out=o, in0=a, in1=b, op=op), reads=rd, writes=wr)
        kcu = load_wblock(l, 6)
        for j in range(8):
            for mt in range(2):
                bi = next_bank(0, 2)
                for c in range(8):
                    kb.op("pe", lambda pe, c=c: pe.matmul(
                        pp[bi][:, 0:256], hT[:, c, slice(1024 * mt + j, 1024 * mt + j + 8 * 127 + 1, 8)],
                        wb[kcu][:, c, :], start=(c == 0), stop=(c == 7)),
                        reads=[("wb", kcu), "hT"], writes=[("pp", bi)])
                kb.op("act" if (j + mt) % 2 == 0 else "dve",
                      lambda e: (e.copy if (j + mt) % 2 == 0 else e.tensor_copy)(
                          out=ucm[mt][:, :, 7 - j, :], in_=pp[bi][:, 0:256].rearrange("p (g c) -> p g c", g=16)),
                      reads=[("pp", bi)], writes=kucm[mt])
        for j in range(2):
            kb.op("pool", lambda g, j=j: g.memset(sres[j][F_, :, 0:1], 0.0), writes=ksres[j])
            kb.op("pool", lambda g, j=j: g.memset(sres[j][Bh, :, 127:128], 0.0), writes=ksres[j])
            kb.op("pool", lambda g, j=j: g.memset(sims[j][F_, :, 0:1], 0.0), writes=ksims[j])
            kb.op("pool", lambda g, j=j: g.memset(sims[j][Bh, :, 127:128], 0.0), writes=ksims[j])
        banks = {}

        def x_front(gb):
            kb.dma(et, etab_d[l, gb], reads=["etab"], writes=ket)
            s0r, s0i = 2, 3
            banks[gb] = (s0r, s0i)
            for gi in range(4):
                g = 4 * gb + gi
                V, kV = Vs[(gb % 2) * 4 + gi], kVs[(gb % 2) * 4 + gi]
                P_, kP_ = Pb[g % 2], kPb[g % 2]
                kb.dma(P_, sblk_d[l, g, :, 3:7, :], reads=["sblk"], writes=kP_)
                bt = next_bank(6, 8)
                ptb = pp[bt][:].bitcast(BF16)
                for mt in range(2):
                    kb.op("pe", lambda pe, mt=mt: pe.transpose(
                        ptb[:, mt * 128:(mt + 1) * 128], ucm[mt][:, g, :, :].rearrange("p j c -> p (j c)"), ident[:]),
                        reads=kucm[mt] + ["ident"], writes=[("pp", bt)])
                kb.op("act", lambda a: a.copy(out=V, in_=ptb[:, 0:256]), reads=[("pp", bt)], writes=kV)
                for (bank, b0) in ((s0r, 0), (s0i, 2)):
                    for sub in range(2):
                        kb.op("pe", lambda pe, sub=sub, bank=bank, b0=b0: pe.matmul(
                            pp[bank][:, gi * 128:(gi + 1) * 128], P_[:, b0 + sub, :], V[:, sub:256:2],
                            start=(sub == 0), stop=(sub == 1), skip_group_check=True),
                            reads=kP_ + kV, writes=[("pp", bank)])
            for (bank, dst, kd) in ((s0r, A_, kA), (s0i, B_, kB_)):
                src = pp[bank][:, :].rearrange("p (g m) -> p g m", g=4)
                kb.op("act", lambda a, src=src, dst=dst: a.copy(out=dst[F_], in_=src[F_]), reads=[("pp", bank)], writes=kd)
                kb.op("act", lambda a, src=src, dst=dst: a.copy(out=dst[Bh], in_=src[Bh, :, ::-1]), reads=[("pp", bank)], writes=kd)

        def x_back(gb, part):
            cs_, sn_ = et[:, :, 0, :], et[:, :, 1, :]
            sre, sim, ksre, ksim = sres[gb % 2], sims[gb % 2], ksres[gb % 2], ksims[gb % 2]
            if part == 0:
                tt2("dve", T1, A_, cs_, ALU.mult, kA + ket, kT1)
                tt2("pool", T2, B_, sn_, ALU.mult, kB_ + ket, kT2)
                tt2("dve", T1, T1, T2, ALU.add, kT1 + kT2, kT1)
                tt2("pool", T3, B_, cs_, ALU.mult, kB_ + ket, kT3)
                tt2("dve", T2, A_, sn_, ALU.mult, kA + ket, kT2)
                tt2("pool", T3, T3, T2, ALU.subtract, kT3 + kT2, kT3)
            elif part == 1:
                for gi in range(4):
                    g = 4 * gb + gi
                    rb_ = rho_sb[:, l, g:g + 1].broadcast_to([128, 128])
                    kb.op("dve", lambda v, gi=gi, rb_=rb_: v.tensor_tensor_scan(
                        out=A_[:, gi, :], data0=rb_, data1=T1[:, gi, :], initial=0.0, op0=ALU.mult, op1=ALU.add),
                        reads=kT1 + ["rho"], writes=kA)
                    kb.op("dve", lambda v, gi=gi, rb_=rb_: v.tensor_tensor_scan(
                        out=B_[:, gi, :], data0=rb_, data1=T3[:, gi, :], initial=0.0, op0=ALU.mult, op1=ALU.add),
                        reads=kT3 + ["rho"], writes=kB_)
            elif part == 2:
                tt2("dve", T1, A_, cs_, ALU.mult, kA + ket, kT1)
                tt2("pool", T2, B_, sn_, ALU.mult, kB_ + ket, kT2)
                tt2("dve", T1, T1, T2, ALU.subtract, kT1 + kT2, kT1)
                tt2("pool", T3, B_, cs_, ALU.mult, kB_ + ket, kT3)
                tt2("dve", T2, A_, sn_, ALU.mult, kA + ket, kT2)
                tt2("pool", T3, T3, T2, ALU.add, kT3 + kT2, kT3)
            else:
                for (src, ksrc, dst, kd) in ((T1, kT1, sre, ksre), (T3, kT3, sim, ksim)):
                    kb.op("act", lambda a, src=src, dst=dst: a.copy(out=dst[F_, :, 1:128], in_=src[F_, :, 0:127]),
                          reads=ksrc, writes=kd)
                    kb.op("pool", lambda g_, src=src, dst=dst: g_.tensor_copy(out=dst[Bh, :, 0:127], in_=src[Bh, :, 126::-1]),
                          reads=ksrc, writes=kd)

        def y_group(gb, gi):
            g = 4 * gb + gi
            V, kV = Vs[(gb % 2) * 4 + gi], kVs[(gb % 2) * 4 + gi]
            sre, sim, ksre, ksim = sres[gb % 2], sims[gb % 2], ksres[gb % 2], ksims[gb % 2]
            M_, kM_ = MC[g % 2], kMC[g % 2]
            Ysb, sq, u_, th_ = ytmp[g % 2]
            kYsb, ksq, ku_, kth_ = kytmp[g % 2]
            kb.dma(M_[:, 0:3, :], sblk_d[l, g, :, 0:3, :], reads=["sblk"], writes=kM_)
            kb.dma(M_[:, 3:7, :], sblk_d[l, g, :, 7:11, :], reads=["sblk"], writes=kM_)
            yb = next_bank(4, 6)
            V0, V1 = V[:, 0:256:2], V[:, 1:256:2]
            plan = ((0, [(0, V0, kV), (2, V1, kV), (3, sre[:, gi, :], ksre), (4, sim[:, gi, :], ksim)]),
                    (1, [(0, V1, kV), (1, V0, kV), (5, sre[:, gi, :], ksre), (6, sim[:, gi, :], ksim)]))
            for so, terms in plan:
                for ti, (bidx, rhs, krhs) in enumerate(terms):
                    kb.op("pe", lambda pe, so=so, ti=ti, bidx=bidx, rhs=rhs: pe.matmul(
                        pp[yb][:, so * 128:(so + 1) * 128], M_[:, bidx, :], rhs, start=(ti == 0), stop=(ti == 3),
                        skip_group_check=True), reads=kM_ + krhs, writes=[("pp", yb)])
            kb.op("act", lambda a: a.copy(out=Ysb, in_=pp[yb][:, 0:256]), reads=[("pp", yb)], writes=kYsb)
            tb = next_bank(6, 8)
            for so in range(2):
                kb.op("pe", lambda pe, so=so: pe.transpose(pp[tb][:, so * 128:(so + 1) * 128],
                                                           Ysb[:, so * 128:(so + 1) * 128], identf[:]),
                      reads=kYsb + ["identf"], writes=[("pp", tb)])
            yy = pp[tb][:, 0:256]
            kb.op("act", lambda a: a.activation(out=sq, in_=yy, func=AF.Square), reads=[("pp", tb)], writes=ksq)
            kb.op("pool", lambda g_: g_.tensor_scalar(out=sq, in0=sq, scalar1=GC2, scalar2=1.0, op0=ALU.mult, op1=ALU.add),
                  reads=ksq, writes=ksq)
            kb.op("dve", lambda v: v.tensor_tensor(out=u_, in0=sq, in1=yy, op=ALU.mult), reads=ksq + [("pp", tb)], writes=ku_)
            kb.op("act", lambda a: a.activation(out=th_, in_=u_, func=AF.Tanh, scale=GC1), reads=ku_, writes=kth_)
            kb.op("dve", lambda v: v.scalar_tensor_tensor(
                out=Gcm[:, :, g * 16:(g + 1) * 16], in0=th_.rearrange("p (s c) -> p s c", c=16), scalar=1.0,
                in1=yy.rearrange("p (s c) -> p s c", c=16), op0=ALU.add, op1=ALU.mult),
                reads=kth_ + [("pp", tb)], writes=kG)

        x_front(0)
        for part in range(4):
            x_back(0, part)
        for gb in range(4):
            if gb + 1 < 4:
                x_front(gb + 1)
            for gi in range(4):
                y_group(gb, gi)
                if gb + 1 < 4:
                    x_back(gb + 1, gi)
        dump("Gcm", Gcm, kG)
        for chc in range(2):
            for half in range(2):
                bt = next_bank(6, 8)
                ptb = pp[bt][:].bitcast(BF16)
                for q in range(8):
                    si = half * 8 + q
                    kb.op("pe", lambda pe, q=q, si=si: pe.transpose(ptb[:, q * 128:(q + 1) * 128],
                                                                    Gcm[:, si, chc * 128:(chc + 1) * 128], ident[:]),
                          reads=kG + ["ident"], writes=[("pp", bt)])
                kb.op("act" if half == 0 else "dve", lambda e: (e.copy if half == 0 else e.tensor_copy)(
                    out=gT[:, chc, :].rearrange("p (m s) -> p s m", s=16)[:, half * 8:half * 8 + 8, :],
                    in_=ptb.rearrange("p (q m) -> p q m", q=8)), reads=[("pp", bt)], writes=kgT)
        dump("gT", gT, kgT)
        gwl = av(o_et, [2, 256], BF16)
        kgwl = ak(o_et, 1024)
        kb.dma(gwl, gwb_d[l], reads=["gwb"], writes=kgwl)
        ttg = [av(o_w + j * 2048, [512], F32) for j in range(2)]
        t2s = [av(o_w + (2 + j) * 2048, [512], F32) for j in range(2)]
        n_ = 0
        for ec in range(2):
            for tq in range(4):
                ts_ = slice(tq * 512, (tq + 1) * 512)
                bi = next_bank(0, 2)
                for cc in range(2):
                    kb.op("pe", lambda pe, cc=cc: pe.matmul(pp[bi][:, :], gwl[:, cc, ec * 128:(ec + 1) * 128], gT[:, cc, ts_],
                                                            start=(cc == 0), stop=(cc == 1)),
                          reads=kgwl + kgT, writes=[("pp", bi)])
                th2, kth2 = ttg[n_ % 2], kwk[n_ % 2]
                t_, kt_ = t2s[n_ % 2], kwk[2 + n_ % 2]
                n_ += 1
                kb.op("act", lambda a: a.activation(out=th2, in_=pp[bi][:, :], func=AF.Tanh, scale=0.25,
                                                    bias=glb[:, l, ec:ec + 1]),
                      reads=[("pp", bi), "glb"], writes=kth2)
                kb.op("dve", lambda v: v.scalar_tensor_tensor(out=t_, in0=th2, scalar=1.0, in1=gT[:, ec, ts_],
                                                               op0=ALU.add, op1=ALU.mult),
                      reads=kth2 + kgT, writes=kt_)
                kb.op("pool", lambda g_: g_.tensor_scalar(out=yT[:, 4 + ec, ts_], in0=t_, scalar1=0.125, scalar2=1.0,
                                                          op0=ALU.mult, op1=ALU.mult),
                      reads=kt_, writes=ky(4 + ec))
        gtt = [av(o_G + j * 2048, [512], F32) for j in range(2)]
        kgtt = [ak(o_G + j * 2048, 2048) for j in range(2)]
        gq = [av(o_G + 4096 + j * 2048, [512], F32) for j in range(2)]
        kgq = [ak(o_G + 4096 + j * 2048, 2048) for j in range(2)]
        gate_apply(l, 7, 4, gtt, kgtt, gq, kgq)

    kb.same_depth = 2
    for s in range(nseq):
        for i in range(NT):
            kb.dma(x_res[:, i, :], x_d[s, i * 128:(i + 1) * 128, :], writes=[("x", i)])
        for l in range(depth):
            rms_all()
            for i in range(NT + 2):
                if i < NT:
                    kb.op("act", lambda a, i=i: a.activation(out=hs[i % 4], in_=x_res[:, i, :], func=AF.Copy,
                                                             scale=small[:, i:i + 1]),
                          reads=[("x", i), "rstd"], writes=khs[i % 4])
                    bi = 4 + i % 4
                    pt = pp[bi][:].bitcast(BF16)
                    for c in range(8):
                        kb.op("pe", lambda pe, c=c, i=i, pt=pt: pe.transpose(
                            pt[:, c * 128:(c + 1) * 128], hs[i % 4][:, c * 128:(c + 1) * 128], ident[:]),
                            reads=khs[i % 4] + ["ident"], writes=[("pp", bi)])
                if i >= 2:
                    j = i - 2
                    bj = 4 + j % 4
                    ptj = pp[bj][:].bitcast(BF16)
                    if j % 2 == 0:
                        kb.op("act", lambda a, j=j, ptj=ptj: a.copy(
                            out=hT[:, :, j * 128:(j + 1) * 128], in_=ptj.rearrange("p (c n) -> p c n", c=8)),
                            reads=[("pp", bj)], writes=["hT"])
                    else:
                        kb.op("dve", lambda v, j=j, ptj=ptj: v.tensor_copy(
                            out=hT[:, :, j * 128:(j + 1) * 128], in_=ptj.rearrange("p (c n) -> p c n", c=8)),
                            reads=[("pp", bj)], writes=["hT"])
            if "a" in mixers:
                mixer_a(l)
            if "b" in mixers:
                mixer_b(l)
            if "c" in mixers:
                mixer_c(l)
            if "d" in mixers:
                mixer_d(l)
            for mi, m in enumerate("abcd"):
                if m not in mixers:
                    kb.op("pool", lambda g, mi=mi: g.memset(yT[:, 2 * mi:2 * mi + 2, :], 0.0), writes=ky(2 * mi, 2 * mi + 2))
            if dbg and s == 0 and l == 0:
                kb.dma(dbg_d, yT[:], reads=ky(0, 8), writes=["dbgout"])
            for h in range(4):
                k = load_woblock(l, h)
                for i in range(NT):
                    bi = next_bank(0, 2)
                    for c in range(8):
                        kb.op("pe", lambda pe, c=c, i=i, bi=bi, k=k: pe.matmul(
                            pp[bi][:, 0:256], yT[:, c, i * 128:(i + 1) * 128], wb[k][:, c, :],
                            start=(c == 0), stop=(c == 7)),
                            reads=[("wb", k)] + ky(c), writes=[("pp", bi)])
                    kb.op("dve", lambda v, i=i, bi=bi, h=h: v.tensor_tensor(
                        out=x_res[:, i, h * 256:(h + 1) * 256], in0=pp[bi][:, 0:256],
                        in1=x_res[:, i, h * 256:(h + 1) * 256], op=ALU.add),
                        reads=[("pp", bi)], writes=[("x", i)])
        fg = av(8192, [D], F32)
        kb.dma(fg, fg_d, writes=ak(8192, 4096))
        rms_all()
        for i in range(NT):
            oo = 28672 + (i % 4) * 4096
            ot = av(oo, [1024], F32)
            kb.op("dve", lambda v, i=i, ot=ot: v.scalar_tensor_tensor(
                out=ot, in0=x_res[:, i, :], scalar=small[:, i:i + 1], in1=fg, op0=ALU.mult, op1=ALU.mult),
                reads=[("x", i), "rstd"] + ak(8192, 4096), writes=ak(oo, 4096))
            kb.dma(y_d[s, i * 128:(i + 1) * 128, :], ot, reads=ak(oo, 4096), writes=["y"])
    kb.finish()
    return nc


def host_prep(inputs):
    f = np.float32
    ng = np.ascontiguousarray(np.asarray(inputs["norm_g"], f).reshape(4, 8, 128).transpose(2, 0, 1))
    fgb = np.ascontiguousarray(np.broadcast_to(np.asarray(inputs["final_g"], f)[None, :], (128, D)))
    pw = np.asarray(inputs["pool_w"], f)
    pwb = np.zeros((128, 4, 2, 128), f)
    for g in range(4):
        cb, h = g // 2, g % 2
        pwb[h * 64:(h + 1) * 64, :, cb, h * 64:(h + 1) * 64] = pw[:, g].transpose(1, 0, 2)
    psc = np.ascontiguousarray(np.asarray(inputs["pool_scale"], f).reshape(4, 2, 128).transpose(2, 0, 1))
    pcn = np.zeros((128, 2, 17), f)
    for g, w in enumerate((2, 4, 8, 16)):
        cb, h = g // 2, g % 2
        t = np.arange(S)
        cnt = np.minimum(t + w // 2, S) - np.maximum(t - w // 2, 0)
        pcn[h * 64:(h + 1) * 64, cb, 0:8] = (w / cnt[0:8])[None, :]
        pcn[h * 64:(h + 1) * 64, cb, 8:16] = (w / cnt[S - 8:S])[None, :]
        pcn[h * 64:(h + 1) * 64, cb, 16] = 1.0 / w
    inv = 10000.0 ** (-np.arange(0, 64, 2, dtype=np.float32) / 64)
    ang = np.arange(S, dtype=np.float32)[:, None] * inv[None, :]
    rope = np.stack([np.cos(ang), np.sin(ang)], 0).astype(f)
    rope_t = np.ascontiguousarray(rope.reshape(2, NT, 128, 32).transpose(2, 0, 1, 3))
    kk = np.arange(128)[:, None]
    cc = np.arange(256)[None, :]
    band = np.where(((cc - kk) >= 0) & ((cc - kk) <= 128), 0.0, -240000.0).astype(ml_dtypes.bfloat16)
    rpb = np.asarray(inputs["na_rpb"], f)
    rpbpad = np.zeros((4, 4, 15, 128), f)
    rpbpad[:, :, :, 48:79] = rpb[:, :, ::-1, :]
    kc = np.arange(64)
    c = 63 - np.arange(64)
    cs = np.clip(c - 8, 0, 48)
    colok = ((kc[:, None] >= cs[None, :]) & (kc[:, None] < cs[None, :] + 16)).astype(f)
    nam = np.zeros((128, 2368), f)
    for krl in range(2):
        for ri in range(9):
            dlt = krl - ri + 3
            if -4 <= dlt <= 3:
                nam[krl * 64:(krl + 1) * 64, ri * 64:(ri + 1) * 64] = colok
        for blk in range(28):
            nam[krl * 64:(krl + 1) * 64, 576 + blk * 64:576 + (blk + 1) * 64] = colok
    are = np.asarray(inputs["ssm_a_re"], f)
    aim = np.asarray(inputs["ssm_a_im"], f)
    ldt = np.asarray(inputs["ssm_log_dt"], f)
    lam = np.stack([are.transpose(0, 1, 3, 2), aim.transpose(0, 1, 3, 2),
                    np.broadcast_to(ldt[:, :, None, :], (4, 2, 64, 16))], axis=-1)
    lam = np.ascontiguousarray(lam.reshape(4, 128, 16, 3))
    bre = np.asarray(inputs["ssm_b_re"], f)
    bim = np.asarray(inputs["ssm_b_im"], f)
    bp1 = np.stack([bre.transpose(0, 2, 1, 3), bim.transpose(0, 2, 1, 3)], axis=-1)
    bp = np.ascontiguousarray(np.concatenate([bp1, bp1], axis=1))
    cre = np.asarray(inputs["ssm_c_re"], f)
    cim = np.asarray(inputs["ssm_c_im"], f)
    cp1 = np.stack([cre.transpose(0, 1, 4, 2, 3), cim.transpose(0, 1, 4, 2, 3)], axis=-1)
    cp = np.ascontiguousarray(cp1.reshape(4, 128, 16, 16, 2))
    sd = np.asarray(inputs["ssm_d"], f).reshape(4, 16, 16)
    sdt = np.ascontiguousarray(sd.transpose(2, 0, 1))
    glw = np.ascontiguousarray(np.asarray(inputs["glu_w"], f).reshape(4, 2, 128, 256).transpose(0, 2, 1, 3))
    glbt = np.ascontiguousarray(np.asarray(inputs["glu_b"], f).reshape(4, 2, 128).transpose(2, 0, 1))
    return {
        "ssm_lam": lam, "ssm_bp": bp, "ssm_cp": cp, "ssm_dt": sdt, "glu_w_t": glw, "glu_b_t": glbt,
        "rpbpad": rpbpad, "na_mask": nam.astype(ml_dtypes.bfloat16),
        "rope_t": rope_t, "band": band,
        "pool_w_blk": pwb, "pool_scale_t": psc, "pool_const": pcn,
        "w_in": np.ascontiguousarray(inputs["w_in"], dtype=f),
        "w_out": np.ascontiguousarray(inputs["w_out"], dtype=f),
        "norm_g_t": ng,
        "final_g_b": fgb,
    }


def kernel(**inputs):
    xp = np.asarray(inputs["x_prompt"], np.float32)
    xs = np.asarray(inputs["x_sample"], np.float32)
    shared = host_prep(inputs)
    nc = build()
    in_maps = []
    for c in range(8):
        xc = np.concatenate([xp[4 * c:4 * c + 4], xs[c:c + 1]], axis=0)
        m = dict(shared)
        m["x"] = np.ascontiguousarray(xc)
        in_maps.append(m)
    res = run_bass_kernel_spmd(nc, in_maps, core_ids=list(range(8)))
    yp = np.empty_like(xp)
    ys = np.empty_like(xs)
    for c in range(8):
        y = res.results[c]["y"]
        yp[4 * c:4 * c + 4] = y[0:4]
        ys[c] = y[4]
    return (yp, ys)
```

```python
import math
from contextlib import ExitStack
import numpy as np
import ml_dtypes
import concourse.bass as bass
import concourse.mybir as mybir
from concourse.bass_utils import run_bass_kernel_spmd

F32 = mybir.dt.float32
BF16 = mybir.dt.bfloat16
I32 = mybir.dt.int32
AF = mybir.ActivationFunctionType
ALU = mybir.AluOpType
AX = mybir.AxisListType

S = 2048
D = 1024
NT = 16
EPS = 1e-6
NDMA = 24


class KB:
    def __init__(self):
        self.nc = bass.Bass("TRN2", target_bir_lowering=False)
        nc = self.nc
        self.es = ExitStack()
        self.eng = {"pe": nc.tensor, "act": nc.scalar, "dve": nc.vector, "pool": nc.gpsimd, "sp": nc.sync}
        self.sem = {}
        for e in ["pe", "act", "dve", "pool"]:
            self.sem[e] = self.es.enter_context(nc.semaphore("s_" + e))
        for i in range(NDMA):
            self.sem[("dma", i)] = self.es.enter_context(nc.semaphore("s_dma%d" % i))
        self.cnt = {e: 0 for e in ["pe", "act", "dve", "pool"]}
        self.seen = {e: {} for e in ["pe", "act", "dve", "pool", "sp"]}
        self.res = {}
        self.ndma = 0
        self.same_eng = {"act", "dve", "pool"}
        self.same_depth = 1000000
        self.clock = {}

    def sb(self, name, shape, dt):
        return self.es.enter_context(self.nc.sbuf_tensor(name, shape, dt))

    def ps(self, name, shape, dt):
        return self.es.enter_context(self.nc.psum_tensor(name, shape, dt))

    def dram(self, name, shape, dt, kind="Internal"):
        return self.nc.dram_tensor(name, shape, dt, kind=kind).ap()

    def _wait(self, e, key, val):
        if key == e:
            if e not in self.same_eng or val < self.cnt[e] - self.same_depth + 1:
                return
        if self.seen[e].get(key, 0) >= val:
            return
        self.eng[e].wait_ge(self.sem[key], val)
        self.seen[e][key] = val
        clk = self.clock.get((key, val))
        if clk:
            se = self.seen[e]
            for k2, v2 in clk.items():
                if se.get(k2, 0) < v2:
                    se[k2] = v2

    def _deps(self, e, reads, writes):
        for r in reads:
            st = self.res.get(r)
            if st:
                for k, v in st["w"].items():
                    self._wait(e, k, v)
        for w in writes:
            st = self.res.get(w)
            if st:
                for k, v in st["w"].items():
                    self._wait(e, k, v)
                for k, v in st["r"].items():
                    self._wait(e, k, v)

    def _mark(self, key, val, reads, writes):
        for r in reads:
            st = self.res.setdefault(r, {"w": {}, "r": {}})
            st["r"][key] = max(st["r"].get(key, 0), val)
        for w in writes:
            st = self.res.setdefault(w, {"w": {}, "r": {}})
            st["w"][key] = max(st["w"].get(key, 0), val)

    def op(self, e, fn, reads=(), writes=()):
        self._deps(e, reads, writes)
        inst = fn(self.eng[e])
        self.cnt[e] += 1
        inst.then_inc(self.sem[e], 1)
        snap = dict(self.seen[e])
        snap[e] = self.cnt[e]
        self.clock[(e, self.cnt[e])] = snap
        self._mark(e, self.cnt[e], reads, writes)

    def dma(self, out, in_, reads=(), writes=(), q="sp"):
        n = self.ndma
        self.ndma += 1
        i = n % NDMA
        key = ("dma", i)
        if n >= NDMA:
            self._wait(q, key, 16 * (n // NDMA))
        self._deps(q, reads, writes)
        self.eng[q].dma_start(out=out, in_=in_).then_inc(self.sem[key], 16)
        self.clock[(key, 16 * (n // NDMA + 1))] = dict(self.seen[q])
        self._mark(key, 16 * (n // NDMA + 1), reads, writes)

    def barrier(self, engines=("pe", "act", "dve", "pool")):
        for e in engines:
            for e2 in engines:
                if e2 != e and self.cnt[e2] > 0:
                    self._wait(e, e2, self.cnt[e2])

    def finish(self):
        for k in list(self.sem.keys()):
            if isinstance(k, tuple):
                i = k[1]
                uses = (self.ndma - 1 - i) // NDMA + 1 if self.ndma > i else 0
                if uses > 0:
                    self._wait("sp", k, 16 * uses)
            else:
                if self.cnt[k] > 0:
                    self._wait("sp", k, self.cnt[k])
        self.es.close()


def build(nseq=5, depth=4, mixers=("a", "b", "c", "d"), dbg=False):
    kb = KB()
    nc = kb.nc
    x_d = nc.dram_tensor("x", [nseq, S, D], F32, kind="ExternalInput").ap()
    y_d = nc.dram_tensor("y", [nseq, S, D], F32, kind="ExternalOutput").ap()
    w_in_d = nc.dram_tensor("w_in", [4, D, 3072], F32, kind="ExternalInput").ap()
    w_out_d = nc.dram_tensor("w_out", [4, D, D], F32, kind="ExternalInput").ap()
    ng_d = nc.dram_tensor("norm_g_t", [128, 4, 8], F32, kind="ExternalInput").ap()
    fg_d = nc.dram_tensor("final_g_b", [128, D], F32, kind="ExternalInput").ap()
    pwb_d = nc.dram_tensor("pool_w_blk", [128, 4, 2, 128], F32, kind="ExternalInput").ap()
    psc_d = nc.dram_tensor("pool_scale_t", [128, 4, 2], F32, kind="ExternalInput").ap()
    pcn_d = nc.dram_tensor("pool_const", [128, 2, 17], F32, kind="ExternalInput").ap()
    dbg_d = nc.dram_tensor("dbg", [128, 8, S], BF16, kind="ExternalOutput").ap() if dbg else None
    rope_d = nc.dram_tensor("rope_t", [128, 2, NT, 32], F32, kind="ExternalInput").ap()
    band_d = nc.dram_tensor("band", [128, 256], BF16, kind="ExternalInput").ap()
    rpb_t = nc.dram_tensor("rpbpad", [4, 4, 15, 128], F32, kind="ExternalInput")
    nam_d = nc.dram_tensor("na_mask", [128, 2368], BF16, kind="ExternalInput").ap()
    et_d = kb.dram("et", [4, 4, 128, 2368], BF16)
    lam_d = nc.dram_tensor("ssm_lam", [4, 128, 16, 3], F32, kind="ExternalInput").ap()
    bp_d = nc.dram_tensor("ssm_bp", [4, 128, 16, 16, 2], F32, kind="ExternalInput").ap()
    cp_d = nc.dram_tensor("ssm_cp", [4, 128, 16, 16, 2], F32, kind="ExternalInput").ap()
    dt_d = nc.dram_tensor("ssm_dt", [16, 4, 16], F32, kind="ExternalInput").ap()
    glw_d = nc.dram_tensor("glu_w_t", [4, 128, 2, 256], F32, kind="ExternalInput").ap()
    glb_d = nc.dram_tensor("glu_b_t", [128, 4, 2], F32, kind="ExternalInput").ap()
    sblk_d = kb.dram("sblk", [4, 16, 128, 11, 128], BF16)
    etab_d = kb.dram("etab", [4, 4, 128, 4, 2, 128], F32)
    ktab_t = nc.dram_tensor("ktab", [4, 16, 31, 16, 16], F32, kind="Internal")
    ktab_d = ktab_t.ap()
    gwb_d = kb.dram("gwb", [4, 128, 2, 256], BF16)
    wib_d = kb.dram("wib", [4, 12, 128, 8, 256], BF16)
    wob_d = kb.dram("wob", [4, 4, 128, 8, 256], BF16)

    x_res = kb.sb("x_res", [128, NT, D], F32)
    hT = kb.sb("hT", [128, 8, S], BF16)
    yT = kb.sb("yT", [128, 8, S], BF16)
    ARENA = 56 * 1024
    arena = kb.sb("arena", [128, ARENA // 2], BF16)
    wb = [kb.sb("wb%d" % i, [128, 8, 256], BF16) for i in range(3)]
    ng = kb.sb("ng", [128, 4, 8], F32)
    ident = kb.sb("ident", [128, 128], BF16)
    identf = kb.sb("identf", [128, 128], F32)
    small = kb.sb("small", [128, 64], F32)
    pwb = kb.sb("pwb", [128, 4, 2, 128], BF16)
    psc = kb.sb("psc", [128, 4, 2], F32)
    pcn = kb.sb("pcn", [128, 2, 17], F32)
    ropet = kb.sb("ropet", [128, 2, NT, 32], F32)
    band = kb.sb("band_sb", [128, 256], BF16)
    rho_sb = kb.sb("rho_sb", [128, 4, 16], F32)
    glb = kb.sb("glb", [128, 4, 2], F32)
    pp = [kb.ps("pp%d" % i, [128, 512], F32) for i in range(8)]

    GR = 256

    def ak(off, nbytes):
        return [("ar", j) for j in range(off // GR, (off + nbytes - 1) // GR + 1)]

    dumped = set()

    def dump(name, src, reads):
        if not dbg or name in dumped:
            return
        dumped.add(name)
        shp = list(src.shape)
        dd = nc.dram_tensor("d_" + name, shp, src.dtype, kind="ExternalOutput").ap()
        kb.dma(dd, src, reads=reads, writes=["dump_" + name])

    def ky(c0, c1=None, h=None):
        c1 = c0 + 1 if c1 is None else c1
        hs_ = (0, 1) if h is None else (h,)
        return [("yT", c, hh) for c in range(c0, c1) for hh in hs_]

    def av(off, shape, dt):
        n = int(np.prod(shape))
        esz = 4 if dt in (F32, I32) else 2
        a = arena[:, off // 2: off // 2 + n * esz // 2]
        if dt != BF16:
            a = a.bitcast(dt)
        if len(shape) == 2:
            return a.rearrange("p (a b) -> p a b", a=shape[0])
        if len(shape) == 3:
            return a.rearrange("p (a b c) -> p a b c", a=shape[0], b=shape[1])
        return a

    pstate = {"n": 0}

    def next_bank(lo=0, hi=2):
        i = lo + pstate.setdefault((lo, hi), 0) % (hi - lo)
        pstate[(lo, hi)] += 1
        return i

    hs = [av(16384 + i * 2048, [D], BF16) for i in range(6)]
    khs = [ak(16384 + i * 2048, 2048) for i in range(6)]
    kb.dma(ng[:], ng_d, writes=["ng"])
    pwst = av(8192, [4 * 2 * 128], F32)
    kb.dma(pwst, pwb_d.rearrange("p a b c -> p (a b c)"), writes=ak(8192, 4096))
    kb.op("dve", lambda v: v.tensor_copy(out=pwb[:].rearrange("p a b c -> p (a b c)"), in_=pwst), reads=ak(8192, 4096), writes=["pwb"])
    kb.dma(psc[:], psc_d, writes=["psc"])
    kb.op("dve", lambda v: v.tensor_scalar(out=psc[:], in0=psc[:], scalar1=0.5, scalar2=None, op0=ALU.mult), reads=["psc"], writes=["psc"])
    kb.dma(pcn[:], pcn_d, writes=["pcn"])
    kb.dma(ropet[:], rope_d, writes=["ropet"])
    kb.dma(band[:], band_d, writes=["band"])
    io = av(0, [128], I32)
    kb.op("pool", lambda g: g.iota(io, [[1, 128]], base=0, channel_multiplier=-1), writes=ak(0, 512))
    kb.op("dve", lambda v: v.tensor_single_scalar(out=identf[:], in_=io, scalar=0, op=ALU.is_equal),
          reads=ak(0, 512), writes=["identf"])
    kb.op("dve", lambda v: v.tensor_copy(out=ident[:], in_=identf[:]), reads=["identf"], writes=["ident"])

    hTf = hT[:].rearrange("p a b -> p (a b)")
    yTf = yT[:].rearrange("p a b -> p (a b)")

    def tv(flat, off, n, dt):
        esz = 4 if dt == F32 else 2
        v = flat[:, off // 2: off // 2 + n * esz // 2]
        return v.bitcast(dt) if dt != BF16 else v

    wtasks = {}
    for l in range(depth):
        lst = []
        for c in range(8):
            def w_in_task(l=l, c=c):
                k = c % 2
                st = tv(hTf, k * 12288, 3072, F32)
                sb_ = tv(yTf, k * 6144, 3072, BF16)
                kst, ksb = [("hTs", k)], [("yTs", k)]
                kb.dma(st, w_in_d[l, c * 128:(c + 1) * 128, :], writes=kst, q="act")
                kb.op("pool" if k else "dve",
                      lambda v: v.tensor_scalar(out=sb_, in0=st, scalar1=ng[:, l, c:c + 1], scalar2=1.0,
                                                op0=ALU.mult, op1=ALU.mult),
                      reads=kst + ["ng"], writes=ksb)
                kb.dma(wib_d[l, :, :, c, :].rearrange("b p n -> p b n"), sb_.rearrange("p (b n) -> p b n", b=12),
                       reads=ksb, writes=["wib"], q="act")

            def w_out_task(l=l, c=c):
                k = c % 2
                st = tv(yTf, 12288 + k * 4096, 1024, F32)
                sb_ = tv(yTf, 20480 + k * 2048, 1024, BF16)
                kst, ksb = [("yTs", 2 + k)], [("yTs", 4 + k)]
                kb.dma(st, w_out_d[l, c * 128:(c + 1) * 128, :], writes=kst, q="act")
                kb.op("dve" if k else "pool", lambda v: v.tensor_copy(out=sb_, in_=st), reads=kst, writes=ksb)
                kb.dma(wob_d[l, :, :, c, :].rearrange("h p n -> p h n"), sb_.rearrange("p (h n) -> p h n", h=4),
                       reads=ksb, writes=["wob"], q="act")
            lst.append(w_in_task)
            lst.append(w_out_task)
        wtasks[l] = lst
    if "c" not in mixers:
        for l in range(depth):
            for t_ in wtasks[l]:
                t_()

    xf = x_res[:].rearrange("p a b -> p (a b)").bitcast(BF16)
    na_tasks = {}
    if "d" in mixers:
        nmask = tv(xf, 0, 2368, BF16)
        kb.dma(nmask, nam_d, writes=[("xs", "m")])
        negm = tv(xf, 33152, 2368, F32)
        knegm = [("xs", "n")]
        kb.op("dve", lambda v: v.tensor_scalar(out=negm, in0=nmask, scalar1=-1.0, scalar2=240000.0, op0=ALU.add, op1=ALU.mult),
              reads=[("xs", "m")], writes=knegm)
        for l in range(depth):
            for h in range(4):
                def na_task(l=l, h=h):
                    k = (l * 4 + h) % 2
                    o_st = 4736 + k * 14208
                    stg = tv(xf, o_st, 2368, F32)
                    eo = tv(xf, o_st + 9472, 2368, BF16)
                    kst, keo = [("xs", "s", k)], [("xs", "e", k)]
                    base = (l * 4 + h) * 15 * 128
                    for krl in range(2):
                        ps_ = slice(krl * 64, krl * 64 + 64)
                        src = bass.AP(rpb_t, base + (4 - krl) * 128, [[1, 64], [128, 9], [1, 64]])
                        kb.dma(stg[ps_, 0:576].rearrange("p (r c) -> p r c", r=9), src, writes=kst)
                        for sr in range(7):
                            r = sr if sr < 4 else 25 + sr
                            if sr < 4:
                                i0 = 1 - krl + r
                                dst = stg[ps_, 576:1600].rearrange("p (u s c) -> p u s c", u=4, s=4)[:, :, sr, :]
                            else:
                                i0 = r - 23 - krl
                                dst = stg[ps_, 1600:2368].rearrange("p (u s c) -> p u s c", u=4, s=3)[:, :, sr - 4, :]
                            src = bass.AP(rpb_t, base + i0 * 128, [[1, 64], [256, 4], [1, 64]])
                            kb.dma(dst, src, writes=kst)
                    st3 = stg.rearrange("p (b c) -> p b c", c=64)
                    kb.op("dve", lambda v: v.scalar_tensor_tensor(
                        out=st3, in0=st3, scalar=8.0, in1=nmask.rearrange("p (b c) -> p b c", c=64),
                        op0=ALU.mult, op1=ALU.mult), reads=kst + [("xs", "m")], writes=kst)
                    kb.op("dve", lambda v: v.tensor_tensor(
                        out=eo.rearrange("p (b c) -> p b c", c=64), in0=st3[:, :, ::-1],
                        in1=negm.rearrange("p (b c) -> p b c", c=64)[:, :, ::-1],
                        op=ALU.add), reads=kst + knegm, writes=keo)
                    kb.dma(et_d[l, h], eo, reads=keo, writes=["et"])
                na_tasks[(l, h)] = na_task
        if "c" not in mixers:
            for l in range(depth):
                for h in range(4):
                    na_tasks[(l, h)]()

    TWO_PI = 2.0 * math.pi

    def s5_precompute(l):
        kb.same_depth = 1000000
        st = {"o": 0}

        def A(shape, dt=F32):
            n = int(np.prod(shape))
            nb = n * (4 if dt in (F32, I32) else 2)
            off = st["o"]
            st["o"] = off + (nb + 63) // 64 * 64
            v = arena[:, off // 2: off // 2 + nb // 2]
            if dt != BF16:
                v = v.bitcast(dt)
            if len(shape) == 2:
                v = v.rearrange("p (a b) -> p a b", a=shape[0])
            elif len(shape) == 3:
                v = v.rearrange("p (a b c) -> p a b c", a=shape[0], b=shape[1])
            return v, ak(off, nb)

        def tt_(e, out, a, b, op, rd, wr):
            kb.op(e, lambda v: v.tensor_tensor(out=out, in0=a, in1=b, op=op), reads=rd, writes=wr)

        def ts_(e, out, a, s1, s2, op0, op1, rd, wr):
            kb.op(e, lambda v: v.tensor_scalar(out=out, in0=a, scalar1=s1, scalar2=s2, op0=op0, op1=op1) if s2 is not None
                  else v.tensor_scalar(out=out, in0=a, scalar1=s1, scalar2=None, op0=op0), reads=rd, writes=wr)

        def frac_(T, kT, TI, kTI, TF, kTF):
            MAGIC = 12582912.0
            ts_("dve", TF, T, MAGIC, None, ALU.add, None, kT, kTF)
            ts_("dve", TF, TF, -MAGIC, None, ALU.add, None, kTF, kTF)
            tt_("dve", T, T, TF, ALU.subtract, kT + kTF, kT)
            kb.op("dve", lambda v: v.tensor_single_scalar(out=TF, in_=T, scalar=0.5, op=ALU.is_gt), reads=kT, writes=kTF)
            tt_("dve", T, T, TF, ALU.subtract, kT + kTF, kT)
            kb.op("dve", lambda v: v.tensor_single_scalar(out=TF, in_=T, scalar=-0.5, op=ALU.is_lt), reads=kT, writes=kTF)
            tt_("dve", T, T, TF, ALU.add, kT + kTF, kT)

        def sincos_(T, kT, SN, kSN, CS, kCS, TI, kTI, TF, kTF):
            frac_(T, kT, TI, kTI, TF, kTF)
            kb.op("act", lambda a: a.activation(out=SN, in_=T, func=AF.Sin, scale=6.28318), reads=kT, writes=kSN)
            ts_("dve", T, T, 0.25, None, ALU.add, None, kT, kT)
            kb.op("dve", lambda v: v.tensor_single_scalar(out=TF, in_=T, scalar=0.5, op=ALU.is_gt), reads=kT, writes=kTF)
            tt_("dve", T, T, TF, ALU.subtract, kT + kTF, kT)
            kb.op("act", lambda a: a.activation(out=CS, in_=T, func=AF.Sin, scale=6.28318), reads=kT, writes=kCS)

        lam, klam = A([16, 3])
        Bp, kBp = A([16, 16, 2])
        Cp, kCp = A([16, 16, 2])
        Dt, kDt = A([16])
        kb.dma(lam, lam_d[l], writes=klam)
        kb.dma(Bp, bp_d[l], writes=kBp)
        kb.dma(Cp, cp_d[l], writes=kCp)
        kb.dma(Dt[0:16, :], dt_d[:, l, :], writes=kDt)
        BBr, kBBr = A([16, 16])
        BBi, kBBi = A([16, 16])
        nBBi, knBBi = A([16, 16])
        WP, kWP = A([2, 16, 16])
        WC, kWC = A([2, 16, 16])
        WK, kWK = A([2, 16, 31])
        mark = st["o"]
        dtt, kdt = A([16])
        xr, kxr = A([16])
        tht, ktht = A([16])
        NNi, kNNi = A([128], I32)
        NN, kNN = A([128])
        TI, kTI = A([512], I32)
        TF, kTF = A([512])
        ARG, kARG = A([16, 17])
        MAG, kMAG = A([16, 17])
        SN, kSN = A([16, 17])
        CS, kCS = A([16, 17])
        WR, kWR = A([16, 17])
        WI, kWI = A([16, 17])
        kb.op("act", lambda a: a.activation(out=dtt, in_=lam[:, :, 2], func=AF.Exp), reads=klam, writes=kdt)
        tt_("dve", xr, lam[:, :, 0], dtt, ALU.mult, klam + kdt, kxr)
        tt_("dve", tht, lam[:, :, 1], dtt, ALU.mult, klam + kdt, ktht)
        ts_("dve", tht, tht, 1.0 / TWO_PI, None, ALU.mult, None, ktht, ktht)
        frac_(tht, ktht, TI[:, 0:16], kTI, TF[:, 0:16], kTF)
        kb.op("pool", lambda g: g.iota(NNi, [[1, 128]], base=0, channel_multiplier=0), writes=kNNi)
        kb.op("dve", lambda v: v.tensor_copy(out=NN, in_=NNi), reads=kNNi, writes=kNN)
        nb17 = NN[:, 0:17].unsqueeze(1).broadcast_to([128, 16, 17])
        tt_("dve", ARG, tht.unsqueeze(2).broadcast_to([128, 16, 17]), nb17, ALU.mult, ktht + kNN, kARG)
        tt_("dve", MAG, xr.unsqueeze(2).broadcast_to([128, 16, 17]), nb17, ALU.mult, kxr + kNN, kMAG)
        kb.op("act", lambda a: a.activation(out=MAG, in_=MAG, func=AF.Exp), reads=kMAG, writes=kMAG)
        f2 = lambda t: t.rearrange("p a b -> p (a b)")
        sincos_(f2(ARG), kARG, f2(SN), kSN, f2(CS), kCS, TI[:, 0:272], kTI, TF[:, 0:272], kTF)
        tt_("dve", WR, MAG, CS, ALU.mult, kMAG + kCS, kWR)
        tt_("dve", WI, MAG, SN, ALU.mult, kMAG + kSN, kWI)
        if l == 0:
            dump("WR", WR, kWR)
            dump("WI", WI, kWI)
            dump("dtt", dtt, kdt)
            dump("xr", xr, kxr)
            dump("tht", tht, ktht)
            dump("NN", NN, kNN)
            dump("MAG", MAG, kMAG)
            dump("SN", SN, kSN)
            dump("CS", CS, kCS)
            dump("lam", lam, klam)
        kb.op("act", lambda a: a.copy(out=rho_sb[:, l, :], in_=MAG[:, :, 16]), reads=kMAG, writes=["rho"])
        den, kden = A([16])
        t1, kt1 = A([16])
        t2, kt2 = A([16])
        gr, kgr = A([16])
        gi, kgi = A([16])
        lr_, li_ = lam[:, :, 0], lam[:, :, 1]
        tt_("dve", den, lr_, lr_, ALU.mult, klam, kden)
        tt_("dve", t1, li_, li_, ALU.mult, klam, kt1)
        tt_("dve", den, den, t1, ALU.add, kden + kt1, kden)
        kb.op("dve", lambda v: v.reciprocal(out=den, in_=den), reads=kden, writes=kden)
        ts_("dve", t1, WR[:, :, 1], -1.0, None, ALU.add, None, kWR, kt1)
        tt_("dve", gr, t1, lr_, ALU.mult, kt1 + klam, kgr)
        tt_("dve", t2, WI[:, :, 1], li_, ALU.mult, kWI + klam, kt2)
        tt_("dve", gr, gr, t2, ALU.add, kgr + kt2, kgr)
        tt_("dve", gr, gr, den, ALU.mult, kgr + kden, kgr)
        tt_("dve", gi, WI[:, :, 1], lr_, ALU.mult, kWI + klam, kgi)
        tt_("dve", t2, t1, li_, ALU.mult, kt1 + klam, kt2)
        tt_("dve", gi, gi, t2, ALU.subtract, kgi + kt2, kgi)
        tt_("dve", gi, gi, den, ALU.mult, kgi + kden, kgi)
        u1, ku1 = A([16, 16])
        grb = gr.unsqueeze(2).broadcast_to([128, 16, 16])
        gib = gi.unsqueeze(2).broadcast_to([128, 16, 16])
        Br_, Bi_ = Bp[:, :, :, 0], Bp[:, :, :, 1]
        tt_("dve", BBr, grb, Br_, ALU.mult, kgr + kBp, kBBr)
        tt_("dve", u1, gib, Bi_, ALU.mult, kgi + kBp, ku1)
        tt_("dve", BBr, BBr, u1, ALU.subtract, kBBr + ku1, kBBr)
        tt_("dve", BBi, grb, Bi_, ALU.mult, kgr + kBp, kBBi)
        tt_("dve", u1, gib, Br_, ALU.mult, kgi + kBp, ku1)
        tt_("dve", BBi, BBi, u1, ALU.add, kBBi + ku1, kBBi)
        ts_("dve", nBBi, BBi, -1.0, None, ALU.mult, None, kBBi, knBBi)
        kb.op("pool", lambda g: g.memset(WK, 0.0), writes=kWK)
        F_, B_ = slice(0, 64), slice(64, 128)
        for ri, W_, kW_ in ((0, WR, kWR), (1, WI, kWI)):
            cp = lambda dst, src, wk: kb.op("act", lambda a: a.copy(out=dst, in_=src), reads=kW_, writes=wk)
            cp(WP[F_, ri, :, 0:8], W_[F_, :, 8:16], kWP)
            cp(WP[F_, ri, :, 8:16], W_[F_, :, 0:8], kWP)
            cp(WP[B_, ri, :, 0:8], W_[B_, :, 7::-1], kWP)
            cp(WP[B_, ri, :, 8:16], W_[B_, :, 15:7:-1], kWP)
            cp(WC[F_, ri, :, :], W_[F_, :, 1:17], kWC)
            cp(WC[B_, ri, :, :], W_[B_, :, 16:0:-1], kWC)
            cp(WK[F_, ri, :, 15:31], W_[F_, :, 0:16], kWK)
            cp(WK[B_, ri, :, 0:16], W_[B_, :, 15::-1], kWK)
        pht, kpht = A([16])
        ts_("dve", pht, tht, 16.0, None, ALU.mult, None, ktht, kpht)
        frac_(pht, kpht, TI[:, 0:16], kTI, TF[:, 0:16], kTF)
        EA, kEA = A([4, 128])
        ES, kES = A([4, 2, 128])
        for q in range(4):
            tt_("dve", EA, pht[:, 4 * q:4 * q + 4].unsqueeze(2).broadcast_to([128, 4, 128]),
                NN.unsqueeze(1).broadcast_to([128, 4, 128]), ALU.mult, kpht + kNN, kEA)
            sincos_(f2(EA), kEA, ES[:, :, 1, :], kES, ES[:, :, 0, :], kES, TI[:, 0:512], kTI, TF[:, 0:512], kTF)
            kb.dma(etab_d[l, q], ES, reads=kES, writes=["etab"])
            if l == 0 and q == 0:
                dump("ES0", ES, kES)
        gst, kgst = A([2, 256])
        gbf, kgbf = A([2, 256], BF16)
        kb.dma(gst, glw_d[l], writes=kgst)
        kb.op("act", lambda a: a.copy(out=gbf, in_=gst), reads=kgst, writes=kgbf)
        kb.dma(gwb_d[l], gbf, reads=kgbf, writes=["gwb"])
        st["o"] = mark
        bufs = []
        for par in range(2):
            d_ = {}
            d_["PPr"], d_["kPPr"] = A([2, 128])
            d_["PPi"], d_["kPPi"] = A([2, 128])
            d_["q1"], d_["kq1"] = A([2, 128])
            d_["q2"], d_["kq2"] = A([2, 128])
            d_["Rr"], d_["kRr"] = A([31, 16])
            d_["Ri"], d_["kRi"] = A([31, 16])
            d_["r1"], d_["kr1"] = A([31, 16])
            d_["r2"], d_["kr2"] = A([31, 16])
            d_["Ks"], d_["kKs"] = A([496])
            d_["Ms"], d_["kMs"] = A([3, 128])
            d_["blk"], d_["kblk"] = A([11, 128], BF16)
            bufs.append(d_)
        assert st["o"] <= ARENA, st["o"]
        kb.same_depth = 3
        for g in range(16):
            wtasks[l][g]()
            if g % 4 == 0 and (l, g // 4) in na_tasks:
                na_tasks[(l, g // 4)]()
            d_ = bufs[g % 2]
            PPr, PPi, q1, q2 = d_["PPr"], d_["PPi"], d_["q1"], d_["q2"]
            kPPr, kPPi, kq1, kq2 = d_["kPPr"], d_["kPPi"], d_["kq1"], d_["kq2"]
            blk, kblk = d_["blk"], d_["kblk"]
            v4 = lambda t: t.rearrange("p s (j c) -> p s j c", j=8)
            wpr = WP[:, 0, g, :].rearrange("p (s j) -> p s j", s=2).unsqueeze(3).broadcast_to([128, 2, 8, 16])
            wpi = WP[:, 1, g, :].rearrange("p (s j) -> p s j", s=2).unsqueeze(3).broadcast_to([128, 2, 8, 16])
            bbr = BBr[:, g, :].unsqueeze(1).unsqueeze(1).broadcast_to([128, 2, 8, 16])
            bbi = BBi[:, g, :].unsqueeze(1).unsqueeze(1).broadcast_to([128, 2, 8, 16])
            e1, e2 = ("dve", "pool") if g % 2 == 0 else ("pool", "dve")
            tt_(e1, v4(q1), wpr, bbr, ALU.mult, kWP + kBBr, kq1)
            tt_(e2, v4(q2), wpi, bbi, ALU.mult, kWP + kBBi, kq2)
            tt_(e1, PPr, q1, q2, ALU.subtract, kq1 + kq2, kPPr)
            tt_(e2, v4(q1), wpr, bbi, ALU.mult, kWP + kBBi, kq1)
            tt_(e1, v4(q2), wpi, bbr, ALU.mult, kWP + kBBr, kq2)
            tt_(e2, PPi, q1, q2, ALU.add, kq1 + kq2, kPPi)
            bt = next_bank(6, 8)
            for j, (src, ksrc) in enumerate(((PPr[:, 0, :], kPPr), (PPr[:, 1, :], kPPr), (PPi[:, 0, :], kPPi), (PPi[:, 1, :], kPPi))):
                kb.op("pe", lambda pe, j=j, src=src: pe.transpose(pp[bt][:, j * 128:(j + 1) * 128], src, identf[:]),
                      reads=ksrc + ["identf"], writes=[("pp", bt)])
            kb.op("act", lambda a: a.copy(out=blk[:, 3:7, :], in_=pp[bt][:, :].rearrange("p (a b) -> p a b", a=4)),
                  reads=[("pp", bt)], writes=kblk)
            c4 = lambda t: t.rearrange("p s (i c) -> p (s i) c", i=8)
            wcr = WC[:, 0, g, :].unsqueeze(2).broadcast_to([128, 16, 16])
            wci = WC[:, 1, g, :].unsqueeze(2).broadcast_to([128, 16, 16])
            cr = Cp[:, g, :, 0].unsqueeze(1).broadcast_to([128, 16, 16])
            ci = Cp[:, g, :, 1].unsqueeze(1).broadcast_to([128, 16, 16])
            tt_(e1, c4(q1), cr, wcr, ALU.mult, kCp + kWC, kq1)
            tt_(e2, c4(q2), ci, wci, ALU.mult, kCp + kWC, kq2)
            tt_(e1, blk[:, 7:10:2, :], q1, q2, ALU.subtract, kq1 + kq2, kblk)
            tt_(e2, c4(q1), cr, wci, ALU.mult, kCp + kWC, kq1)
            tt_(e1, c4(q2), ci, wcr, ALU.mult, kCp + kWC, kq2)
            kb.op("dve", lambda v: v.scalar_tensor_tensor(out=blk[:, 8:11:2, :], in0=q1, scalar=-1.0, in1=q2,
                                                           op0=ALU.mult, op1=ALU.subtract),
                  reads=kq1 + kq2, writes=kblk)
            Rr, Ri, r1, r2 = d_["Rr"], d_["Ri"], d_["r1"], d_["r2"]
            kRr, kRi, kr1, kr2 = d_["kRr"], d_["kRi"], d_["kr1"], d_["kr2"]
            wkr = WK[:, 0, g, :].unsqueeze(2).broadcast_to([128, 31, 16])
            wki = WK[:, 1, g, :].unsqueeze(2).broadcast_to([128, 31, 16])
            cr3 = Cp[:, g, :, 0].unsqueeze(1).broadcast_to([128, 31, 16])
            ci3 = Cp[:, g, :, 1].unsqueeze(1).broadcast_to([128, 31, 16])
            tt_(e1, r1, cr3, wkr, ALU.mult, kCp + kWK, kr1)
            tt_(e2, r2, ci3, wki, ALU.mult, kCp + kWK, kr2)
            tt_(e1, Rr, r1, r2, ALU.subtract, kr1 + kr2, kRr)
            tt_(e2, r1, cr3, wki, ALU.mult, kCp + kWK, kr1)
            tt_(e1, r2, ci3, wkr, ALU.mult, kCp + kWK, kr2)
            tt_(e2, Ri, r1, r2, ALU.add, kr1 + kr2, kRi)
            bk = next_bank(4, 6)
            kb.op("pe", lambda pe: pe.matmul(pp[bk][0:16, 0:496], BBr[:, g, :], Rr.rearrange("p a b -> p (a b)"),
                                             start=True, stop=False), reads=kBBr + kRr, writes=[("pp", bk)])
            kb.op("pe", lambda pe: pe.matmul(pp[bk][0:16, 0:496], nBBi[:, g, :], Ri.rearrange("p a b -> p (a b)"),
                                             start=False, stop=True), reads=knBBi + kRi, writes=[("pp", bk)])
            Ks, kKs = d_["Ks"], d_["kKs"]
            kb.op("act", lambda a: a.copy(out=Ks[0:16, :], in_=pp[bk][0:16, 0:496]), reads=[("pp", bk)], writes=kKs)
            kb.op("dve", lambda v: v.scalar_tensor_tensor(out=Ks[0:16, 240:256], in0=identf[0:16, 0:16],
                                                           scalar=Dt[0:16, g:g + 1], in1=Ks[0:16, 240:256],
                                                           op0=ALU.mult, op1=ALU.add),
                  reads=kKs + kDt + ["identf"], writes=kKs)
            kb.dma(ktab_d[l, g].rearrange("i c o -> c i o"), Ks[0:16, :].rearrange("p (i o) -> p i o", i=31),
                   reads=kKs, writes=[("ktab", l, g)])
            Ms, kMs = d_["Ms"], d_["kMs"]
            kbase = (l * 16 + g) * 31 * 256
            for bi_, boff in enumerate((8, 16, 0)):
                src = bass.AP(ktab_t, kbase + boff * 256, [[16, 128], [256, 8], [1, 16]])
                kb.dma(Ms[:, bi_, :].rearrange("p (i o) -> p i o", i=8), src, reads=[("ktab", l, g)], writes=kMs)
            kb.op("act", lambda a: a.copy(out=blk[:, 0:3, :], in_=Ms), reads=kMs, writes=kblk)
            kb.dma(sblk_d[l, g], blk, reads=kblk, writes=["sblk"])
            if l == 0 and g == 0:
                dump("blk0", blk, kblk)
                dump("Ks0", Ks[0:16, :], kKs)
                dump("BBr", BBr, kBBr)
                dump("BBi", BBi, kBBi)
                dump("WP", WP, kWP)
                dump("WC", WC, kWC)
                dump("WK", WK, kWK)
                dump("PPr", PPr, kPPr)

    if "c" in mixers:
        kb.dma(glb[:], glb_d, writes=["glb"])
        kb.op("dve", lambda v: v.tensor_scalar(out=glb[:], in0=glb[:], scalar1=0.5, scalar2=None, op0=ALU.mult),
              reads=["glb"], writes=["glb"])
        for l in range(depth):
            s5_precompute(l)

    wstate = {"n": 0}

    def load_wblock(l, b):
        k = wstate["n"] % 3
        wstate["n"] += 1
        kb.dma(wb[k][:], wib_d[l, b], reads=["wib"], writes=[("wb", k)])
        return k

    def load_woblock(l, h):
        k = wstate["n"] % 3
        wstate["n"] += 1
        kb.dma(wb[k][:], wob_d[l, h], reads=["wob"], writes=[("wb", k)])
        return k


    def proj_fm(k, cb, evac, tqs=range(4), M=128, moff=0):
        for tq in tqs:
            bi = next_bank(0, 2)
            for c in range(8):
                kb.op("pe", lambda pe, c=c, bi=bi, tq=tq: pe.matmul(
                    pp[bi][0:M, :], wb[k][:, c, cb * 128 + moff: cb * 128 + moff + M], hT[:, c, tq * 512:(tq + 1) * 512],
                    start=(c == 0), stop=(c == 7)),
                    reads=[("wb", k), "hT"], writes=[("pp", bi)])
            evac(tq, bi)

    def proj_tm(k, tok_ap_fn, ntiles, evac, ncols=256):
        for i in range(ntiles):
            bi = next_bank(0, 2)
            for c in range(8):
                kb.op("pe", lambda pe, c=c, bi=bi, i=i: pe.matmul(
                    pp[bi][:, 0:ncols], tok_ap_fn(c, i), wb[k][:, c, 0:ncols], start=(c == 0), stop=(c == 7)),
                    reads=[("wb", k), "hT"], writes=[("pp", bi)])
            evac(i, bi)

    def rms_all():
        for i in range(NT):
            junk = hs[4 + i % 2]
            kb.op("act", lambda a, i=i, junk=junk: a.activation(out=junk, in_=x_res[:, i, :], func=AF.Square,
                                                                accum_out=small[:, 16 + i:17 + i]),
                  reads=[("x", i)], writes=khs[4 + i % 2] + ["ss"])
        kb.op("dve", lambda v: v.tensor_scalar(out=small[:, 32:48], in0=small[:, 16:32],
                                               scalar1=1.0 / D, scalar2=EPS, op0=ALU.mult, op1=ALU.add),
              reads=["ss"], writes=["ms"])
        kb.op("pool", lambda g: g.tensor_tensor(out=small[:, 0:16], in0=small[:, 32:48],
                                                in1=small[:, 48:49].broadcast_to([128, 16]), op=ALU.pow),
              reads=["ms", "mhalf"], writes=["rstd"])

    kb.op("dve", lambda v: v.memset(small[:, 48:49], -0.5), writes=["mhalf"])
    nlh = small[:, 49:50]
    kb.op("dve", lambda v: v.memset(nlh, -math.log(2.0)), writes=["nlh"])

    W = S + 32

    def mixer_a(l):
        kU, kA, kB, kg2, kDm = ak(0, 8320), ak(8320, 8320), ak(16640, 8320), ak(24960, 4096), ak(29056, 4096)
        ktt = [ak(33152 + j * 2048, 2048) for j in range(2)]
        U = av(0, [W], F32)
        A = av(8320, [W], F32)
        B = av(16640, [W], F32)
        g2 = av(24960, [S], BF16)
        Dm = av(29056, [S], BF16)
        tt = [av(33152 + j * 2048, [512], F32) for j in range(2)]
        for cb in range(2):
            kv = load_wblock(l, 0)
            kg = load_wblock(l, 1)
            kb.op("pool", lambda g: g.memset(U[:, 0:16], 0.0), writes=kU)
            kb.op("pool", lambda g: g.memset(U[:, 16 + S:W], 0.0), writes=kU)

            def ev_u(tq, bi):
                kb.op("act", lambda a: a.copy(out=U[:, 16 + tq * 512:16 + (tq + 1) * 512], in_=pp[bi][:, :]),
                      reads=[("pp", bi)], writes=kU)
            proj_fm(kv, cb, ev_u)
            kb.op("pool", lambda g: g.tensor_tensor(out=A[:, 1:W], in0=U[:, 0:W - 1], in1=U[:, 1:W], op=ALU.add),
                  reads=kU, writes=kA)
            kb.op("pool", lambda g: g.tensor_tensor(out=B[:, 2:W - 1], in0=A[:, 1:W - 2], in1=A[:, 3:W], op=ALU.add),
                  reads=kA, writes=kB)
            if cb == 1:
                kb.op("pool", lambda g: g.tensor_tensor(out=A[:, 4:W - 3], in0=B[:, 2:W - 5], in1=B[:, 6:W - 1], op=ALU.add),
                      reads=kB, writes=kA)
                kb.op("pool", lambda g: g.tensor_tensor(out=B[64:128, 8:W - 7], in0=A[64:128, 4:W - 11],
                                                        in1=A[64:128, 12:W - 3], op=ALU.add),
                      reads=kA, writes=kB)
            for (buf, nm, p0) in ((A, kA, 0), (B, kB, 64)):
                sl = slice(p0, p0 + 64)
                kb.op("dve", lambda v, buf=buf, sl=sl: v.tensor_tensor(
                    out=buf[sl, 16:24], in0=buf[sl, 16:24], in1=pcn[sl, cb, 0:8], op=ALU.mult),
                    reads=nm + ["pcn"], writes=nm)
                kb.op("dve", lambda v, buf=buf, sl=sl: v.tensor_tensor(
                    out=buf[sl, 8 + S:16 + S], in0=buf[sl, 8 + S:16 + S], in1=pcn[sl, cb, 8:16], op=ALU.mult),
                    reads=nm + ["pcn"], writes=nm)
                kb.op("dve", lambda v, buf=buf, sl=sl: v.scalar_tensor_tensor(
                    out=Dm[sl, :], in0=buf[sl, 16:16 + S], scalar=pcn[sl, cb, 16:17], in1=U[sl, 16:16 + S],
                    op0=ALU.mult, op1=ALU.subtract),
                    reads=nm + kU + ["pcn"], writes=kDm)

            def ev_g(tq, bi):
                t = tt[tq % 2]
                kb.op("act", lambda a: a.activation(out=t, in_=pp[bi][:, :], func=AF.Tanh, scale=0.5),
                      reads=[("pp", bi)], writes=ktt[tq % 2])
                kb.op("dve", lambda v: v.scalar_tensor_tensor(
                    out=g2[:, tq * 512:(tq + 1) * 512], in0=t, scalar=1.0, in1=pp[bi][:, :], op0=ALU.add, op1=ALU.mult),
                    reads=ktt[tq % 2] + [("pp", bi)], writes=kg2)
            proj_fm(kg, cb, ev_g)
            for tq in range(4):
                bi = next_bank(0, 2)
                kb.op("pe", lambda pe: pe.matmul(pp[bi][:, :], pwb[:, l, cb, :], Dm[:, tq * 512:(tq + 1) * 512],
                                                 start=True, stop=True),
                      reads=["pwb"] + kDm, writes=[("pp", bi)])
                kb.op("dve", lambda v: v.scalar_tensor_tensor(
                    out=yT[:, cb, tq * 512:(tq + 1) * 512], in0=pp[bi][:, :], scalar=psc[:, l, cb:cb + 1],
                    in1=g2[:, tq * 512:(tq + 1) * 512], op0=ALU.mult, op1=ALU.mult),
                    reads=[("pp", bi), "psc"] + kg2, writes=ky(cb))

    def run_pipeline(tasks, la):
        n = len(tasks)
        for i in range(n + la):
            if i < n:
                t = tasks[i]
                t["slot"] = i
                if t.get("pre"):
                    t["pre"]()
                t["s1"]()
            if i >= la:
                t = tasks[i - la]
                t["s2"]()
                if t.get("post"):
                    t["post"]()

    def run_pipeline_b(tasks, bs):
        n = len(tasks)
        nb = (n + bs - 1) // bs
        for b in range(nb + 1):
            if b < nb:
                for i in range(b * bs, min(n, (b + 1) * bs)):
                    t = tasks[i]
                    t["slot"] = i
                    if t.get("pre"):
                        t["pre"]()
                for i in range(b * bs, min(n, (b + 1) * bs)):
                    tasks[i]["s1a"]()
                for i in range(b * bs, min(n, (b + 1) * bs)):
                    tasks[i]["s1b"]()
            if b >= 1:
                for i in range((b - 1) * bs, min(n, b * bs)):
                    t = tasks[i]
                    t["s2"]()
                    if t.get("post"):
                        t["post"]()

    def gate_fm(k, kg2, g2, ktt, tt):
        for cb in range(2):
            def ev_g(tq, bi):
                t = tt[tq % 2]
                kb.op("act", lambda a: a.activation(out=t, in_=pp[bi][:, :], func=AF.Tanh, scale=0.5),
                      reads=[("pp", bi)], writes=ktt[tq % 2])
                kb.op("dve", lambda v: v.scalar_tensor_tensor(
                    out=g2[:, cb, tq * 512:(tq + 1) * 512], in0=t, scalar=1.0, in1=pp[bi][:, :], op0=ALU.add, op1=ALU.mult),
                    reads=ktt[tq % 2] + [("pp", bi)], writes=kg2)
            proj_fm(k, cb, ev_g)

    def psl(p0, n, d, n_sub):
        st = (p0 % n_sub) * d + p0 // n_sub
        return slice(st, st + (n - 1) * d + 1, d)

    def attn_finalize(h, acc, kacc, g2, kg2, rc, krc, tmp, ktmp, chunk0):
        nr = slice((h % 2) * 64, (h % 2) * 64 + 64)
        dr = slice(((h + 1) % 2) * 64, ((h + 1) % 2) * 64 + 64)
        for tq in range(8):
            ts_ = slice(tq * 256, (tq + 1) * 256)
            kb.op("dve", lambda v: v.reciprocal(out=rc[tq % 2][nr, :], in_=acc[dr, ts_]),
                  reads=kacc, writes=krc[tq % 2])
            kb.op("dve", lambda v: v.scalar_tensor_tensor(out=tmp[tq % 2][nr, :], in0=acc[nr, ts_], scalar=0.5,
                                                           in1=rc[tq % 2][nr, :], op0=ALU.mult, op1=ALU.mult),
                  reads=kacc + krc[tq % 2], writes=ktmp[tq % 2])
            kb.op("pool", lambda g: g.tensor_tensor(out=yT[nr, chunk0 + h // 2, ts_], in0=tmp[tq % 2][nr, :],
                                                    in1=g2[nr, h // 2, ts_], op=ALU.mult),
                  reads=ktmp[tq % 2] + kg2, writes=ky(chunk0 + h // 2, h=h % 2))

    def gate_apply(l, blk, chunk0, tt, ktt, gq, kgq):
        k = load_wblock(l, blk)
        for cb in range(2):
            def ev_g(tq, bi):
                t = tt[tq % 2]
                g_ = gq[tq % 2]
                ts_ = slice(tq * 512, (tq + 1) * 512)
                kb.op("act", lambda a: a.activation(out=t, in_=pp[bi][:, :], func=AF.Tanh, scale=0.5),
                      reads=[("pp", bi)], writes=ktt[tq % 2])
                kb.op("dve", lambda v: v.scalar_tensor_tensor(out=g_, in0=t, scalar=1.0, in1=pp[bi][:, :],
                                                               op0=ALU.add, op1=ALU.mult),
                      reads=ktt[tq % 2] + [("pp", bi)], writes=kgq[tq % 2])
                kb.op("pool", lambda g: g.tensor_tensor(out=yT[:, chunk0 + cb, ts_], in0=g_, in1=yT[:, chunk0 + cb, ts_],
                                                        op=ALU.mult),
                      reads=kgq[tq % 2] + ky(chunk0 + cb), writes=ky(chunk0 + cb))
            proj_fm(k, cb, ev_g)

    def mixer_b(l):
        o_q, o_kz, o_v, o_acc, o_pt, o_rc = 0, 8192, 24576, 32768, 49152, 51200
        qT = av(o_q, [2, S], BF16)
        kqT = ak(o_q, 8192)
        kTz = [av(o_kz + h * 4096, [S], BF16) for h in range(4)]
        kkTz = [ak(o_kz + h * 4096, 4096) for h in range(4)]
        Va = [av(o_v + j * 4096, [16, 128], BF16) for j in range(2)]
        kVa = [ak(o_v + j * 4096, 4096) for j in range(2)]
        accs = [av(o_acc + j * 8192, [S], F32) for j in range(2)]
        kaccs = [ak(o_acc + j * 8192, 8192) for j in range(2)]
        pt = [av(o_pt + j * 512, [256], BF16) for j in range(4)]
        kpt = [ak(o_pt + j * 512, 512) for j in range(4)]
        rcq = [av(o_rc + j * 1024, [256], F32) for j in range(2)]
        krcq = [ak(o_rc + j * 1024, 1024) for j in range(2)]
        for h in range(4):
            oh = slice(((h + 1) % 2) * 64, ((h + 1) % 2) * 64 + 64)
            kb.op("pool", lambda g, h=h, oh=oh: g.memset(kTz[h][oh, :], 0.0), writes=kkTz[h])
        rts = [[av(o_acc + sl * 2560 + j * 512, [128], F32) for j in range(4)] for sl in range(3)]
        krts = [[ak(o_acc + sl * 2560 + j * 512, 512) for j in range(4)] for sl in range(3)]
        qrs = [av(o_acc + sl * 2560 + 2048, [256], BF16) for sl in range(3)]
        kqrs = [ak(o_acc + sl * 2560 + 2048, 512) for sl in range(3)]
        rtasks = []
        for blk in (2, 3):
            k = load_wblock(l, blk)
            for i in range(NT):
                t = {"pre": None, "post": None}

                def s1(t=t, i=i, k=k):
                    sl = t["slot"] % 3
                    bi = (0, 1, 2)[sl]
                    for c in range(8):
                        kb.op("pe", lambda pe, c=c: pe.matmul(pp[bi][:, 0:256], hT[:, c, i * 128:(i + 1) * 128],
                                                              wb[k][:, c, 0:256], start=(c == 0), stop=(c == 7)),
                              reads=[("wb", k), "hT"], writes=[("pp", bi)])
                    z4 = pp[bi][:, 0:256].rearrange("p (h t f) -> p h t f", h=4, t=2)
                    x1, x2 = z4[:, :, 0, :], z4[:, :, 1, :]
                    cs = ropet[:, 0, i, :].unsqueeze(1).broadcast_to([128, 4, 32])
                    sn = ropet[:, 1, i, :].unsqueeze(1).broadcast_to([128, 4, 32])
                    r4 = [r.rearrange("p (h f) -> p h f", h=4) for r in rts[sl]]
                    q4 = qrs[sl].rearrange("p (h t f) -> p h t f", h=4, t=2)
                    for j, (a_, b_) in enumerate(((x1, cs), (x2, sn), (x2, cs), (x1, sn))):
                        kb.op("dve", lambda v, a_=a_, b_=b_, j=j: v.tensor_tensor(out=r4[j], in0=a_, in1=b_, op=ALU.mult),
                              reads=[("pp", bi), "ropet"], writes=krts[sl][j])
                    kb.op("pool", lambda g: g.tensor_tensor(out=q4[:, :, 0, :], in0=r4[0], in1=r4[1], op=ALU.subtract),
                          reads=krts[sl][0] + krts[sl][1], writes=kqrs[sl])
                    kb.op("pool", lambda g: g.tensor_tensor(out=q4[:, :, 1, :], in0=r4[2], in1=r4[3], op=ALU.add),
                          reads=krts[sl][2] + krts[sl][3], writes=kqrs[sl])

                def s2(t=t, i=i, blk=blk):
                    sl = t["slot"] % 3
                    b2 = next_bank(6, 8)
                    ptb = pp[b2][:].bitcast(BF16)
                    for pr in range(2):
                        kb.op("pe", lambda pe, pr=pr: pe.transpose(ptb[:, pr * 128:(pr + 1) * 128],
                                                                   qrs[sl][:, pr * 128:(pr + 1) * 128], ident[:]),
                              reads=kqrs[sl] + ["ident"], writes=[("pp", b2)])
                    if blk == 2:
                        kb.op("act", lambda a: a.copy(out=qT[:, :, i * 128:(i + 1) * 128],
                                                      in_=ptb[:, 0:256].rearrange("p (c n) -> p c n", c=2)),
                              reads=[("pp", b2)], writes=kqT)
                    else:
                        for h in range(4):
                            hp = slice((h % 2) * 64, (h % 2) * 64 + 64)
                            pr = h // 2
                            kb.op("act" if h % 2 == 0 else "dve",
                                  lambda e, h=h, hp=hp, pr=pr: (e.copy if h % 2 == 0 else e.tensor_copy)(
                                      out=kTz[h][hp, i * 128:(i + 1) * 128], in_=ptb[hp, pr * 128:(pr + 1) * 128]),
                                  reads=[("pp", b2)], writes=kkTz[h])
                t["s1"], t["s2"] = s1, s2
                rtasks.append(t)
        run_pipeline(rtasks, 2)
        kv = load_wblock(l, 4)
        for cb in range(2):
            def ev_vt(tq, bi):
                kb.op("act", lambda a: a.copy(out=yT[:, 2 + cb, tq * 512:(tq + 1) * 512], in_=pp[bi][:, :]),
                      reads=[("pp", bi)], writes=ky(2 + cb))
            proj_fm(kv, cb, ev_vt)
        tasks = []
        nva = 0
        for h in range(4):
            acc, kacc = accs[h % 2], kaccs[h % 2]
            voff = 0 if h % 2 == 0 else 64
            for pi, (d, n_sub) in enumerate(((1, 2048), (4, 512), (16, 128))):
                V, kV = Va[nva % 2], kVa[nva % 2]
                nva += 1

                def pre_v(V=V, kV=kV, h=h, voff=voff, d=d, n_sub=n_sub, pi=pi):
                    if pi < 2:
                        kb.op("pool", lambda g: g.memset(V[:, :, 64 - voff:128 - voff], 1.0), writes=kV)
                    for half in range(2):
                        b2 = next_bank(0, 2)
                        ptb = pp[b2][:].bitcast(BF16)
                        for q in range(8):
                            j = half * 8 + q
                            kb.op("pe", lambda pe, q=q, j=j: pe.transpose(
                                ptb[:, q * 128:(q + 1) * 128], yT[:, 2 + h // 2, psl(128 * j, 128, d, n_sub)], ident[:, :]),
                                reads=ky(2 + h // 2) + ["ident"], writes=[("pp", b2)])
                        kb.op("act", lambda a: a.copy(
                            out=V[:, half * 8:half * 8 + 8, voff:voff + 64],
                            in_=ptb.rearrange("p (q e) -> p q e", q=8)[:, :, voff:voff + 64]),
                            reads=[("pp", b2)], writes=kV)
                first_of_pattern = True
                for qb in range(4):
                    ob = next_bank(4, 6)
                    js = []
                    for j in range(max(0, 4 * qb - 1), min(16, 4 * qb + 5)):
                        slo = (128 * j // n_sub) * n_sub
                        qlo = max(128 * j - 64, slo, 512 * qb)
                        qhi = min(128 * j + 192, slo + n_sub, 512 * qb + 512)
                        if qlo < qhi:
                            js.append((j, qlo, qhi))
                    for ji, (j, qlo, qhi) in enumerate(js):
                        t = {}
                        t["pre"] = pre_v if first_of_pattern else None
                        first_of_pattern = False

                        def s1(t=t, j=j, qlo=qlo, qhi=qhi, h=h, d=d, n_sub=n_sub):
                            n = qhi - qlo
                            ns = t["slot"] % 4
                            sbk = (2, 3, 6, 7)[ns]
                            p_, kp_ = pt[ns], kpt[ns]
                            mo = qlo - (128 * j - 64)
                            kb.op("pe", lambda pe: pe.matmul(pp[sbk][:, 0:n], kTz[h][:, psl(128 * j, 128, d, n_sub)],
                                                             qT[:, h // 2, psl(qlo, n, d, n_sub)], start=True, stop=False),
                                  reads=kkTz[h] + kqT, writes=[("pp", sbk)])
                            kb.op("pe", lambda pe: pe.matmul(pp[sbk][:, 0:n], ident[:, :], band[:, mo:mo + n],
                                                             start=False, stop=True),
                                  reads=["ident", "band"], writes=[("pp", sbk)])
                            kb.op("act", lambda a: a.activation(out=p_[:, 0:n], in_=pp[sbk][:, 0:n], func=AF.Exp, scale=0.125),
                                  reads=[("pp", sbk)], writes=kp_)

                        def s2(t=t, j=j, qlo=qlo, qhi=qhi, qb=qb, ob=ob, V=V, kV=kV, first=(ji == 0)):
                            n = qhi - qlo
                            ns = t["slot"] % 4
                            p_, kp_ = pt[ns], kpt[ns]
                            kb.op("pe", lambda pe: pe.matmul(pp[ob][:, qlo - 512 * qb:qhi - 512 * qb], V[:, j, :], p_[:, 0:n],
                                                             start=first, stop=False, skip_group_check=True),
                                  reads=kV + kp_, writes=[("pp", ob)])
                        t["s1"], t["s2"], t["post"] = s1, s2, None
                        if ji == len(js) - 1:
                            def post(qb=qb, ob=ob, d=d, pi=pi, acc=acc, kacc=kacc, h=h):
                                if d == 1:
                                    dst = acc[:, 512 * qb:512 * qb + 512]
                                    src = pp[ob][:, :]
                                elif d == 4:
                                    dst = acc.rearrange("p (l x) -> p x l", x=4)[:, qb, :]
                                    src = pp[ob][:, :]
                                else:
                                    dst = acc.rearrange("p (l x) -> p x l", x=16)[:, 4 * qb:4 * qb + 4, :]
                                    src = pp[ob][:, :].rearrange("p (r l) -> p r l", r=4)
                                if pi == 0:
                                    kb.op("dve", lambda v: v.tensor_copy(out=dst, in_=src), reads=[("pp", ob)], writes=kacc)
                                else:
                                    kb.op("dve", lambda v: v.tensor_tensor(out=dst, in0=src, in1=dst, op=ALU.add),
                                          reads=[("pp", ob)] + kacc, writes=kacc)
                                if pi == 2 and qb == 3:
                                    nr = slice((h % 2) * 64, (h % 2) * 64 + 64)
                                    dr = slice(((h + 1) % 2) * 64, ((h + 1) % 2) * 64 + 64)
                                    for tq in range(8):
                                        ts_ = slice(tq * 256, (tq + 1) * 256)
                                        rc, krc = rcq[tq % 2], krcq[tq % 2]
                                        kb.op("act", lambda a: a.activation(out=rc[nr, :], in_=acc[dr, ts_], func=AF.Ln),
                                              reads=kacc, writes=krc)
                                        kb.op("act", lambda a: a.activation(out=rc[nr, :], in_=rc[nr, :], func=AF.Exp, scale=-1.0,
                                                                            bias=nlh[nr, :]),
                                              reads=krc + ["nlh"], writes=krc)
                                        kb.op("pool", lambda g: g.tensor_tensor(out=yT[nr, 2 + h // 2, ts_], in0=acc[nr, ts_],
                                                                                in1=rc[nr, :], op=ALU.mult),
                                              reads=kacc + krc, writes=ky(2 + h // 2, h=h % 2))
                            t["post"] = post
                        tasks.append(t)
        run_pipeline(tasks, 3)
        tt = [av(o_acc + j * 2048, [512], F32) for j in range(2)]
        ktt = [ak(o_acc + j * 2048, 2048) for j in range(2)]
        gq = [av(o_acc + 4096 + j * 2048, [512], F32) for j in range(2)]
        kgq = [ak(o_acc + 4096 + j * 2048, 2048) for j in range(2)]
        gate_apply(l, 5, 2, tt, ktt, gq, kgq)

    def na_rows(kt):
        rows = []
        for r in range(32):
            rs = min(max(r - 4, 0), 24)
            if any(rs <= 2 * kt + krl < rs + 8 for krl in range(2)):
                rows.append(r)
        return rows[0], rows[-1]

    def mixer_d(l):
        o_q, o_kz, o_v, o_e, o_tt, o_pt = 0, 8192, 24576, 32768, 42240, 46336
        qT = av(o_q, [2, S], BF16)
        kqT = ak(o_q, 8192)
        kTz = [av(o_kz + h * 4096, [S], BF16) for h in range(4)]
        kkTz = [ak(o_kz + h * 4096, 4096) for h in range(4)]
        Va = [av(o_v + j * 4096, [16, 128], BF16) for j in range(2)]
        kVa = [ak(o_v + j * 4096, 4096) for j in range(2)]
        Eb = [av(o_e + j * 4736, [2368], BF16) for j in range(2)]
        kEb = [ak(o_e + j * 4736, 4736) for j in range(2)]
        tt = [av(o_tt + j * 2048, [512], F32) for j in range(2)]
        ktt = [ak(o_tt + j * 2048, 2048) for j in range(2)]
        pt = [av(o_pt + j * 1024, [512], BF16) for j in range(4)]
        kpt = [ak(o_pt + j * 1024, 1024) for j in range(4)]
        for h in range(4):
            oh = slice(((h + 1) % 2) * 64, ((h + 1) % 2) * 64 + 64)
            kb.op("pool", lambda g, h=h, oh=oh: g.memset(kTz[h][oh, :], 0.0), writes=kkTz[h])
        k = load_wblock(l, 8)
        for cb in range(2):
            def ev_q(tq, bi):
                kb.op("act", lambda a: a.copy(out=qT[:, cb, tq * 512:(tq + 1) * 512], in_=pp[bi][:, :]),
                      reads=[("pp", bi)], writes=kqT)
            proj_fm(k, cb, ev_q)
        k = load_wblock(l, 9)
        for cb in range(2):
            def ev_k(tq, bi):
                for hh in range(2):
                    h = 2 * cb + hh
                    hp = slice(hh * 64, hh * 64 + 64)
                    kb.op("act" if hh == 0 else "dve",
                          lambda e, h=h, hp=hp, hh=hh: (e.copy if hh == 0 else e.tensor_copy)(
                              out=kTz[h][hp, tq * 512:(tq + 1) * 512], in_=pp[bi][hp, :]),
                          reads=[("pp", bi)], writes=kkTz[h])
            proj_fm(k, cb, ev_k)
        kv = load_wblock(l, 10)
        for cb in range(2):
            def ev_vt(tq, bi):
                kb.op("act", lambda a: a.copy(out=yT[:, 6 + cb, tq * 512:(tq + 1) * 512], in_=pp[bi][:, :]),
                      reads=[("pp", bi)], writes=ky(6 + cb))
            proj_fm(kv, cb, ev_vt)
        tasks = []
        for h in range(4):
            nr = slice((h % 2) * 64, (h % 2) * 64 + 64)
            dr = slice(((h + 1) % 2) * 64, ((h + 1) % 2) * 64 + 64)
            voff = 0 if h % 2 == 0 else 64
            V, kV = Va[h % 2], kVa[h % 2]
            E, kE = Eb[h % 2], kEb[h % 2]

            def pre_h(h=h, voff=voff, V=V, kV=kV, E=E, kE=kE):
                kb.dma(E, et_d[l, h], reads=["et"], writes=kE)
                if h < 2:
                    kb.op("pool", lambda g: g.memset(V[:, :, 64 - voff:128 - voff], 1.0), writes=kV)
                for half in range(2):
                    b2 = next_bank(0, 2)
                    ptb = pp[b2][:].bitcast(BF16)
                    for q in range(8):
                        j = half * 8 + q
                        kb.op("pe", lambda pe, q=q, j=j: pe.transpose(
                            ptb[:, q * 128:(q + 1) * 128], yT[:, 6 + h // 2, 128 * j:128 * j + 128], ident[:, :]),
                            reads=ky(6 + h // 2) + ["ident"], writes=[("pp", b2)])
                    kb.op("act", lambda a: a.copy(out=V[:, half * 8:half * 8 + 8, voff:voff + 64],
                                                  in_=ptb.rearrange("p (q e) -> p q e", q=8)[:, :, voff:voff + 64]),
                          reads=[("pp", b2)], writes=kV)
            first_of_head = True
            for qb in range(4):
                ob = next_bank(4, 6)
                kts = []
                for kt in range(16):
                    ra, rb = na_rows(kt)
                    ra, rb = max(ra, 8 * qb), min(rb, 8 * qb + 7)
                    if ra <= rb:
                        kts.append((kt, ra, rb))
                for ki, (kt, ra, rb) in enumerate(kts):
                    t = {"pre": pre_h if first_of_head else None, "post": None}
                    first_of_head = False

                    def s1(t=t, kt=kt, ra=ra, rb=rb, h=h, E=E, kE=kE):
                        n = 64 * (rb - ra + 1)
                        ns = t["slot"]
                        sbk = (2, 3, 6, 7)[ns % 4]
                        p_, kp_ = pt[ns % 4], kpt[ns % 4]
                        kb.op("pe", lambda pe: pe.matmul(pp[sbk][:, 0:n], kTz[h][:, 128 * kt:128 * kt + 128],
                                                         qT[:, h // 2, 64 * ra:64 * (rb + 1)], start=True, stop=True,
                                                         skip_group_check=True),
                              reads=kkTz[h] + kqT, writes=[("pp", sbk)])
                        segs = []
                        if ra <= 3:
                            r1 = min(rb, 3)
                            segs.append((ra, r1, 576 + (3 - kt) * 256 + ra * 64))
                        if max(ra, 4) <= min(rb, 28):
                            r0, r1 = max(ra, 4), min(rb, 28)
                            segs.append((r0, r1, (r0 - 2 * kt + 3) * 64))
                        if rb >= 29:
                            r0 = max(ra, 29)
                            segs.append((r0, rb, 1600 + (15 - kt) * 192 + (r0 - 29) * 64))
                        for si, (r0, r1, eoff) in enumerate(segs):
                            c0, c1 = 64 * (r0 - ra), 64 * (r1 - ra + 1)
                            kb.op("pe", lambda pe, c0=c0, c1=c1, eoff=eoff, si=si: pe.matmul(
                                pp[sbk][:, c0:c1], ident[:, :], E[:, eoff:eoff + c1 - c0], start=False,
                                stop=True, skip_group_check=True),
                                reads=["ident"] + kE, writes=[("pp", sbk)])
                        kb.op("act", lambda a: a.activation(out=p_[:, 0:n], in_=pp[sbk][:, 0:n], func=AF.Exp, scale=0.125),
                              reads=[("pp", sbk)], writes=kp_)

                    def s2(t=t, kt=kt, ra=ra, rb=rb, qb=qb, ob=ob, V=V, kV=kV, first=(ki == 0)):
                        n = 64 * (rb - ra + 1)
                        ns = t["slot"]
                        p_, kp_ = pt[ns % 4], kpt[ns % 4]
                        kb.op("pe", lambda pe: pe.matmul(pp[ob][:, 64 * ra - 512 * qb:64 * (rb + 1) - 512 * qb], V[:, kt, :],
                                                         p_[:, 0:n], start=first, stop=False, skip_group_check=True),
                              reads=kV + kp_, writes=[("pp", ob)])
                    t["s1"], t["s2"] = s1, s2
                    if ki == len(kts) - 1:
                        def post(qb=qb, ob=ob, h=h, nr=nr, dr=dr):
                            ts_ = slice(qb * 512, (qb + 1) * 512)
                            rc, krc = tt[qb % 2], ktt[qb % 2]
                            kb.op("act", lambda a: a.activation(out=rc[nr, :], in_=pp[ob][dr, :], func=AF.Ln),
                                  reads=[("pp", ob)], writes=krc)
                            kb.op("act", lambda a: a.activation(out=rc[nr, :], in_=rc[nr, :], func=AF.Exp, scale=-1.0,
                                                                bias=nlh[nr, :]),
                                  reads=krc + ["nlh"], writes=krc)
                            kb.op("dve", lambda v: v.tensor_tensor(out=yT[nr, 6 + h // 2, ts_], in0=pp[ob][nr, :],
                                                                   in1=rc[nr, :], op=ALU.mult),
                                  reads=[("pp", ob)] + krc, writes=ky(6 + h // 2, h=h % 2))
                        t["post"] = post
                    tasks.append(t)
        run_pipeline(tasks, 3)
        ttg = [av(o_v + j * 2048, [512], F32) for j in range(2)]
        kttg = [ak(o_v + j * 2048, 2048) for j in range(2)]
        gq = [av(o_v + 4096 + j * 2048, [512], F32) for j in range(2)]
        kgq = [ak(o_v + 4096 + j * 2048, 2048) for j in range(2)]
        gate_apply(l, 11, 6, ttg, kttg, gq, kgq)

    GC1 = math.sqrt(2.0 / math.pi)
    GC2 = 0.044715

    def mixer_c(l):
        o_u, o_G, o_V, o_P, o_MC, o_et, o_w, o_sin, o_y = 0, 8192, 16384, 20480, 22528, 26112, 30208, 40448, 44544
        ucm = [av(o_u + t * 4096, [16, 8, 16], BF16) for t in range(2)]
        kucm = [ak(o_u + t * 4096, 4096) for t in range(2)]
        Gcm = av(o_G, [16, 256], BF16)
        kG = ak(o_G, 8192)
        gT = av(o_u, [2, S], BF16)
        kgT = ak(o_u, 8192)
        Vs = [av(o_V + j * 512, [256], BF16) for j in range(8)]
        kVs = [ak(o_V + j * 512, 512) for j in range(8)]
        Pb = [av(o_P + j * 1024, [4, 128], BF16) for j in range(2)]
        kPb = [ak(o_P + j * 1024, 1024) for j in range(2)]
        MC = [av(o_MC + j * 1792, [7, 128], BF16) for j in range(2)]
        kMC = [ak(o_MC + j * 1792, 1792) for j in range(2)]
        et = av(o_et, [4, 2, 128], F32)
        ket = ak(o_et, 4096)
        wk_ = [av(o_w + j * 2048, [4, 128], F32) for j in range(5)]
        kwk = [ak(o_w + j * 2048, 2048) for j in range(5)]
        A_, B_, T1, T2, T3 = wk_
        kA, kB_, kT1, kT2, kT3 = kwk
        sres = [av(o_sin + j * 2048, [4, 128], BF16) for j in range(2)]
        sims = [av(o_sin + j * 2048 + 1024, [4, 128], BF16) for j in range(2)]
        ksres = [ak(o_sin + j * 2048, 1024) for j in range(2)]
        ksims = [ak(o_sin + j * 2048 + 1024, 1024) for j in range(2)]
        ytmp = [[av(o_y + s_ * 4096 + j * 1024, [256], F32) for j in range(4)] for s_ in range(2)]
        kytmp = [[ak(o_y + s_ * 4096 + j * 1024, 1024) for j in range(4)] for s_ in range(2)]
        F_, Bh = slice(0, 64), slice(64, 128)
        tt2 = lambda e, o, a, b, op, rd, wr: kb.op(e, lambda v: v.tensor_tensor(out=o, in0=a, in1=b, op=op), reads=rd, writes=wr)
        kcu = load_wblock(l, 6)
        for j in range(8):
            for mt in range(2):
                bi = next_bank(0, 2)
                for c in range(8):
                    kb.op("pe", lambda pe, c=c: pe.matmul(
                        pp[bi][:, 0:256], hT[:, c, slice(1024 * mt + j, 1024 * mt + j + 8 * 127 + 1, 8)],
                        wb[kcu][:, c, :], start=(c == 0), stop=(c == 7)),
                        reads=[("wb", kcu), "hT"], writes=[("pp", bi)])
                kb.op("act" if (j + mt) % 2 == 0 else "dve",
                      lambda e: (e.copy if (j + mt) % 2 == 0 else e.tensor_copy)(
                          out=ucm[mt][:, :, 7 - j, :], in_=pp[bi][:, 0:256].rearrange("p (g c) -> p g c", g=16)),
                      reads=[("pp", bi)], writes=kucm[mt])
        for j in range(2):
            kb.op("pool", lambda g, j=j: g.memset(sres[j][F_, :, 0:1], 0.0), writes=ksres[j])
            kb.op("pool", lambda g, j=j: g.memset(sres[j][Bh, :, 127:128], 0.0), writes=ksres[j])
            kb.op("pool", lambda g, j=j: g.memset(sims[j][F_, :, 0:1], 0.0), writes=ksims[j])
            kb.op("pool", lambda g, j=j: g.memset(sims[j][Bh, :, 127:128], 0.0), writes=ksims[j])
        banks = {}

        def x_front(gb):
            kb.dma(et, etab_d[l, gb], reads=["etab"], writes=ket)
            s0r, s0i = 2, 3
            banks[gb] = (s0r, s0i)
            for gi in range(4):
                g = 4 * gb + gi
                V, kV = Vs[(gb % 2) * 4 + gi], kVs[(gb % 2) * 4 + gi]
                P_, kP_ = Pb[g % 2], kPb[g % 2]
                kb.dma(P_, sblk_d[l, g, :, 3:7, :], reads=["sblk"], writes=kP_)
                bt = next_bank(6, 8)
                ptb = pp[bt][:].bitcast(BF16)
                for mt in range(2):
                    kb.op("pe", lambda pe, mt=mt: pe.transpose(
                        ptb[:, mt * 128:(mt + 1) * 128], ucm[mt][:, g, :, :].rearrange("p j c -> p (j c)"), ident[:]),
                        reads=kucm[mt] + ["ident"], writes=[("pp", bt)])
                kb.op("act", lambda a: a.copy(out=V, in_=ptb[:, 0:256]), reads=[("pp", bt)], writes=kV)
                for (bank, b0) in ((s0r, 0), (s0i, 2)):
                    for sub in range(2):
                        kb.op("pe", lambda pe, sub=sub, bank=bank, b0=b0: pe.matmul(
                            pp[bank][:, gi * 128:(gi + 1) * 128], P_[:, b0 + sub, :], V[:, sub:256:2],
                            start=(sub == 0), stop=(sub == 1), skip_group_check=True),
                            reads=kP_ + kV, writes=[("pp", bank)])
            for (bank, dst, kd) in ((s0r, A_, kA), (s0i, B_, kB_)):
                src = pp[bank][:, :].rearrange("p (g m) -> p g m", g=4)
                kb.op("act", lambda a, src=src, dst=dst: a.copy(out=dst[F_], in_=src[F_]), reads=[("pp", bank)], writes=kd)
                kb.op("act", lambda a, src=src, dst=dst: a.copy(out=dst[Bh], in_=src[Bh, :, ::-1]), reads=[("pp", bank)], writes=kd)

        def x_back(gb, part):
            cs_, sn_ = et[:, :, 0, :], et[:, :, 1, :]
            sre, sim, ksre, ksim = sres[gb % 2], sims[gb % 2], ksres[gb % 2], ksims[gb % 2]
            if part == 0:
                tt2("dve", T1, A_, cs_, ALU.mult, kA + ket, kT1)
                tt2("pool", T2, B_, sn_, ALU.mult, kB_ + ket, kT2)
                tt2("dve", T1, T1, T2, ALU.add, kT1 + kT2, kT1)
                tt2("pool", T3, B_, cs_, ALU.mult, kB_ + ket, kT3)
                tt2("dve", T2, A_, sn_, ALU.mult, kA + ket, kT2)
                tt2("pool", T3, T3, T2, ALU.subtract, kT3 + kT2, kT3)
            elif part == 1:
                for gi in range(4):
                    g = 4 * gb + gi
                    rb_ = rho_sb[:, l, g:g + 1].broadcast_to([128, 128])
                    kb.op("dve", lambda v, gi=gi, rb_=rb_: v.tensor_tensor_scan(
                        out=A_[:, gi, :], data0=rb_, data1=T1[:, gi, :], initial=0.0, op0=ALU.mult, op1=ALU.add),
                        reads=kT1 + ["rho"], writes=kA)
                    kb.op("dve", lambda v, gi=gi, rb_=rb_: v.tensor_tensor_scan(
                        out=B_[:, gi, :], data0=rb_, data1=T3[:, gi, :], initial=0.0, op0=ALU.mult, op1=ALU.add),
                        reads=kT3 + ["rho"], writes=kB_)
            elif part == 2:
                tt2("dve", T1, A_, cs_, ALU.mult, kA + ket, kT1)
                tt2("pool", T2, B_, sn_, ALU.mult, kB_ + ket, kT2)
                tt2("dve", T1, T1, T2, ALU.subtract, kT1 + kT2, kT1)
                tt2("pool", T3, B_, cs_, ALU.mult, kB_ + ket, kT3)
                tt2("dve", T2, A_, sn_, ALU.mult, kA + ket, kT2)
                tt2("pool", T3, T3, T2, ALU.add, kT3 + kT2, kT3)
            else:
                for (src, ksrc, dst, kd) in ((T1, kT1, sre, ksre), (T3, kT3, sim, ksim)):
                    kb.op("act", lambda a, src=src, dst=dst: a.copy(out=dst[F_, :, 1:128], in_=src[F_, :, 0:127]),
                          reads=ksrc, writes=kd)
                    kb.op("pool", lambda g_, src=src, dst=dst: g_.tensor_copy(out=dst[Bh, :, 0:127], in_=src[Bh, :, 126::-1]),
                          reads=ksrc, writes=kd)

        def y_group(gb, gi):
            g = 4 * gb + gi
            V, kV = Vs[(gb % 2) * 4 + gi], kVs[(gb % 2) * 4 + gi]
            sre, sim, ksre, ksim = sres[gb % 2], sims[gb % 2], ksres[gb % 2], ksims[gb % 2]
            M_, kM_ = MC[g % 2], kMC[g % 2]
            Ysb, sq, u_, th_ = ytmp[g % 2]
            kYsb, ksq, ku_, kth_ = kytmp[g % 2]
            kb.dma(M_[:, 0:3, :], sblk_d[l, g, :, 0:3, :], reads=["sblk"], writes=kM_)
            kb.dma(M_[:, 3:7, :], sblk_d[l, g, :, 7:11, :], reads=["sblk"], writes=kM_)
            yb = next_bank(4, 6)
            V0, V1 = V[:, 0:256:2], V[:, 1:256:2]
            plan = ((0, [(0, V0, kV), (2, V1, kV), (3, sre[:, gi, :], ksre), (4, sim[:, gi, :], ksim)]),
                    (1, [(0, V1, kV), (1, V0, kV), (5, sre[:, gi, :], ksre), (6, sim[:, gi, :], ksim)]))
            for so, terms in plan:
                for ti, (bidx, rhs, krhs) in enumerate(terms):
                    kb.op("pe", lambda pe, so=so, ti=ti, bidx=bidx, rhs=rhs: pe.matmul(
                        pp[yb][:, so * 128:(so + 1) * 128], M_[:, bidx, :], rhs, start=(ti == 0), stop=(ti == 3),
                        skip_group_check=True), reads=kM_ + krhs, writes=[("pp", yb)])
            kb.op("act", lambda a: a.copy(out=Ysb, in_=pp[yb][:, 0:256]), reads=[("pp", yb)], writes=kYsb)
            tb = next_bank(6, 8)
            for so in range(2):
                kb.op("pe", lambda pe, so=so: pe.transpose(pp[tb][:, so * 128:(so + 1) * 128],
                                                           Ysb[:, so * 128:(so + 1) * 128], identf[:]),
                      reads=kYsb + ["identf"], writes=[("pp", tb)])
            yy = pp[tb][:, 0:256]
            kb.op("act", lambda a: a.activation(out=sq, in_=yy, func=AF.Square), reads=[("pp", tb)], writes=ksq)
            kb.op("pool", lambda g_: g_.tensor_scalar(out=sq, in0=sq, scalar1=GC2, scalar2=1.0, op0=ALU.mult, op1=ALU.add),
                  reads=ksq, writes=ksq)
            kb.op("dve", lambda v: v.tensor_tensor(out=u_, in0=sq, in1=yy, op=ALU.mult), reads=ksq + [("pp", tb)], writes=ku_)
            kb.op("act", lambda a: a.activation(out=th_, in_=u_, func=AF.Tanh, scale=GC1), reads=ku_, writes=kth_)
            kb.op("dve", lambda v: v.scalar_tensor_tensor(
                out=Gcm[:, :, g * 16:(g + 1) * 16], in0=th_.rearrange("p (s c) -> p s c", c=16), scalar=1.0,
                in1=yy.rearrange("p (s c) -> p s c", c=16), op0=ALU.add, op1=ALU.mult),
                reads=kth_ + [("pp", tb)], writes=kG)

        x_front(0)
        for part in range(4):
            x_back(0, part)
        for gb in range(4):
            if gb + 1 < 4:
                x_front(gb + 1)
            for gi in range(4):
                y_group(gb, gi)
                if gb + 1 < 4:
                    x_back(gb + 1, gi)
        dump("Gcm", Gcm, kG)
        for chc in range(2):
            for half in range(2):
                bt = next_bank(6, 8)
                ptb = pp[bt][:].bitcast(BF16)
                for q in range(8):
                    si = half * 8 + q
                    kb.op("pe", lambda pe, q=q, si=si: pe.transpose(ptb[:, q * 128:(q + 1) * 128],
                                                                    Gcm[:, si, chc * 128:(chc + 1) * 128], ident[:]),
                          reads=kG + ["ident"], writes=[("pp", bt)])
                kb.op("act" if half == 0 else "dve", lambda e: (e.copy if half == 0 else e.tensor_copy)(
                    out=gT[:, chc, :].rearrange("p (m s) -> p s m", s=16)[:, half * 8:half * 8 + 8, :],
                    in_=ptb.rearrange("p (q m) -> p q m", q=8)), reads=[("pp", bt)], writes=kgT)
        dump("gT", gT, kgT)
        gwl = av(o_et, [2, 256], BF16)
        kgwl = ak(o_et, 1024)
        kb.dma(gwl, gwb_d[l], reads=["gwb"], writes=kgwl)
        ttg = [av(o_w + j * 2048, [512], F32) for j in range(2)]
        t2s = [av(o_w + (2 + j) * 2048, [512], F32) for j in range(2)]
        n_ = 0
        for ec in range(2):
            for tq in range(4):
                ts_ = slice(tq * 512, (tq + 1) * 512)
                bi = next_bank(0, 2)
                for cc in range(2):
                    kb.op("pe", lambda pe, cc=cc: pe.matmul(pp[bi][:, :], gwl[:, cc, ec * 128:(ec + 1) * 128], gT[:, cc, ts_],
                                                            start=(cc == 0), stop=(cc == 1)),
                          reads=kgwl + kgT, writes=[("pp", bi)])
                th2, kth2 = ttg[n_ % 2], kwk[n_ % 2]
                t_, kt_ = t2s[n_ % 2], kwk[2 + n_ % 2]
                n_ += 1
                kb.op("act", lambda a: a.activation(out=th2, in_=pp[bi][:, :], func=AF.Tanh, scale=0.25,
                                                    bias=glb[:, l, ec:ec + 1]),
                      reads=[("pp", bi), "glb"], writes=kth2)
                kb.op("dve", lambda v: v.scalar_tensor_tensor(out=t_, in0=th2, scalar=1.0, in1=gT[:, ec, ts_],
                                                               op0=ALU.add, op1=ALU.mult),
                      reads=kth2 + kgT, writes=kt_)
                kb.op("pool", lambda g_: g_.tensor_scalar(out=yT[:, 4 + ec, ts_], in0=t_, scalar1=0.125, scalar2=1.0,
                                                          op0=ALU.mult, op1=ALU.mult),
                      reads=kt_, writes=ky(4 + ec))
        gtt = [av(o_G + j * 2048, [512], F32) for j in range(2)]
        kgtt = [ak(o_G + j * 2048, 2048) for j in range(2)]
        gq = [av(o_G + 4096 + j * 2048, [512], F32) for j in range(2)]
        kgq = [ak(o_G + 4096 + j * 2048, 2048) for j in range(2)]
        gate_apply(l, 7, 4, gtt, kgtt, gq, kgq)

    kb.same_depth = 2
    stg_keys = [("hTs", 0), ("hTs", 1)] + [("yTs", j) for j in range(6)]
    stg_keys += [("xs", "m"), ("xs", "n"), ("xs", "s", 0), ("xs", "s", 1), ("xs", "e", 0), ("xs", "e", 1)]
    for e_ in ("act", "dve", "pool"):
        kb.op(e_, (lambda a: a.copy(out=small[:, 50:51], in_=small[:, 49:50])) if e_ == "act" else
              (lambda v: v.tensor_copy(out=small[:, 51 + (0 if e_ == "dve" else 1):52 + (0 if e_ == "dve" else 1)], in_=small[:, 49:50])),
              reads=stg_keys + ["nlh"], writes=["hT"] + ky(0, 8) + [("x", i_) for i_ in range(NT)])
    for s in range(nseq):
        for i in range(NT):
            kb.dma(x_res[:, i, :], x_d[s, i * 128:(i + 1) * 128, :], writes=[("x", i)])
        for l in range(depth):
            rms_all()
            for i in range(NT + 2):
                if i < NT:
                    kb.op("act", lambda a, i=i: a.activation(out=hs[i % 4], in_=x_res[:, i, :], func=AF.Copy,
                                                             scale=small[:, i:i + 1]),
                          reads=[("x", i), "rstd"], writes=khs[i % 4])
                    bi = 4 + i % 4
                    pt = pp[bi][:].bitcast(BF16)
                    for c in range(8):
                        kb.op("pe", lambda pe, c=c, i=i, pt=pt: pe.transpose(
                            pt[:, c * 128:(c + 1) * 128], hs[i % 4][:, c * 128:(c + 1) * 128], ident[:]),
                            reads=khs[i % 4] + ["ident"], writes=[("pp", bi)])
                if i >= 2:
                    j = i - 2
                    bj = 4 + j % 4
                    ptj = pp[bj][:].bitcast(BF16)
                    if j % 2 == 0:
                        kb.op("act", lambda a, j=j, ptj=ptj: a.copy(
                            out=hT[:, :, j * 128:(j + 1) * 128], in_=ptj.rearrange("p (c n) -> p c n", c=8)),
                            reads=[("pp", bj)], writes=["hT"])
                    else:
                        kb.op("dve", lambda v, j=j, ptj=ptj: v.tensor_copy(
                            out=hT[:, :, j * 128:(j + 1) * 128], in_=ptj.rearrange("p (c n) -> p c n", c=8)),
                            reads=[("pp", bj)], writes=["hT"])
            if "a" in mixers:
                mixer_a(l)
            if "b" in mixers:
                mixer_b(l)
            if "c" in mixers:
                mixer_c(l)
            if "d" in mixers:
                mixer_d(l)
            for mi, m in enumerate("abcd"):
                if m not in mixers:
                    kb.op("pool", lambda g, mi=mi: g.memset(yT[:, 2 * mi:2 * mi + 2, :], 0.0), writes=ky(2 * mi, 2 * mi + 2))
            if dbg and s == 0 and l == 0:
                kb.dma(dbg_d, yT[:], reads=ky(0, 8), writes=["dbgout"])
            for h in range(4):
                k = load_woblock(l, h)
                for i in range(NT):
                    bi = next_bank(0, 2)
                    for c in range(8):
                        kb.op("pe", lambda pe, c=c, i=i, bi=bi, k=k: pe.matmul(
                            pp[bi][:, 0:256], yT[:, c, i * 128:(i + 1) * 128], wb[k][:, c, :],
                            start=(c == 0), stop=(c == 7)),
                            reads=[("wb", k)] + ky(c), writes=[("pp", bi)])
                    kb.op("dve", lambda v, i=i, bi=bi, h=h: v.tensor_tensor(
                        out=x_res[:, i, h * 256:(h + 1) * 256], in0=pp[bi][:, 0:256],
                        in1=x_res[:, i, h * 256:(h + 1) * 256], op=ALU.add),
                        reads=[("pp", bi)], writes=[("x", i)])
        fg = av(8192, [D], F32)
        kb.dma(fg, fg_d, writes=ak(8192, 4096))
        rms_all()
        for i in range(NT):
            oo = 28672 + (i % 4) * 4096
            ot = av(oo, [1024], F32)
            kb.op("dve", lambda v, i=i, ot=ot: v.scalar_tensor_tensor(
                out=ot, in0=x_res[:, i, :], scalar=small[:, i:i + 1], in1=fg, op0=ALU.mult, op1=ALU.mult),
                reads=[("x", i), "rstd"] + ak(8192, 4096), writes=ak(oo, 4096))
            kb.dma(y_d[s, i * 128:(i + 1) * 128, :], ot, reads=ak(oo, 4096), writes=["y"])
    kb.finish()
    return nc


def host_prep(inputs):
    f = np.float32
    ng = np.ascontiguousarray(np.asarray(inputs["norm_g"], f).reshape(4, 8, 128).transpose(2, 0, 1))
    fgb = np.ascontiguousarray(np.broadcast_to(np.asarray(inputs["final_g"], f)[None, :], (128, D)))
    pw = np.asarray(inputs["pool_w"], f)
    pwb = np.zeros((128, 4, 2, 128), f)
    for g in range(4):
        cb, h = g // 2, g % 2
        pwb[h * 64:(h + 1) * 64, :, cb, h * 64:(h + 1) * 64] = pw[:, g].transpose(1, 0, 2)
    psc = np.ascontiguousarray(np.asarray(inputs["pool_scale"], f).reshape(4, 2, 128).transpose(2, 0, 1))
    pcn = np.zeros((128, 2, 17), f)
    for g, w in enumerate((2, 4, 8, 16)):
        cb, h = g // 2, g % 2
        t = np.arange(S)
        cnt = np.minimum(t + w // 2, S) - np.maximum(t - w // 2, 0)
        pcn[h * 64:(h + 1) * 64, cb, 0:8] = (w / cnt[0:8])[None, :]
        pcn[h * 64:(h + 1) * 64, cb, 8:16] = (w / cnt[S - 8:S])[None, :]
        pcn[h * 64:(h + 1) * 64, cb, 16] = 1.0 / w
    inv = 10000.0 ** (-np.arange(0, 64, 2, dtype=np.float32) / 64)
    ang = np.arange(S, dtype=np.float32)[:, None] * inv[None, :]
    rope = np.stack([np.cos(ang), np.sin(ang)], 0).astype(f)
    rope_t = np.ascontiguousarray(rope.reshape(2, NT, 128, 32).transpose(2, 0, 1, 3))
    kk = np.arange(128)[:, None]
    cc = np.arange(256)[None, :]
    band = np.where(((cc - kk) >= 0) & ((cc - kk) <= 128), 0.0, -240000.0).astype(ml_dtypes.bfloat16)
    rpb = np.asarray(inputs["na_rpb"], f)
    rpbpad = np.zeros((4, 4, 15, 128), f)
    rpbpad[:, :, :, 48:79] = rpb[:, :, ::-1, :]
    kc = np.arange(64)
    c = 63 - np.arange(64)
    cs = np.clip(c - 8, 0, 48)
    colok = ((kc[:, None] >= cs[None, :]) & (kc[:, None] < cs[None, :] + 16)).astype(f)
    nam = np.zeros((128, 2368), f)
    for krl in range(2):
        for ri in range(9):
            dlt = krl - ri + 3
            if -4 <= dlt <= 3:
                nam[krl * 64:(krl + 1) * 64, ri * 64:(ri + 1) * 64] = colok
        for blk in range(28):
            nam[krl * 64:(krl + 1) * 64, 576 + blk * 64:576 + (blk + 1) * 64] = colok
    are = np.asarray(inputs["ssm_a_re"], f)
    aim = np.asarray(inputs["ssm_a_im"], f)
    ldt = np.asarray(inputs["ssm_log_dt"], f)
    lam = np.stack([are.transpose(0, 1, 3, 2), aim.transpose(0, 1, 3, 2),
                    np.broadcast_to(ldt[:, :, None, :], (4, 2, 64, 16))], axis=-1)
    lam = np.ascontiguousarray(lam.reshape(4, 128, 16, 3))
    bre = np.asarray(inputs["ssm_b_re"], f)
    bim = np.asarray(inputs["ssm_b_im"], f)
    bp1 = np.stack([bre.transpose(0, 2, 1, 3), bim.transpose(0, 2, 1, 3)], axis=-1)
    bp = np.ascontiguousarray(np.concatenate([bp1, bp1], axis=1))
    cre = np.asarray(inputs["ssm_c_re"], f)
    cim = np.asarray(inputs["ssm_c_im"], f)
    cp1 = np.stack([cre.transpose(0, 1, 4, 2, 3), cim.transpose(0, 1, 4, 2, 3)], axis=-1)
    cp = np.ascontiguousarray(cp1.reshape(4, 128, 16, 16, 2))
    sd = np.asarray(inputs["ssm_d"], f).reshape(4, 16, 16)
    sdt = np.ascontiguousarray(sd.transpose(2, 0, 1))
    glw = np.ascontiguousarray(np.asarray(inputs["glu_w"], f).reshape(4, 2, 128, 256).transpose(0, 2, 1, 3))
    glbt = np.ascontiguousarray(np.asarray(inputs["glu_b"], f).reshape(4, 2, 128).transpose(2, 0, 1))
    return {
        "ssm_lam": lam, "ssm_bp": bp, "ssm_cp": cp, "ssm_dt": sdt, "glu_w_t": glw, "glu_b_t": glbt,
        "rpbpad": rpbpad, "na_mask": nam.astype(ml_dtypes.bfloat16),
        "rope_t": rope_t, "band": band,
        "pool_w_blk": pwb, "pool_scale_t": psc, "pool_const": pcn,
        "w_in": np.ascontiguousarray(inputs["w_in"], dtype=f),
        "w_out": np.ascontiguousarray(inputs["w_out"], dtype=f),
        "norm_g_t": ng,
        "final_g_b": fgb,
    }


def kernel(**inputs):
    xp = np.asarray(inputs["x_prompt"], np.float32)
    xs = np.asarray(inputs["x_sample"], np.float32)
    shared = host_prep(inputs)
    nc = build()
    in_maps = []
    for c in range(8):
        xc = np.concatenate([xp[4 * c:4 * c + 4], xs[c:c + 1]], axis=0)
        m = dict(shared)
        m["x"] = np.ascontiguousarray(xc)
        in_maps.append(m)
    res = run_bass_kernel_spmd(nc, in_maps, core_ids=list(range(8)))
    yp = np.empty_like(xp)
    ys = np.empty_like(xs)
    for c in range(8):
        y = res.results[c]["y"]
        yp[4 * c:4 * c + 4] = y[0:4]
        ys[c] = y[4]
    return (yp, ys)
```

```python
import math
from contextlib import ExitStack
import numpy as np
import ml_dtypes
import concourse.bass as bass
import concourse.mybir as mybir
from concourse.bass_utils import run_bass_kernel_spmd

F32 = mybir.dt.float32
BF16 = mybir.dt.bfloat16
I32 = mybir.dt.int32
AF = mybir.ActivationFunctionType
ALU = mybir.AluOpType
AX = mybir.AxisListType

S = 2048
D = 1024
NT = 16
EPS = 1e-6
NDMA = 24


class KB:
    def __init__(self):
        self.nc = bass.Bass("TRN2", target_bir_lowering=False)
        nc = self.nc
        self.es = ExitStack()
        self.eng = {"pe": nc.tensor, "act": nc.scalar, "dve": nc.vector, "pool": nc.gpsimd, "sp": nc.sync}
        self.sem = {}
        for e in ["pe", "act", "dve", "pool"]:
            self.sem[e] = self.es.enter_context(nc.semaphore("s_" + e))
        for i in range(NDMA):
            self.sem[("dma", i)] = self.es.enter_context(nc.semaphore("s_dma%d" % i))
        self.cnt = {e: 0 for e in ["pe", "act", "dve", "pool"]}
        self.seen = {e: {} for e in ["pe", "act", "dve", "pool", "sp"]}
        self.res = {}
        self.ndma = 0
        self.same_eng = {"act", "dve", "pool"}
        self.same_depth = 1000000
        self.clock = {}

    def sb(self, name, shape, dt):
        return self.es.enter_context(self.nc.sbuf_tensor(name, shape, dt))

    def ps(self, name, shape, dt):
        return self.es.enter_context(self.nc.psum_tensor(name, shape, dt))

    def dram(self, name, shape, dt, kind="Internal"):
        return self.nc.dram_tensor(name, shape, dt, kind=kind).ap()

    def _wait(self, e, key, val):
        if key == e:
            if e not in self.same_eng or val < self.cnt[e] - self.same_depth + 1:
                return
        if self.seen[e].get(key, 0) >= val:
            return
        self.eng[e].wait_ge(self.sem[key], val)
        self.seen[e][key] = val
        clk = self.clock.get((key, val))
        if clk:
            se = self.seen[e]
            for k2, v2 in clk.items():
                if se.get(k2, 0) < v2:
                    se[k2] = v2

    def _deps(self, e, reads, writes):
        for r in reads:
            st = self.res.get(r)
            if st:
                for k, v in st["w"].items():
                    self._wait(e, k, v)
        for w in writes:
            st = self.res.get(w)
            if st:
                for k, v in st["w"].items():
                    self._wait(e, k, v)
                for k, v in st["r"].items():
                    self._wait(e, k, v)

    def _mark(self, key, val, reads, writes):
        for r in reads:
            st = self.res.setdefault(r, {"w": {}, "r": {}})
            st["r"][key] = max(st["r"].get(key, 0), val)
        for w in writes:
            st = self.res.setdefault(w, {"w": {}, "r": {}})
            st["w"][key] = max(st["w"].get(key, 0), val)

    def op(self, e, fn, reads=(), writes=()):
        self._deps(e, reads, writes)
        inst = fn(self.eng[e])
        self.cnt[e] += 1
        inst.then_inc(self.sem[e], 1)
        snap = dict(self.seen[e])
        snap[e] = self.cnt[e]
        self.clock[(e, self.cnt[e])] = snap
        self._mark(e, self.cnt[e], reads, writes)

    def dma(self, out, in_, reads=(), writes=(), q="sp"):
        n = self.ndma
        self.ndma += 1
        i = n % NDMA
        key = ("dma", i)
        if n >= NDMA:
            self._wait(q, key, 16 * (n // NDMA))
        self._deps(q, reads, writes)
        self.eng[q].dma_start(out=out, in_=in_).then_inc(self.sem[key], 16)
        self.clock[(key, 16 * (n // NDMA + 1))] = dict(self.seen[q])
        self._mark(key, 16 * (n // NDMA + 1), reads, writes)

    def barrier(self, engines=("pe", "act", "dve", "pool")):
        for e in engines:
            for e2 in engines:
                if e2 != e and self.cnt[e2] > 0:
                    self._wait(e, e2, self.cnt[e2])

    def finish(self):
        for k in list(self.sem.keys()):
            if isinstance(k, tuple):
                i = k[1]
                uses = (self.ndma - 1 - i) // NDMA + 1 if self.ndma > i else 0
                if uses > 0:
                    self._wait("sp", k, 16 * uses)
            else:
                if self.cnt[k] > 0:
                    self._wait("sp", k, self.cnt[k])
        self.es.close()


def build(nseq=5, depth=4, mixers=("a", "b", "c", "d"), dbg=False):
    kb = KB()
    nc = kb.nc
    x_d = nc.dram_tensor("x", [nseq, S, D], F32, kind="ExternalInput").ap()
    y_d = nc.dram_tensor("y", [nseq, S, D], F32, kind="ExternalOutput").ap()
    w_in_d = nc.dram_tensor("w_in", [4, D, 3072], F32, kind="ExternalInput").ap()
    w_out_d = nc.dram_tensor("w_out", [4, D, D], F32, kind="ExternalInput").ap()
    ng_d = nc.dram_tensor("norm_g_t", [128, 4, 8], F32, kind="ExternalInput").ap()
    fg_d = nc.dram_tensor("final_g_b", [128, D], F32, kind="ExternalInput").ap()
    pwb_d = nc.dram_tensor("pool_w_blk", [128, 4, 2, 128], F32, kind="ExternalInput").ap()
    psc_d = nc.dram_tensor("pool_scale_t", [128, 4, 2], F32, kind="ExternalInput").ap()
    pcn_d = nc.dram_tensor("pool_const", [128, 2, 17], F32, kind="ExternalInput").ap()
    dbg_d = nc.dram_tensor("dbg", [128, 8, S], BF16, kind="ExternalOutput").ap() if dbg else None
    rope_d = nc.dram_tensor("rope_t", [128, 2, NT, 32], F32, kind="ExternalInput").ap()
    band_d = nc.dram_tensor("band", [128, 256], BF16, kind="ExternalInput").ap()
    rpb_t = nc.dram_tensor("rpbpad", [4, 4, 15, 128], F32, kind="ExternalInput")
    nam_d = nc.dram_tensor("na_mask", [128, 2368], BF16, kind="ExternalInput").ap()
    et_d = kb.dram("et", [4, 4, 128, 2368], BF16)
    lam_d = nc.dram_tensor("ssm_lam", [4, 128, 16, 3], F32, kind="ExternalInput").ap()
    bp_d = nc.dram_tensor("ssm_bp", [4, 128, 16, 16, 2], F32, kind="ExternalInput").ap()
    cp_d = nc.dram_tensor("ssm_cp", [4, 128, 16, 16, 2], F32, kind="ExternalInput").ap()
    dt_d = nc.dram_tensor("ssm_dt", [16, 4, 16], F32, kind="ExternalInput").ap()
    glw_d = nc.dram_tensor("glu_w_t", [4, 128, 2, 256], F32, kind="ExternalInput").ap()
    glb_d = nc.dram_tensor("glu_b_t", [128, 4, 2], F32, kind="ExternalInput").ap()
    sblk_d = kb.dram("sblk", [4, 16, 128, 11, 128], BF16)
    etab_d = kb.dram("etab", [4, 4, 128, 4, 2, 128], F32)
    ktab_t = nc.dram_tensor("ktab", [4, 16, 31, 16, 16], F32, kind="Internal")
    ktab_d = ktab_t.ap()
    gwb_d = kb.dram("gwb", [4, 128, 2, 256], BF16)
    wib_d = kb.dram("wib", [4, 12, 128, 8, 256], BF16)
    wob_d = kb.dram("wob", [4, 4, 128, 8, 256], BF16)

    x_res = kb.sb("x_res", [128, NT, D], F32)
    hT = kb.sb("hT", [128, 8, S], BF16)
    yT = kb.sb("yT", [128, 8, S], BF16)
    ARENA = 56 * 1024
    arena = kb.sb("arena", [128, ARENA // 2], BF16)
    wb = [kb.sb("wb%d" % i, [128, 8, 256], BF16) for i in range(3)]
    ng = kb.sb("ng", [128, 4, 8], F32)
    ident = kb.sb("ident", [128, 128], BF16)
    identf = kb.sb("identf", [128, 128], F32)
    small = kb.sb("small", [128, 64], F32)
    pwb = kb.sb("pwb", [128, 4, 2, 128], BF16)
    psc = kb.sb("psc", [128, 4, 2], F32)
    pcn = kb.sb("pcn", [128, 2, 17], F32)
    ropet = kb.sb("ropet", [128, 2, NT, 32], F32)
    band = kb.sb("band_sb", [128, 256], BF16)
    rho_sb = kb.sb("rho_sb", [128, 4, 16], F32)
    glb = kb.sb("glb", [128, 4, 2], F32)
    pp = [kb.ps("pp%d" % i, [128, 512], F32) for i in range(8)]

    GR = 256

    def ak(off, nbytes):
        return [("ar", j) for j in range(off // GR, (off + nbytes - 1) // GR + 1)]

    dumped = set()

    def dump(name, src, reads):
        if not dbg or name in dumped:
            return
        dumped.add(name)
        shp = list(src.shape)
        dd = nc.dram_tensor("d_" + name, shp, src.dtype, kind="ExternalOutput").ap()
        kb.dma(dd, src, reads=reads, writes=["dump_" + name])

    def ky(c0, c1=None, h=None):
        c1 = c0 + 1 if c1 is None else c1
        hs_ = (0, 1) if h is None else (h,)
        return [("yT", c, hh) for c in range(c0, c1) for hh in hs_]

    def av(off, shape, dt):
        n = int(np.prod(shape))
        esz = 4 if dt in (F32, I32) else 2
        a = arena[:, off // 2: off // 2 + n * esz // 2]
        if dt != BF16:
            a = a.bitcast(dt)
        if len(shape) == 2:
            return a.rearrange("p (a b) -> p a b", a=shape[0])
        if len(shape) == 3:
            return a.rearrange("p (a b c) -> p a b c", a=shape[0], b=shape[1])
        return a

    pstate = {"n": 0}

    def next_bank(lo=0, hi=2):
        i = lo + pstate.setdefault((lo, hi), 0) % (hi - lo)
        pstate[(lo, hi)] += 1
        return i

    hs = [av(16384 + i * 2048, [D], BF16) for i in range(6)]
    khs = [ak(16384 + i * 2048, 2048) for i in range(6)]
    kb.dma(ng[:], ng_d, writes=["ng"])
    pwst = av(8192, [4 * 2 * 128], F32)
    kb.dma(pwst, pwb_d.rearrange("p a b c -> p (a b c)"), writes=ak(8192, 4096))
    kb.op("dve", lambda v: v.tensor_copy(out=pwb[:].rearrange("p a b c -> p (a b c)"), in_=pwst), reads=ak(8192, 4096), writes=["pwb"])
    kb.dma(psc[:], psc_d, writes=["psc"])
    kb.op("dve", lambda v: v.tensor_scalar(out=psc[:], in0=psc[:], scalar1=0.5, scalar2=None, op0=ALU.mult), reads=["psc"], writes=["psc"])
    kb.dma(pcn[:], pcn_d, writes=["pcn"])
    kb.dma(ropet[:], rope_d, writes=["ropet"])
    kb.dma(band[:], band_d, writes=["band"])
    io = av(0, [128], I32)
    kb.op("pool", lambda g: g.iota(io, [[1, 128]], base=0, channel_multiplier=-1), writes=ak(0, 512))
    kb.op("dve", lambda v: v.tensor_single_scalar(out=identf[:], in_=io, scalar=0, op=ALU.is_equal),
          reads=ak(0, 512), writes=["identf"])
    kb.op("dve", lambda v: v.tensor_copy(out=ident[:], in_=identf[:]), reads=["identf"], writes=["ident"])

    hTf = hT[:].rearrange("p a b -> p (a b)")
    yTf = yT[:].rearrange("p a b -> p (a b)")

    def tv(flat, off, n, dt):
        esz = 4 if dt == F32 else 2
        v = flat[:, off // 2: off // 2 + n * esz // 2]
        return v.bitcast(dt) if dt != BF16 else v

    wtasks = {}
    for l in range(depth):
        lst = []
        for c in range(8):
            def w_in_task(l=l, c=c):
                k = c % 2
                st = tv(hTf, k * 12288, 3072, F32)
                sb_ = tv(yTf, k * 6144, 3072, BF16)
                kst, ksb = [("hTs", k)], [("yTs", k)]
                kb.dma(st, w_in_d[l, c * 128:(c + 1) * 128, :], writes=kst, q="act")
                kb.op("pool" if k else "dve",
                      lambda v: v.tensor_scalar(out=sb_, in0=st, scalar1=ng[:, l, c:c + 1], scalar2=1.0,
                                                op0=ALU.mult, op1=ALU.mult),
                      reads=kst + ["ng"], writes=ksb)
                kb.dma(wib_d[l, :, :, c, :].rearrange("b p n -> p b n"), sb_.rearrange("p (b n) -> p b n", b=12),
                       reads=ksb, writes=["wib"], q="act")

            def w_out_task(l=l, c=c):
                k = c % 2
                st = tv(yTf, 12288 + k * 4096, 1024, F32)
                sb_ = tv(yTf, 20480 + k * 2048, 1024, BF16)
                kst, ksb = [("yTs", 2 + k)], [("yTs", 4 + k)]
                kb.dma(st, w_out_d[l, c * 128:(c + 1) * 128, :], writes=kst, q="act")
                kb.op("dve" if k else "pool", lambda v: v.tensor_copy(out=sb_, in_=st), reads=kst, writes=ksb)
                kb.dma(wob_d[l, :, :, c, :].rearrange("h p n -> p h n"), sb_.rearrange("p (h n) -> p h n", h=4),
                       reads=ksb, writes=["wob"], q="act")
            lst.append(w_in_task)
            lst.append(w_out_task)
        wtasks[l] = lst
    if "c" not in mixers:
        for l in range(depth):
            for t_ in wtasks[l]:
                t_()

    xf = x_res[:].rearrange("p a b -> p (a b)").bitcast(BF16)
    na_tasks = {}
    if "d" in mixers:
        nmask = tv(xf, 0, 2368, BF16)
        kb.dma(nmask, nam_d, writes=[("xs", "m")])
        negm = tv(xf, 33152, 2368, F32)
        knegm = [("xs", "n")]
        kb.op("dve", lambda v: v.tensor_scalar(out=negm, in0=nmask, scalar1=-1.0, scalar2=240000.0, op0=ALU.add, op1=ALU.mult),
              reads=[("xs", "m")], writes=knegm)
        for l in range(depth):
            for h in range(4):
                def na_task(l=l, h=h):
                    k = (l * 4 + h) % 2
                    o_st = 4736 + k * 14208
                    stg = tv(xf, o_st, 2368, F32)
                    eo = tv(xf, o_st + 9472, 2368, BF16)
                    kst, keo = [("xs", "s", k)], [("xs", "e", k)]
                    base = (l * 4 + h) * 15 * 128
                    for krl in range(2):
                        ps_ = slice(krl * 64, krl * 64 + 64)
                        src = bass.AP(rpb_t, base + (4 - krl) * 128, [[1, 64], [128, 9], [1, 64]])
                        kb.dma(stg[ps_, 0:576].rearrange("p (r c) -> p r c", r=9), src, writes=kst)
                        for sr in range(7):
                            r = sr if sr < 4 else 25 + sr
                            if sr < 4:
                                i0 = 1 - krl + r
                                dst = stg[ps_, 576:1600].rearrange("p (u s c) -> p u s c", u=4, s=4)[:, :, sr, :]
                            else:
                                i0 = r - 23 - krl
                                dst = stg[ps_, 1600:2368].rearrange("p (u s c) -> p u s c", u=4, s=3)[:, :, sr - 4, :]
                            src = bass.AP(rpb_t, base + i0 * 128, [[1, 64], [256, 4], [1, 64]])
                            kb.dma(dst, src, writes=kst)
                    st3 = stg.rearrange("p (b c) -> p b c", c=64)
                    kb.op("dve", lambda v: v.scalar_tensor_tensor(
                        out=st3, in0=st3, scalar=8.0, in1=nmask.rearrange("p (b c) -> p b c", c=64),
                        op0=ALU.mult, op1=ALU.mult), reads=kst + [("xs", "m")], writes=kst)
                    kb.op("dve", lambda v: v.tensor_tensor(
                        out=eo.rearrange("p (b c) -> p b c", c=64), in0=st3[:, :, ::-1],
                        in1=negm.rearrange("p (b c) -> p b c", c=64)[:, :, ::-1],
                        op=ALU.add), reads=kst + knegm, writes=keo)
                    kb.dma(et_d[l, h], eo, reads=keo, writes=["et"])
                na_tasks[(l, h)] = na_task
        if "c" not in mixers:
            for l in range(depth):
                for h in range(4):
                    na_tasks[(l, h)]()

    TWO_PI = 2.0 * math.pi

    def s5_precompute(l):
        kb.same_depth = 1000000
        st = {"o": 0}

        def A(shape, dt=F32):
            n = int(np.prod(shape))
            nb = n * (4 if dt in (F32, I32) else 2)
            off = st["o"]
            st["o"] = off + (nb + 63) // 64 * 64
            v = arena[:, off // 2: off // 2 + nb // 2]
            if dt != BF16:
                v = v.bitcast(dt)
            if len(shape) == 2:
                v = v.rearrange("p (a b) -> p a b", a=shape[0])
            elif len(shape) == 3:
                v = v.rearrange("p (a b c) -> p a b c", a=shape[0], b=shape[1])
            return v, ak(off, nb)

        def tt_(e, out, a, b, op, rd, wr):
            kb.op(e, lambda v: v.tensor_tensor(out=out, in0=a, in1=b, op=op), reads=rd, writes=wr)

        def ts_(e, out, a, s1, s2, op0, op1, rd, wr):
            kb.op(e, lambda v: v.tensor_scalar(out=out, in0=a, scalar1=s1, scalar2=s2, op0=op0, op1=op1) if s2 is not None
                  else v.tensor_scalar(out=out, in0=a, scalar1=s1, scalar2=None, op0=op0), reads=rd, writes=wr)

        def frac_(T, kT, TI, kTI, TF, kTF):
            MAGIC = 12582912.0
            ts_("dve", TF, T, MAGIC, None, ALU.add, None, kT, kTF)
            ts_("dve", TF, TF, -MAGIC, None, ALU.add, None, kTF, kTF)
            tt_("dve", T, T, TF, ALU.subtract, kT + kTF, kT)
            kb.op("dve", lambda v: v.tensor_single_scalar(out=TF, in_=T, scalar=0.5, op=ALU.is_gt), reads=kT, writes=kTF)
            tt_("dve", T, T, TF, ALU.subtract, kT + kTF, kT)
            kb.op("dve", lambda v: v.tensor_single_scalar(out=TF, in_=T, scalar=-0.5, op=ALU.is_lt), reads=kT, writes=kTF)
            tt_("dve", T, T, TF, ALU.add, kT + kTF, kT)

        def sincos_(T, kT, SN, kSN, CS, kCS, TI, kTI, TF, kTF):
            frac_(T, kT, TI, kTI, TF, kTF)
            kb.op("act", lambda a: a.activation(out=SN, in_=T, func=AF.Sin, scale=6.28318), reads=kT, writes=kSN)
            ts_("dve", T, T, 0.25, None, ALU.add, None, kT, kT)
            kb.op("dve", lambda v: v.tensor_single_scalar(out=TF, in_=T, scalar=0.5, op=ALU.is_gt), reads=kT, writes=kTF)
            tt_("dve", T, T, TF, ALU.subtract, kT + kTF, kT)
            kb.op("act", lambda a: a.activation(out=CS, in_=T, func=AF.Sin, scale=6.28318), reads=kT, writes=kCS)

        lam, klam = A([16, 3])
        Bp, kBp = A([16, 16, 2])
        Cp, kCp = A([16, 16, 2])
        Dt, kDt = A([16])
        kb.dma(lam, lam_d[l], writes=klam)
        kb.dma(Bp, bp_d[l], writes=kBp)
        kb.dma(Cp, cp_d[l], writes=kCp)
        kb.dma(Dt[0:16, :], dt_d[:, l, :], writes=kDt)
        BBr, kBBr = A([16, 16])
        BBi, kBBi = A([16, 16])
        nBBi, knBBi = A([16, 16])
        WP, kWP = A([2, 16, 16])
        WC, kWC = A([2, 16, 16])
        WK, kWK = A([2, 16, 31])
        mark = st["o"]
        dtt, kdt = A([16])
        xr, kxr = A([16])
        tht, ktht = A([16])
        NNi, kNNi = A([128], I32)
        NN, kNN = A([128])
        TI, kTI = A([512], I32)
        TF, kTF = A([512])
        ARG, kARG = A([16, 17])
        MAG, kMAG = A([16, 17])
        SN, kSN = A([16, 17])
        CS, kCS = A([16, 17])
        WR, kWR = A([16, 17])
        WI, kWI = A([16, 17])
        kb.op("act", lambda a: a.activation(out=dtt, in_=lam[:, :, 2], func=AF.Exp), reads=klam, writes=kdt)
        tt_("dve", xr, lam[:, :, 0], dtt, ALU.mult, klam + kdt, kxr)
        tt_("dve", tht, lam[:, :, 1], dtt, ALU.mult, klam + kdt, ktht)
        ts_("dve", tht, tht, 1.0 / TWO_PI, None, ALU.mult, None, ktht, ktht)
        frac_(tht, ktht, TI[:, 0:16], kTI, TF[:, 0:16], kTF)
        kb.op("pool", lambda g: g.iota(NNi, [[1, 128]], base=0, channel_multiplier=0), writes=kNNi)
        kb.op("dve", lambda v: v.tensor_copy(out=NN, in_=NNi), reads=kNNi, writes=kNN)
        nb17 = NN[:, 0:17].unsqueeze(1).broadcast_to([128, 16, 17])
        tt_("dve", ARG, tht.unsqueeze(2).broadcast_to([128, 16, 17]), nb17, ALU.mult, ktht + kNN, kARG)
        tt_("dve", MAG, xr.unsqueeze(2).broadcast_to([128, 16, 17]), nb17, ALU.mult, kxr + kNN, kMAG)
        kb.op("act", lambda a: a.activation(out=MAG, in_=MAG, func=AF.Exp), reads=kMAG, writes=kMAG)
        f2 = lambda t: t.rearrange("p a b -> p (a b)")
        sincos_(f2(ARG), kARG, f2(SN), kSN, f2(CS), kCS, TI[:, 0:272], kTI, TF[:, 0:272], kTF)
        tt_("dve", WR, MAG, CS, ALU.mult, kMAG + kCS, kWR)
        tt_("dve", WI, MAG, SN, ALU.mult, kMAG + kSN, kWI)
        if l == 0:
            dump("WR", WR, kWR)
            dump("WI", WI, kWI)
            dump("dtt", dtt, kdt)
            dump("xr", xr, kxr)
            dump("tht", tht, ktht)
            dump("NN", NN, kNN)
            dump("MAG", MAG, kMAG)
            dump("SN", SN, kSN)
            dump("CS", CS, kCS)
            dump("lam", lam, klam)
        kb.op("act", lambda a: a.copy(out=rho_sb[:, l, :], in_=MAG[:, :, 16]), reads=kMAG, writes=["rho"])
        den, kden = A([16])
        t1, kt1 = A([16])
        t2, kt2 = A([16])
        gr, kgr = A([16])
        gi, kgi = A([16])
        lr_, li_ = lam[:, :, 0], lam[:, :, 1]
        tt_("dve", den, lr_, lr_, ALU.mult, klam, kden)
        tt_("dve", t1, li_, li_, ALU.mult, klam, kt1)
        tt_("dve", den, den, t1, ALU.add, kden + kt1, kden)
        kb.op("dve", lambda v: v.reciprocal(out=den, in_=den), reads=kden, writes=kden)
        ts_("dve", t1, WR[:, :, 1], -1.0, None, ALU.add, None, kWR, kt1)
        tt_("dve", gr, t1, lr_, ALU.mult, kt1 + klam, kgr)
        tt_("dve", t2, WI[:, :, 1], li_, ALU.mult, kWI + klam, kt2)
        tt_("dve", gr, gr, t2, ALU.add, kgr + kt2, kgr)
        tt_("dve", gr, gr, den, ALU.mult, kgr + kden, kgr)
        tt_("dve", gi, WI[:, :, 1], lr_, ALU.mult, kWI + klam, kgi)
        tt_("dve", t2, t1, li_, ALU.mult, kt1 + klam, kt2)
        tt_("dve", gi, gi, t2, ALU.subtract, kgi + kt2, kgi)
        tt_("dve", gi, gi, den, ALU.mult, kgi + kden, kgi)
        u1, ku1 = A([16, 16])
        grb = gr.unsqueeze(2).broadcast_to([128, 16, 16])
        gib = gi.unsqueeze(2).broadcast_to([128, 16, 16])
        Br_, Bi_ = Bp[:, :, :, 0], Bp[:, :, :, 1]
        tt_("dve", BBr, grb, Br_, ALU.mult, kgr + kBp, kBBr)
        tt_("dve", u1, gib, Bi_, ALU.mult, kgi + kBp, ku1)
        tt_("dve", BBr, BBr, u1, ALU.subtract, kBBr + ku1, kBBr)
        tt_("dve", BBi, grb, Bi_, ALU.mult, kgr + kBp, kBBi)
        tt_("dve", u1, gib, Br_, ALU.mult, kgi + kBp, ku1)
        tt_("dve", BBi, BBi, u1, ALU.add, kBBi + ku1, kBBi)
        ts_("dve", nBBi, BBi, -1.0, None, ALU.mult, None, kBBi, knBBi)
        kb.op("pool", lambda g: g.memset(WK, 0.0), writes=kWK)
        F_, B_ = slice(0, 64), slice(64, 128)
        for ri, W_, kW_ in ((0, WR, kWR), (1, WI, kWI)):
            cp = lambda dst, src, wk: kb.op("act", lambda a: a.copy(out=dst, in_=src), reads=kW_, writes=wk)
            cp(WP[F_, ri, :, 0:8], W_[F_, :, 8:16], kWP)
            cp(WP[F_, ri, :, 8:16], W_[F_, :, 0:8], kWP)
            cp(WP[B_, ri, :, 0:8], W_[B_, :, 7::-1], kWP)
            cp(WP[B_, ri, :, 8:16], W_[B_, :, 15:7:-1], kWP)
            cp(WC[F_, ri, :, :], W_[F_, :, 1:17], kWC)
            cp(WC[B_, ri, :, :], W_[B_, :, 16:0:-1], kWC)
            cp(WK[F_, ri, :, 15:31], W_[F_, :, 0:16], kWK)
            cp(WK[B_, ri, :, 0:16], W_[B_, :, 15::-1], kWK)
        pht, kpht = A([16])
        ts_("dve", pht, tht, 16.0, None, ALU.mult, None, ktht, kpht)
        frac_(pht, kpht, TI[:, 0:16], kTI, TF[:, 0:16], kTF)
        EA, kEA = A([4, 128])
        ES, kES = A([4, 2, 128])
        for q in range(4):
            tt_("dve", EA, pht[:, 4 * q:4 * q + 4].unsqueeze(2).broadcast_to([128, 4, 128]),
                NN.unsqueeze(1).broadcast_to([128, 4, 128]), ALU.mult, kpht + kNN, kEA)
            sincos_(f2(EA), kEA, ES[:, :, 1, :], kES, ES[:, :, 0, :], kES, TI[:, 0:512], kTI, TF[:, 0:512], kTF)
            kb.dma(etab_d[l, q], ES, reads=kES, writes=["etab"])
            if l == 0 and q == 0:
                dump("ES0", ES, kES)
        gst, kgst = A([2, 256])
        gbf, kgbf = A([2, 256], BF16)
        kb.dma(gst, glw_d[l], writes=kgst)
        kb.op("act", lambda a: a.copy(out=gbf, in_=gst), reads=kgst, writes=kgbf)
        kb.dma(gwb_d[l], gbf, reads=kgbf, writes=["gwb"])
        st["o"] = mark
        bufs = []
        for par in range(2):
            d_ = {}
            d_["PPr"], d_["kPPr"] = A([2, 128])
            d_["PPi"], d_["kPPi"] = A([2, 128])
            d_["q1"], d_["kq1"] = A([2, 128])
            d_["q2"], d_["kq2"] = A([2, 128])
            d_["Rr"], d_["kRr"] = A([31, 16])
            d_["Ri"], d_["kRi"] = A([31, 16])
            d_["r1"], d_["kr1"] = A([31, 16])
            d_["r2"], d_["kr2"] = A([31, 16])
            d_["Ks"], d_["kKs"] = A([496])
            d_["Ms"], d_["kMs"] = A([3, 128])
            d_["blk"], d_["kblk"] = A([11, 128], BF16)
            bufs.append(d_)
        assert st["o"] <= ARENA, st["o"]
        kb.same_depth = 3
        for g in range(16):
            wtasks[l][g]()
            if g % 4 == 0 and (l, g // 4) in na_tasks:
                na_tasks[(l, g // 4)]()
            d_ = bufs[g % 2]
            PPr, PPi, q1, q2 = d_["PPr"], d_["PPi"], d_["q1"], d_["q2"]
            kPPr, kPPi, kq1, kq2 = d_["kPPr"], d_["kPPi"], d_["kq1"], d_["kq2"]
            blk, kblk = d_["blk"], d_["kblk"]
            v4 = lambda t: t.rearrange("p s (j c) -> p s j c", j=8)
            wpr = WP[:, 0, g, :].rearrange("p (s j) -> p s j", s=2).unsqueeze(3).broadcast_to([128, 2, 8, 16])
            wpi = WP[:, 1, g, :].rearrange("p (s j) -> p s j", s=2).unsqueeze(3).broadcast_to([128, 2, 8, 16])
            bbr = BBr[:, g, :].unsqueeze(1).unsqueeze(1).broadcast_to([128, 2, 8, 16])
            bbi = BBi[:, g, :].unsqueeze(1).unsqueeze(1).broadcast_to([128, 2, 8, 16])
            e1, e2 = ("dve", "pool") if g % 2 == 0 else ("pool", "dve")
            tt_(e1, v4(q1), wpr, bbr, ALU.mult, kWP + kBBr, kq1)
            tt_(e2, v4(q2), wpi, bbi, ALU.mult, kWP + kBBi, kq2)
            tt_(e1, PPr, q1, q2, ALU.subtract, kq1 + kq2, kPPr)
            tt_(e2, v4(q1), wpr, bbi, ALU.mult, kWP + kBBi, kq1)
            tt_(e1, v4(q2), wpi, bbr, ALU.mult, kWP + kBBr, kq2)
            tt_(e2, PPi, q1, q2, ALU.add, kq1 + kq2, kPPi)
            bt = next_bank(6, 8)
            for j, (src, ksrc) in enumerate(((PPr[:, 0, :], kPPr), (PPr[:, 1, :], kPPr), (PPi[:, 0, :], kPPi), (PPi[:, 1, :], kPPi))):
                kb.op("pe", lambda pe, j=j, src=src: pe.transpose(pp[bt][:, j * 128:(j + 1) * 128], src, identf[:]),
                      reads=ksrc + ["identf"], writes=[("pp", bt)])
            kb.op("act", lambda a: a.copy(out=blk[:, 3:7, :], in_=pp[bt][:, :].rearrange("p (a b) -> p a b", a=4)),
                  reads=[("pp", bt)], writes=kblk)
            c4 = lambda t: t.rearrange("p s (i c) -> p (s i) c", i=8)
            wcr = WC[:, 0, g, :].unsqueeze(2).broadcast_to([128, 16, 16])
            wci = WC[:, 1, g, :].unsqueeze(2).broadcast_to([128, 16, 16])
            cr = Cp[:, g, :, 0].unsqueeze(1).broadcast_to([128, 16, 16])
            ci = Cp[:, g, :, 1].unsqueeze(1).broadcast_to([128, 16, 16])
            tt_(e1, c4(q1), cr, wcr, ALU.mult, kCp + kWC, kq1)
            tt_(e2, c4(q2), ci, wci, ALU.mult, kCp + kWC, kq2)
            tt_(e1, blk[:, 7:10:2, :], q1, q2, ALU.subtract, kq1 + kq2, kblk)
            tt_(e2, c4(q1), cr, wci, ALU.mult, kCp + kWC, kq1)
            tt_(e1, c4(q2), ci, wcr, ALU.mult, kCp + kWC, kq2)
            kb.op("dve", lambda v: v.scalar_tensor_tensor(out=blk[:, 8:11:2, :], in0=q1, scalar=-1.0, in1=q2,
                                                           op0=ALU.mult, op1=ALU.subtract),
                  reads=kq1 + kq2, writes=kblk)
            Rr, Ri, r1, r2 = d_["Rr"], d_["Ri"], d_["r1"], d_["r2"]
            kRr, kRi, kr1, kr2 = d_["kRr"], d_["kRi"], d_["kr1"], d_["kr2"]
            wkr = WK[:, 0, g, :].unsqueeze(2).broadcast_to([128, 31, 16])
            wki = WK[:, 1, g, :].unsqueeze(2).broadcast_to([128, 31, 16])
            cr3 = Cp[:, g, :, 0].unsqueeze(1).broadcast_to([128, 31, 16])
            ci3 = Cp[:, g, :, 1].unsqueeze(1).broadcast_to([128, 31, 16])
            tt_(e1, r1, cr3, wkr, ALU.mult, kCp + kWK, kr1)
            tt_(e2, r2, ci3, wki, ALU.mult, kCp + kWK, kr2)
            tt_(e1, Rr, r1, r2, ALU.subtract, kr1 + kr2, kRr)
            tt_(e2, r1, cr3, wki, ALU.mult, kCp + kWK, kr1)
            tt_(e1, r2, ci3, wkr, ALU.mult, kCp + kWK, kr2)
            tt_(e2, Ri, r1, r2, ALU.add, kr1 + kr2, kRi)
            bk = next_bank(4, 6)
            kb.op("pe", lambda pe: pe.matmul(pp[bk][0:16, 0:496], BBr[:, g, :], Rr.rearrange("p a b -> p (a b)"),
                                             start=True, stop=False), reads=kBBr + kRr, writes=[("pp", bk)])
            kb.op("pe", lambda pe: pe.matmul(pp[bk][0:16, 0:496], nBBi[:, g, :], Ri.rearrange("p a b -> p (a b)"),
                                             start=False, stop=True), reads=knBBi + kRi, writes=[("pp", bk)])
            Ks, kKs = d_["Ks"], d_["kKs"]
            kb.op("act", lambda a: a.copy(out=Ks[0:16, :], in_=pp[bk][0:16, 0:496]), reads=[("pp", bk)], writes=kKs)
            kb.op("dve", lambda v: v.scalar_tensor_tensor(out=Ks[0:16, 240:256], in0=identf[0:16, 0:16],
                                                           scalar=Dt[0:16, g:g + 1], in1=Ks[0:16, 240:256],
                                                           op0=ALU.mult, op1=ALU.add),
                  reads=kKs + kDt + ["identf"], writes=kKs)
            kb.dma(ktab_d[l, g].rearrange("i c o -> c i o"), Ks[0:16, :].rearrange("p (i o) -> p i o", i=31),
                   reads=kKs, writes=[("ktab", l, g)])
            Ms, kMs = d_["Ms"], d_["kMs"]
            kbase = (l * 16 + g) * 31 * 256
            for bi_, boff in enumerate((8, 16, 0)):
                src = bass.AP(ktab_t, kbase + boff * 256, [[16, 128], [256, 8], [1, 16]])
                kb.dma(Ms[:, bi_, :].rearrange("p (i o) -> p i o", i=8), src, reads=[("ktab", l, g)], writes=kMs)
            kb.op("act", lambda a: a.copy(out=blk[:, 0:3, :], in_=Ms), reads=kMs, writes=kblk)
            kb.dma(sblk_d[l, g], blk, reads=kblk, writes=["sblk"])
            if l == 0 and g == 0:
                dump("blk0", blk, kblk)
                dump("Ks0", Ks[0:16, :], kKs)
                dump("BBr", BBr, kBBr)
                dump("BBi", BBi, kBBi)
                dump("WP", WP, kWP)
                dump("WC", WC, kWC)
                dump("WK", WK, kWK)
                dump("PPr", PPr, kPPr)

    if "c" in mixers:
        kb.dma(glb[:], glb_d, writes=["glb"])
        kb.op("dve", lambda v: v.tensor_scalar(out=glb[:], in0=glb[:], scalar1=0.5, scalar2=None, op0=ALU.mult),
              reads=["glb"], writes=["glb"])
        for l in range(depth):
            s5_precompute(l)

    wstate = {"n": 0}

    def load_wblock(l, b):
        k = wstate["n"] % 3
        wstate["n"] += 1
        kb.dma(wb[k][:], wib_d[l, b], reads=["wib"], writes=[("wb", k)])
        return k

    def load_woblock(l, h):
        k = wstate["n"] % 3
        wstate["n"] += 1
        kb.dma(wb[k][:], wob_d[l, h], reads=["wob"], writes=[("wb", k)])
        return k


    def proj_fm(k, cb, evac, tqs=range(4), M=128, moff=0):
        for tq in tqs:
            bi = next_bank(0, 2)
            for c in range(8):
                kb.op("pe", lambda pe, c=c, bi=bi, tq=tq: pe.matmul(
                    pp[bi][0:M, :], wb[k][:, c, cb * 128 + moff: cb * 128 + moff + M], hT[:, c, tq * 512:(tq + 1) * 512],
                    start=(c == 0), stop=(c == 7)),
                    reads=[("wb", k), "hT"], writes=[("pp", bi)])
            evac(tq, bi)

    def proj_tm(k, tok_ap_fn, ntiles, evac, ncols=256):
        for i in range(ntiles):
            bi = next_bank(0, 2)
            for c in range(8):
                kb.op("pe", lambda pe, c=c, bi=bi, i=i: pe.matmul(
                    pp[bi][:, 0:ncols], tok_ap_fn(c, i), wb[k][:, c, 0:ncols], start=(c == 0), stop=(c == 7)),
                    reads=[("wb", k), "hT"], writes=[("pp", bi)])
            evac(i, bi)

    def rms_sq(i):
        junk = hs[4 + i % 2]
        kb.op("act", lambda a, i=i, junk=junk: a.activation(out=junk, in_=x_res[:, i, :], func=AF.Square,
                                                            accum_out=small[:, 16 + i:17 + i]),
              reads=[("x", i)], writes=khs[4 + i % 2] + ["ss"])

    def rms_fin():
        kb.op("dve", lambda v: v.tensor_scalar(out=small[:, 32:48], in0=small[:, 16:32],
                                               scalar1=1.0 / D, scalar2=EPS, op0=ALU.mult, op1=ALU.add),
              reads=["ss"], writes=["ms"])
        kb.op("pool", lambda g: g.tensor_tensor(out=small[:, 0:16], in0=small[:, 32:48],
                                                in1=small[:, 48:49].broadcast_to([128, 16]), op=ALU.pow),
              reads=["ms", "mhalf"], writes=["rstd"])

    kb.op("dve", lambda v: v.memset(small[:, 48:49], -0.5), writes=["mhalf"])
    nlh = small[:, 49:50]
    kb.op("dve", lambda v: v.memset(nlh, -math.log(2.0)), writes=["nlh"])

    W = S + 32

    def mixer_a(l):
        kU, kA, kB, kg2, kDm = ak(0, 8320), ak(8320, 8320), ak(16640, 8320), ak(24960, 4096), ak(29056, 4096)
        ktt = [ak(33152 + j * 2048, 2048) for j in range(2)]
        U = av(0, [W], F32)
        A = av(8320, [W], F32)
        B = av(16640, [W], F32)
        g2 = av(24960, [S], BF16)
        Dm = av(29056, [S], BF16)
        tt = [av(33152 + j * 2048, [512], F32) for j in range(2)]
        for cb in range(2):
            kv = load_wblock(l, 0)
            kg = load_wblock(l, 1)
            kb.op("pool", lambda g: g.memset(U[:, 0:16], 0.0), writes=kU)
            kb.op("pool", lambda g: g.memset(U[:, 16 + S:W], 0.0), writes=kU)

            def ev_u(tq, bi):
                kb.op("act", lambda a: a.copy(out=U[:, 16 + tq * 512:16 + (tq + 1) * 512], in_=pp[bi][:, :]),
                      reads=[("pp", bi)], writes=kU)
            proj_fm(kv, cb, ev_u)
            kb.op("pool", lambda g: g.tensor_tensor(out=A[:, 1:W], in0=U[:, 0:W - 1], in1=U[:, 1:W], op=ALU.add),
                  reads=kU, writes=kA)
            kb.op("pool", lambda g: g.tensor_tensor(out=B[:, 2:W - 1], in0=A[:, 1:W - 2], in1=A[:, 3:W], op=ALU.add),
                  reads=kA, writes=kB)
            if cb == 1:
                kb.op("pool", lambda g: g.tensor_tensor(out=A[:, 4:W - 3], in0=B[:, 2:W - 5], in1=B[:, 6:W - 1], op=ALU.add),
                      reads=kB, writes=kA)
                kb.op("pool", lambda g: g.tensor_tensor(out=B[64:128, 8:W - 7], in0=A[64:128, 4:W - 11],
                                                        in1=A[64:128, 12:W - 3], op=ALU.add),
                      reads=kA, writes=kB)
            for (buf, nm, p0) in ((A, kA, 0), (B, kB, 64)):
                sl = slice(p0, p0 + 64)
                kb.op("dve", lambda v, buf=buf, sl=sl: v.tensor_tensor(
                    out=buf[sl, 16:24], in0=buf[sl, 16:24], in1=pcn[sl, cb, 0:8], op=ALU.mult),
                    reads=nm + ["pcn"], writes=nm)
                kb.op("dve", lambda v, buf=buf, sl=sl: v.tensor_tensor(
                    out=buf[sl, 8 + S:16 + S], in0=buf[sl, 8 + S:16 + S], in1=pcn[sl, cb, 8:16], op=ALU.mult),
                    reads=nm + ["pcn"], writes=nm)
                kb.op("dve", lambda v, buf=buf, sl=sl: v.scalar_tensor_tensor(
                    out=Dm[sl, :], in0=buf[sl, 16:16 + S], scalar=pcn[sl, cb, 16:17], in1=U[sl, 16:16 + S],
                    op0=ALU.mult, op1=ALU.subtract),
                    reads=nm + kU + ["pcn"], writes=kDm)

            def ev_g(tq, bi):
                t = tt[tq % 2]
                kb.op("act", lambda a: a.activation(out=t, in_=pp[bi][:, :], func=AF.Tanh, scale=0.5),
                      reads=[("pp", bi)], writes=ktt[tq % 2])
                kb.op("dve", lambda v: v.scalar_tensor_tensor(
                    out=g2[:, tq * 512:(tq + 1) * 512], in0=t, scalar=1.0, in1=pp[bi][:, :], op0=ALU.add, op1=ALU.mult),
                    reads=ktt[tq % 2] + [("pp", bi)], writes=kg2)
            proj_fm(kg, cb, ev_g)
            for tq in range(4):
                bi = next_bank(0, 2)
                kb.op("pe", lambda pe: pe.matmul(pp[bi][:, :], pwb[:, l, cb, :], Dm[:, tq * 512:(tq + 1) * 512],
                                                 start=True, stop=True),
                      reads=["pwb"] + kDm, writes=[("pp", bi)])
                kb.op("dve", lambda v: v.scalar_tensor_tensor(
                    out=yT[:, cb, tq * 512:(tq + 1) * 512], in0=pp[bi][:, :], scalar=psc[:, l, cb:cb + 1],
                    in1=g2[:, tq * 512:(tq + 1) * 512], op0=ALU.mult, op1=ALU.mult),
                    reads=[("pp", bi), "psc"] + kg2, writes=ky(cb))

    def run_pipeline(tasks, la):
        n = len(tasks)
        for i in range(n + la):
            if i < n:
                t = tasks[i]
                t["slot"] = i
                if t.get("pre"):
                    t["pre"]()
                t["s1"]()
            if i >= la:
                t = tasks[i - la]
                t["s2"]()
                if t.get("post"):
                    t["post"]()

    def run_pipeline_b(tasks, bs):
        n = len(tasks)
        nb = (n + bs - 1) // bs
        for b in range(nb + 1):
            if b < nb:
                for i in range(b * bs, min(n, (b + 1) * bs)):
                    t = tasks[i]
                    t["slot"] = i
                    if t.get("pre"):
                        t["pre"]()
                for i in range(b * bs, min(n, (b + 1) * bs)):
                    tasks[i]["s1a"]()
                for i in range(b * bs, min(n, (b + 1) * bs)):
                    tasks[i]["s1b"]()
            if b >= 1:
                for i in range((b - 1) * bs, min(n, b * bs)):
                    t = tasks[i]
                    t["s2"]()
                    if t.get("post"):
                        t["post"]()

    def gate_fm(k, kg2, g2, ktt, tt):
        for cb in range(2):
            def ev_g(tq, bi):
                t = tt[tq % 2]
                kb.op("act", lambda a: a.activation(out=t, in_=pp[bi][:, :], func=AF.Tanh, scale=0.5),
                      reads=[("pp", bi)], writes=ktt[tq % 2])
                kb.op("dve", lambda v: v.scalar_tensor_tensor(
                    out=g2[:, cb, tq * 512:(tq + 1) * 512], in0=t, scalar=1.0, in1=pp[bi][:, :], op0=ALU.add, op1=ALU.mult),
                    reads=ktt[tq % 2] + [("pp", bi)], writes=kg2)
            proj_fm(k, cb, ev_g)

    def psl(p0, n, d, n_sub):
        st = (p0 % n_sub) * d + p0 // n_sub
        return slice(st, st + (n - 1) * d + 1, d)

    def attn_finalize(h, acc, kacc, g2, kg2, rc, krc, tmp, ktmp, chunk0):
        nr = slice((h % 2) * 64, (h % 2) * 64 + 64)
        dr = slice(((h + 1) % 2) * 64, ((h + 1) % 2) * 64 + 64)
        for tq in range(8):
            ts_ = slice(tq * 256, (tq + 1) * 256)
            kb.op("dve", lambda v: v.reciprocal(out=rc[tq % 2][nr, :], in_=acc[dr, ts_]),
                  reads=kacc, writes=krc[tq % 2])
            kb.op("dve", lambda v: v.scalar_tensor_tensor(out=tmp[tq % 2][nr, :], in0=acc[nr, ts_], scalar=0.5,
                                                           in1=rc[tq % 2][nr, :], op0=ALU.mult, op1=ALU.mult),
                  reads=kacc + krc[tq % 2], writes=ktmp[tq % 2])
            kb.op("pool", lambda g: g.tensor_tensor(out=yT[nr, chunk0 + h // 2, ts_], in0=tmp[tq % 2][nr, :],
                                                    in1=g2[nr, h // 2, ts_], op=ALU.mult),
                  reads=ktmp[tq % 2] + kg2, writes=ky(chunk0 + h // 2, h=h % 2))

    def gate_apply(l, blk, chunk0, tt, ktt, gq, kgq):
        k = load_wblock(l, blk)
        for cb in range(2):
            def ev_g(tq, bi):
                t = tt[tq % 2]
                g_ = gq[tq % 2]
                ts_ = slice(tq * 512, (tq + 1) * 512)
                kb.op("act", lambda a: a.activation(out=t, in_=pp[bi][:, :], func=AF.Tanh, scale=0.5),
                      reads=[("pp", bi)], writes=ktt[tq % 2])
                kb.op("dve", lambda v: v.scalar_tensor_tensor(out=g_, in0=t, scalar=1.0, in1=pp[bi][:, :],
                                                               op0=ALU.add, op1=ALU.mult),
                      reads=ktt[tq % 2] + [("pp", bi)], writes=kgq[tq % 2])
                kb.op("pool", lambda g: g.tensor_tensor(out=yT[:, chunk0 + cb, ts_], in0=g_, in1=yT[:, chunk0 + cb, ts_],
                                                        op=ALU.mult),
                      reads=kgq[tq % 2] + ky(chunk0 + cb), writes=ky(chunk0 + cb))
            proj_fm(k, cb, ev_g)

    def mixer_b(l):
        o_q, o_kz, o_v, o_acc, o_pt, o_rc = 0, 8192, 24576, 32768, 49152, 51200
        qT = av(o_q, [2, S], BF16)
        kqT = ak(o_q, 8192)
        kTz = [av(o_kz + h * 4096, [S], BF16) for h in range(4)]
        kkTz = [ak(o_kz + h * 4096, 4096) for h in range(4)]
        Va = [av(o_v + j * 4096, [16, 128], BF16) for j in range(2)]
        kVa = [ak(o_v + j * 4096, 4096) for j in range(2)]
        accs = [av(o_acc + j * 8192, [S], F32) for j in range(2)]
        kaccs = [ak(o_acc + j * 8192, 8192) for j in range(2)]
        pt = [av(o_pt + j * 512, [256], BF16) for j in range(4)]
        kpt = [ak(o_pt + j * 512, 512) for j in range(4)]
        rcq = [av(o_rc + j * 1024, [256], F32) for j in range(2)]
        krcq = [ak(o_rc + j * 1024, 1024) for j in range(2)]
        for h in range(4):
            oh = slice(((h + 1) % 2) * 64, ((h + 1) % 2) * 64 + 64)
            kb.op("pool", lambda g, h=h, oh=oh: g.memset(kTz[h][oh, :], 0.0), writes=kkTz[h])
        rts = [[av(o_acc + sl * 2560 + j * 512, [128], F32) for j in range(4)] for sl in range(3)]
        krts = [[ak(o_acc + sl * 2560 + j * 512, 512) for j in range(4)] for sl in range(3)]
        qrs = [av(o_acc + sl * 2560 + 2048, [256], BF16) for sl in range(3)]
        kqrs = [ak(o_acc + sl * 2560 + 2048, 512) for sl in range(3)]
        rtasks = []
        for blk in (2, 3):
            k = load_wblock(l, blk)
            for i in range(NT):
                t = {"pre": None, "post": None}

                def s1(t=t, i=i, k=k):
                    sl = t["slot"] % 3
                    bi = (0, 1, 2)[sl]
                    for c in range(8):
                        kb.op("pe", lambda pe, c=c: pe.matmul(pp[bi][:, 0:256], hT[:, c, i * 128:(i + 1) * 128],
                                                              wb[k][:, c, 0:256], start=(c == 0), stop=(c == 7)),
                              reads=[("wb", k), "hT"], writes=[("pp", bi)])
                    z4 = pp[bi][:, 0:256].rearrange("p (h t f) -> p h t f", h=4, t=2)
                    x1, x2 = z4[:, :, 0, :], z4[:, :, 1, :]
                    cs = ropet[:, 0, i, :].unsqueeze(1).broadcast_to([128, 4, 32])
                    sn = ropet[:, 1, i, :].unsqueeze(1).broadcast_to([128, 4, 32])
                    r4 = [r.rearrange("p (h f) -> p h f", h=4) for r in rts[sl]]
                    q4 = qrs[sl].rearrange("p (h t f) -> p h t f", h=4, t=2)
                    for j, (a_, b_) in enumerate(((x1, cs), (x2, sn), (x2, cs), (x1, sn))):
                        kb.op("dve", lambda v, a_=a_, b_=b_, j=j: v.tensor_tensor(out=r4[j], in0=a_, in1=b_, op=ALU.mult),
                              reads=[("pp", bi), "ropet"], writes=krts[sl][j])
                    kb.op("pool", lambda g: g.tensor_tensor(out=q4[:, :, 0, :], in0=r4[0], in1=r4[1], op=ALU.subtract),
                          reads=krts[sl][0] + krts[sl][1], writes=kqrs[sl])
                    kb.op("pool", lambda g: g.tensor_tensor(out=q4[:, :, 1, :], in0=r4[2], in1=r4[3], op=ALU.add),
                          reads=krts[sl][2] + krts[sl][3], writes=kqrs[sl])

                def s2(t=t, i=i, blk=blk):
                    sl = t["slot"] % 3
                    b2 = next_bank(6, 8)
                    ptb = pp[b2][:].bitcast(BF16)
                    for pr in range(2):
                        kb.op("pe", lambda pe, pr=pr: pe.transpose(ptb[:, pr * 128:(pr + 1) * 128],
                                                                   qrs[sl][:, pr * 128:(pr + 1) * 128], ident[:]),
                              reads=kqrs[sl] + ["ident"], writes=[("pp", b2)])
                    if blk == 2:
                        kb.op("act", lambda a: a.copy(out=qT[:, :, i * 128:(i + 1) * 128],
                                                      in_=ptb[:, 0:256].rearrange("p (c n) -> p c n", c=2)),
                              reads=[("pp", b2)], writes=kqT)
                    else:
                        for h in range(4):
                            hp = slice((h % 2) * 64, (h % 2) * 64 + 64)
                            pr = h // 2
                            kb.op("act" if h % 2 == 0 else "dve",
                                  lambda e, h=h, hp=hp, pr=pr: (e.copy if h % 2 == 0 else e.tensor_copy)(
                                      out=kTz[h][hp, i * 128:(i + 1) * 128], in_=ptb[hp, pr * 128:(pr + 1) * 128]),
                                  reads=[("pp", b2)], writes=kkTz[h])
                t["s1"], t["s2"] = s1, s2
                rtasks.append(t)
        run_pipeline(rtasks, 2)
        kv = load_wblock(l, 4)
        for cb in range(2):
            def ev_vt(tq, bi):
                kb.op("act", lambda a: a.copy(out=yT[:, 2 + cb, tq * 512:(tq + 1) * 512], in_=pp[bi][:, :]),
                      reads=[("pp", bi)], writes=ky(2 + cb))
            proj_fm(kv, cb, ev_vt)
        tasks = []
        nva = 0
        for h in range(4):
            acc, kacc = accs[h % 2], kaccs[h % 2]
            voff = 0 if h % 2 == 0 else 64
            for pi, (d, n_sub) in enumerate(((1, 2048), (4, 512), (16, 128))):
                V, kV = Va[nva % 2], kVa[nva % 2]
                nva += 1

                def pre_v(V=V, kV=kV, h=h, voff=voff, d=d, n_sub=n_sub, pi=pi):
                    if pi < 2:
                        kb.op("pool", lambda g: g.memset(V[:, :, 64 - voff:128 - voff], 1.0), writes=kV)
                    for half in range(2):
                        b2 = next_bank(0, 2)
                        ptb = pp[b2][:].bitcast(BF16)
                        for q in range(8):
                            j = half * 8 + q
                            kb.op("pe", lambda pe, q=q, j=j: pe.transpose(
                                ptb[:, q * 128:(q + 1) * 128], yT[:, 2 + h // 2, psl(128 * j, 128, d, n_sub)], ident[:, :]),
                                reads=ky(2 + h // 2) + ["ident"], writes=[("pp", b2)])
                        kb.op("act", lambda a: a.copy(
                            out=V[:, half * 8:half * 8 + 8, voff:voff + 64],
                            in_=ptb.rearrange("p (q e) -> p q e", q=8)[:, :, voff:voff + 64]),
                            reads=[("pp", b2)], writes=kV)
                first_of_pattern = True
                for qb in range(4):
                    ob = next_bank(4, 6)
                    js = []
                    for j in range(max(0, 4 * qb - 1), min(16, 4 * qb + 5)):
                        slo = (128 * j // n_sub) * n_sub
                        qlo = max(128 * j - 64, slo, 512 * qb)
                        qhi = min(128 * j + 192, slo + n_sub, 512 * qb + 512)
                        if qlo < qhi:
                            js.append((j, qlo, qhi))
                    for ji, (j, qlo, qhi) in enumerate(js):
                        t = {}
                        t["pre"] = pre_v if first_of_pattern else None
                        first_of_pattern = False

                        def s1(t=t, j=j, qlo=qlo, qhi=qhi, h=h, d=d, n_sub=n_sub):
                            n = qhi - qlo
                            ns = t["slot"] % 4
                            sbk = (2, 3, 6, 7)[ns]
                            p_, kp_ = pt[ns], kpt[ns]
                            mo = qlo - (128 * j - 64)
                            kb.op("pe", lambda pe: pe.matmul(pp[sbk][:, 0:n], kTz[h][:, psl(128 * j, 128, d, n_sub)],
                                                             qT[:, h // 2, psl(qlo, n, d, n_sub)], start=True, stop=False),
                                  reads=kkTz[h] + kqT, writes=[("pp", sbk)])
                            kb.op("pe", lambda pe: pe.matmul(pp[sbk][:, 0:n], ident[:, :], band[:, mo:mo + n],
                                                             start=False, stop=True),
                                  reads=["ident", "band"], writes=[("pp", sbk)])
                            kb.op("act", lambda a: a.activation(out=p_[:, 0:n], in_=pp[sbk][:, 0:n], func=AF.Exp, scale=0.125),
                                  reads=[("pp", sbk)], writes=kp_)

                        def s2(t=t, j=j, qlo=qlo, qhi=qhi, qb=qb, ob=ob, V=V, kV=kV, first=(ji == 0)):
                            n = qhi - qlo
                            ns = t["slot"] % 4
                            p_, kp_ = pt[ns], kpt[ns]
                            kb.op("pe", lambda pe: pe.matmul(pp[ob][:, qlo - 512 * qb:qhi - 512 * qb], V[:, j, :], p_[:, 0:n],
                                                             start=first, stop=False, skip_group_check=True),
                                  reads=kV + kp_, writes=[("pp", ob)])
                        t["s1"], t["s2"], t["post"] = s1, s2, None
                        if ji == len(js) - 1:
                            def post(qb=qb, ob=ob, d=d, pi=pi, acc=acc, kacc=kacc, h=h):
                                if d == 1:
                                    dst = acc[:, 512 * qb:512 * qb + 512]
                                    src = pp[ob][:, :]
                                elif d == 4:
                                    dst = acc.rearrange("p (l x) -> p x l", x=4)[:, qb, :]
                                    src = pp[ob][:, :]
                                else:
                                    dst = acc.rearrange("p (l x) -> p x l", x=16)[:, 4 * qb:4 * qb + 4, :]
                                    src = pp[ob][:, :].rearrange("p (r l) -> p r l", r=4)
                                if pi == 0:
                                    kb.op("dve", lambda v: v.tensor_copy(out=dst, in_=src), reads=[("pp", ob)], writes=kacc)
                                else:
                                    kb.op("dve", lambda v: v.tensor_tensor(out=dst, in0=src, in1=dst, op=ALU.add),
                                          reads=[("pp", ob)] + kacc, writes=kacc)
                                if pi == 2 and qb == 3:
                                    nr = slice((h % 2) * 64, (h % 2) * 64 + 64)
                                    dr = slice(((h + 1) % 2) * 64, ((h + 1) % 2) * 64 + 64)
                                    for tq in range(8):
                                        ts_ = slice(tq * 256, (tq + 1) * 256)
                                        rc, krc = rcq[tq % 2], krcq[tq % 2]
                                        kb.op("act", lambda a: a.activation(out=rc[nr, :], in_=acc[dr, ts_], func=AF.Ln),
                                              reads=kacc, writes=krc)
                                        kb.op("act", lambda a: a.activation(out=rc[nr, :], in_=rc[nr, :], func=AF.Exp, scale=-1.0,
                                                                            bias=nlh[nr, :]),
                                              reads=krc + ["nlh"], writes=krc)
                                        kb.op("pool", lambda g: g.tensor_tensor(out=yT[nr, 2 + h // 2, ts_], in0=acc[nr, ts_],
                                                                                in1=rc[nr, :], op=ALU.mult),
                                              reads=kacc + krc, writes=ky(2 + h // 2, h=h % 2))
                            t["post"] = post
                        tasks.append(t)
        run_pipeline(tasks, 3)
        tt = [av(o_acc + j * 2048, [512], F32) for j in range(2)]
        ktt = [ak(o_acc + j * 2048, 2048) for j in range(2)]
        gq = [av(o_acc + 4096 + j * 2048, [512], F32) for j in range(2)]
        kgq = [ak(o_acc + 4096 + j * 2048, 2048) for j in range(2)]
        gate_apply(l, 5, 2, tt, ktt, gq, kgq)

    def na_rows(kt):
        rows = []
        for r in range(32):
            rs = min(max(r - 4, 0), 24)
            if any(rs <= 2 * kt + krl < rs + 8 for krl in range(2)):
                rows.append(r)
        return rows[0], rows[-1]

    def mixer_d(l):
        o_q, o_kz, o_v, o_e, o_tt, o_pt = 0, 8192, 24576, 32768, 42240, 46336
        qT = av(o_q, [2, S], BF16)
        kqT = ak(o_q, 8192)
        kTz = [av(o_kz + h * 4096, [S], BF16) for h in range(4)]
        kkTz = [ak(o_kz + h * 4096, 4096) for h in range(4)]
        Va = [av(o_v + j * 4096, [16, 128], BF16) for j in range(2)]
        kVa = [ak(o_v + j * 4096, 4096) for j in range(2)]
        Eb = [av(o_e + j * 4736, [2368], BF16) for j in range(2)]
        kEb = [ak(o_e + j * 4736, 4736) for j in range(2)]
        tt = [av(o_tt + j * 2048, [512], F32) for j in range(2)]
        ktt = [ak(o_tt + j * 2048, 2048) for j in range(2)]
        pt = [av(o_pt + j * 1024, [512], BF16) for j in range(4)]
        kpt = [ak(o_pt + j * 1024, 1024) for j in range(4)]
        for h in range(4):
            oh = slice(((h + 1) % 2) * 64, ((h + 1) % 2) * 64 + 64)
            kb.op("pool", lambda g, h=h, oh=oh: g.memset(kTz[h][oh, :], 0.0), writes=kkTz[h])
        k = load_wblock(l, 8)
        for cb in range(2):
            def ev_q(tq, bi):
                kb.op("act", lambda a: a.copy(out=qT[:, cb, tq * 512:(tq + 1) * 512], in_=pp[bi][:, :]),
                      reads=[("pp", bi)], writes=kqT)
            proj_fm(k, cb, ev_q)
        k = load_wblock(l, 9)
        for cb in range(2):
            def ev_k(tq, bi):
                for hh in range(2):
                    h = 2 * cb + hh
                    hp = slice(hh * 64, hh * 64 + 64)
                    kb.op("act" if hh == 0 else "dve",
                          lambda e, h=h, hp=hp, hh=hh: (e.copy if hh == 0 else e.tensor_copy)(
                              out=kTz[h][hp, tq * 512:(tq + 1) * 512], in_=pp[bi][hp, :]),
                          reads=[("pp", bi)], writes=kkTz[h])
            proj_fm(k, cb, ev_k)
        kv = load_wblock(l, 10)
        for cb in range(2):
            def ev_vt(tq, bi):
                kb.op("act", lambda a: a.copy(out=yT[:, 6 + cb, tq * 512:(tq + 1) * 512], in_=pp[bi][:, :]),
                      reads=[("pp", bi)], writes=ky(6 + cb))
            proj_fm(kv, cb, ev_vt)
        tasks = []
        for h in range(4):
            nr = slice((h % 2) * 64, (h % 2) * 64 + 64)
            dr = slice(((h + 1) % 2) * 64, ((h + 1) % 2) * 64 + 64)
            voff = 0 if h % 2 == 0 else 64
            V, kV = Va[h % 2], kVa[h % 2]
            E, kE = Eb[h % 2], kEb[h % 2]

            def pre_h(h=h, voff=voff, V=V, kV=kV, E=E, kE=kE):
                kb.dma(E, et_d[l, h], reads=["et"], writes=kE)
                if h < 2:
                    kb.op("pool", lambda g: g.memset(V[:, :, 64 - voff:128 - voff], 1.0), writes=kV)
                for half in range(2):
                    b2 = next_bank(0, 2)
                    ptb = pp[b2][:].bitcast(BF16)
                    for q in range(8):
                        j = half * 8 + q
                        kb.op("pe", lambda pe, q=q, j=j: pe.transpose(
                            ptb[:, q * 128:(q + 1) * 128], yT[:, 6 + h // 2, 128 * j:128 * j + 128], ident[:, :]),
                            reads=ky(6 + h // 2) + ["ident"], writes=[("pp", b2)])
                    kb.op("act", lambda a: a.copy(out=V[:, half * 8:half * 8 + 8, voff:voff + 64],
                                                  in_=ptb.rearrange("p (q e) -> p q e", q=8)[:, :, voff:voff + 64]),
                          reads=[("pp", b2)], writes=kV)
            first_of_head = True
            for qb in range(4):
                ob = next_bank(4, 6)
                kts = []
                for kt in range(16):
                    ra, rb = na_rows(kt)
                    ra, rb = max(ra, 8 * qb), min(rb, 8 * qb + 7)
                    if ra <= rb:
                        kts.append((kt, ra, rb))
                for ki, (kt, ra, rb) in enumerate(kts):
                    t = {"pre": pre_h if first_of_head else None, "post": None}
                    first_of_head = False

                    def s1(t=t, kt=kt, ra=ra, rb=rb, h=h, E=E, kE=kE):
                        n = 64 * (rb - ra + 1)
                        ns = t["slot"]
                        sbk = (2, 3, 6, 7)[ns % 4]
                        p_, kp_ = pt[ns % 4], kpt[ns % 4]
                        kb.op("pe", lambda pe: pe.matmul(pp[sbk][:, 0:n], kTz[h][:, 128 * kt:128 * kt + 128],
                                                         qT[:, h // 2, 64 * ra:64 * (rb + 1)], start=True, stop=True,
                                                         skip_group_check=True),
                              reads=kkTz[h] + kqT, writes=[("pp", sbk)])
                        segs = []
                        if ra <= 3:
                            r1 = min(rb, 3)
                            segs.append((ra, r1, 576 + (3 - kt) * 256 + ra * 64))
                        if max(ra, 4) <= min(rb, 28):
                            r0, r1 = max(ra, 4), min(rb, 28)
                            segs.append((r0, r1, (r0 - 2 * kt + 3) * 64))
                        if rb >= 29:
                            r0 = max(ra, 29)
                            segs.append((r0, rb, 1600 + (15 - kt) * 192 + (r0 - 29) * 64))
                        for si, (r0, r1, eoff) in enumerate(segs):
                            c0, c1 = 64 * (r0 - ra), 64 * (r1 - ra + 1)
                            kb.op("pe", lambda pe, c0=c0, c1=c1, eoff=eoff, si=si: pe.matmul(
                                pp[sbk][:, c0:c1], ident[:, :], E[:, eoff:eoff + c1 - c0], start=False,
                                stop=True, skip_group_check=True),
                                reads=["ident"] + kE, writes=[("pp", sbk)])
                        kb.op("act", lambda a: a.activation(out=p_[:, 0:n], in_=pp[sbk][:, 0:n], func=AF.Exp, scale=0.125),
                              reads=[("pp", sbk)], writes=kp_)

                    def s2(t=t, kt=kt, ra=ra, rb=rb, qb=qb, ob=ob, V=V, kV=kV, first=(ki == 0)):
                        n = 64 * (rb - ra + 1)
                        ns = t["slot"]
                        p_, kp_ = pt[ns % 4], kpt[ns % 4]
                        kb.op("pe", lambda pe: pe.matmul(pp[ob][:, 64 * ra - 512 * qb:64 * (rb + 1) - 512 * qb], V[:, kt, :],
                                                         p_[:, 0:n], start=first, stop=False, skip_group_check=True),
                              reads=kV + kp_, writes=[("pp", ob)])
                    t["s1"], t["s2"] = s1, s2
                    if ki == len(kts) - 1:
                        def post(qb=qb, ob=ob, h=h, nr=nr, dr=dr):
                            ts_ = slice(qb * 512, (qb + 1) * 512)
                            rc, krc = tt[qb % 2], ktt[qb % 2]
                            kb.op("act", lambda a: a.activation(out=rc[nr, :], in_=pp[ob][dr, :], func=AF.Ln),
                                  reads=[("pp", ob)], writes=krc)
                            kb.op("act", lambda a: a.activation(out=rc[nr, :], in_=rc[nr, :], func=AF.Exp, scale=-1.0,
                                                                bias=nlh[nr, :]),
                                  reads=krc + ["nlh"], writes=krc)
                            kb.op("dve", lambda v: v.tensor_tensor(out=yT[nr, 6 + h // 2, ts_], in0=pp[ob][nr, :],
                                                                   in1=rc[nr, :], op=ALU.mult),
                                  reads=[("pp", ob)] + krc, writes=ky(6 + h // 2, h=h % 2))
                        t["post"] = post
                    tasks.append(t)
        run_pipeline(tasks, 3)
        ttg = [av(o_v + j * 2048, [512], F32) for j in range(2)]
        kttg = [ak(o_v + j * 2048, 2048) for j in range(2)]
        gq = [av(o_v + 4096 + j * 2048, [512], F32) for j in range(2)]
        kgq = [ak(o_v + 4096 + j * 2048, 2048) for j in range(2)]
        gate_apply(l, 11, 6, ttg, kttg, gq, kgq)

    GC1 = math.sqrt(2.0 / math.pi)
    GC2 = 0.044715

    def mixer_c(l):
        o_u, o_G, o_V, o_P, o_MC, o_et, o_w, o_sin, o_y = 0, 8192, 16384, 20480, 22528, 26112, 30208, 40448, 44544
        ucm = [av(o_u + t * 4096, [16, 8, 16], BF16) for t in range(2)]
        kucm = [ak(o_u + t * 4096, 4096) for t in range(2)]
        Gcm = av(o_G, [16, 256], BF16)
        kG = ak(o_G, 8192)
        gT = av(o_u, [2, S], BF16)
        kgT = ak(o_u, 8192)
        Vs = [av(o_V + j * 512, [256], BF16) for j in range(8)]
        kVs = [ak(o_V + j * 512, 512) for j in range(8)]
        Pb = [av(o_P + j * 1024, [4, 128], BF16) for j in range(2)]
        kPb = [ak(o_P + j * 1024, 1024) for j in range(2)]
        MC = [av(o_MC + j * 1792, [7, 128], BF16) for j in range(2)]
        kMC = [ak(o_MC + j * 1792, 1792) for j in range(2)]
        et = av(o_et, [4, 2, 128], F32)
        ket = ak(o_et, 4096)
        wk_ = [av(o_w + j * 2048, [4, 128], F32) for j in range(5)]
        kwk = [ak(o_w + j * 2048, 2048) for j in range(5)]
        A_, B_, T1, T2, T3 = wk_
        kA, kB_, kT1, kT2, kT3 = kwk
        sres = [av(o_sin + j * 2048, [4, 128], BF16) for j in range(2)]
        sims = [av(o_sin + j * 2048 + 1024, [4, 128], BF16) for j in range(2)]
        ksres = [ak(o_sin + j * 2048, 1024) for j in range(2)]
        ksims = [ak(o_sin + j * 2048 + 1024, 1024) for j in range(2)]
        ytmp = [[av(o_y + s_ * 4096 + j * 1024, [256], F32) for j in range(4)] for s_ in range(2)]
        kytmp = [[ak(o_y + s_ * 4096 + j * 1024, 1024) for j in range(4)] for s_ in range(2)]
        F_, Bh = slice(0, 64), slice(64, 128)
        tt2 = lambda e, o, a, b, op, rd, wr: kb.op(e, lambda v: v.tensor_tensor(out=o, in0=a, in1=b, op=op), reads=rd, writes=wr)
        kcu = load_wblock(l, 6)
        for j in range(8):
            for mt in range(2):
                bi = next_bank(0, 2)
                for c in range(8):
                    kb.op("pe", lambda pe, c=c: pe.matmul(
                        pp[bi][:, 0:256], hT[:, c, slice(1024 * mt + j, 1024 * mt + j + 8 * 127 + 1, 8)],
                        wb[kcu][:, c, :], start=(c == 0), stop=(c == 7)),
                        reads=[("wb", kcu), "hT"], writes=[("pp", bi)])
                kb.op("act" if (j + mt) % 2 == 0 else "dve",
                      lambda e: (e.copy if (j + mt) % 2 == 0 else e.tensor_copy)(
                          out=ucm[mt][:, :, 7 - j, :], in_=pp[bi][:, 0:256].rearrange("p (g c) -> p g c", g=16)),
                      reads=[("pp", bi)], writes=kucm[mt])
        for j in range(2):
            kb.op("pool", lambda g, j=j: g.memset(sres[j][F_, :, 0:1], 0.0), writes=ksres[j])
            kb.op("pool", lambda g, j=j: g.memset(sres[j][Bh, :, 127:128], 0.0), writes=ksres[j])
            kb.op("pool", lambda g, j=j: g.memset(sims[j][F_, :, 0:1], 0.0), writes=ksims[j])
            kb.op("pool", lambda g, j=j: g.memset(sims[j][Bh, :, 127:128], 0.0), writes=ksims[j])
        banks = {}

        def x_front(gb):
            kb.dma(et, etab_d[l, gb], reads=["etab"], writes=ket)
            s0r, s0i = 2, 3
            banks[gb] = (s0r, s0i)
            for gi in range(4):
                g = 4 * gb + gi
                V, kV = Vs[(gb % 2) * 4 + gi], kVs[(gb % 2) * 4 + gi]
                P_, kP_ = Pb[g % 2], kPb[g % 2]
                kb.dma(P_, sblk_d[l, g, :, 3:7, :], reads=["sblk"], writes=kP_)
                bt = next_bank(6, 8)
                ptb = pp[bt][:].bitcast(BF16)
                for mt in range(2):
                    kb.op("pe", lambda pe, mt=mt: pe.transpose(
                        ptb[:, mt * 128:(mt + 1) * 128], ucm[mt][:, g, :, :].rearrange("p j c -> p (j c)"), ident[:]),
                        reads=kucm[mt] + ["ident"], writes=[("pp", bt)])
                kb.op("act", lambda a: a.copy(out=V, in_=ptb[:, 0:256]), reads=[("pp", bt)], writes=kV)
                for (bank, b0) in ((s0r, 0), (s0i, 2)):
                    for sub in range(2):
                        kb.op("pe", lambda pe, sub=sub, bank=bank, b0=b0: pe.matmul(
                            pp[bank][:, gi * 128:(gi + 1) * 128], P_[:, b0 + sub, :], V[:, sub:256:2],
                            start=(sub == 0), stop=(sub == 1), skip_group_check=True),
                            reads=kP_ + kV, writes=[("pp", bank)])
            for (bank, dst, kd) in ((s0r, A_, kA), (s0i, B_, kB_)):
                src = pp[bank][:, :].rearrange("p (g m) -> p g m", g=4)
                kb.op("act", lambda a, src=src, dst=dst: a.copy(out=dst[F_], in_=src[F_]), reads=[("pp", bank)], writes=kd)
                kb.op("act", lambda a, src=src, dst=dst: a.copy(out=dst[Bh], in_=src[Bh, :, ::-1]), reads=[("pp", bank)], writes=kd)

        def x_back(gb, part):
            cs_, sn_ = et[:, :, 0, :], et[:, :, 1, :]
            sre, sim, ksre, ksim = sres[gb % 2], sims[gb % 2], ksres[gb % 2], ksims[gb % 2]
            if part == 0:
                tt2("dve", T1, A_, cs_, ALU.mult, kA + ket, kT1)
                tt2("pool", T2, B_, sn_, ALU.mult, kB_ + ket, kT2)
                tt2("dve", T1, T1, T2, ALU.add, kT1 + kT2, kT1)
                tt2("pool", T3, B_, cs_, ALU.mult, kB_ + ket, kT3)
                tt2("dve", T2, A_, sn_, ALU.mult, kA + ket, kT2)
                tt2("pool", T3, T3, T2, ALU.subtract, kT3 + kT2, kT3)
            elif part == 1:
                for gi in range(4):
                    g = 4 * gb + gi
                    rb_ = rho_sb[:, l, g:g + 1].broadcast_to([128, 128])
                    kb.op("dve", lambda v, gi=gi, rb_=rb_: v.tensor_tensor_scan(
                        out=A_[:, gi, :], data0=rb_, data1=T1[:, gi, :], initial=0.0, op0=ALU.mult, op1=ALU.add),
                        reads=kT1 + ["rho"], writes=kA)
                    kb.op("dve", lambda v, gi=gi, rb_=rb_: v.tensor_tensor_scan(
                        out=B_[:, gi, :], data0=rb_, data1=T3[:, gi, :], initial=0.0, op0=ALU.mult, op1=ALU.add),
                        reads=kT3 + ["rho"], writes=kB_)
            elif part == 2:
                tt2("dve", T1, A_, cs_, ALU.mult, kA + ket, kT1)
                tt2("pool", T2, B_, sn_, ALU.mult, kB_ + ket, kT2)
                tt2("dve", T1, T1, T2, ALU.subtract, kT1 + kT2, kT1)
                tt2("pool", T3, B_, cs_, ALU.mult, kB_ + ket, kT3)
                tt2("dve", T2, A_, sn_, ALU.mult, kA + ket, kT2)
                tt2("pool", T3, T3, T2, ALU.add, kT3 + kT2, kT3)
            else:
                for (src, ksrc, dst, kd) in ((T1, kT1, sre, ksre), (T3, kT3, sim, ksim)):
                    kb.op("act", lambda a, src=src, dst=dst: a.copy(out=dst[F_, :, 1:128], in_=src[F_, :, 0:127]),
                          reads=ksrc, writes=kd)
                    kb.op("pool", lambda g_, src=src, dst=dst: g_.tensor_copy(out=dst[Bh, :, 0:127], in_=src[Bh, :, 126::-1]),
                          reads=ksrc, writes=kd)

        def y_group(gb, gi):
            g = 4 * gb + gi
            V, kV = Vs[(gb % 2) * 4 + gi], kVs[(gb % 2) * 4 + gi]
            sre, sim, ksre, ksim = sres[gb % 2], sims[gb % 2], ksres[gb % 2], ksims[gb % 2]
            M_, kM_ = MC[g % 2], kMC[g % 2]
            Ysb, sq, u_, th_ = ytmp[g % 2]
            kYsb, ksq, ku_, kth_ = kytmp[g % 2]
            kb.dma(M_[:, 0:3, :], sblk_d[l, g, :, 0:3, :], reads=["sblk"], writes=kM_)
            kb.dma(M_[:, 3:7, :], sblk_d[l, g, :, 7:11, :], reads=["sblk"], writes=kM_)
            yb = next_bank(4, 6)
            V0, V1 = V[:, 0:256:2], V[:, 1:256:2]
            plan = ((0, [(0, V0, kV), (2, V1, kV), (3, sre[:, gi, :], ksre), (4, sim[:, gi, :], ksim)]),
                    (1, [(0, V1, kV), (1, V0, kV), (5, sre[:, gi, :], ksre), (6, sim[:, gi, :], ksim)]))
            for so, terms in plan:
                for ti, (bidx, rhs, krhs) in enumerate(terms):
                    kb.op("pe", lambda pe, so=so, ti=ti, bidx=bidx, rhs=rhs: pe.matmul(
                        pp[yb][:, so * 128:(so + 1) * 128], M_[:, bidx, :], rhs, start=(ti == 0), stop=(ti == 3),
                        skip_group_check=True), reads=kM_ + krhs, writes=[("pp", yb)])
            kb.op("act", lambda a: a.copy(out=Ysb, in_=pp[yb][:, 0:256]), reads=[("pp", yb)], writes=kYsb)
            tb = next_bank(6, 8)
            for so in range(2):
                kb.op("pe", lambda pe, so=so: pe.transpose(pp[tb][:, so * 128:(so + 1) * 128],
                                                           Ysb[:, so * 128:(so + 1) * 128], identf[:]),
                      reads=kYsb + ["identf"], writes=[("pp", tb)])
            yy = pp[tb][:, 0:256]
            kb.op("act", lambda a: a.activation(out=sq, in_=yy, func=AF.Square), reads=[("pp", tb)], writes=ksq)
            kb.op("pool", lambda g_: g_.tensor_scalar(out=sq, in0=sq, scalar1=GC2, scalar2=1.0, op0=ALU.mult, op1=ALU.add),
                  reads=ksq, writes=ksq)
            kb.op("dve", lambda v: v.tensor_tensor(out=u_, in0=sq, in1=yy, op=ALU.mult), reads=ksq + [("pp", tb)], writes=ku_)
            kb.op("act", lambda a: a.activation(out=th_, in_=u_, func=AF.Tanh, scale=GC1), reads=ku_, writes=kth_)
            kb.op("dve", lambda v: v.scalar_tensor_tensor(
                out=Gcm[:, :, g * 16:(g + 1) * 16], in0=th_.rearrange("p (s c) -> p s c", c=16), scalar=1.0,
                in1=yy.rearrange("p (s c) -> p s c", c=16), op0=ALU.add, op1=ALU.mult),
                reads=kth_ + [("pp", tb)], writes=kG)

        x_front(0)
        for part in range(4):
            x_back(0, part)
        for gb in range(4):
            if gb + 1 < 4:
                x_front(gb + 1)
            for gi in range(4):
                y_group(gb, gi)
                if gb + 1 < 4:
                    x_back(gb + 1, gi)
        dump("Gcm", Gcm, kG)
        for chc in range(2):
            for half in range(2):
                bt = next_bank(6, 8)
                ptb = pp[bt][:].bitcast(BF16)
                for q in range(8):
                    si = half * 8 + q
                    kb.op("pe", lambda pe, q=q, si=si: pe.transpose(ptb[:, q * 128:(q + 1) * 128],
                                                                    Gcm[:, si, chc * 128:(chc + 1) * 128], ident[:]),
                          reads=kG + ["ident"], writes=[("pp", bt)])
                kb.op("act" if half == 0 else "dve", lambda e: (e.copy if half == 0 else e.tensor_copy)(
                    out=gT[:, chc, :].rearrange("p (m s) -> p s m", s=16)[:, half * 8:half * 8 + 8, :],
                    in_=ptb.rearrange("p (q m) -> p q m", q=8)), reads=[("pp", bt)], writes=kgT)
        dump("gT", gT, kgT)
        gwl = av(o_et, [2, 256], BF16)
        kgwl = ak(o_et, 1024)
        kb.dma(gwl, gwb_d[l], reads=["gwb"], writes=kgwl)
        ttg = [av(o_w + j * 2048, [512], F32) for j in range(2)]
        t2s = [av(o_w + (2 + j) * 2048, [512], F32) for j in range(2)]
        n_ = 0
        for ec in range(2):
            for tq in range(4):
                ts_ = slice(tq * 512, (tq + 1) * 512)
                bi = next_bank(0, 2)
                for cc in range(2):
                    kb.op("pe", lambda pe, cc=cc: pe.matmul(pp[bi][:, :], gwl[:, cc, ec * 128:(ec + 1) * 128], gT[:, cc, ts_],
                                                            start=(cc == 0), stop=(cc == 1)),
                          reads=kgwl + kgT, writes=[("pp", bi)])
                th2, kth2 = ttg[n_ % 2], kwk[n_ % 2]
                t_, kt_ = t2s[n_ % 2], kwk[2 + n_ % 2]
                n_ += 1
                kb.op("act", lambda a: a.activation(out=th2, in_=pp[bi][:, :], func=AF.Tanh, scale=0.25,
                                                    bias=glb[:, l, ec:ec + 1]),
                      reads=[("pp", bi), "glb"], writes=kth2)
                kb.op("dve", lambda v: v.scalar_tensor_tensor(out=t_, in0=th2, scalar=1.0, in1=gT[:, ec, ts_],
                                                               op0=ALU.add, op1=ALU.mult),
                      reads=kth2 + kgT, writes=kt_)
                kb.op("pool", lambda g_: g_.tensor_scalar(out=yT[:, 4 + ec, ts_], in0=t_, scalar1=0.125, scalar2=1.0,
                                                          op0=ALU.mult, op1=ALU.mult),
                      reads=kt_, writes=ky(4 + ec))
        gtt = [av(o_G + j * 2048, [512], F32) for j in range(2)]
        kgtt = [ak(o_G + j * 2048, 2048) for j in range(2)]
        gq = [av(o_G + 4096 + j * 2048, [512], F32) for j in range(2)]
        kgq = [ak(o_G + 4096 + j * 2048, 2048) for j in range(2)]
        gate_apply(l, 7, 4, gtt, kgtt, gq, kgq)

    kb.same_depth = 2
    stg_keys = [("hTs", 0), ("hTs", 1)] + [("yTs", j) for j in range(6)]
    stg_keys += [("xs", "m"), ("xs", "n"), ("xs", "s", 0), ("xs", "s", 1), ("xs", "e", 0), ("xs", "e", 1)]
    for e_ in ("act", "dve", "pool"):
        kb.op(e_, (lambda a: a.copy(out=small[:, 50:51], in_=small[:, 49:50])) if e_ == "act" else
              (lambda v: v.tensor_copy(out=small[:, 51 + (0 if e_ == "dve" else 1):52 + (0 if e_ == "dve" else 1)], in_=small[:, 49:50])),
              reads=stg_keys + ["nlh"], writes=["hT"] + ky(0, 8) + [("x", i_) for i_ in range(NT)])
    for s in range(nseq):
        for i in range(NT):
            kb.dma(x_res[:, i, :], x_d[s, i * 128:(i + 1) * 128, :], writes=[("x", i)])
        for l in range(depth):
            if l == 0:
                for i in range(NT):
                    rms_sq(i)
            rms_fin()
            for i in range(NT + 2):
                if i < NT:
                    kb.op("act", lambda a, i=i: a.activation(out=hs[i % 4], in_=x_res[:, i, :], func=AF.Copy,
                                                             scale=small[:, i:i + 1]),
                          reads=[("x", i), "rstd"], writes=khs[i % 4])
                    bi = 4 + i % 4
                    pt = pp[bi][:].bitcast(BF16)
                    for c in range(8):
                        kb.op("pe", lambda pe, c=c, i=i, pt=pt: pe.transpose(
                            pt[:, c * 128:(c + 1) * 128], hs[i % 4][:, c * 128:(c + 1) * 128], ident[:]),
                            reads=khs[i % 4] + ["ident"], writes=[("pp", bi)])
                if i >= 2:
                    j = i - 2
                    bj = 4 + j % 4
                    ptj = pp[bj][:].bitcast(BF16)
                    if j % 2 == 0:
                        kb.op("act", lambda a, j=j, ptj=ptj: a.copy(
                            out=hT[:, :, j * 128:(j + 1) * 128], in_=ptj.rearrange("p (c n) -> p c n", c=8)),
                            reads=[("pp", bj)], writes=["hT"])
                    else:
                        kb.op("dve", lambda v, j=j, ptj=ptj: v.tensor_copy(
                            out=hT[:, :, j * 128:(j + 1) * 128], in_=ptj.rearrange("p (c n) -> p c n", c=8)),
                            reads=[("pp", bj)], writes=["hT"])
            if "a" in mixers:
                mixer_a(l)
            if "b" in mixers:
                mixer_b(l)
            if "c" in mixers:
                mixer_c(l)
            if "d" in mixers:
                mixer_d(l)
            for mi, m in enumerate("abcd"):
                if m not in mixers:
                    kb.op("pool", lambda g, mi=mi: g.memset(yT[:, 2 * mi:2 * mi + 2, :], 0.0), writes=ky(2 * mi, 2 * mi + 2))
            if dbg and s == 0 and l == 0:
                kb.dma(dbg_d, yT[:], reads=ky(0, 8), writes=["dbgout"])
            wo, kwo = [], []
            for h in range(4):
                wo.append(av(28672 + h * 4096, [8, 256], BF16))
                kwo.append(ak(28672 + h * 4096, 4096))
                kb.dma(wo[h], wob_d[l, h], reads=["wob"], writes=kwo[h])
            for i in range(NT):
                for h in range(4):
                    bi = next_bank(0, 2)
                    for c in range(8):
                        kb.op("pe", lambda pe, c=c, i=i, bi=bi, h=h: pe.matmul(
                            pp[bi][:, 0:256], yT[:, c, i * 128:(i + 1) * 128], wo[h][:, c, :],
                            start=(c == 0), stop=(c == 7)),
                            reads=kwo[h] + ky(c), writes=[("pp", bi)])
                    kb.op("dve", lambda v, i=i, bi=bi, h=h: v.tensor_tensor(
                        out=x_res[:, i, h * 256:(h + 1) * 256], in0=pp[bi][:, 0:256],
                        in1=x_res[:, i, h * 256:(h + 1) * 256], op=ALU.add),
                        reads=[("pp", bi)], writes=[("x", i)])
                rms_sq(i)
        fg = av(8192, [D], F32)
        kb.dma(fg, fg_d, writes=ak(8192, 4096))
        rms_fin()
        for i in range(NT):
            oo = 28672 + (i % 4) * 4096
            ot = av(oo, [1024], F32)
            kb.op("dve", lambda v, i=i, ot=ot: v.scalar_tensor_tensor(
                out=ot, in0=x_res[:, i, :], scalar=small[:, i:i + 1], in1=fg, op0=ALU.mult, op1=ALU.mult),
                reads=[("x", i), "rstd"] + ak(8192, 4096), writes=ak(oo, 4096))
            kb.dma(y_d[s, i * 128:(i + 1) * 128, :], ot, reads=ak(oo, 4096), writes=["y"])
    kb.finish()
    return nc


def host_prep(inputs):
    f = np.float32
    ng = np.ascontiguousarray(np.asarray(inputs["norm_g"], f).reshape(4, 8, 128).transpose(2, 0, 1))
    fgb = np.ascontiguousarray(np.broadcast_to(np.asarray(inputs["final_g"], f)[None, :], (128, D)))
    pw = np.asarray(inputs["pool_w"], f)
    pwb = np.zeros((128, 4, 2, 128), f)
    for g in range(4):
        cb, h = g // 2, g % 2
        pwb[h * 64:(h + 1) * 64, :, cb, h * 64:(h + 1) * 64] = pw[:, g].transpose(1, 0, 2)
    psc = np.ascontiguousarray(np.asarray(inputs["pool_scale"], f).reshape(4, 2, 128).transpose(2, 0, 1))
    pcn = np.zeros((128, 2, 17), f)
    for g, w in enumerate((2, 4, 8, 16)):
        cb, h = g // 2, g % 2
        t = np.arange(S)
        cnt = np.minimum(t + w // 2, S) - np.maximum(t - w // 2, 0)
        pcn[h * 64:(h + 1) * 64, cb, 0:8] = (w / cnt[0:8])[None, :]
        pcn[h * 64:(h + 1) * 64, cb, 8:16] = (w / cnt[S - 8:S])[None, :]
        pcn[h * 64:(h + 1) * 64, cb, 16] = 1.0 / w
    inv = 10000.0 ** (-np.arange(0, 64, 2, dtype=np.float32) / 64)
    ang = np.arange(S, dtype=np.float32)[:, None] * inv[None, :]
    rope = np.stack([np.cos(ang), np.sin(ang)], 0).astype(f)
    rope_t = np.ascontiguousarray(rope.reshape(2, NT, 128, 32).transpose(2, 0, 1, 3))
    kk = np.arange(128)[:, None]
    cc = np.arange(256)[None, :]
    band = np.where(((cc - kk) >= 0) & ((cc - kk) <= 128), 0.0, -240000.0).astype(ml_dtypes.bfloat16)
    rpb = np.asarray(inputs["na_rpb"], f)
    rpbpad = np.zeros((4, 4, 15, 128), f)
    rpbpad[:, :, :, 48:79] = rpb[:, :, ::-1, :]
    kc = np.arange(64)
    c = 63 - np.arange(64)
    cs = np.clip(c - 8, 0, 48)
    colok = ((kc[:, None] >= cs[None, :]) & (kc[:, None] < cs[None, :] + 16)).astype(f)
    nam = np.zeros((128, 2368), f)
    for krl in range(2):
        for ri in range(9):
            dlt = krl - ri + 3
            if -4 <= dlt <= 3:
                nam[krl * 64:(krl + 1) * 64, ri * 64:(ri + 1) * 64] = colok
        for blk in range(28):
            nam[krl * 64:(krl + 1) * 64, 576 + blk * 64:576 + (blk + 1) * 64] = colok
    are = np.asarray(inputs["ssm_a_re"], f)
    aim = np.asarray(inputs["ssm_a_im"], f)
    ldt = np.asarray(inputs["ssm_log_dt"], f)
    lam = np.stack([are.transpose(0, 1, 3, 2), aim.transpose(0, 1, 3, 2),
                    np.broadcast_to(ldt[:, :, None, :], (4, 2, 64, 16))], axis=-1)
    lam = np.ascontiguousarray(lam.reshape(4, 128, 16, 3))
    bre = np.asarray(inputs["ssm_b_re"], f)
    bim = np.asarray(inputs["ssm_b_im"], f)
    bp1 = np.stack([bre.transpose(0, 2, 1, 3), bim.transpose(0, 2, 1, 3)], axis=-1)
    bp = np.ascontiguousarray(np.concatenate([bp1, bp1], axis=1))
    cre = np.asarray(inputs["ssm_c_re"], f)
    cim = np.asarray(inputs["ssm_c_im"], f)
    cp1 = np.stack([cre.transpose(0, 1, 4, 2, 3), cim.transpose(0, 1, 4, 2, 3)], axis=-1)
    cp = np.ascontiguousarray(cp1.reshape(4, 128, 16, 16, 2))
    sd = np.asarray(inputs["ssm_d"], f).reshape(4, 16, 16)
    sdt = np.ascontiguousarray(sd.transpose(2, 0, 1))
    glw = np.ascontiguousarray(np.asarray(inputs["glu_w"], f).reshape(4, 2, 128, 256).transpose(0, 2, 1, 3))
    glbt = np.ascontiguousarray(np.asarray(inputs["glu_b"], f).reshape(4, 2, 128).transpose(2, 0, 1))
    return {
        "ssm_lam": lam, "ssm_bp": bp, "ssm_cp": cp, "ssm_dt": sdt, "glu_w_t": glw, "glu_b_t": glbt,
        "rpbpad": rpbpad, "na_mask": nam.astype(ml_dtypes.bfloat16),
        "rope_t": rope_t, "band": band,
        "pool_w_blk": pwb, "pool_scale_t": psc, "pool_const": pcn,
        "w_in": np.ascontiguousarray(inputs["w_in"], dtype=f),
        "w_out": np.ascontiguousarray(inputs["w_out"], dtype=f),
        "norm_g_t": ng,
        "final_g_b": fgb,
    }


def kernel(**inputs):
    xp = np.asarray(inputs["x_prompt"], np.float32)
    xs = np.asarray(inputs["x_sample"], np.float32)
    shared = host_prep(inputs)
    nc = build()
    in_maps = []
    for c in range(8):
        xc = np.concatenate([xp[4 * c:4 * c + 4], xs[c:c + 1]], axis=0)
        m = dict(shared)
        m["x"] = np.ascontiguousarray(xc)
        in_maps.append(m)
    res = run_bass_kernel_spmd(nc, in_maps, core_ids=list(range(8)))
    yp = np.empty_like(xp)
    ys = np.empty_like(xs)
    for c in range(8):
        y = res.results[c]["y"]
        yp[4 * c:4 * c + 4] = y[0:4]
        ys[c] = y[4]
    return (yp, ys)
```

```python
import math
from contextlib import ExitStack
import numpy as np
import ml_dtypes
import concourse.bass as bass
import concourse.mybir as mybir
from concourse.bass_utils import run_bass_kernel_spmd

F32 = mybir.dt.float32
BF16 = mybir.dt.bfloat16
I32 = mybir.dt.int32
AF = mybir.ActivationFunctionType
ALU = mybir.AluOpType
AX = mybir.AxisListType

S = 2048
D = 1024
NT = 16
EPS = 1e-6
NDMA = 24


class KB:
    def __init__(self):
        self.nc = bass.Bass("TRN2", target_bir_lowering=False)
        nc = self.nc
        self.es = ExitStack()
        self.eng = {"pe": nc.tensor, "act": nc.scalar, "dve": nc.vector, "pool": nc.gpsimd, "sp": nc.sync}
        self.sem = {}
        for e in ["pe", "act", "dve", "pool"]:
            self.sem[e] = self.es.enter_context(nc.semaphore("s_" + e))
        for i in range(NDMA):
            self.sem[("dma", i)] = self.es.enter_context(nc.semaphore("s_dma%d" % i))
        self.cnt = {e: 0 for e in ["pe", "act", "dve", "pool"]}
        self.seen = {e: {} for e in ["pe", "act", "dve", "pool", "sp"]}
        self.res = {}
        self.ndma = 0
        self.same_eng = {"act", "dve", "pool"}
        self.same_depth = 1000000
        self.clock = {}

    def sb(self, name, shape, dt):
        return self.es.enter_context(self.nc.sbuf_tensor(name, shape, dt))

    def ps(self, name, shape, dt):
        return self.es.enter_context(self.nc.psum_tensor(name, shape, dt))

    def dram(self, name, shape, dt, kind="Internal"):
        return self.nc.dram_tensor(name, shape, dt, kind=kind).ap()

    def _wait(self, e, key, val):
        if key == e:
            if e not in self.same_eng or val < self.cnt[e] - self.same_depth + 1:
                return
        if self.seen[e].get(key, 0) >= val:
            return
        self.eng[e].wait_ge(self.sem[key], val)
        self.seen[e][key] = val
        clk = self.clock.get((key, val))
        if clk:
            se = self.seen[e]
            for k2, v2 in clk.items():
                if se.get(k2, 0) < v2:
                    se[k2] = v2

    def _deps(self, e, reads, writes):
        for r in reads:
            st = self.res.get(r)
            if st:
                for k, v in st["w"].items():
                    self._wait(e, k, v)
        for w in writes:
            st = self.res.get(w)
            if st:
                for k, v in st["w"].items():
                    self._wait(e, k, v)
                for k, v in st["r"].items():
                    self._wait(e, k, v)

    def _mark(self, key, val, reads, writes):
        for r in reads:
            st = self.res.setdefault(r, {"w": {}, "r": {}})
            st["r"][key] = max(st["r"].get(key, 0), val)
        for w in writes:
            st = self.res.setdefault(w, {"w": {}, "r": {}})
            st["w"][key] = max(st["w"].get(key, 0), val)

    def op(self, e, fn, reads=(), writes=()):
        self._deps(e, reads, writes)
        inst = fn(self.eng[e])
        self.cnt[e] += 1
        inst.then_inc(self.sem[e], 1)
        snap = dict(self.seen[e])
        snap[e] = self.cnt[e]
        self.clock[(e, self.cnt[e])] = snap
        self._mark(e, self.cnt[e], reads, writes)

    def dma(self, out, in_, reads=(), writes=(), q="sp"):
        n = self.ndma
        self.ndma += 1
        i = n % NDMA
        key = ("dma", i)
        if n >= NDMA:
            self._wait(q, key, 16 * (n // NDMA))
        self._deps(q, reads, writes)
        self.eng[q].dma_start(out=out, in_=in_).then_inc(self.sem[key], 16)
        self.clock[(key, 16 * (n // NDMA + 1))] = dict(self.seen[q])
        self._mark(key, 16 * (n // NDMA + 1), reads, writes)

    def barrier(self, engines=("pe", "act", "dve", "pool")):
        for e in engines:
            for e2 in engines:
                if e2 != e and self.cnt[e2] > 0:
                    self._wait(e, e2, self.cnt[e2])

    def finish(self):
        for k in list(self.sem.keys()):
            if isinstance(k, tuple):
                i = k[1]
                uses = (self.ndma - 1 - i) // NDMA + 1 if self.ndma > i else 0
                if uses > 0:
                    self._wait("sp", k, 16 * uses)
            else:
                if self.cnt[k] > 0:
                    self._wait("sp", k, self.cnt[k])
        self.es.close()


def build(nseq=5, depth=4, mixers=("a", "b", "c", "d"), dbg=False):
    kb = KB()
    nc = kb.nc
    x_d = nc.dram_tensor("x", [nseq, S, D], F32, kind="ExternalInput").ap()
    y_d = nc.dram_tensor("y", [nseq, S, D], F32, kind="ExternalOutput").ap()
    w_in_d = nc.dram_tensor("w_in", [4, D, 3072], F32, kind="ExternalInput").ap()
    w_out_d = nc.dram_tensor("w_out", [4, D, D], F32, kind="ExternalInput").ap()
    ng_d = nc.dram_tensor("norm_g_t", [128, 4, 8], F32, kind="ExternalInput").ap()
    fg_d = nc.dram_tensor("final_g_b", [128, D], F32, kind="ExternalInput").ap()
    pwb_d = nc.dram_tensor("pool_w_blk", [128, 4, 2, 128], F32, kind="ExternalInput").ap()
    psc_d = nc.dram_tensor("pool_scale_t", [128, 4, 2], F32, kind="ExternalInput").ap()
    pcn_d = nc.dram_tensor("pool_const", [128, 2, 17], F32, kind="ExternalInput").ap()
    dbg_d = nc.dram_tensor("dbg", [128, 8, S], BF16, kind="ExternalOutput").ap() if dbg else None
    rope_d = nc.dram_tensor("rope_t", [128, 2, NT, 32], F32, kind="ExternalInput").ap()
    band_d = nc.dram_tensor("band", [128, 256], BF16, kind="ExternalInput").ap()
    rpb_t = nc.dram_tensor("rpbpad", [4, 4, 15, 128], F32, kind="ExternalInput")
    nam_d = nc.dram_tensor("na_mask", [128, 2368], BF16, kind="ExternalInput").ap()
    et_d = kb.dram("et", [4, 4, 128, 2368], BF16)
    lam_d = nc.dram_tensor("ssm_lam", [4, 128, 16, 3], F32, kind="ExternalInput").ap()
    bp_d = nc.dram_tensor("ssm_bp", [4, 128, 16, 16, 2], F32, kind="ExternalInput").ap()
    cp_d = nc.dram_tensor("ssm_cp", [4, 128, 16, 16, 2], F32, kind="ExternalInput").ap()
    dt_d = nc.dram_tensor("ssm_dt", [16, 4, 16], F32, kind="ExternalInput").ap()
    glw_d = nc.dram_tensor("glu_w_t", [4, 128, 2, 256], F32, kind="ExternalInput").ap()
    glb_d = nc.dram_tensor("glu_b_t", [128, 4, 2], F32, kind="ExternalInput").ap()
    sblk_d = kb.dram("sblk", [4, 16, 128, 11, 128], BF16)
    etab_d = kb.dram("etab", [4, 4, 128, 4, 2, 128], F32)
    ktab_t = nc.dram_tensor("ktab", [4, 16, 31, 16, 16], F32, kind="Internal")
    ktab_d = ktab_t.ap()
    gwb_d = kb.dram("gwb", [4, 128, 2, 256], BF16)
    wib_d = kb.dram("wib", [4, 12, 128, 8, 256], BF16)
    wob_d = kb.dram("wob", [4, 4, 128, 8, 256], BF16)

    x_res = kb.sb("x_res", [128, NT, D], F32)
    hT = kb.sb("hT", [128, 8, S], BF16)
    yT = kb.sb("yT", [128, 8, S], BF16)
    ARENA = 56 * 1024
    arena = kb.sb("arena", [128, ARENA // 2], BF16)
    wb = [kb.sb("wb%d" % i, [128, 8, 256], BF16) for i in range(3)]
    ng = kb.sb("ng", [128, 4, 8], F32)
    ident = kb.sb("ident", [128, 128], BF16)
    identf = kb.sb("identf", [128, 128], F32)
    small = kb.sb("small", [128, 64], F32)
    pwb = kb.sb("pwb", [128, 4, 2, 128], BF16)
    psc = kb.sb("psc", [128, 4, 2], F32)
    pcn = kb.sb("pcn", [128, 2, 17], F32)
    ropet = kb.sb("ropet", [128, 2, NT, 32], F32)
    band = kb.sb("band_sb", [128, 256], BF16)
    rho_sb = kb.sb("rho_sb", [128, 4, 16], F32)
    glb = kb.sb("glb", [128, 4, 2], F32)
    pp = [kb.ps("pp%d" % i, [128, 512], F32) for i in range(8)]

    GR = 256

    def ak(off, nbytes):
        return [("ar", j) for j in range(off // GR, (off + nbytes - 1) // GR + 1)]

    dumped = set()

    def dump(name, src, reads):
        if not dbg or name in dumped:
            return
        dumped.add(name)
        shp = list(src.shape)
        dd = nc.dram_tensor("d_" + name, shp, src.dtype, kind="ExternalOutput").ap()
        kb.dma(dd, src, reads=reads, writes=["dump_" + name])

    def ky(c0, c1=None, h=None):
        c1 = c0 + 1 if c1 is None else c1
        hs_ = (0, 1) if h is None else (h,)
        return [("yT", c, hh) for c in range(c0, c1) for hh in hs_]

    def av(off, shape, dt):
        n = int(np.prod(shape))
        esz = 4 if dt in (F32, I32) else 2
        a = arena[:, off // 2: off // 2 + n * esz // 2]
        if dt != BF16:
            a = a.bitcast(dt)
        if len(shape) == 2:
            return a.rearrange("p (a b) -> p a b", a=shape[0])
        if len(shape) == 3:
            return a.rearrange("p (a b c) -> p a b c", a=shape[0], b=shape[1])
        return a

    pstate = {"n": 0}

    def next_bank(lo=0, hi=2):
        i = lo + pstate.setdefault((lo, hi), 0) % (hi - lo)
        pstate[(lo, hi)] += 1
        return i

    hs = [av(16384 + i * 2048, [D], BF16) for i in range(6)]
    khs = [ak(16384 + i * 2048, 2048) for i in range(6)]
    kb.dma(ng[:], ng_d, writes=["ng"])
    pwst = av(8192, [4 * 2 * 128], F32)
    kb.dma(pwst, pwb_d.rearrange("p a b c -> p (a b c)"), writes=ak(8192, 4096))
    kb.op("dve", lambda v: v.tensor_copy(out=pwb[:].rearrange("p a b c -> p (a b c)"), in_=pwst), reads=ak(8192, 4096), writes=["pwb"])
    kb.dma(psc[:], psc_d, writes=["psc"])
    kb.op("dve", lambda v: v.tensor_scalar(out=psc[:], in0=psc[:], scalar1=0.5, scalar2=None, op0=ALU.mult), reads=["psc"], writes=["psc"])
    kb.dma(pcn[:], pcn_d, writes=["pcn"])
    kb.dma(ropet[:], rope_d, writes=["ropet"])
    kb.dma(band[:], band_d, writes=["band"])
    io = av(0, [128], I32)
    kb.op("pool", lambda g: g.iota(io, [[1, 128]], base=0, channel_multiplier=-1), writes=ak(0, 512))
    kb.op("dve", lambda v: v.tensor_single_scalar(out=identf[:], in_=io, scalar=0, op=ALU.is_equal),
          reads=ak(0, 512), writes=["identf"])
    kb.op("dve", lambda v: v.tensor_copy(out=ident[:], in_=identf[:]), reads=["identf"], writes=["ident"])

    hTf = hT[:].rearrange("p a b -> p (a b)")
    yTf = yT[:].rearrange("p a b -> p (a b)")

    def tv(flat, off, n, dt):
        esz = 4 if dt == F32 else 2
        v = flat[:, off // 2: off // 2 + n * esz // 2]
        return v.bitcast(dt) if dt != BF16 else v

    wtasks = {}
    for l in range(depth):
        lst = []
        for c in range(8):
            def w_in_task(l=l, c=c):
                k = c % 2
                st = tv(hTf, k * 12288, 3072, F32)
                sb_ = tv(yTf, k * 6144, 3072, BF16)
                kst, ksb = [("hTs", k)], [("yTs", k)]
                kb.dma(st, w_in_d[l, c * 128:(c + 1) * 128, :], writes=kst, q="act")
                kb.op("pool" if k else "dve",
                      lambda v: v.tensor_scalar(out=sb_, in0=st, scalar1=ng[:, l, c:c + 1], scalar2=1.0,
                                                op0=ALU.mult, op1=ALU.mult),
                      reads=kst + ["ng"], writes=ksb)
                kb.dma(wib_d[l, :, :, c, :].rearrange("b p n -> p b n"), sb_.rearrange("p (b n) -> p b n", b=12),
                       reads=ksb, writes=["wib"], q="act")

            def w_out_task(l=l, c=c):
                k = c % 2
                st = tv(yTf, 12288 + k * 4096, 1024, F32)
                sb_ = tv(yTf, 20480 + k * 2048, 1024, BF16)
                kst, ksb = [("yTs", 2 + k)], [("yTs", 4 + k)]
                kb.dma(st, w_out_d[l, c * 128:(c + 1) * 128, :], writes=kst, q="act")
                kb.op("dve" if k else "pool", lambda v: v.tensor_copy(out=sb_, in_=st), reads=kst, writes=ksb)
                kb.dma(wob_d[l, :, :, c, :].rearrange("h p n -> p h n"), sb_.rearrange("p (h n) -> p h n", h=4),
                       reads=ksb, writes=["wob"], q="act")
            lst.append(w_in_task)
            lst.append(w_out_task)
        wtasks[l] = lst
    if "c" not in mixers:
        for l in range(depth):
            for t_ in wtasks[l]:
                t_()

    xf = x_res[:].rearrange("p a b -> p (a b)").bitcast(BF16)
    na_tasks = {}
    if "d" in mixers:
        nmask = tv(xf, 0, 2368, BF16)
        kb.dma(nmask, nam_d, writes=[("xs", "m")])
        negm = tv(xf, 33152, 2368, F32)
        knegm = [("xs", "n")]
        kb.op("dve", lambda v: v.tensor_scalar(out=negm, in0=nmask, scalar1=-1.0, scalar2=240000.0, op0=ALU.add, op1=ALU.mult),
              reads=[("xs", "m")], writes=knegm)
        for l in range(depth):
            for h in range(4):
                def na_task(l=l, h=h):
                    k = (l * 4 + h) % 2
                    o_st = 4736 + k * 14208
                    stg = tv(xf, o_st, 2368, F32)
                    eo = tv(xf, o_st + 9472, 2368, BF16)
                    kst, keo = [("xs", "s", k)], [("xs", "e", k)]
                    base = (l * 4 + h) * 15 * 128
                    for krl in range(2):
                        ps_ = slice(krl * 64, krl * 64 + 64)
                        src = bass.AP(rpb_t, base + (4 - krl) * 128, [[1, 64], [128, 9], [1, 64]])
                        kb.dma(stg[ps_, 0:576].rearrange("p (r c) -> p r c", r=9), src, writes=kst)
                        for sr in range(7):
                            r = sr if sr < 4 else 25 + sr
                            if sr < 4:
                                i0 = 1 - krl + r
                                dst = stg[ps_, 576:1600].rearrange("p (u s c) -> p u s c", u=4, s=4)[:, :, sr, :]
                            else:
                                i0 = r - 23 - krl
                                dst = stg[ps_, 1600:2368].rearrange("p (u s c) -> p u s c", u=4, s=3)[:, :, sr - 4, :]
                            src = bass.AP(rpb_t, base + i0 * 128, [[1, 64], [256, 4], [1, 64]])
                            kb.dma(dst, src, writes=kst)
                    st3 = stg.rearrange("p (b c) -> p b c", c=64)
                    kb.op("dve", lambda v: v.scalar_tensor_tensor(
                        out=st3, in0=st3, scalar=8.0, in1=nmask.rearrange("p (b c) -> p b c", c=64),
                        op0=ALU.mult, op1=ALU.mult), reads=kst + [("xs", "m")], writes=kst)
                    kb.op("dve", lambda v: v.tensor_tensor(
                        out=eo.rearrange("p (b c) -> p b c", c=64), in0=st3[:, :, ::-1],
                        in1=negm.rearrange("p (b c) -> p b c", c=64)[:, :, ::-1],
                        op=ALU.add), reads=kst + knegm, writes=keo)
                    kb.dma(et_d[l, h], eo, reads=keo, writes=["et"])
                na_tasks[(l, h)] = na_task
        if "c" not in mixers:
            for l in range(depth):
                for h in range(4):
                    na_tasks[(l, h)]()

    TWO_PI = 2.0 * math.pi

    def s5_precompute(l):
        kb.same_depth = 1000000
        st = {"o": 0}

        def A(shape, dt=F32):
            n = int(np.prod(shape))
            nb = n * (4 if dt in (F32, I32) else 2)
            off = st["o"]
            st["o"] = off + (nb + 63) // 64 * 64
            v = arena[:, off // 2: off // 2 + nb // 2]
            if dt != BF16:
                v = v.bitcast(dt)
            if len(shape) == 2:
                v = v.rearrange("p (a b) -> p a b", a=shape[0])
            elif len(shape) == 3:
                v = v.rearrange("p (a b c) -> p a b c", a=shape[0], b=shape[1])
            return v, ak(off, nb)

        def tt_(e, out, a, b, op, rd, wr):
            kb.op(e, lambda v: v.tensor_tensor(out=out, in0=a, in1=b, op=op), reads=rd, writes=wr)

        def ts_(e, out, a, s1, s2, op0, op1, rd, wr):
            kb.op(e, lambda v: v.tensor_scalar(out=out, in0=a, scalar1=s1, scalar2=s2, op0=op0, op1=op1) if s2 is not None
                  else v.tensor_scalar(out=out, in0=a, scalar1=s1, scalar2=None, op0=op0), reads=rd, writes=wr)

        def frac_(T, kT, TI, kTI, TF, kTF):
            MAGIC = 12582912.0
            ts_("dve", TF, T, MAGIC, None, ALU.add, None, kT, kTF)
            ts_("dve", TF, TF, -MAGIC, None, ALU.add, None, kTF, kTF)
            tt_("dve", T, T, TF, ALU.subtract, kT + kTF, kT)
            kb.op("dve", lambda v: v.tensor_single_scalar(out=TF, in_=T, scalar=0.5, op=ALU.is_gt), reads=kT, writes=kTF)
            tt_("dve", T, T, TF, ALU.subtract, kT + kTF, kT)
            kb.op("dve", lambda v: v.tensor_single_scalar(out=TF, in_=T, scalar=-0.5, op=ALU.is_lt), reads=kT, writes=kTF)
            tt_("dve", T, T, TF, ALU.add, kT + kTF, kT)

        def sincos_(T, kT, SN, kSN, CS, kCS, TI, kTI, TF, kTF):
            frac_(T, kT, TI, kTI, TF, kTF)
            kb.op("act", lambda a: a.activation(out=SN, in_=T, func=AF.Sin, scale=6.28318), reads=kT, writes=kSN)
            ts_("dve", T, T, 0.25, None, ALU.add, None, kT, kT)
            kb.op("dve", lambda v: v.tensor_single_scalar(out=TF, in_=T, scalar=0.5, op=ALU.is_gt), reads=kT, writes=kTF)
            tt_("dve", T, T, TF, ALU.subtract, kT + kTF, kT)
            kb.op("act", lambda a: a.activation(out=CS, in_=T, func=AF.Sin, scale=6.28318), reads=kT, writes=kCS)

        lam, klam = A([16, 3])
        Bp, kBp = A([16, 16, 2])
        Cp, kCp = A([16, 16, 2])
        Dt, kDt = A([16])
        kb.dma(lam, lam_d[l], writes=klam)
        kb.dma(Bp, bp_d[l], writes=kBp)
        kb.dma(Cp, cp_d[l], writes=kCp)
        kb.dma(Dt[0:16, :], dt_d[:, l, :], writes=kDt)
        BBr, kBBr = A([16, 16])
        BBi, kBBi = A([16, 16])
        nBBi, knBBi = A([16, 16])
        WP, kWP = A([2, 16, 16])
        WC, kWC = A([2, 16, 16])
        WK, kWK = A([2, 16, 31])
        Dd, kDd = A([16, 16])
        E15, kE15 = A([496])
        kb.op("dve", lambda v: v.tensor_tensor(out=Dd[0:16], in0=identf[0:16, 0:16].unsqueeze(1).broadcast_to([16, 16, 16]),
                                               in1=Dt[0:16, :].unsqueeze(2).broadcast_to([16, 16, 16]), op=ALU.mult),
              reads=kDt + ["identf"], writes=kDd)
        kb.op("pool", lambda g_: g_.memset(E15[0:16, :], 0.0), writes=kE15)
        kb.op("pool", lambda g_: g_.tensor_copy(out=E15[0:16, 240:256], in_=identf[0:16, 0:16]), reads=["identf"], writes=kE15)
        mark = st["o"]
        dtt, kdt = A([16])
        xr, kxr = A([16])
        tht, ktht = A([16])
        NNi, kNNi = A([128], I32)
        NN, kNN = A([128])
        TI, kTI = A([512], I32)
        TF, kTF = A([512])
        ARG, kARG = A([16, 17])
        MAG, kMAG = A([16, 17])
        SN, kSN = A([16, 17])
        CS, kCS = A([16, 17])
        WR, kWR = A([16, 17])
        WI, kWI = A([16, 17])
        kb.op("act", lambda a: a.activation(out=dtt, in_=lam[:, :, 2], func=AF.Exp), reads=klam, writes=kdt)
        tt_("dve", xr, lam[:, :, 0], dtt, ALU.mult, klam + kdt, kxr)
        tt_("dve", tht, lam[:, :, 1], dtt, ALU.mult, klam + kdt, ktht)
        ts_("dve", tht, tht, 1.0 / TWO_PI, None, ALU.mult, None, ktht, ktht)
        frac_(tht, ktht, TI[:, 0:16], kTI, TF[:, 0:16], kTF)
        kb.op("pool", lambda g: g.iota(NNi, [[1, 128]], base=0, channel_multiplier=0), writes=kNNi)
        kb.op("dve", lambda v: v.tensor_copy(out=NN, in_=NNi), reads=kNNi, writes=kNN)
        nb17 = NN[:, 0:17].unsqueeze(1).broadcast_to([128, 16, 17])
        tt_("dve", ARG, tht.unsqueeze(2).broadcast_to([128, 16, 17]), nb17, ALU.mult, ktht + kNN, kARG)
        tt_("dve", MAG, xr.unsqueeze(2).broadcast_to([128, 16, 17]), nb17, ALU.mult, kxr + kNN, kMAG)
        kb.op("act", lambda a: a.activation(out=MAG, in_=MAG, func=AF.Exp), reads=kMAG, writes=kMAG)
        f2 = lambda t: t.rearrange("p a b -> p (a b)")
        sincos_(f2(ARG), kARG, f2(SN), kSN, f2(CS), kCS, TI[:, 0:272], kTI, TF[:, 0:272], kTF)
        tt_("dve", WR, MAG, CS, ALU.mult, kMAG + kCS, kWR)
        tt_("dve", WI, MAG, SN, ALU.mult, kMAG + kSN, kWI)
        if l == 0:
            dump("WR", WR, kWR)
            dump("WI", WI, kWI)
            dump("dtt", dtt, kdt)
            dump("xr", xr, kxr)
            dump("tht", tht, ktht)
            dump("NN", NN, kNN)
            dump("MAG", MAG, kMAG)
            dump("SN", SN, kSN)
            dump("CS", CS, kCS)
            dump("lam", lam, klam)
        kb.op("act", lambda a: a.copy(out=rho_sb[:, l, :], in_=MAG[:, :, 16]), reads=kMAG, writes=["rho"])
        den, kden = A([16])
        t1, kt1 = A([16])
        t2, kt2 = A([16])
        gr, kgr = A([16])
        gi, kgi = A([16])
        lr_, li_ = lam[:, :, 0], lam[:, :, 1]
        tt_("dve", den, lr_, lr_, ALU.mult, klam, kden)
        tt_("dve", t1, li_, li_, ALU.mult, klam, kt1)
        tt_("dve", den, den, t1, ALU.add, kden + kt1, kden)
        kb.op("dve", lambda v: v.reciprocal(out=den, in_=den), reads=kden, writes=kden)
        ts_("dve", t1, WR[:, :, 1], -1.0, None, ALU.add, None, kWR, kt1)
        tt_("dve", gr, t1, lr_, ALU.mult, kt1 + klam, kgr)
        tt_("dve", t2, WI[:, :, 1], li_, ALU.mult, kWI + klam, kt2)
        tt_("dve", gr, gr, t2, ALU.add, kgr + kt2, kgr)
        tt_("dve", gr, gr, den, ALU.mult, kgr + kden, kgr)
        tt_("dve", gi, WI[:, :, 1], lr_, ALU.mult, kWI + klam, kgi)
        tt_("dve", t2, t1, li_, ALU.mult, kt1 + klam, kt2)
        tt_("dve", gi, gi, t2, ALU.subtract, kgi + kt2, kgi)
        tt_("dve", gi, gi, den, ALU.mult, kgi + kden, kgi)
        u1, ku1 = A([16, 16])
        grb = gr.unsqueeze(2).broadcast_to([128, 16, 16])
        gib = gi.unsqueeze(2).broadcast_to([128, 16, 16])
        Br_, Bi_ = Bp[:, :, :, 0], Bp[:, :, :, 1]
        tt_("dve", BBr, grb, Br_, ALU.mult, kgr + kBp, kBBr)
        tt_("dve", u1, gib, Bi_, ALU.mult, kgi + kBp, ku1)
        tt_("dve", BBr, BBr, u1, ALU.subtract, kBBr + ku1, kBBr)
        tt_("dve", BBi, grb, Bi_, ALU.mult, kgr + kBp, kBBi)
        tt_("dve", u1, gib, Br_, ALU.mult, kgi + kBp, ku1)
        tt_("dve", BBi, BBi, u1, ALU.add, kBBi + ku1, kBBi)
        ts_("dve", nBBi, BBi, -1.0, None, ALU.mult, None, kBBi, knBBi)
        kb.op("pool", lambda g: g.memset(WK, 0.0), writes=kWK)
        F_, B_ = slice(0, 64), slice(64, 128)
        for ri, W_, kW_ in ((0, WR, kWR), (1, WI, kWI)):
            cp = lambda dst, src, wk: kb.op("act", lambda a: a.copy(out=dst, in_=src), reads=kW_, writes=wk)
            cp(WP[F_, ri, :, 0:8], W_[F_, :, 8:16], kWP)
            cp(WP[F_, ri, :, 8:16], W_[F_, :, 0:8], kWP)
            cp(WP[B_, ri, :, 0:8], W_[B_, :, 7::-1], kWP)
            cp(WP[B_, ri, :, 8:16], W_[B_, :, 15:7:-1], kWP)
            cp(WC[F_, ri, :, :], W_[F_, :, 1:17], kWC)
            cp(WC[B_, ri, :, :], W_[B_, :, 16:0:-1], kWC)
            cp(WK[F_, ri, :, 15:31], W_[F_, :, 0:16], kWK)
            cp(WK[B_, ri, :, 0:16], W_[B_, :, 15::-1], kWK)
        pht, kpht = A([16])
        ts_("dve", pht, tht, 16.0, None, ALU.mult, None, ktht, kpht)
        frac_(pht, kpht, TI[:, 0:16], kTI, TF[:, 0:16], kTF)
        EA, kEA = A([4, 128])
        ES, kES = A([4, 2, 128])
        for q in range(4):
            tt_("dve", EA, pht[:, 4 * q:4 * q + 4].unsqueeze(2).broadcast_to([128, 4, 128]),
                NN.unsqueeze(1).broadcast_to([128, 4, 128]), ALU.mult, kpht + kNN, kEA)
            sincos_(f2(EA), kEA, ES[:, :, 1, :], kES, ES[:, :, 0, :], kES, TI[:, 0:512], kTI, TF[:, 0:512], kTF)
            kb.dma(etab_d[l, q], ES, reads=kES, writes=["etab"])
            if l == 0 and q == 0:
                dump("ES0", ES, kES)
        gst, kgst = A([2, 256])
        gbf, kgbf = A([2, 256], BF16)
        kb.dma(gst, glw_d[l], writes=kgst)
        kb.op("act", lambda a: a.copy(out=gbf, in_=gst), reads=kgst, writes=kgbf)
        kb.dma(gwb_d[l], gbf, reads=kgbf, writes=["gwb"])
        st["o"] = mark
        bufs = []
        for par in range(2):
            d_ = {}
            d_["PPr"], d_["kPPr"] = A([2, 128])
            d_["PPi"], d_["kPPi"] = A([2, 128])
            d_["q1"], d_["kq1"] = A([2, 128])
            d_["q2"], d_["kq2"] = A([2, 128])
            d_["Rr"], d_["kRr"] = A([31, 16])
            d_["Ri"], d_["kRi"] = A([31, 16])
            d_["r1"], d_["kr1"] = A([31, 16])
            d_["r2"], d_["kr2"] = A([31, 16])
            d_["Ks"], d_["kKs"] = A([496])
            d_["Ms"], d_["kMs"] = A([3, 128])
            d_["blk"], d_["kblk"] = A([11, 128], BF16)
            bufs.append(d_)
        assert st["o"] <= ARENA, st["o"]
        kb.same_depth = 3
        pend_tail = [None]
        for g in range(16):
            wtasks[l][g]()
            if g % 4 == 0 and (l, g // 4) in na_tasks:
                na_tasks[(l, g // 4)]()
            d_ = bufs[g % 2]
            PPr, PPi, q1, q2 = d_["PPr"], d_["PPi"], d_["q1"], d_["q2"]
            kPPr, kPPi, kq1, kq2 = d_["kPPr"], d_["kPPi"], d_["kq1"], d_["kq2"]
            blk, kblk = d_["blk"], d_["kblk"]
            v4 = lambda t: t.rearrange("p s (j c) -> p s j c", j=8)
            wpr = WP[:, 0, g, :].rearrange("p (s j) -> p s j", s=2).unsqueeze(3).broadcast_to([128, 2, 8, 16])
            wpi = WP[:, 1, g, :].rearrange("p (s j) -> p s j", s=2).unsqueeze(3).broadcast_to([128, 2, 8, 16])
            bbr = BBr[:, g, :].unsqueeze(1).unsqueeze(1).broadcast_to([128, 2, 8, 16])
            bbi = BBi[:, g, :].unsqueeze(1).unsqueeze(1).broadcast_to([128, 2, 8, 16])
            e1, e2 = ("dve", "pool") if g % 2 == 0 else ("pool", "dve")
            tt_(e1, v4(q1), wpr, bbr, ALU.mult, kWP + kBBr, kq1)
            tt_(e2, v4(q2), wpi, bbi, ALU.mult, kWP + kBBi, kq2)
            tt_(e1, PPr, q1, q2, ALU.subtract, kq1 + kq2, kPPr)
            tt_(e2, v4(q1), wpr, bbi, ALU.mult, kWP + kBBi, kq1)
            tt_(e1, v4(q2), wpi, bbr, ALU.mult, kWP + kBBr, kq2)
            tt_(e2, PPi, q1, q2, ALU.add, kq1 + kq2, kPPi)
            bt = next_bank(6, 8)
            for j, (src, ksrc) in enumerate(((PPr[:, 0, :], kPPr), (PPr[:, 1, :], kPPr), (PPi[:, 0, :], kPPi), (PPi[:, 1, :], kPPi))):
                kb.op("pe", lambda pe, j=j, src=src: pe.transpose(pp[bt][:, j * 128:(j + 1) * 128], src, identf[:]),
                      reads=ksrc + ["identf"], writes=[("pp", bt)])
            kb.op("act", lambda a: a.copy(out=blk[:, 3:7, :], in_=pp[bt][:, :].rearrange("p (a b) -> p a b", a=4)),
                  reads=[("pp", bt)], writes=kblk)
            c4 = lambda t: t.rearrange("p s (i c) -> p (s i) c", i=8)
            wcr = WC[:, 0, g, :].unsqueeze(2).broadcast_to([128, 16, 16])
            wci = WC[:, 1, g, :].unsqueeze(2).broadcast_to([128, 16, 16])
            cr = Cp[:, g, :, 0].unsqueeze(1).broadcast_to([128, 16, 16])
            ci = Cp[:, g, :, 1].unsqueeze(1).broadcast_to([128, 16, 16])
            tt_(e1, c4(q1), cr, wcr, ALU.mult, kCp + kWC, kq1)
            tt_(e2, c4(q2), ci, wci, ALU.mult, kCp + kWC, kq2)
            tt_(e1, blk[:, 7:10:2, :], q1, q2, ALU.subtract, kq1 + kq2, kblk)
            tt_(e2, c4(q1), cr, wci, ALU.mult, kCp + kWC, kq1)
            tt_(e1, c4(q2), ci, wcr, ALU.mult, kCp + kWC, kq2)
            kb.op("dve", lambda v: v.scalar_tensor_tensor(out=blk[:, 8:11:2, :], in0=q1, scalar=-1.0, in1=q2,
                                                           op0=ALU.mult, op1=ALU.subtract),
                  reads=kq1 + kq2, writes=kblk)
            Rr, Ri, r1, r2 = d_["Rr"], d_["Ri"], d_["r1"], d_["r2"]
            kRr, kRi, kr1, kr2 = d_["kRr"], d_["kRi"], d_["kr1"], d_["kr2"]
            wkr = WK[:, 0, g, :].unsqueeze(2).broadcast_to([128, 31, 16])
            wki = WK[:, 1, g, :].unsqueeze(2).broadcast_to([128, 31, 16])
            cr3 = Cp[:, g, :, 0].unsqueeze(1).broadcast_to([128, 31, 16])
            ci3 = Cp[:, g, :, 1].unsqueeze(1).broadcast_to([128, 31, 16])
            tt_(e1, r1, cr3, wkr, ALU.mult, kCp + kWK, kr1)
            tt_(e2, r2, ci3, wki, ALU.mult, kCp + kWK, kr2)
            tt_(e1, Rr, r1, r2, ALU.subtract, kr1 + kr2, kRr)
            tt_(e2, r1, cr3, wki, ALU.mult, kCp + kWK, kr1)
            tt_(e1, r2, ci3, wkr, ALU.mult, kCp + kWK, kr2)
            tt_(e2, Ri, r1, r2, ALU.add, kr1 + kr2, kRi)
            bk = next_bank(4, 6)
            kb.op("pe", lambda pe: pe.matmul(pp[bk][0:16, 0:496], BBr[:, g, :], Rr.rearrange("p a b -> p (a b)"),
                                             start=True, stop=False), reads=kBBr + kRr, writes=[("pp", bk)])
            kb.op("pe", lambda pe: pe.matmul(pp[bk][0:16, 0:496], nBBi[:, g, :], Ri.rearrange("p a b -> p (a b)"),
                                             start=False, stop=False), reads=knBBi + kRi, writes=[("pp", bk)])
            kb.op("pe", lambda pe: pe.matmul(pp[bk][0:16, 0:496], Dd[0:16, g, :], E15[0:16, :],
                                             start=False, stop=True), reads=kDd + kE15, writes=[("pp", bk)])
            Ks, kKs = d_["Ks"], d_["kKs"]
            kb.op("act", lambda a: a.copy(out=Ks[0:16, :], in_=pp[bk][0:16, 0:496]), reads=[("pp", bk)], writes=kKs)
            kb.dma(ktab_d[l, g].rearrange("i c o -> c i o"), Ks[0:16, :].rearrange("p (i o) -> p i o", i=31),
                   reads=kKs, writes=[("ktab", l, g)])
            Ms, kMs = d_["Ms"], d_["kMs"]
            kbase = (l * 16 + g) * 31 * 256
            for bi_, boff in enumerate((8, 16, 0)):
                src = bass.AP(ktab_t, kbase + boff * 256, [[16, 128], [256, 8], [1, 16]])
                kb.dma(Ms[:, bi_, :].rearrange("p (i o) -> p i o", i=8), src, reads=[("ktab", l, g)], writes=kMs)
            def tail(g=g, blk=blk, kblk=kblk, Ms=Ms, kMs=kMs):
                kb.op("act", lambda a: a.copy(out=blk[:, 0:3, :], in_=Ms), reads=kMs, writes=kblk)
                kb.dma(sblk_d[l, g], blk, reads=kblk, writes=["sblk"])
            if pend_tail[0] is not None:
                pend_tail[0]()
            pend_tail[0] = tail
        pend_tail[0]()

    if "c" in mixers:
        kb.dma(glb[:], glb_d, writes=["glb"])
        kb.op("dve", lambda v: v.tensor_scalar(out=glb[:], in0=glb[:], scalar1=0.5, scalar2=None, op0=ALU.mult),
              reads=["glb"], writes=["glb"])
        for l in range(depth):
            s5_precompute(l)

    wstate = {"n": 0}

    def load_wblock(l, b):
        k = wstate["n"] % 3
        wstate["n"] += 1
        kb.dma(wb[k][:], wib_d[l, b], reads=["wib"], writes=[("wb", k)])
        return k

    def load_woblock(l, h):
        k = wstate["n"] % 3
        wstate["n"] += 1
        kb.dma(wb[k][:], wob_d[l, h], reads=["wob"], writes=[("wb", k)])
        return k


    def proj_fm(k, cb, evac, tqs=range(4), M=128, moff=0):
        for tq in tqs:
            bi = next_bank(0, 2)
            for c in range(8):
                kb.op("pe", lambda pe, c=c, bi=bi, tq=tq: pe.matmul(
                    pp[bi][0:M, :], wb[k][:, c, cb * 128 + moff: cb * 128 + moff + M], hT[:, c, tq * 512:(tq + 1) * 512],
                    start=(c == 0), stop=(c == 7)),
                    reads=[("wb", k), "hT"], writes=[("pp", bi)])
            evac(tq, bi)

    def proj_tm(k, tok_ap_fn, ntiles, evac, ncols=256):
        for i in range(ntiles):
            bi = next_bank(0, 2)
            for c in range(8):
                kb.op("pe", lambda pe, c=c, bi=bi, i=i: pe.matmul(
                    pp[bi][:, 0:ncols], tok_ap_fn(c, i), wb[k][:, c, 0:ncols], start=(c == 0), stop=(c == 7)),
                    reads=[("wb", k), "hT"], writes=[("pp", bi)])
            evac(i, bi)

    def rms_sq(i):
        junk = hs[4 + i % 2]
        kb.op("act", lambda a, i=i, junk=junk: a.activation(out=junk, in_=x_res[:, i, :], func=AF.Square,
                                                            accum_out=small[:, 16 + i:17 + i]),
              reads=[("x", i)], writes=khs[4 + i % 2] + ["ss"])

    def rms_fin():
        kb.op("dve", lambda v: v.tensor_scalar(out=small[:, 32:48], in0=small[:, 16:32],
                                               scalar1=1.0 / D, scalar2=EPS, op0=ALU.mult, op1=ALU.add),
              reads=["ss"], writes=["ms"])
        kb.op("pool", lambda g: g.tensor_tensor(out=small[:, 0:16], in0=small[:, 32:48],
                                                in1=small[:, 48:49].broadcast_to([128, 16]), op=ALU.pow),
              reads=["ms", "mhalf"], writes=["rstd"])

    kb.op("dve", lambda v: v.memset(small[:, 48:49], -0.5), writes=["mhalf"])
    nlh = small[:, 49:50]
    kb.op("dve", lambda v: v.memset(nlh, -math.log(2.0)), writes=["nlh"])

    W = S + 32

    def mixer_a(l):
        kU, kA, kB, kg2, kDm = ak(0, 8320), ak(8320, 8320), ak(16640, 8320), ak(24960, 4096), ak(29056, 4096)
        ktt = [ak(33152 + j * 2048, 2048) for j in range(2)]
        U = av(0, [W], F32)
        A = av(8320, [W], F32)
        B = av(16640, [W], F32)
        g2 = av(24960, [S], BF16)
        Dm = av(29056, [S], BF16)
        tt = [av(33152 + j * 2048, [512], F32) for j in range(2)]
        for cb in range(2):
            kv = load_wblock(l, 0)
            kg = load_wblock(l, 1)
            kb.op("pool", lambda g: g.memset(U[:, 0:16], 0.0), writes=kU)
            kb.op("pool", lambda g: g.memset(U[:, 16 + S:W], 0.0), writes=kU)

            def ev_u(tq, bi):
                kb.op("act", lambda a: a.copy(out=U[:, 16 + tq * 512:16 + (tq + 1) * 512], in_=pp[bi][:, :]),
                      reads=[("pp", bi)], writes=kU)
            proj_fm(kv, cb, ev_u)
            kb.op("pool", lambda g: g.tensor_tensor(out=A[:, 1:W], in0=U[:, 0:W - 1], in1=U[:, 1:W], op=ALU.add),
                  reads=kU, writes=kA)
            kb.op("pool", lambda g: g.tensor_tensor(out=B[:, 2:W - 1], in0=A[:, 1:W - 2], in1=A[:, 3:W], op=ALU.add),
                  reads=kA, writes=kB)
            if cb == 1:
                kb.op("pool", lambda g: g.tensor_tensor(out=A[:, 4:W - 3], in0=B[:, 2:W - 5], in1=B[:, 6:W - 1], op=ALU.add),
                      reads=kB, writes=kA)
                kb.op("pool", lambda g: g.tensor_tensor(out=B[64:128, 8:W - 7], in0=A[64:128, 4:W - 11],
                                                        in1=A[64:128, 12:W - 3], op=ALU.add),
                      reads=kA, writes=kB)
            for (buf, nm, p0) in ((A, kA, 0), (B, kB, 64)):
                sl = slice(p0, p0 + 64)
                kb.op("dve", lambda v, buf=buf, sl=sl: v.tensor_tensor(
                    out=buf[sl, 16:24], in0=buf[sl, 16:24], in1=pcn[sl, cb, 0:8], op=ALU.mult),
                    reads=nm + ["pcn"], writes=nm)
                kb.op("dve", lambda v, buf=buf, sl=sl: v.tensor_tensor(
                    out=buf[sl, 8 + S:16 + S], in0=buf[sl, 8 + S:16 + S], in1=pcn[sl, cb, 8:16], op=ALU.mult),
                    reads=nm + ["pcn"], writes=nm)
                kb.op("dve", lambda v, buf=buf, sl=sl: v.scalar_tensor_tensor(
                    out=Dm[sl, :], in0=buf[sl, 16:16 + S], scalar=pcn[sl, cb, 16:17], in1=U[sl, 16:16 + S],
                    op0=ALU.mult, op1=ALU.subtract),
                    reads=nm + kU + ["pcn"], writes=kDm)

            def ev_g(tq, bi):
                t = tt[tq % 2]
                kb.op("act", lambda a: a.activation(out=t, in_=pp[bi][:, :], func=AF.Tanh, scale=0.5),
                      reads=[("pp", bi)], writes=ktt[tq % 2])
                kb.op("dve", lambda v: v.scalar_tensor_tensor(
                    out=g2[:, tq * 512:(tq + 1) * 512], in0=t, scalar=1.0, in1=pp[bi][:, :], op0=ALU.add, op1=ALU.mult),
                    reads=ktt[tq % 2] + [("pp", bi)], writes=kg2)
            proj_fm(kg, cb, ev_g)
            for tq in range(4):
                bi = next_bank(0, 2)
                kb.op("pe", lambda pe: pe.matmul(pp[bi][:, :], pwb[:, l, cb, :], Dm[:, tq * 512:(tq + 1) * 512],
                                                 start=True, stop=True),
                      reads=["pwb"] + kDm, writes=[("pp", bi)])
                kb.op("dve", lambda v: v.scalar_tensor_tensor(
                    out=yT[:, cb, tq * 512:(tq + 1) * 512], in0=pp[bi][:, :], scalar=psc[:, l, cb:cb + 1],
                    in1=g2[:, tq * 512:(tq + 1) * 512], op0=ALU.mult, op1=ALU.mult),
                    reads=[("pp", bi), "psc"] + kg2, writes=ky(cb))

    def run_pipeline(tasks, la):
        n = len(tasks)
        for i in range(n + la):
            if i < n:
                t = tasks[i]
                t["slot"] = i
                if t.get("pre"):
                    t["pre"]()
                t["s1"]()
            if i >= la:
                t = tasks[i - la]
                t["s2"]()
                if t.get("post"):
                    t["post"]()

    def run_pipeline_b(tasks, bs):
        n = len(tasks)
        nb = (n + bs - 1) // bs
        for b in range(nb + 1):
            if b < nb:
                for i in range(b * bs, min(n, (b + 1) * bs)):
                    t = tasks[i]
                    t["slot"] = i
                    if t.get("pre"):
                        t["pre"]()
                for i in range(b * bs, min(n, (b + 1) * bs)):
                    tasks[i]["s1a"]()
                for i in range(b * bs, min(n, (b + 1) * bs)):
                    tasks[i]["s1b"]()
            if b >= 1:
                for i in range((b - 1) * bs, min(n, b * bs)):
                    t = tasks[i]
                    t["s2"]()
                    if t.get("post"):
                        t["post"]()

    def gate_fm(k, kg2, g2, ktt, tt):
        for cb in range(2):
            def ev_g(tq, bi):
                t = tt[tq % 2]
                kb.op("act", lambda a: a.activation(out=t, in_=pp[bi][:, :], func=AF.Tanh, scale=0.5),
                      reads=[("pp", bi)], writes=ktt[tq % 2])
                kb.op("dve", lambda v: v.scalar_tensor_tensor(
                    out=g2[:, cb, tq * 512:(tq + 1) * 512], in0=t, scalar=1.0, in1=pp[bi][:, :], op0=ALU.add, op1=ALU.mult),
                    reads=ktt[tq % 2] + [("pp", bi)], writes=kg2)
            proj_fm(k, cb, ev_g)

    def psl(p0, n, d, n_sub):
        st = (p0 % n_sub) * d + p0 // n_sub
        return slice(st, st + (n - 1) * d + 1, d)

    def attn_finalize(h, acc, kacc, g2, kg2, rc, krc, tmp, ktmp, chunk0):
        nr = slice((h % 2) * 64, (h % 2) * 64 + 64)
        dr = slice(((h + 1) % 2) * 64, ((h + 1) % 2) * 64 + 64)
        for tq in range(8):
            ts_ = slice(tq * 256, (tq + 1) * 256)
            kb.op("dve", lambda v: v.reciprocal(out=rc[tq % 2][nr, :], in_=acc[dr, ts_]),
                  reads=kacc, writes=krc[tq % 2])
            kb.op("dve", lambda v: v.scalar_tensor_tensor(out=tmp[tq % 2][nr, :], in0=acc[nr, ts_], scalar=0.5,
                                                           in1=rc[tq % 2][nr, :], op0=ALU.mult, op1=ALU.mult),
                  reads=kacc + krc[tq % 2], writes=ktmp[tq % 2])
            kb.op("pool", lambda g: g.tensor_tensor(out=yT[nr, chunk0 + h // 2, ts_], in0=tmp[tq % 2][nr, :],
                                                    in1=g2[nr, h // 2, ts_], op=ALU.mult),
                  reads=ktmp[tq % 2] + kg2, writes=ky(chunk0 + h // 2, h=h % 2))

    def gate_apply(l, blk, chunk0, tt, ktt, gq, kgq):
        k = load_wblock(l, blk)
        for cb in range(2):
            def ev_g(tq, bi):
                t = tt[tq % 2]
                g_ = gq[tq % 2]
                ts_ = slice(tq * 512, (tq + 1) * 512)
                kb.op("act", lambda a: a.activation(out=t, in_=pp[bi][:, :], func=AF.Tanh, scale=0.5),
                      reads=[("pp", bi)], writes=ktt[tq % 2])
                kb.op("dve", lambda v: v.scalar_tensor_tensor(out=g_, in0=t, scalar=1.0, in1=pp[bi][:, :],
                                                               op0=ALU.add, op1=ALU.mult),
                      reads=ktt[tq % 2] + [("pp", bi)], writes=kgq[tq % 2])
                kb.op("pool", lambda g: g.tensor_tensor(out=yT[:, chunk0 + cb, ts_], in0=g_, in1=yT[:, chunk0 + cb, ts_],
                                                        op=ALU.mult),
                      reads=kgq[tq % 2] + ky(chunk0 + cb), writes=ky(chunk0 + cb))
            proj_fm(k, cb, ev_g)

    def mixer_b(l):
        o_q, o_kz, o_v, o_acc, o_pt, o_rc = 0, 8192, 24576, 32768, 49152, 51200
        qT = av(o_q, [2, S], BF16)
        kqT = ak(o_q, 8192)
        kTz = [av(o_kz + h * 4096, [S], BF16) for h in range(4)]
        kkTz = [ak(o_kz + h * 4096, 4096) for h in range(4)]
        Va = [av(o_v + j * 4096, [16, 128], BF16) for j in range(2)]
        kVa = [ak(o_v + j * 4096, 4096) for j in range(2)]
        accs = [av(o_acc + j * 8192, [S], F32) for j in range(2)]
        kaccs = [ak(o_acc + j * 8192, 8192) for j in range(2)]
        pt = [av(o_pt + j * 512, [256], BF16) for j in range(4)]
        kpt = [ak(o_pt + j * 512, 512) for j in range(4)]
        rcq = [av(o_rc + j * 1024, [256], F32) for j in range(2)]
        krcq = [ak(o_rc + j * 1024, 1024) for j in range(2)]
        for h in range(4):
            oh = slice(((h + 1) % 2) * 64, ((h + 1) % 2) * 64 + 64)
            kb.op("pool", lambda g, h=h, oh=oh: g.memset(kTz[h][oh, :], 0.0), writes=kkTz[h])
        rts = [[av(o_acc + sl * 2560 + j * 512, [128], F32) for j in range(4)] for sl in range(3)]
        krts = [[ak(o_acc + sl * 2560 + j * 512, 512) for j in range(4)] for sl in range(3)]
        qrs = [av(o_acc + sl * 2560 + 2048, [256], BF16) for sl in range(3)]
        kqrs = [ak(o_acc + sl * 2560 + 2048, 512) for sl in range(3)]
        rtasks = []
        for blk in (2, 3):
            k = load_wblock(l, blk)
            for i in range(NT):
                t = {"pre": None, "post": None}

                def s1(t=t, i=i, k=k):
                    sl = t["slot"] % 3
                    bi = (0, 1, 2)[sl]
                    for c in range(8):
                        kb.op("pe", lambda pe, c=c: pe.matmul(pp[bi][:, 0:256], hT[:, c, i * 128:(i + 1) * 128],
                                                              wb[k][:, c, 0:256], start=(c == 0), stop=(c == 7)),
                              reads=[("wb", k), "hT"], writes=[("pp", bi)])
                    z4 = pp[bi][:, 0:256].rearrange("p (h t f) -> p h t f", h=4, t=2)
                    x1, x2 = z4[:, :, 0, :], z4[:, :, 1, :]
                    cs = ropet[:, 0, i, :].unsqueeze(1).broadcast_to([128, 4, 32])
                    sn = ropet[:, 1, i, :].unsqueeze(1).broadcast_to([128, 4, 32])
                    r4 = [r.rearrange("p (h f) -> p h f", h=4) for r in rts[sl]]
                    q4 = qrs[sl].rearrange("p (h t f) -> p h t f", h=4, t=2)
                    for j, (a_, b_) in enumerate(((x1, cs), (x2, sn), (x2, cs), (x1, sn))):
                        kb.op("dve", lambda v, a_=a_, b_=b_, j=j: v.tensor_tensor(out=r4[j], in0=a_, in1=b_, op=ALU.mult),
                              reads=[("pp", bi), "ropet"], writes=krts[sl][j])
                    kb.op("pool", lambda g: g.tensor_tensor(out=q4[:, :, 0, :], in0=r4[0], in1=r4[1], op=ALU.subtract),
                          reads=krts[sl][0] + krts[sl][1], writes=kqrs[sl])
                    kb.op("pool", lambda g: g.tensor_tensor(out=q4[:, :, 1, :], in0=r4[2], in1=r4[3], op=ALU.add),
                          reads=krts[sl][2] + krts[sl][3], writes=kqrs[sl])

                def s2(t=t, i=i, blk=blk):
                    sl = t["slot"] % 3
                    b2 = next_bank(6, 8)
                    ptb = pp[b2][:].bitcast(BF16)
                    for pr in range(2):
                        kb.op("pe", lambda pe, pr=pr: pe.transpose(ptb[:, pr * 128:(pr + 1) * 128],
                                                                   qrs[sl][:, pr * 128:(pr + 1) * 128], ident[:]),
                              reads=kqrs[sl] + ["ident"], writes=[("pp", b2)])
                    if blk == 2:
                        kb.op("act", lambda a: a.copy(out=qT[:, :, i * 128:(i + 1) * 128],
                                                      in_=ptb[:, 0:256].rearrange("p (c n) -> p c n", c=2)),
                              reads=[("pp", b2)], writes=kqT)
                    else:
                        for h in range(4):
                            hp = slice((h % 2) * 64, (h % 2) * 64 + 64)
                            pr = h // 2
                            kb.op("act" if h % 2 == 0 else "dve",
                                  lambda e, h=h, hp=hp, pr=pr: (e.copy if h % 2 == 0 else e.tensor_copy)(
                                      out=kTz[h][hp, i * 128:(i + 1) * 128], in_=ptb[hp, pr * 128:(pr + 1) * 128]),
                                  reads=[("pp", b2)], writes=kkTz[h])
                t["s1"], t["s2"] = s1, s2
                rtasks.append(t)
        run_pipeline(rtasks, 2)
        kv = load_wblock(l, 4)
        for cb in range(2):
            def ev_vt(tq, bi):
                kb.op("act", lambda a: a.copy(out=yT[:, 2 + cb, tq * 512:(tq + 1) * 512], in_=pp[bi][:, :]),
                      reads=[("pp", bi)], writes=ky(2 + cb))
            proj_fm(kv, cb, ev_vt)
        tasks = []
        nva = 0
        for h in range(4):
            acc, kacc = accs[h % 2], kaccs[h % 2]
            voff = 0 if h % 2 == 0 else 64
            for pi, (d, n_sub) in enumerate(((1, 2048), (4, 512), (16, 128))):
                V, kV = Va[nva % 2], kVa[nva % 2]
                nva += 1

                def pre_v(V=V, kV=kV, h=h, voff=voff, d=d, n_sub=n_sub, pi=pi):
                    if pi < 2:
                        kb.op("pool", lambda g: g.memset(V[:, :, 64 - voff:128 - voff], 1.0), writes=kV)
                    for half in range(2):
                        b2 = next_bank(0, 2)
                        ptb = pp[b2][:].bitcast(BF16)
                        for q in range(8):
                            j = half * 8 + q
                            kb.op("pe", lambda pe, q=q, j=j: pe.transpose(
                                ptb[:, q * 128:(q + 1) * 128], yT[:, 2 + h // 2, psl(128 * j, 128, d, n_sub)], ident[:, :]),
                                reads=ky(2 + h // 2) + ["ident"], writes=[("pp", b2)])
                        kb.op("act", lambda a: a.copy(
                            out=V[:, half * 8:half * 8 + 8, voff:voff + 64],
                            in_=ptb.rearrange("p (q e) -> p q e", q=8)[:, :, voff:voff + 64]),
                            reads=[("pp", b2)], writes=kV)
                first_of_pattern = True
                for qb in range(4):
                    ob = next_bank(4, 6)
                    js = []
                    for j in range(max(0, 4 * qb - 1), min(16, 4 * qb + 5)):
                        slo = (128 * j // n_sub) * n_sub
                        qlo = max(128 * j - 64, slo, 512 * qb)
                        qhi = min(128 * j + 192, slo + n_sub, 512 * qb + 512)
                        if qlo < qhi:
                            js.append((j, qlo, qhi))
                    for ji, (j, qlo, qhi) in enumerate(js):
                        t = {}
                        t["pre"] = pre_v if first_of_pattern else None
                        first_of_pattern = False

                        def s1(t=t, j=j, qlo=qlo, qhi=qhi, h=h, d=d, n_sub=n_sub):
                            n = qhi - qlo
                            ns = t["slot"] % 4
                            sbk = (2, 3, 6, 7)[ns]
                            p_, kp_ = pt[ns], kpt[ns]
                            mo = qlo - (128 * j - 64)
                            kb.op("pe", lambda pe: pe.matmul(pp[sbk][:, 0:n], kTz[h][:, psl(128 * j, 128, d, n_sub)],
                                                             qT[:, h // 2, psl(qlo, n, d, n_sub)], start=True, stop=False),
                                  reads=kkTz[h] + kqT, writes=[("pp", sbk)])
                            kb.op("pe", lambda pe: pe.matmul(pp[sbk][:, 0:n], ident[:, :], band[:, mo:mo + n],
                                                             start=False, stop=True),
                                  reads=["ident", "band"], writes=[("pp", sbk)])
                            kb.op("act", lambda a: a.activation(out=p_[:, 0:n], in_=pp[sbk][:, 0:n], func=AF.Exp, scale=0.125),
                                  reads=[("pp", sbk)], writes=kp_)

                        def s2(t=t, j=j, qlo=qlo, qhi=qhi, qb=qb, ob=ob, V=V, kV=kV, first=(ji == 0)):
                            n = qhi - qlo
                            ns = t["slot"] % 4
                            p_, kp_ = pt[ns], kpt[ns]
                            kb.op("pe", lambda pe: pe.matmul(pp[ob][:, qlo - 512 * qb:qhi - 512 * qb], V[:, j, :], p_[:, 0:n],
                                                             start=first, stop=False, skip_group_check=True),
                                  reads=kV + kp_, writes=[("pp", ob)])
                        t["s1"], t["s2"], t["post"] = s1, s2, None
                        if ji == len(js) - 1:
                            def post(qb=qb, ob=ob, d=d, pi=pi, acc=acc, kacc=kacc, h=h):
                                if d == 1:
                                    dst = acc[:, 512 * qb:512 * qb + 512]
                                    src = pp[ob][:, :]
                                elif d == 4:
                                    dst = acc.rearrange("p (l x) -> p x l", x=4)[:, qb, :]
                                    src = pp[ob][:, :]
                                else:
                                    dst = acc.rearrange("p (l x) -> p x l", x=16)[:, 4 * qb:4 * qb + 4, :]
                                    src = pp[ob][:, :].rearrange("p (r l) -> p r l", r=4)
                                if pi == 0:
                                    kb.op("dve", lambda v: v.tensor_copy(out=dst, in_=src), reads=[("pp", ob)], writes=kacc)
                                else:
                                    kb.op("dve", lambda v: v.tensor_tensor(out=dst, in0=src, in1=dst, op=ALU.add),
                                          reads=[("pp", ob)] + kacc, writes=kacc)
                                if pi == 2 and qb == 3:
                                    nr = slice((h % 2) * 64, (h % 2) * 64 + 64)
                                    dr = slice(((h + 1) % 2) * 64, ((h + 1) % 2) * 64 + 64)
                                    for tq in range(8):
                                        ts_ = slice(tq * 256, (tq + 1) * 256)
                                        rc, krc = rcq[tq % 2], krcq[tq % 2]
                                        kb.op("act", lambda a: a.activation(out=rc[nr, :], in_=acc[dr, ts_], func=AF.Ln),
                                              reads=kacc, writes=krc)
                                        kb.op("act", lambda a: a.activation(out=rc[nr, :], in_=rc[nr, :], func=AF.Exp, scale=-1.0,
                                                                            bias=nlh[nr, :]),
                                              reads=krc + ["nlh"], writes=krc)
                                        kb.op("pool", lambda g: g.tensor_tensor(out=yT[nr, 2 + h // 2, ts_], in0=acc[nr, ts_],
                                                                                in1=rc[nr, :], op=ALU.mult),
                                              reads=kacc + krc, writes=ky(2 + h // 2, h=h % 2))
                            t["post"] = post
                        tasks.append(t)
        run_pipeline(tasks, 3)
        tt = [av(o_acc + j * 2048, [512], F32) for j in range(2)]
        ktt = [ak(o_acc + j * 2048, 2048) for j in range(2)]
        gq = [av(o_acc + 4096 + j * 2048, [512], F32) for j in range(2)]
        kgq = [ak(o_acc + 4096 + j * 2048, 2048) for j in range(2)]
        gate_apply(l, 5, 2, tt, ktt, gq, kgq)

    def na_rows(kt):
        rows = []
        for r in range(32):
            rs = min(max(r - 4, 0), 24)
            if any(rs <= 2 * kt + krl < rs + 8 for krl in range(2)):
                rows.append(r)
        return rows[0], rows[-1]

    def mixer_d(l):
        o_q, o_kz, o_v, o_e, o_tt, o_pt = 0, 8192, 24576, 32768, 42240, 46336
        qT = av(o_q, [2, S], BF16)
        kqT = ak(o_q, 8192)
        kTz = [av(o_kz + h * 4096, [S], BF16) for h in range(4)]
        kkTz = [ak(o_kz + h * 4096, 4096) for h in range(4)]
        Va = [av(o_v + j * 4096, [16, 128], BF16) for j in range(2)]
        kVa = [ak(o_v + j * 4096, 4096) for j in range(2)]
        Eb = [av(o_e + j * 4736, [2368], BF16) for j in range(2)]
        kEb = [ak(o_e + j * 4736, 4736) for j in range(2)]
        tt = [av(o_tt + j * 2048, [512], F32) for j in range(2)]
        ktt = [ak(o_tt + j * 2048, 2048) for j in range(2)]
        pt = [av(o_pt + j * 1024, [512], BF16) for j in range(4)]
        kpt = [ak(o_pt + j * 1024, 1024) for j in range(4)]
        for h in range(4):
            oh = slice(((h + 1) % 2) * 64, ((h + 1) % 2) * 64 + 64)
            kb.op("pool", lambda g, h=h, oh=oh: g.memset(kTz[h][oh, :], 0.0), writes=kkTz[h])
        k = load_wblock(l, 8)
        for cb in range(2):
            def ev_q(tq, bi):
                kb.op("act", lambda a: a.copy(out=qT[:, cb, tq * 512:(tq + 1) * 512], in_=pp[bi][:, :]),
                      reads=[("pp", bi)], writes=kqT)
            proj_fm(k, cb, ev_q)
        k = load_wblock(l, 9)
        for cb in range(2):
            def ev_k(tq, bi):
                for hh in range(2):
                    h = 2 * cb + hh
                    hp = slice(hh * 64, hh * 64 + 64)
                    kb.op("act" if hh == 0 else "dve",
                          lambda e, h=h, hp=hp, hh=hh: (e.copy if hh == 0 else e.tensor_copy)(
                              out=kTz[h][hp, tq * 512:(tq + 1) * 512], in_=pp[bi][hp, :]),
                          reads=[("pp", bi)], writes=kkTz[h])
            proj_fm(k, cb, ev_k)
        kv = load_wblock(l, 10)
        for cb in range(2):
            def ev_vt(tq, bi):
                kb.op("act", lambda a: a.copy(out=yT[:, 6 + cb, tq * 512:(tq + 1) * 512], in_=pp[bi][:, :]),
                      reads=[("pp", bi)], writes=ky(6 + cb))
            proj_fm(kv, cb, ev_vt)
        tasks = []
        for h in range(4):
            nr = slice((h % 2) * 64, (h % 2) * 64 + 64)
            dr = slice(((h + 1) % 2) * 64, ((h + 1) % 2) * 64 + 64)
            voff = 0 if h % 2 == 0 else 64
            V, kV = Va[h % 2], kVa[h % 2]
            E, kE = Eb[h % 2], kEb[h % 2]

            def pre_h(h=h, voff=voff, V=V, kV=kV, E=E, kE=kE):
                kb.dma(E, et_d[l, h], reads=["et"], writes=kE)
                if h < 2:
                    kb.op("pool", lambda g: g.memset(V[:, :, 64 - voff:128 - voff], 1.0), writes=kV)
                for half in range(2):
                    b2 = next_bank(0, 2)
                    ptb = pp[b2][:].bitcast(BF16)
                    for q in range(8):
                        j = half * 8 + q
                        kb.op("pe", lambda pe, q=q, j=j: pe.transpose(
                            ptb[:, q * 128:(q + 1) * 128], yT[:, 6 + h // 2, 128 * j:128 * j + 128], ident[:, :]),
                            reads=ky(6 + h // 2) + ["ident"], writes=[("pp", b2)])
                    kb.op("act", lambda a: a.copy(out=V[:, half * 8:half * 8 + 8, voff:voff + 64],
                                                  in_=ptb.rearrange("p (q e) -> p q e", q=8)[:, :, voff:voff + 64]),
                          reads=[("pp", b2)], writes=kV)
            first_of_head = True
            for qb in range(4):
                ob = next_bank(4, 6)
                kts = []
                for kt in range(16):
                    ra, rb = na_rows(kt)
                    ra, rb = max(ra, 8 * qb), min(rb, 8 * qb + 7)
                    if ra <= rb:
                        kts.append((kt, ra, rb))
                for ki, (kt, ra, rb) in enumerate(kts):
                    t = {"pre": pre_h if first_of_head else None, "post": None}
                    first_of_head = False

                    def s1(t=t, kt=kt, ra=ra, rb=rb, h=h, E=E, kE=kE):
                        n = 64 * (rb - ra + 1)
                        ns = t["slot"]
                        sbk = (2, 3, 6, 7)[ns % 4]
                        p_, kp_ = pt[ns % 4], kpt[ns % 4]
                        kb.op("pe", lambda pe: pe.matmul(pp[sbk][:, 0:n], kTz[h][:, 128 * kt:128 * kt + 128],
                                                         qT[:, h // 2, 64 * ra:64 * (rb + 1)], start=True, stop=True,
                                                         skip_group_check=True),
                              reads=kkTz[h] + kqT, writes=[("pp", sbk)])
                        segs = []
                        if ra <= 3:
                            r1 = min(rb, 3)
                            segs.append((ra, r1, 576 + (3 - kt) * 256 + ra * 64))
                        if max(ra, 4) <= min(rb, 28):
                            r0, r1 = max(ra, 4), min(rb, 28)
                            segs.append((r0, r1, (r0 - 2 * kt + 3) * 64))
                        if rb >= 29:
                            r0 = max(ra, 29)
                            segs.append((r0, rb, 1600 + (15 - kt) * 192 + (r0 - 29) * 64))
                        for si, (r0, r1, eoff) in enumerate(segs):
                            c0, c1 = 64 * (r0 - ra), 64 * (r1 - ra + 1)
                            kb.op("pe", lambda pe, c0=c0, c1=c1, eoff=eoff, si=si: pe.matmul(
                                pp[sbk][:, c0:c1], ident[:, :], E[:, eoff:eoff + c1 - c0], start=False,
                                stop=True, skip_group_check=True),
                                reads=["ident"] + kE, writes=[("pp", sbk)])
                        kb.op("act", lambda a: a.activation(out=p_[:, 0:n], in_=pp[sbk][:, 0:n], func=AF.Exp, scale=0.125),
                              reads=[("pp", sbk)], writes=kp_)

                    def s2(t=t, kt=kt, ra=ra, rb=rb, qb=qb, ob=ob, V=V, kV=kV, first=(ki == 0)):
                        n = 64 * (rb - ra + 1)
                        ns = t["slot"]
                        p_, kp_ = pt[ns % 4], kpt[ns % 4]
                        kb.op("pe", lambda pe: pe.matmul(pp[ob][:, 64 * ra - 512 * qb:64 * (rb + 1) - 512 * qb], V[:, kt, :],
                                                         p_[:, 0:n], start=first, stop=False, skip_group_check=True),
                              reads=kV + kp_, writes=[("pp", ob)])
                    t["s1"], t["s2"] = s1, s2
                    if ki == len(kts) - 1:
                        def post(qb=qb, ob=ob, h=h, nr=nr, dr=dr):
                            ts_ = slice(qb * 512, (qb + 1) * 512)
                            rc, krc = tt[qb % 2], ktt[qb % 2]
                            kb.op("act", lambda a: a.activation(out=rc[nr, :], in_=pp[ob][dr, :], func=AF.Ln),
                                  reads=[("pp", ob)], writes=krc)
                            kb.op("act", lambda a: a.activation(out=rc[nr, :], in_=rc[nr, :], func=AF.Exp, scale=-1.0,
                                                                bias=nlh[nr, :]),
                                  reads=krc + ["nlh"], writes=krc)
                            kb.op("dve", lambda v: v.tensor_tensor(out=yT[nr, 6 + h // 2, ts_], in0=pp[ob][nr, :],
                                                                   in1=rc[nr, :], op=ALU.mult),
                                  reads=[("pp", ob)] + krc, writes=ky(6 + h // 2, h=h % 2))
                        t["post"] = post
                    tasks.append(t)
        run_pipeline(tasks, 3)
        ttg = [av(o_v + j * 2048, [512], F32) for j in range(2)]
        kttg = [ak(o_v + j * 2048, 2048) for j in range(2)]
        gq = [av(o_v + 4096 + j * 2048, [512], F32) for j in range(2)]
        kgq = [ak(o_v + 4096 + j * 2048, 2048) for j in range(2)]
        gate_apply(l, 11, 6, ttg, kttg, gq, kgq)

    GC1 = math.sqrt(2.0 / math.pi)
    GC2 = 0.044715

    def mixer_c(l):
        o_u, o_G, o_V, o_P, o_MC, o_et, o_w, o_sin, o_y = 0, 8192, 16384, 20480, 22528, 26112, 30208, 40448, 44544
        ucm = [av(o_u + t * 4096, [16, 8, 16], BF16) for t in range(2)]
        kucm = [ak(o_u + t * 4096, 4096) for t in range(2)]
        Gcm = av(o_G, [16, 256], BF16)
        kG = ak(o_G, 8192)
        gT = av(o_u, [2, S], BF16)
        kgT = ak(o_u, 8192)
        Vs = [av(o_V + j * 512, [256], BF16) for j in range(8)]
        kVs = [ak(o_V + j * 512, 512) for j in range(8)]
        Pb = [av(o_P + j * 1024, [4, 128], BF16) for j in range(2)]
        kPb = [ak(o_P + j * 1024, 1024) for j in range(2)]
        MC = [av(o_MC + j * 1792, [7, 128], BF16) for j in range(2)]
        kMC = [ak(o_MC + j * 1792, 1792) for j in range(2)]
        et = av(o_et, [4, 2, 128], F32)
        ket = ak(o_et, 4096)
        wk_ = [av(o_w + j * 2048, [4, 128], F32) for j in range(5)]
        kwk = [ak(o_w + j * 2048, 2048) for j in range(5)]
        A_, B_, T1, T2, T3 = wk_
        kA, kB_, kT1, kT2, kT3 = kwk
        sres = [av(o_sin + j * 2048, [4, 128], BF16) for j in range(2)]
        sims = [av(o_sin + j * 2048 + 1024, [4, 128], BF16) for j in range(2)]
        ksres = [ak(o_sin + j * 2048, 1024) for j in range(2)]
        ksims = [ak(o_sin + j * 2048 + 1024, 1024) for j in range(2)]
        ytmp = [[av(o_y + s_ * 4096 + j * 1024, [256], F32) for j in range(4)] for s_ in range(2)]
        kytmp = [[ak(o_y + s_ * 4096 + j * 1024, 1024) for j in range(4)] for s_ in range(2)]
        F_, Bh = slice(0, 64), slice(64, 128)
        tt2 = lambda e, o, a, b, op, rd, wr: kb.op(e, lambda v: v.tensor_tensor(out=o, in0=a, in1=b, op=op), reads=rd, writes=wr)
        kcu = load_wblock(l, 6)
        for j in range(8):
            for mt in range(2):
                bi = next_bank(0, 2)
                for c in range(8):
                    kb.op("pe", lambda pe, c=c: pe.matmul(
                        pp[bi][:, 0:256], hT[:, c, slice(1024 * mt + j, 1024 * mt + j + 8 * 127 + 1, 8)],
                        wb[kcu][:, c, :], start=(c == 0), stop=(c == 7)),
                        reads=[("wb", kcu), "hT"], writes=[("pp", bi)])
                kb.op("act" if (j + mt) % 2 == 0 else "dve",
                      lambda e: (e.copy if (j + mt) % 2 == 0 else e.tensor_copy)(
                          out=ucm[mt][:, :, 7 - j, :], in_=pp[bi][:, 0:256].rearrange("p (g c) -> p g c", g=16)),
                      reads=[("pp", bi)], writes=kucm[mt])
        for j in range(2):
            kb.op("pool", lambda g, j=j: g.memset(sres[j][F_, :, 0:1], 0.0), writes=ksres[j])
            kb.op("pool", lambda g, j=j: g.memset(sres[j][Bh, :, 127:128], 0.0), writes=ksres[j])
            kb.op("pool", lambda g, j=j: g.memset(sims[j][F_, :, 0:1], 0.0), writes=ksims[j])
            kb.op("pool", lambda g, j=j: g.memset(sims[j][Bh, :, 127:128], 0.0), writes=ksims[j])
        banks = {}

        def x_front(gb):
            kb.dma(et, etab_d[l, gb], reads=["etab"], writes=ket)
            s0r, s0i = 2, 3
            banks[gb] = (s0r, s0i)
            for gi in range(4):
                g = 4 * gb + gi
                V, kV = Vs[(gb % 2) * 4 + gi], kVs[(gb % 2) * 4 + gi]
                P_, kP_ = Pb[g % 2], kPb[g % 2]
                kb.dma(P_, sblk_d[l, g, :, 3:7, :], reads=["sblk"], writes=kP_)
                bt = next_bank(6, 8)
                ptb = pp[bt][:].bitcast(BF16)
                for mt in range(2):
                    kb.op("pe", lambda pe, mt=mt: pe.transpose(
                        ptb[:, mt * 128:(mt + 1) * 128], ucm[mt][:, g, :, :].rearrange("p j c -> p (j c)"), ident[:]),
                        reads=kucm[mt] + ["ident"], writes=[("pp", bt)])
                kb.op("act", lambda a: a.copy(out=V, in_=ptb[:, 0:256]), reads=[("pp", bt)], writes=kV)
                for (bank, b0) in ((s0r, 0), (s0i, 2)):
                    for sub in range(2):
                        kb.op("pe", lambda pe, sub=sub, bank=bank, b0=b0: pe.matmul(
                            pp[bank][:, gi * 128:(gi + 1) * 128], P_[:, b0 + sub, :], V[:, sub:256:2],
                            start=(sub == 0), stop=(sub == 1), skip_group_check=True),
                            reads=kP_ + kV, writes=[("pp", bank)])
            for (bank, dst, kd) in ((s0r, A_, kA), (s0i, B_, kB_)):
                src = pp[bank][:, :].rearrange("p (g m) -> p g m", g=4)
                kb.op("act", lambda a, src=src, dst=dst: a.copy(out=dst[F_], in_=src[F_]), reads=[("pp", bank)], writes=kd)
                kb.op("act", lambda a, src=src, dst=dst: a.copy(out=dst[Bh], in_=src[Bh, :, ::-1]), reads=[("pp", bank)], writes=kd)

        def x_back(gb, part):
            cs_, sn_ = et[:, :, 0, :], et[:, :, 1, :]
            sre, sim, ksre, ksim = sres[gb % 2], sims[gb % 2], ksres[gb % 2], ksims[gb % 2]
            if part == 0:
                tt2("dve", T1, A_, cs_, ALU.mult, kA + ket, kT1)
                tt2("pool", T2, B_, sn_, ALU.mult, kB_ + ket, kT2)
                tt2("dve", T1, T1, T2, ALU.add, kT1 + kT2, kT1)
                tt2("pool", T3, B_, cs_, ALU.mult, kB_ + ket, kT3)
                tt2("dve", T2, A_, sn_, ALU.mult, kA + ket, kT2)
                tt2("pool", T3, T3, T2, ALU.subtract, kT3 + kT2, kT3)
            elif part == 1:
                for gi in range(4):
                    g = 4 * gb + gi
                    rb_ = rho_sb[:, l, g:g + 1].broadcast_to([128, 128])
                    kb.op("dve", lambda v, gi=gi, rb_=rb_: v.tensor_tensor_scan(
                        out=A_[:, gi, :], data0=rb_, data1=T1[:, gi, :], initial=0.0, op0=ALU.mult, op1=ALU.add),
                        reads=kT1 + ["rho"], writes=kA)
                    kb.op("dve", lambda v, gi=gi, rb_=rb_: v.tensor_tensor_scan(
                        out=B_[:, gi, :], data0=rb_, data1=T3[:, gi, :], initial=0.0, op0=ALU.mult, op1=ALU.add),
                        reads=kT3 + ["rho"], writes=kB_)
            elif part == 2:
                tt2("dve", T1, A_, cs_, ALU.mult, kA + ket, kT1)
                tt2("pool", T2, B_, sn_, ALU.mult, kB_ + ket, kT2)
                tt2("dve", T1, T1, T2, ALU.subtract, kT1 + kT2, kT1)
                tt2("pool", T3, B_, cs_, ALU.mult, kB_ + ket, kT3)
                tt2("dve", T2, A_, sn_, ALU.mult, kA + ket, kT2)
                tt2("pool", T3, T3, T2, ALU.add, kT3 + kT2, kT3)
            else:
                for (src, ksrc, dst, kd) in ((T1, kT1, sre, ksre), (T3, kT3, sim, ksim)):
                    kb.op("act", lambda a, src=src, dst=dst: a.copy(out=dst[F_, :, 1:128], in_=src[F_, :, 0:127]),
                          reads=ksrc, writes=kd)
                    kb.op("pool", lambda g_, src=src, dst=dst: g_.tensor_copy(out=dst[Bh, :, 0:127], in_=src[Bh, :, 126::-1]),
                          reads=ksrc, writes=kd)

        def y_group(gb, gi):
            g = 4 * gb + gi
            V, kV = Vs[(gb % 2) * 4 + gi], kVs[(gb % 2) * 4 + gi]
            sre, sim, ksre, ksim = sres[gb % 2], sims[gb % 2], ksres[gb % 2], ksims[gb % 2]
            M_, kM_ = MC[g % 2], kMC[g % 2]
            Ysb, sq, u_, th_ = ytmp[g % 2]
            kYsb, ksq, ku_, kth_ = kytmp[g % 2]
            kb.dma(M_[:, 0:3, :], sblk_d[l, g, :, 0:3, :], reads=["sblk"], writes=kM_)
            kb.dma(M_[:, 3:7, :], sblk_d[l, g, :, 7:11, :], reads=["sblk"], writes=kM_)
            yb = next_bank(4, 6)
            V0, V1 = V[:, 0:256:2], V[:, 1:256:2]
            plan = ((0, [(0, V0, kV), (2, V1, kV), (3, sre[:, gi, :], ksre), (4, sim[:, gi, :], ksim)]),
                    (1, [(0, V1, kV), (1, V0, kV), (5, sre[:, gi, :], ksre), (6, sim[:, gi, :], ksim)]))
            for so, terms in plan:
                for ti, (bidx, rhs, krhs) in enumerate(terms):
                    kb.op("pe", lambda pe, so=so, ti=ti, bidx=bidx, rhs=rhs: pe.matmul(
                        pp[yb][:, so * 128:(so + 1) * 128], M_[:, bidx, :], rhs, start=(ti == 0), stop=(ti == 3),
                        skip_group_check=True), reads=kM_ + krhs, writes=[("pp", yb)])
            kb.op("act", lambda a: a.copy(out=Ysb, in_=pp[yb][:, 0:256]), reads=[("pp", yb)], writes=kYsb)
            tb = next_bank(6, 8)
            for so in range(2):
                kb.op("pe", lambda pe, so=so: pe.transpose(pp[tb][:, so * 128:(so + 1) * 128],
                                                           Ysb[:, so * 128:(so + 1) * 128], identf[:]),
                      reads=kYsb + ["identf"], writes=[("pp", tb)])
            yy = pp[tb][:, 0:256]
            kb.op("act", lambda a: a.activation(out=sq, in_=yy, func=AF.Square), reads=[("pp", tb)], writes=ksq)
            kb.op("pool", lambda g_: g_.tensor_scalar(out=sq, in0=sq, scalar1=GC2, scalar2=1.0, op0=ALU.mult, op1=ALU.add),
                  reads=ksq, writes=ksq)
            kb.op("dve", lambda v: v.tensor_tensor(out=u_, in0=sq, in1=yy, op=ALU.mult), reads=ksq + [("pp", tb)], writes=ku_)
            kb.op("act", lambda a: a.activation(out=th_, in_=u_, func=AF.Tanh, scale=GC1), reads=ku_, writes=kth_)
            kb.op("dve", lambda v: v.scalar_tensor_tensor(
                out=Gcm[:, :, g * 16:(g + 1) * 16], in0=th_.rearrange("p (s c) -> p s c", c=16), scalar=1.0,
                in1=yy.rearrange("p (s c) -> p s c", c=16), op0=ALU.add, op1=ALU.mult),
                reads=kth_ + [("pp", tb)], writes=kG)

        x_front(0)
        for part in range(4):
            x_back(0, part)
        for gb in range(4):
            if gb + 1 < 4:
                x_front(gb + 1)
            for gi in range(4):
                y_group(gb, gi)
                if gb + 1 < 4:
                    x_back(gb + 1, gi)
        dump("Gcm", Gcm, kG)
        for chc in range(2):
            for half in range(2):
                bt = next_bank(6, 8)
                ptb = pp[bt][:].bitcast(BF16)
                for q in range(8):
                    si = half * 8 + q
                    kb.op("pe", lambda pe, q=q, si=si: pe.transpose(ptb[:, q * 128:(q + 1) * 128],
                                                                    Gcm[:, si, chc * 128:(chc + 1) * 128], ident[:]),
                          reads=kG + ["ident"], writes=[("pp", bt)])
                kb.op("act" if half == 0 else "dve", lambda e: (e.copy if half == 0 else e.tensor_copy)(
                    out=gT[:, chc, :].rearrange("p (m s) -> p s m", s=16)[:, half * 8:half * 8 + 8, :],
                    in_=ptb.rearrange("p (q m) -> p q m", q=8)), reads=[("pp", bt)], writes=kgT)
        dump("gT", gT, kgT)
        gwl = av(o_et, [2, 256], BF16)
        kgwl = ak(o_et, 1024)
        kb.dma(gwl, gwb_d[l], reads=["gwb"], writes=kgwl)
        ttg = [av(o_w + j * 2048, [512], F32) for j in range(2)]
        t2s = [av(o_w + (2 + j) * 2048, [512], F32) for j in range(2)]
        n_ = 0
        for ec in range(2):
            for tq in range(4):
                ts_ = slice(tq * 512, (tq + 1) * 512)
                bi = next_bank(0, 2)
                for cc in range(2):
                    kb.op("pe", lambda pe, cc=cc: pe.matmul(pp[bi][:, :], gwl[:, cc, ec * 128:(ec + 1) * 128], gT[:, cc, ts_],
                                                            start=(cc == 0), stop=(cc == 1)),
                          reads=kgwl + kgT, writes=[("pp", bi)])
                th2, kth2 = ttg[n_ % 2], kwk[n_ % 2]
                t_, kt_ = t2s[n_ % 2], kwk[2 + n_ % 2]
                n_ += 1
                kb.op("act", lambda a: a.activation(out=th2, in_=pp[bi][:, :], func=AF.Tanh, scale=0.25,
                                                    bias=glb[:, l, ec:ec + 1]),
                      reads=[("pp", bi), "glb"], writes=kth2)
                kb.op("dve", lambda v: v.scalar_tensor_tensor(out=t_, in0=th2, scalar=1.0, in1=gT[:, ec, ts_],
                                                               op0=ALU.add, op1=ALU.mult),
                      reads=kth2 + kgT, writes=kt_)
                kb.op("pool", lambda g_: g_.tensor_scalar(out=yT[:, 4 + ec, ts_], in0=t_, scalar1=0.125, scalar2=1.0,
                                                          op0=ALU.mult, op1=ALU.mult),
                      reads=kt_, writes=ky(4 + ec))
        gtt = [av(o_G + j * 2048, [512], F32) for j in range(2)]
        kgtt = [ak(o_G + j * 2048, 2048) for j in range(2)]
        gq = [av(o_G + 4096 + j * 2048, [512], F32) for j in range(2)]
        kgq = [ak(o_G + 4096 + j * 2048, 2048) for j in range(2)]
        gate_apply(l, 7, 4, gtt, kgtt, gq, kgq)

    kb.same_depth = 2
    stg_keys = [("hTs", 0), ("hTs", 1)] + [("yTs", j) for j in range(6)]
    stg_keys += [("xs", "m"), ("xs", "n"), ("xs", "s", 0), ("xs", "s", 1), ("xs", "e", 0), ("xs", "e", 1)]
    for e_ in ("act", "dve", "pool"):
        kb.op(e_, (lambda a: a.copy(out=small[:, 50:51], in_=small[:, 49:50])) if e_ == "act" else
              (lambda v: v.tensor_copy(out=small[:, 51 + (0 if e_ == "dve" else 1):52 + (0 if e_ == "dve" else 1)], in_=small[:, 49:50])),
              reads=stg_keys + ["nlh"], writes=["hT"] + ky(0, 8) + [("x", i_) for i_ in range(NT)])
    for s in range(nseq):
        for i in range(NT):
            kb.dma(x_res[:, i, :], x_d[s, i * 128:(i + 1) * 128, :], writes=[("x", i)])
        for l in range(depth):
            if l == 0:
                for i in range(NT):
                    rms_sq(i)
            rms_fin()
            for i in range(NT + 2):
                if i < NT:
                    kb.op("act", lambda a, i=i: a.activation(out=hs[i % 4], in_=x_res[:, i, :], func=AF.Copy,
                                                             scale=small[:, i:i + 1]),
                          reads=[("x", i), "rstd"], writes=khs[i % 4])
                    bi = 4 + i % 4
                    pt = pp[bi][:].bitcast(BF16)
                    for c in range(8):
                        kb.op("pe", lambda pe, c=c, i=i, pt=pt: pe.transpose(
                            pt[:, c * 128:(c + 1) * 128], hs[i % 4][:, c * 128:(c + 1) * 128], ident[:]),
                            reads=khs[i % 4] + ["ident"], writes=[("pp", bi)])
                if i >= 2:
                    j = i - 2
                    bj = 4 + j % 4
                    ptj = pp[bj][:].bitcast(BF16)
                    if j % 2 == 0:
                        kb.op("act", lambda a, j=j, ptj=ptj: a.copy(
                            out=hT[:, :, j * 128:(j + 1) * 128], in_=ptj.rearrange("p (c n) -> p c n", c=8)),
                            reads=[("pp", bj)], writes=["hT"])
                    else:
                        kb.op("dve", lambda v, j=j, ptj=ptj: v.tensor_copy(
                            out=hT[:, :, j * 128:(j + 1) * 128], in_=ptj.rearrange("p (c n) -> p c n", c=8)),
                            reads=[("pp", bj)], writes=["hT"])
            if "a" in mixers:
                mixer_a(l)
            if "b" in mixers:
                mixer_b(l)
            if "c" in mixers:
                mixer_c(l)
            if "d" in mixers:
                mixer_d(l)
            for mi, m in enumerate("abcd"):
                if m not in mixers:
                    kb.op("pool", lambda g, mi=mi: g.memset(yT[:, 2 * mi:2 * mi + 2, :], 0.0), writes=ky(2 * mi, 2 * mi + 2))
            if dbg and s == 0 and l == 0:
                kb.dma(dbg_d, yT[:], reads=ky(0, 8), writes=["dbgout"])
            wo, kwo = [], []
            for h in range(4):
                wo.append(av(28672 + h * 4096, [8, 256], BF16))
                kwo.append(ak(28672 + h * 4096, 4096))
                kb.dma(wo[h], wob_d[l, h], reads=["wob"], writes=kwo[h])
            for i in range(NT):
                for h in range(4):
                    bi = next_bank(0, 2)
                    for c in range(8):
                        kb.op("pe", lambda pe, c=c, i=i, bi=bi, h=h: pe.matmul(
                            pp[bi][:, 0:256], yT[:, c, i * 128:(i + 1) * 128], wo[h][:, c, :],
                            start=(c == 0), stop=(c == 7)),
                            reads=kwo[h] + ky(c), writes=[("pp", bi)])
                    kb.op("dve", lambda v, i=i, bi=bi, h=h: v.tensor_tensor(
                        out=x_res[:, i, h * 256:(h + 1) * 256], in0=pp[bi][:, 0:256],
                        in1=x_res[:, i, h * 256:(h + 1) * 256], op=ALU.add),
                        reads=[("pp", bi)], writes=[("x", i)])
                rms_sq(i)
        fg = av(8192, [D], F32)
        kb.dma(fg, fg_d, writes=ak(8192, 4096))
        rms_fin()
        for i in range(NT):
            oo = 28672 + (i % 4) * 4096
            ot = av(oo, [1024], F32)
            kb.op("dve", lambda v, i=i, ot=ot: v.scalar_tensor_tensor(
                out=ot, in0=x_res[:, i, :], scalar=small[:, i:i + 1], in1=fg, op0=ALU.mult, op1=ALU.mult),
                reads=[("x", i), "rstd"] + ak(8192, 4096), writes=ak(oo, 4096))
            kb.dma(y_d[s, i * 128:(i + 1) * 128, :], ot, reads=ak(oo, 4096), writes=["y"])
    kb.finish()
    return nc


def host_prep(inputs):
    f = np.float32
    ng = np.ascontiguousarray(np.asarray(inputs["norm_g"], f).reshape(4, 8, 128).transpose(2, 0, 1))
    fgb = np.ascontiguousarray(np.broadcast_to(np.asarray(inputs["final_g"], f)[None, :], (128, D)))
    pw = np.asarray(inputs["pool_w"], f)
    pwb = np.zeros((128, 4, 2, 128), f)
    for g in range(4):
        cb, h = g // 2, g % 2
        pwb[h * 64:(h + 1) * 64, :, cb, h * 64:(h + 1) * 64] = pw[:, g].transpose(1, 0, 2)
    psc = np.ascontiguousarray(np.asarray(inputs["pool_scale"], f).reshape(4, 2, 128).transpose(2, 0, 1))
    pcn = np.zeros((128, 2, 17), f)
    for g, w in enumerate((2, 4, 8, 16)):
        cb, h = g // 2, g % 2
        t = np.arange(S)
        cnt = np.minimum(t + w // 2, S) - np.maximum(t - w // 2, 0)
        pcn[h * 64:(h + 1) * 64, cb, 0:8] = (w / cnt[0:8])[None, :]
        pcn[h * 64:(h + 1) * 64, cb, 8:16] = (w / cnt[S - 8:S])[None, :]
        pcn[h * 64:(h + 1) * 64, cb, 16] = 1.0 / w
    inv = 10000.0 ** (-np.arange(0, 64, 2, dtype=np.float32) / 64)
    ang = np.arange(S, dtype=np.float32)[:, None] * inv[None, :]
    rope = np.stack([np.cos(ang), np.sin(ang)], 0).astype(f)
    rope_t = np.ascontiguousarray(rope.reshape(2, NT, 128, 32).transpose(2, 0, 1, 3))
    kk = np.arange(128)[:, None]
    cc = np.arange(256)[None, :]
    band = np.where(((cc - kk) >= 0) & ((cc - kk) <= 128), 0.0, -240000.0).astype(ml_dtypes.bfloat16)
    rpb = np.asarray(inputs["na_rpb"], f)
    rpbpad = np.zeros((4, 4, 15, 128), f)
    rpbpad[:, :, :, 48:79] = rpb[:, :, ::-1, :]
    kc = np.arange(64)
    c = 63 - np.arange(64)
    cs = np.clip(c - 8, 0, 48)
    colok = ((kc[:, None] >= cs[None, :]) & (kc[:, None] < cs[None, :] + 16)).astype(f)
    nam = np.zeros((128, 2368), f)
    for krl in range(2):
        for ri in range(9):
            dlt = krl - ri + 3
            if -4 <= dlt <= 3:
                nam[krl * 64:(krl + 1) * 64, ri * 64:(ri + 1) * 64] = colok
        for blk in range(28):
            nam[krl * 64:(krl + 1) * 64, 576 + blk * 64:576 + (blk + 1) * 64] = colok
    are = np.asarray(inputs["ssm_a_re"], f)
    aim = np.asarray(inputs["ssm_a_im"], f)
    ldt = np.asarray(inputs["ssm_log_dt"], f)
    lam = np.stack([are.transpose(0, 1, 3, 2), aim.transpose(0, 1, 3, 2),
                    np.broadcast_to(ldt[:, :, None, :], (4, 2, 64, 16))], axis=-1)
    lam = np.ascontiguousarray(lam.reshape(4, 128, 16, 3))
    bre = np.asarray(inputs["ssm_b_re"], f)
    bim = np.asarray(inputs["ssm_b_im"], f)
    bp1 = np.stack([bre.transpose(0, 2, 1, 3), bim.transpose(0, 2, 1, 3)], axis=-1)
    bp = np.ascontiguousarray(np.concatenate([bp1, bp1], axis=1))
    cre = np.asarray(inputs["ssm_c_re"], f)
    cim = np.asarray(inputs["ssm_c_im"], f)
    cp1 = np.stack([cre.transpose(0, 1, 4, 2, 3), cim.transpose(0, 1, 4, 2, 3)], axis=-1)
    cp = np.ascontiguousarray(cp1.reshape(4, 128, 16, 16, 2))
    sd = np.asarray(inputs["ssm_d"], f).reshape(4, 16, 16)
    sdt = np.ascontiguousarray(sd.transpose(2, 0, 1))
    glw = np.ascontiguousarray(np.asarray(inputs["glu_w"], f).reshape(4, 2, 128, 256).transpose(0, 2, 1, 3))
    glbt = np.ascontiguousarray(np.asarray(inputs["glu_b"], f).reshape(4, 2, 128).transpose(2, 0, 1))
    return {
        "ssm_lam": lam, "ssm_bp": bp, "ssm_cp": cp, "ssm_dt": sdt, "glu_w_t": glw, "glu_b_t": glbt,
        "rpbpad": rpbpad, "na_mask": nam.astype(ml_dtypes.bfloat16),
        "rope_t": rope_t, "band": band,
        "pool_w_blk": pwb, "pool_scale_t": psc, "pool_const": pcn,
        "w_in": np.ascontiguousarray(inputs["w_in"], dtype=f),
        "w_out": np.ascontiguousarray(inputs["w_out"], dtype=f),
        "norm_g_t": ng,
        "final_g_b": fgb,
    }


def kernel(**inputs):
    xp = np.asarray(inputs["x_prompt"], np.float32)
    xs = np.asarray(inputs["x_sample"], np.float32)
    shared = host_prep(inputs)
    nc = build()
    in_maps = []
    for c in range(8):
        xc = np.concatenate([xp[4 * c:4 * c + 4], xs[c:c + 1]], axis=0)
        m = dict(shared)
        m["x"] = np.ascontiguousarray(xc)
        in_maps.append(m)
    res = run_bass_kernel_spmd(nc, in_maps, core_ids=list(range(8)))
    yp = np.empty_like(xp)
    ys = np.empty_like(xs)
    for c in range(8):
        y = res.results[c]["y"]
        yp[4 * c:4 * c + 4] = y[0:4]
        ys[c] = y[4]
    return (yp, ys)
```

```python
import math
from contextlib import ExitStack
import numpy as np
import ml_dtypes
import concourse.bass as bass
import concourse.mybir as mybir
from concourse.bass_utils import run_bass_kernel_spmd

F32 = mybir.dt.float32
BF16 = mybir.dt.bfloat16
I32 = mybir.dt.int32
AF = mybir.ActivationFunctionType
ALU = mybir.AluOpType
AX = mybir.AxisListType

S = 2048
D = 1024
NT = 16
EPS = 1e-6
NDMA = 24


class KB:
    def __init__(self):
        self.nc = bass.Bass("TRN2", target_bir_lowering=False)
        nc = self.nc
        self.es = ExitStack()
        self.eng = {"pe": nc.tensor, "act": nc.scalar, "dve": nc.vector, "pool": nc.gpsimd, "sp": nc.sync}
        self.sem = {}
        for e in ["pe", "act", "dve", "pool"]:
            self.sem[e] = self.es.enter_context(nc.semaphore("s_" + e))
        for i in range(NDMA):
            self.sem[("dma", i)] = self.es.enter_context(nc.semaphore("s_dma%d" % i))
        self.cnt = {e: 0 for e in ["pe", "act", "dve", "pool"]}
        self.seen = {e: {} for e in ["pe", "act", "dve", "pool", "sp"]}
        self.res = {}
        self.ndma = 0
        self.same_eng = {"act", "dve", "pool"}
        self.same_depth = 1000000
        self.clock = {}

    def sb(self, name, shape, dt):
        return self.es.enter_context(self.nc.sbuf_tensor(name, shape, dt))

    def ps(self, name, shape, dt):
        return self.es.enter_context(self.nc.psum_tensor(name, shape, dt))

    def dram(self, name, shape, dt, kind="Internal"):
        return self.nc.dram_tensor(name, shape, dt, kind=kind).ap()

    def _wait(self, e, key, val):
        if key == e:
            if e not in self.same_eng or val < self.cnt[e] - self.same_depth + 1:
                return
        if self.seen[e].get(key, 0) >= val:
            return
        self.eng[e].wait_ge(self.sem[key], val)
        self.seen[e][key] = val
        clk = self.clock.get((key, val))
        if clk:
            se = self.seen[e]
            for k2, v2 in clk.items():
                if se.get(k2, 0) < v2:
                    se[k2] = v2

    def _deps(self, e, reads, writes):
        for r in reads:
            st = self.res.get(r)
            if st:
                for k, v in st["w"].items():
                    self._wait(e, k, v)
        for w in writes:
            st = self.res.get(w)
            if st:
                for k, v in st["w"].items():
                    self._wait(e, k, v)
                for k, v in st["r"].items():
                    self._wait(e, k, v)

    def _mark(self, key, val, reads, writes):
        for r in reads:
            st = self.res.setdefault(r, {"w": {}, "r": {}})
            st["r"][key] = max(st["r"].get(key, 0), val)
        for w in writes:
            st = self.res.setdefault(w, {"w": {}, "r": {}})
            st["w"][key] = max(st["w"].get(key, 0), val)

    def op(self, e, fn, reads=(), writes=()):
        self._deps(e, reads, writes)
        inst = fn(self.eng[e])
        self.cnt[e] += 1
        inst.then_inc(self.sem[e], 1)
        snap = dict(self.seen[e])
        snap[e] = self.cnt[e]
        self.clock[(e, self.cnt[e])] = snap
        self._mark(e, self.cnt[e], reads, writes)

    def dma(self, out, in_, reads=(), writes=(), q="sp"):
        n = self.ndma
        self.ndma += 1
        i = n % NDMA
        key = ("dma", i)
        if n >= NDMA:
            self._wait(q, key, 16 * (n // NDMA))
        self._deps(q, reads, writes)
        self.eng[q].dma_start(out=out, in_=in_).then_inc(self.sem[key], 16)
        self.clock[(key, 16 * (n // NDMA + 1))] = dict(self.seen[q])
        self._mark(key, 16 * (n // NDMA + 1), reads, writes)

    def barrier(self, engines=("pe", "act", "dve", "pool")):
        for e in engines:
            for e2 in engines:
                if e2 != e and self.cnt[e2] > 0:
                    self._wait(e, e2, self.cnt[e2])

    def finish(self):
        for k in list(self.sem.keys()):
            if isinstance(k, tuple):
                i = k[1]
                uses = (self.ndma - 1 - i) // NDMA + 1 if self.ndma > i else 0
                if uses > 0:
                    self._wait("sp", k, 16 * uses)
            else:
                if self.cnt[k] > 0:
                    self._wait("sp", k, self.cnt[k])
        self.es.close()


def build(nseq=5, depth=4, mixers=("a", "b", "c", "d"), dbg=False):
    kb = KB()
    nc = kb.nc
    x_d = nc.dram_tensor("x", [nseq, S, D], F32, kind="ExternalInput").ap()
    y_d = nc.dram_tensor("y", [nseq, S, D], F32, kind="ExternalOutput").ap()
    w_in_d = nc.dram_tensor("w_in", [4, D, 3072], F32, kind="ExternalInput").ap()
    w_out_d = nc.dram_tensor("w_out", [4, D, D], F32, kind="ExternalInput").ap()
    ng_d = nc.dram_tensor("norm_g_t", [128, 4, 8], F32, kind="ExternalInput").ap()
    fg_d = nc.dram_tensor("final_g_b", [128, D], F32, kind="ExternalInput").ap()
    pwb_d = nc.dram_tensor("pool_w_blk", [128, 4, 2, 128], F32, kind="ExternalInput").ap()
    psc_d = nc.dram_tensor("pool_scale_t", [128, 4, 2], F32, kind="ExternalInput").ap()
    pcn_d = nc.dram_tensor("pool_const", [128, 2, 17], F32, kind="ExternalInput").ap()
    dbg_d = nc.dram_tensor("dbg", [128, 8, S], BF16, kind="ExternalOutput").ap() if dbg else None
    rope_d = nc.dram_tensor("rope_t", [128, 2, NT, 32], F32, kind="ExternalInput").ap()
    band_d = nc.dram_tensor("band", [128, 256], BF16, kind="ExternalInput").ap()
    rpb_t = nc.dram_tensor("rpbpad", [4, 4, 15, 128], F32, kind="ExternalInput")
    nam_d = nc.dram_tensor("na_mask", [128, 2368], BF16, kind="ExternalInput").ap()
    et_d = kb.dram("et", [4, 4, 128, 2368], BF16)
    lam_d = nc.dram_tensor("ssm_lam", [4, 128, 16, 3], F32, kind="ExternalInput").ap()
    bp_d = nc.dram_tensor("ssm_bp", [4, 128, 16, 16, 2], F32, kind="ExternalInput").ap()
    cp_d = nc.dram_tensor("ssm_cp", [4, 128, 16, 16, 2], F32, kind="ExternalInput").ap()
    dt_d = nc.dram_tensor("ssm_dt", [16, 4, 16], F32, kind="ExternalInput").ap()
    glw_d = nc.dram_tensor("glu_w_t", [4, 128, 2, 256], F32, kind="ExternalInput").ap()
    glb_d = nc.dram_tensor("glu_b_t", [128, 4, 2], F32, kind="ExternalInput").ap()
    sblk_d = kb.dram("sblk", [4, 16, 128, 11, 128], BF16)
    etab_d = kb.dram("etab", [4, 4, 128, 4, 2, 128], F32)
    ktab_t = nc.dram_tensor("ktab", [4, 16, 31, 16, 16], F32, kind="Internal")
    ktab_d = ktab_t.ap()
    gwb_d = kb.dram("gwb", [4, 128, 2, 256], BF16)
    wib_d = kb.dram("wib", [4, 12, 128, 8, 256], BF16)
    wob_d = kb.dram("wob", [4, 4, 128, 8, 256], BF16)

    x_res = kb.sb("x_res", [128, NT, D], F32)
    hT = kb.sb("hT", [128, 8, S], BF16)
    yT = kb.sb("yT", [128, 8, S], BF16)
    ARENA = 56 * 1024
    arena = kb.sb("arena", [128, ARENA // 2], BF16)
    wb = [kb.sb("wb%d" % i, [128, 8, 256], BF16) for i in range(3)]
    ng = kb.sb("ng", [128, 4, 8], F32)
    ident = kb.sb("ident", [128, 128], BF16)
    identf = kb.sb("identf", [128, 128], F32)
    small = kb.sb("small", [128, 64], F32)
    pwb = kb.sb("pwb", [128, 4, 2, 128], BF16)
    psc = kb.sb("psc", [128, 4, 2], F32)
    pcn = kb.sb("pcn", [128, 2, 17], F32)
    ropet = kb.sb("ropet", [128, 2, NT, 32], F32)
    band = kb.sb("band_sb", [128, 256], BF16)
    rho_sb = kb.sb("rho_sb", [128, 4, 16], F32)
    glb = kb.sb("glb", [128, 4, 2], F32)
    pp = [kb.ps("pp%d" % i, [128, 512], F32) for i in range(8)]

    GR = 256

    def ak(off, nbytes):
        return [("ar", j) for j in range(off // GR, (off + nbytes - 1) // GR + 1)]

    dumped = set()

    def dump(name, src, reads):
        if not dbg or name in dumped:
            return
        dumped.add(name)
        shp = list(src.shape)
        dd = nc.dram_tensor("d_" + name, shp, src.dtype, kind="ExternalOutput").ap()
        kb.dma(dd, src, reads=reads, writes=["dump_" + name])

    def ky(c0, c1=None, h=None):
        c1 = c0 + 1 if c1 is None else c1
        hs_ = (0, 1) if h is None else (h,)
        return [("yT", c, hh) for c in range(c0, c1) for hh in hs_]

    def av(off, shape, dt):
        n = int(np.prod(shape))
        esz = 4 if dt in (F32, I32) else 2
        a = arena[:, off // 2: off // 2 + n * esz // 2]
        if dt != BF16:
            a = a.bitcast(dt)
        if len(shape) == 2:
            return a.rearrange("p (a b) -> p a b", a=shape[0])
        if len(shape) == 3:
            return a.rearrange("p (a b c) -> p a b c", a=shape[0], b=shape[1])
        return a

    pstate = {"n": 0}

    def next_bank(lo=0, hi=2):
        i = lo + pstate.setdefault((lo, hi), 0) % (hi - lo)
        pstate[(lo, hi)] += 1
        return i

    hs = [av(16384 + i * 2048, [D], BF16) for i in range(6)]
    khs = [ak(16384 + i * 2048, 2048) for i in range(6)]
    kb.dma(ng[:], ng_d, writes=["ng"])
    pwst = av(8192, [4 * 2 * 128], F32)
    kb.dma(pwst, pwb_d.rearrange("p a b c -> p (a b c)"), writes=ak(8192, 4096))
    kb.op("dve", lambda v: v.tensor_copy(out=pwb[:].rearrange("p a b c -> p (a b c)"), in_=pwst), reads=ak(8192, 4096), writes=["pwb"])
    kb.dma(psc[:], psc_d, writes=["psc"])
    kb.op("dve", lambda v: v.tensor_scalar(out=psc[:], in0=psc[:], scalar1=0.5, scalar2=None, op0=ALU.mult), reads=["psc"], writes=["psc"])
    kb.dma(pcn[:], pcn_d, writes=["pcn"])
    kb.dma(ropet[:], rope_d, writes=["ropet"])
    kb.dma(band[:], band_d, writes=["band"])
    io = av(0, [128], I32)
    kb.op("pool", lambda g: g.iota(io, [[1, 128]], base=0, channel_multiplier=-1), writes=ak(0, 512))
    kb.op("dve", lambda v: v.tensor_single_scalar(out=identf[:], in_=io, scalar=0, op=ALU.is_equal),
          reads=ak(0, 512), writes=["identf"])
    kb.op("dve", lambda v: v.tensor_copy(out=ident[:], in_=identf[:]), reads=["identf"], writes=["ident"])

    hTf = hT[:].rearrange("p a b -> p (a b)")
    yTf = yT[:].rearrange("p a b -> p (a b)")

    def tv(flat, off, n, dt):
        esz = 4 if dt == F32 else 2
        v = flat[:, off // 2: off // 2 + n * esz // 2]
        return v.bitcast(dt) if dt != BF16 else v

    wtasks = {}
    for l in range(depth):
        lst = []
        for c in range(8):
            def w_in_task(l=l, c=c):
                k = c % 2
                st = tv(hTf, k * 12288, 3072, F32)
                sb_ = tv(yTf, k * 6144, 3072, BF16)
                kst, ksb = [("hTs", k)], [("yTs", k)]
                kb.dma(st, w_in_d[l, c * 128:(c + 1) * 128, :], writes=kst, q="act")
                kb.op("pool" if k else "dve",
                      lambda v: v.tensor_scalar(out=sb_, in0=st, scalar1=ng[:, l, c:c + 1], scalar2=1.0,
                                                op0=ALU.mult, op1=ALU.mult),
                      reads=kst + ["ng"], writes=ksb)
                kb.dma(wib_d[l, :, :, c, :].rearrange("b p n -> p b n"), sb_.rearrange("p (b n) -> p b n", b=12),
                       reads=ksb, writes=["wib"], q="act")

            def w_out_task(l=l, c=c):
                k = c % 2
                st = tv(yTf, 12288 + k * 4096, 1024, F32)
                sb_ = tv(yTf, 20480 + k * 2048, 1024, BF16)
                kst, ksb = [("yTs", 2 + k)], [("yTs", 4 + k)]
                kb.dma(st, w_out_d[l, c * 128:(c + 1) * 128, :], writes=kst, q="act")
                kb.op("dve" if k else "pool", lambda v: v.tensor_copy(out=sb_, in_=st), reads=kst, writes=ksb)
                kb.dma(wob_d[l, :, :, c, :].rearrange("h p n -> p h n"), sb_.rearrange("p (h n) -> p h n", h=4),
                       reads=ksb, writes=["wob"], q="act")
            lst.append(w_in_task)
            lst.append(w_out_task)
        wtasks[l] = lst
    if "c" not in mixers:
        for l in range(depth):
            for t_ in wtasks[l]:
                t_()

    xf = x_res[:].rearrange("p a b -> p (a b)").bitcast(BF16)
    na_tasks = {}
    if "d" in mixers:
        nmask = tv(xf, 0, 2368, BF16)
        kb.dma(nmask, nam_d, writes=[("xs", "m")])
        negm = tv(xf, 33152, 2368, F32)
        knegm = [("xs", "n")]
        kb.op("dve", lambda v: v.tensor_scalar(out=negm, in0=nmask, scalar1=-1.0, scalar2=240000.0, op0=ALU.add, op1=ALU.mult),
              reads=[("xs", "m")], writes=knegm)
        for l in range(depth):
            for h in range(4):
                def na_task(l=l, h=h):
                    k = (l * 4 + h) % 2
                    o_st = 4736 + k * 14208
                    stg = tv(xf, o_st, 2368, F32)
                    eo = tv(xf, o_st + 9472, 2368, BF16)
                    kst, keo = [("xs", "s", k)], [("xs", "e", k)]
                    base = (l * 4 + h) * 15 * 128
                    for krl in range(2):
                        ps_ = slice(krl * 64, krl * 64 + 64)
                        src = bass.AP(rpb_t, base + (4 - krl) * 128, [[1, 64], [128, 9], [1, 64]])
                        kb.dma(stg[ps_, 0:576].rearrange("p (r c) -> p r c", r=9), src, writes=kst)
                        for sr in range(7):
                            r = sr if sr < 4 else 25 + sr
                            if sr < 4:
                                i0 = 1 - krl + r
                                dst = stg[ps_, 576:1600].rearrange("p (u s c) -> p u s c", u=4, s=4)[:, :, sr, :]
                            else:
                                i0 = r - 23 - krl
                                dst = stg[ps_, 1600:2368].rearrange("p (u s c) -> p u s c", u=4, s=3)[:, :, sr - 4, :]
                            src = bass.AP(rpb_t, base + i0 * 128, [[1, 64], [256, 4], [1, 64]])
                            kb.dma(dst, src, writes=kst)
                    st3 = stg.rearrange("p (b c) -> p b c", c=64)
                    kb.op("dve", lambda v: v.scalar_tensor_tensor(
                        out=st3, in0=st3, scalar=8.0, in1=nmask.rearrange("p (b c) -> p b c", c=64),
                        op0=ALU.mult, op1=ALU.mult), reads=kst + [("xs", "m")], writes=kst)
                    kb.op("dve", lambda v: v.tensor_tensor(
                        out=eo.rearrange("p (b c) -> p b c", c=64), in0=st3[:, :, ::-1],
                        in1=negm.rearrange("p (b c) -> p b c", c=64)[:, :, ::-1],
                        op=ALU.add), reads=kst + knegm, writes=keo)
                    kb.dma(et_d[l, h], eo, reads=keo, writes=["et"])
                na_tasks[(l, h)] = na_task
        if "c" not in mixers:
            for l in range(depth):
                for h in range(4):
                    na_tasks[(l, h)]()

    TWO_PI = 2.0 * math.pi

    def s5_precompute(l):
        kb.same_depth = 1000000
        st = {"o": 0}

        def A(shape, dt=F32):
            n = int(np.prod(shape))
            nb = n * (4 if dt in (F32, I32) else 2)
            off = st["o"]
            st["o"] = off + (nb + 63) // 64 * 64
            v = arena[:, off // 2: off // 2 + nb // 2]
            if dt != BF16:
                v = v.bitcast(dt)
            if len(shape) == 2:
                v = v.rearrange("p (a b) -> p a b", a=shape[0])
            elif len(shape) == 3:
                v = v.rearrange("p (a b c) -> p a b c", a=shape[0], b=shape[1])
            return v, ak(off, nb)

        def tt_(e, out, a, b, op, rd, wr):
            kb.op(e, lambda v: v.tensor_tensor(out=out, in0=a, in1=b, op=op), reads=rd, writes=wr)

        def ts_(e, out, a, s1, s2, op0, op1, rd, wr):
            kb.op(e, lambda v: v.tensor_scalar(out=out, in0=a, scalar1=s1, scalar2=s2, op0=op0, op1=op1) if s2 is not None
                  else v.tensor_scalar(out=out, in0=a, scalar1=s1, scalar2=None, op0=op0), reads=rd, writes=wr)

        def frac_(T, kT, TI, kTI, TF, kTF):
            MAGIC = 12582912.0
            ts_("dve", TF, T, MAGIC, None, ALU.add, None, kT, kTF)
            ts_("dve", TF, TF, -MAGIC, None, ALU.add, None, kTF, kTF)
            tt_("dve", T, T, TF, ALU.subtract, kT + kTF, kT)
            kb.op("dve", lambda v: v.tensor_single_scalar(out=TF, in_=T, scalar=0.5, op=ALU.is_gt), reads=kT, writes=kTF)
            tt_("dve", T, T, TF, ALU.subtract, kT + kTF, kT)
            kb.op("dve", lambda v: v.tensor_single_scalar(out=TF, in_=T, scalar=-0.5, op=ALU.is_lt), reads=kT, writes=kTF)
            tt_("dve", T, T, TF, ALU.add, kT + kTF, kT)

        def sincos_(T, kT, SN, kSN, CS, kCS, TI, kTI, TF, kTF):
            frac_(T, kT, TI, kTI, TF, kTF)
            kb.op("act", lambda a: a.activation(out=SN, in_=T, func=AF.Sin, scale=6.28318), reads=kT, writes=kSN)
            ts_("dve", T, T, 0.25, None, ALU.add, None, kT, kT)
            kb.op("dve", lambda v: v.tensor_single_scalar(out=TF, in_=T, scalar=0.5, op=ALU.is_gt), reads=kT, writes=kTF)
            tt_("dve", T, T, TF, ALU.subtract, kT + kTF, kT)
            kb.op("act", lambda a: a.activation(out=CS, in_=T, func=AF.Sin, scale=6.28318), reads=kT, writes=kCS)

        lam, klam = A([16, 3])
        Bp, kBp = A([16, 16, 2])
        Cp, kCp = A([16, 16, 2])
        Dt, kDt = A([16])
        kb.dma(lam, lam_d[l], writes=klam)
        kb.dma(Bp, bp_d[l], writes=kBp)
        kb.dma(Cp, cp_d[l], writes=kCp)
        kb.dma(Dt[0:16, :], dt_d[:, l, :], writes=kDt)
        BBr, kBBr = A([16, 16])
        BBi, kBBi = A([16, 16])
        nBBi, knBBi = A([16, 16])
        WP, kWP = A([2, 16, 16])
        WC, kWC = A([2, 16, 16])
        WK, kWK = A([2, 16, 31])
        Dd, kDd = A([16, 16])
        E15, kE15 = A([496])
        kb.op("dve", lambda v: v.tensor_tensor(out=Dd[0:16], in0=identf[0:16, 0:16].unsqueeze(1).broadcast_to([16, 16, 16]),
                                               in1=Dt[0:16, :].unsqueeze(2).broadcast_to([16, 16, 16]), op=ALU.mult),
              reads=kDt + ["identf"], writes=kDd)
        kb.op("pool", lambda g_: g_.memset(E15[0:16, :], 0.0), writes=kE15)
        kb.op("pool", lambda g_: g_.tensor_copy(out=E15[0:16, 240:256], in_=identf[0:16, 0:16]), reads=["identf"], writes=kE15)
        mark = st["o"]
        dtt, kdt = A([16])
        xr, kxr = A([16])
        tht, ktht = A([16])
        NNi, kNNi = A([128], I32)
        NN, kNN = A([128])
        TI, kTI = A([512], I32)
        TF, kTF = A([512])
        ARG, kARG = A([16, 17])
        MAG, kMAG = A([16, 17])
        SN, kSN = A([16, 17])
        CS, kCS = A([16, 17])
        WR, kWR = A([16, 17])
        WI, kWI = A([16, 17])
        kb.op("act", lambda a: a.activation(out=dtt, in_=lam[:, :, 2], func=AF.Exp), reads=klam, writes=kdt)
        tt_("dve", xr, lam[:, :, 0], dtt, ALU.mult, klam + kdt, kxr)
        tt_("dve", tht, lam[:, :, 1], dtt, ALU.mult, klam + kdt, ktht)
        ts_("dve", tht, tht, 1.0 / TWO_PI, None, ALU.mult, None, ktht, ktht)
        frac_(tht, ktht, TI[:, 0:16], kTI, TF[:, 0:16], kTF)
        kb.op("pool", lambda g: g.iota(NNi, [[1, 128]], base=0, channel_multiplier=0), writes=kNNi)
        kb.op("dve", lambda v: v.tensor_copy(out=NN, in_=NNi), reads=kNNi, writes=kNN)
        nb17 = NN[:, 0:17].unsqueeze(1).broadcast_to([128, 16, 17])
        tt_("dve", ARG, tht.unsqueeze(2).broadcast_to([128, 16, 17]), nb17, ALU.mult, ktht + kNN, kARG)
        tt_("dve", MAG, xr.unsqueeze(2).broadcast_to([128, 16, 17]), nb17, ALU.mult, kxr + kNN, kMAG)
        kb.op("act", lambda a: a.activation(out=MAG, in_=MAG, func=AF.Exp), reads=kMAG, writes=kMAG)
        f2 = lambda t: t.rearrange("p a b -> p (a b)")
        sincos_(f2(ARG), kARG, f2(SN), kSN, f2(CS), kCS, TI[:, 0:272], kTI, TF[:, 0:272], kTF)
        tt_("dve", WR, MAG, CS, ALU.mult, kMAG + kCS, kWR)
        tt_("dve", WI, MAG, SN, ALU.mult, kMAG + kSN, kWI)
        if l == 0:
            dump("WR", WR, kWR)
            dump("WI", WI, kWI)
            dump("dtt", dtt, kdt)
            dump("xr", xr, kxr)
            dump("tht", tht, ktht)
            dump("NN", NN, kNN)
            dump("MAG", MAG, kMAG)
            dump("SN", SN, kSN)
            dump("CS", CS, kCS)
            dump("lam", lam, klam)
        kb.op("act", lambda a: a.copy(out=rho_sb[:, l, :], in_=MAG[:, :, 16]), reads=kMAG, writes=["rho"])
        den, kden = A([16])
        t1, kt1 = A([16])
        t2, kt2 = A([16])
        gr, kgr = A([16])
        gi, kgi = A([16])
        lr_, li_ = lam[:, :, 0], lam[:, :, 1]
        tt_("dve", den, lr_, lr_, ALU.mult, klam, kden)
        tt_("dve", t1, li_, li_, ALU.mult, klam, kt1)
        tt_("dve", den, den, t1, ALU.add, kden + kt1, kden)
        kb.op("dve", lambda v: v.reciprocal(out=den, in_=den), reads=kden, writes=kden)
        ts_("dve", t1, WR[:, :, 1], -1.0, None, ALU.add, None, kWR, kt1)
        tt_("dve", gr, t1, lr_, ALU.mult, kt1 + klam, kgr)
        tt_("dve", t2, WI[:, :, 1], li_, ALU.mult, kWI + klam, kt2)
        tt_("dve", gr, gr, t2, ALU.add, kgr + kt2, kgr)
        tt_("dve", gr, gr, den, ALU.mult, kgr + kden, kgr)
        tt_("dve", gi, WI[:, :, 1], lr_, ALU.mult, kWI + klam, kgi)
        tt_("dve", t2, t1, li_, ALU.mult, kt1 + klam, kt2)
        tt_("dve", gi, gi, t2, ALU.subtract, kgi + kt2, kgi)
        tt_("dve", gi, gi, den, ALU.mult, kgi + kden, kgi)
        u1, ku1 = A([16, 16])
        grb = gr.unsqueeze(2).broadcast_to([128, 16, 16])
        gib = gi.unsqueeze(2).broadcast_to([128, 16, 16])
        Br_, Bi_ = Bp[:, :, :, 0], Bp[:, :, :, 1]
        tt_("dve", BBr, grb, Br_, ALU.mult, kgr + kBp, kBBr)
        tt_("dve", u1, gib, Bi_, ALU.mult, kgi + kBp, ku1)
        tt_("dve", BBr, BBr, u1, ALU.subtract, kBBr + ku1, kBBr)
        tt_("dve", BBi, grb, Bi_, ALU.mult, kgr + kBp, kBBi)
        tt_("dve", u1, gib, Br_, ALU.mult, kgi + kBp, ku1)
        tt_("dve", BBi, BBi, u1, ALU.add, kBBi + ku1, kBBi)
        ts_("dve", nBBi, BBi, -1.0, None, ALU.mult, None, kBBi, knBBi)
        kb.op("pool", lambda g: g.memset(WK, 0.0), writes=kWK)
        F_, B_ = slice(0, 64), slice(64, 128)
        for ri, W_, kW_ in ((0, WR, kWR), (1, WI, kWI)):
            cp = lambda dst, src, wk: kb.op("act", lambda a: a.copy(out=dst, in_=src), reads=kW_, writes=wk)
            cp(WP[F_, ri, :, 0:8], W_[F_, :, 8:16], kWP)
            cp(WP[F_, ri, :, 8:16], W_[F_, :, 0:8], kWP)
            cp(WP[B_, ri, :, 0:8], W_[B_, :, 7::-1], kWP)
            cp(WP[B_, ri, :, 8:16], W_[B_, :, 15:7:-1], kWP)
            cp(WC[F_, ri, :, :], W_[F_, :, 1:17], kWC)
            cp(WC[B_, ri, :, :], W_[B_, :, 16:0:-1], kWC)
            cp(WK[F_, ri, :, 15:31], W_[F_, :, 0:16], kWK)
            cp(WK[B_, ri, :, 0:16], W_[B_, :, 15::-1], kWK)
        pht, kpht = A([16])
        ts_("dve", pht, tht, 16.0, None, ALU.mult, None, ktht, kpht)
        frac_(pht, kpht, TI[:, 0:16], kTI, TF[:, 0:16], kTF)
        EA, kEA = A([4, 128])
        ES, kES = A([4, 2, 128])
        for q in range(4):
            tt_("dve", EA, pht[:, 4 * q:4 * q + 4].unsqueeze(2).broadcast_to([128, 4, 128]),
                NN.unsqueeze(1).broadcast_to([128, 4, 128]), ALU.mult, kpht + kNN, kEA)
            sincos_(f2(EA), kEA, ES[:, :, 1, :], kES, ES[:, :, 0, :], kES, TI[:, 0:512], kTI, TF[:, 0:512], kTF)
            kb.dma(etab_d[l, q], ES, reads=kES, writes=["etab"])
            if l == 0 and q == 0:
                dump("ES0", ES, kES)
        gst, kgst = A([2, 256])
        gbf, kgbf = A([2, 256], BF16)
        kb.dma(gst, glw_d[l], writes=kgst)
        kb.op("act", lambda a: a.copy(out=gbf, in_=gst), reads=kgst, writes=kgbf)
        kb.dma(gwb_d[l], gbf, reads=kgbf, writes=["gwb"])
        st["o"] = mark
        bufs = []
        for par in range(2):
            d_ = {}
            d_["PPr"], d_["kPPr"] = A([2, 128])
            d_["PPi"], d_["kPPi"] = A([2, 128])
            d_["q1"], d_["kq1"] = A([2, 128])
            d_["q2"], d_["kq2"] = A([2, 128])
            d_["Rr"], d_["kRr"] = A([31, 16])
            d_["Ri"], d_["kRi"] = A([31, 16])
            d_["r1"], d_["kr1"] = A([31, 16])
            d_["r2"], d_["kr2"] = A([31, 16])
            d_["Ks"], d_["kKs"] = A([496])
            d_["Ms"], d_["kMs"] = A([3, 128])
            d_["blk"], d_["kblk"] = A([11, 128], BF16)
            bufs.append(d_)
        assert st["o"] <= ARENA, st["o"]
        kb.same_depth = 3
        pend_tail = [None]
        for g in range(16):
            wtasks[l][g]()
            if g % 4 == 0 and (l, g // 4) in na_tasks:
                na_tasks[(l, g // 4)]()
            d_ = bufs[g % 2]
            PPr, PPi, q1, q2 = d_["PPr"], d_["PPi"], d_["q1"], d_["q2"]
            kPPr, kPPi, kq1, kq2 = d_["kPPr"], d_["kPPi"], d_["kq1"], d_["kq2"]
            blk, kblk = d_["blk"], d_["kblk"]
            v4 = lambda t: t.rearrange("p s (j c) -> p s j c", j=8)
            wpr = WP[:, 0, g, :].rearrange("p (s j) -> p s j", s=2).unsqueeze(3).broadcast_to([128, 2, 8, 16])
            wpi = WP[:, 1, g, :].rearrange("p (s j) -> p s j", s=2).unsqueeze(3).broadcast_to([128, 2, 8, 16])
            bbr = BBr[:, g, :].unsqueeze(1).unsqueeze(1).broadcast_to([128, 2, 8, 16])
            bbi = BBi[:, g, :].unsqueeze(1).unsqueeze(1).broadcast_to([128, 2, 8, 16])
            e1, e2 = ("dve", "pool") if g % 2 == 0 else ("pool", "dve")
            tt_(e1, v4(q1), wpr, bbr, ALU.mult, kWP + kBBr, kq1)
            tt_(e2, v4(q2), wpi, bbi, ALU.mult, kWP + kBBi, kq2)
            tt_(e1, PPr, q1, q2, ALU.subtract, kq1 + kq2, kPPr)
            tt_(e2, v4(q1), wpr, bbi, ALU.mult, kWP + kBBi, kq1)
            tt_(e1, v4(q2), wpi, bbr, ALU.mult, kWP + kBBr, kq2)
            tt_(e2, PPi, q1, q2, ALU.add, kq1 + kq2, kPPi)
            bt = next_bank(6, 8)
            for j, (src, ksrc) in enumerate(((PPr[:, 0, :], kPPr), (PPr[:, 1, :], kPPr), (PPi[:, 0, :], kPPi), (PPi[:, 1, :], kPPi))):
                kb.op("pe", lambda pe, j=j, src=src: pe.transpose(pp[bt][:, j * 128:(j + 1) * 128], src, identf[:]),
                      reads=ksrc + ["identf"], writes=[("pp", bt)])
            kb.op("act", lambda a: a.copy(out=blk[:, 3:7, :], in_=pp[bt][:, :].rearrange("p (a b) -> p a b", a=4)),
                  reads=[("pp", bt)], writes=kblk)
            c4 = lambda t: t.rearrange("p s (i c) -> p (s i) c", i=8)
            wcr = WC[:, 0, g, :].unsqueeze(2).broadcast_to([128, 16, 16])
            wci = WC[:, 1, g, :].unsqueeze(2).broadcast_to([128, 16, 16])
            cr = Cp[:, g, :, 0].unsqueeze(1).broadcast_to([128, 16, 16])
            ci = Cp[:, g, :, 1].unsqueeze(1).broadcast_to([128, 16, 16])
            tt_(e1, c4(q1), cr, wcr, ALU.mult, kCp + kWC, kq1)
            tt_(e2, c4(q2), ci, wci, ALU.mult, kCp + kWC, kq2)
            tt_(e1, blk[:, 7:10:2, :], q1, q2, ALU.subtract, kq1 + kq2, kblk)
            tt_(e2, c4(q1), cr, wci, ALU.mult, kCp + kWC, kq1)
            tt_(e1, c4(q2), ci, wcr, ALU.mult, kCp + kWC, kq2)
            kb.op("dve", lambda v: v.scalar_tensor_tensor(out=blk[:, 8:11:2, :], in0=q1, scalar=-1.0, in1=q2,
                                                           op0=ALU.mult, op1=ALU.subtract),
                  reads=kq1 + kq2, writes=kblk)
            Rr, Ri, r1, r2 = d_["Rr"], d_["Ri"], d_["r1"], d_["r2"]
            kRr, kRi, kr1, kr2 = d_["kRr"], d_["kRi"], d_["kr1"], d_["kr2"]
            wkr = WK[:, 0, g, :].unsqueeze(2).broadcast_to([128, 31, 16])
            wki = WK[:, 1, g, :].unsqueeze(2).broadcast_to([128, 31, 16])
            cr3 = Cp[:, g, :, 0].unsqueeze(1).broadcast_to([128, 31, 16])
            ci3 = Cp[:, g, :, 1].unsqueeze(1).broadcast_to([128, 31, 16])
            tt_(e1, r1, cr3, wkr, ALU.mult, kCp + kWK, kr1)
            tt_(e2, r2, ci3, wki, ALU.mult, kCp + kWK, kr2)
            tt_(e1, Rr, r1, r2, ALU.subtract, kr1 + kr2, kRr)
            tt_(e2, r1, cr3, wki, ALU.mult, kCp + kWK, kr1)
            tt_(e1, r2, ci3, wkr, ALU.mult, kCp + kWK, kr2)
            tt_(e2, Ri, r1, r2, ALU.add, kr1 + kr2, kRi)
            bk = next_bank(4, 6)
            kb.op("pe", lambda pe: pe.matmul(pp[bk][0:16, 0:496], BBr[:, g, :], Rr.rearrange("p a b -> p (a b)"),
                                             start=True, stop=False), reads=kBBr + kRr, writes=[("pp", bk)])
            kb.op("pe", lambda pe: pe.matmul(pp[bk][0:16, 0:496], nBBi[:, g, :], Ri.rearrange("p a b -> p (a b)"),
                                             start=False, stop=False), reads=knBBi + kRi, writes=[("pp", bk)])
            kb.op("pe", lambda pe: pe.matmul(pp[bk][0:16, 0:496], Dd[0:16, g, :], E15[0:16, :],
                                             start=False, stop=True), reads=kDd + kE15, writes=[("pp", bk)])
            Ks, kKs = d_["Ks"], d_["kKs"]
            kb.op("act", lambda a: a.copy(out=Ks[0:16, :], in_=pp[bk][0:16, 0:496]), reads=[("pp", bk)], writes=kKs)
            kb.dma(ktab_d[l, g].rearrange("i c o -> c i o"), Ks[0:16, :].rearrange("p (i o) -> p i o", i=31),
                   reads=kKs, writes=[("ktab", l, g)])
            Ms, kMs = d_["Ms"], d_["kMs"]
            kbase = (l * 16 + g) * 31 * 256
            for bi_, boff in enumerate((8, 16, 0)):
                src = bass.AP(ktab_t, kbase + boff * 256, [[16, 128], [256, 8], [1, 16]])
                kb.dma(Ms[:, bi_, :].rearrange("p (i o) -> p i o", i=8), src, reads=[("ktab", l, g)], writes=kMs)
            def tail(g=g, blk=blk, kblk=kblk, Ms=Ms, kMs=kMs):
                kb.op("act", lambda a: a.copy(out=blk[:, 0:3, :], in_=Ms), reads=kMs, writes=kblk)
                kb.dma(sblk_d[l, g], blk, reads=kblk, writes=["sblk"])
            if pend_tail[0] is not None:
                pend_tail[0]()
            pend_tail[0] = tail
        pend_tail[0]()

    if "c" in mixers:
        kb.dma(glb[:], glb_d, writes=["glb"])
        kb.op("dve", lambda v: v.tensor_scalar(out=glb[:], in0=glb[:], scalar1=0.5, scalar2=None, op0=ALU.mult),
              reads=["glb"], writes=["glb"])
        for l in range(depth):
            s5_precompute(l)

    wstate = {"n": 0}

    def load_wblock(l, b):
        k = wstate["n"] % 3
        wstate["n"] += 1
        kb.dma(wb[k][:], wib_d[l, b], reads=["wib"], writes=[("wb", k)])
        return k

    def load_woblock(l, h):
        k = wstate["n"] % 3
        wstate["n"] += 1
        kb.dma(wb[k][:], wob_d[l, h], reads=["wob"], writes=[("wb", k)])
        return k


    def proj_fm(k, cb, evac, tqs=range(4), M=128, moff=0):
        for tq in tqs:
            bi = next_bank(0, 2)
            for c in range(8):
                kb.op("pe", lambda pe, c=c, bi=bi, tq=tq: pe.matmul(
                    pp[bi][0:M, :], wb[k][:, c, cb * 128 + moff: cb * 128 + moff + M], hT[:, c, tq * 512:(tq + 1) * 512],
                    start=(c == 0), stop=(c == 7)),
                    reads=[("wb", k), "hT"], writes=[("pp", bi)])
            evac(tq, bi)

    def proj_tm(k, tok_ap_fn, ntiles, evac, ncols=256):
        for i in range(ntiles):
            bi = next_bank(0, 2)
            for c in range(8):
                kb.op("pe", lambda pe, c=c, bi=bi, i=i: pe.matmul(
                    pp[bi][:, 0:ncols], tok_ap_fn(c, i), wb[k][:, c, 0:ncols], start=(c == 0), stop=(c == 7)),
                    reads=[("wb", k), "hT"], writes=[("pp", bi)])
            evac(i, bi)

    def rms_sq(i):
        junk = hs[4 + i % 2]
        kb.op("act", lambda a, i=i, junk=junk: a.activation(out=junk, in_=x_res[:, i, :], func=AF.Square,
                                                            accum_out=small[:, 16 + i:17 + i]),
              reads=[("x", i)], writes=khs[4 + i % 2] + ["ss"])

    def rms_fin():
        kb.op("dve", lambda v: v.tensor_scalar(out=small[:, 32:48], in0=small[:, 16:32],
                                               scalar1=1.0 / D, scalar2=EPS, op0=ALU.mult, op1=ALU.add),
              reads=["ss"], writes=["ms"])
        kb.op("pool", lambda g: g.tensor_tensor(out=small[:, 0:16], in0=small[:, 32:48],
                                                in1=small[:, 48:49].broadcast_to([128, 16]), op=ALU.pow),
              reads=["ms", "mhalf"], writes=["rstd"])

    kb.op("dve", lambda v: v.memset(small[:, 48:49], -0.5), writes=["mhalf"])
    nlh = small[:, 49:50]
    kb.op("dve", lambda v: v.memset(nlh, -math.log(2.0)), writes=["nlh"])

    W = S + 32

    def mixer_a(l):
        kUs = [ak(0, 8320), ak(8320, 8320)]
        kA, kB, kg2, kDm = ak(16640, 8320), ak(24960, 8320), ak(33280, 4096), ak(37376, 4096)
        ktt = [ak(41472 + j * 2048, 2048) for j in range(2)]
        Us = [av(0, [W], F32), av(8320, [W], F32)]
        A = av(16640, [W], F32)
        B = av(24960, [W], F32)
        g2 = av(33280, [S], BF16)
        Dm = av(37376, [S], BF16)
        tt = [av(41472 + j * 2048, [512], F32) for j in range(2)]
        kv = load_wblock(l, 0)
        kg = load_wblock(l, 1)
        for cb in range(2):
            U, kU = Us[cb], kUs[cb]
            kb.op("pool", lambda g: g.memset(U[:, 0:16], 0.0), writes=kU)
            kb.op("pool", lambda g: g.memset(U[:, 16 + S:W], 0.0), writes=kU)

            def ev_u(tq, bi, U=U, kU=kU):
                kb.op("act", lambda a: a.copy(out=U[:, 16 + tq * 512:16 + (tq + 1) * 512], in_=pp[bi][:, :]),
                      reads=[("pp", bi)], writes=kU)
            proj_fm(kv, cb, ev_u)
        for cb in range(2):
            U, kU = Us[cb], kUs[cb]
            kb.op("dve", lambda g: g.tensor_tensor(out=A[:, 1:W], in0=U[:, 0:W - 1], in1=U[:, 1:W], op=ALU.add),
                  reads=kU, writes=kA)
            kb.op("dve", lambda g: g.tensor_tensor(out=B[:, 2:W - 1], in0=A[:, 1:W - 2], in1=A[:, 3:W], op=ALU.add),
                  reads=kA, writes=kB)
            if cb == 1:
                kb.op("dve", lambda g: g.tensor_tensor(out=A[:, 4:W - 3], in0=B[:, 2:W - 5], in1=B[:, 6:W - 1], op=ALU.add),
                      reads=kB, writes=kA)
                kb.op("dve", lambda g: g.tensor_tensor(out=B[64:128, 8:W - 7], in0=A[64:128, 4:W - 11],
                                                        in1=A[64:128, 12:W - 3], op=ALU.add),
                      reads=kA, writes=kB)
            for (buf, nm, p0) in ((A, kA, 0), (B, kB, 64)):
                sl = slice(p0, p0 + 64)
                kb.op("dve", lambda v, buf=buf, sl=sl: v.tensor_tensor(
                    out=buf[sl, 16:24], in0=buf[sl, 16:24], in1=pcn[sl, cb, 0:8], op=ALU.mult),
                    reads=nm + ["pcn"], writes=nm)
                kb.op("dve", lambda v, buf=buf, sl=sl: v.tensor_tensor(
                    out=buf[sl, 8 + S:16 + S], in0=buf[sl, 8 + S:16 + S], in1=pcn[sl, cb, 8:16], op=ALU.mult),
                    reads=nm + ["pcn"], writes=nm)
                kb.op("dve", lambda v, buf=buf, sl=sl: v.scalar_tensor_tensor(
                    out=Dm[sl, :], in0=buf[sl, 16:16 + S], scalar=pcn[sl, cb, 16:17], in1=U[sl, 16:16 + S],
                    op0=ALU.mult, op1=ALU.subtract),
                    reads=nm + kU + ["pcn"], writes=kDm)

            def ev_g(tq, bi):
                t = tt[tq % 2]
                kb.op("act", lambda a: a.activation(out=t, in_=pp[bi][:, :], func=AF.Tanh, scale=0.5),
                      reads=[("pp", bi)], writes=ktt[tq % 2])
                kb.op("dve", lambda v: v.scalar_tensor_tensor(
                    out=g2[:, tq * 512:(tq + 1) * 512], in0=t, scalar=1.0, in1=pp[bi][:, :], op0=ALU.add, op1=ALU.mult),
                    reads=ktt[tq % 2] + [("pp", bi)], writes=kg2)
            proj_fm(kg, cb, ev_g)
            for tq in range(4):
                bi = next_bank(0, 2)
                kb.op("pe", lambda pe: pe.matmul(pp[bi][:, :], pwb[:, l, cb, :], Dm[:, tq * 512:(tq + 1) * 512],
                                                 start=True, stop=True),
                      reads=["pwb"] + kDm, writes=[("pp", bi)])
                kb.op("dve", lambda v: v.scalar_tensor_tensor(
                    out=yT[:, cb, tq * 512:(tq + 1) * 512], in0=pp[bi][:, :], scalar=psc[:, l, cb:cb + 1],
                    in1=g2[:, tq * 512:(tq + 1) * 512], op0=ALU.mult, op1=ALU.mult),
                    reads=[("pp", bi), "psc"] + kg2, writes=ky(cb))

    def run_pipeline(tasks, la):
        n = len(tasks)
        for i in range(n + la):
            if i < n:
                t = tasks[i]
                t["slot"] = i
                if t.get("pre"):
                    t["pre"]()
                t["s1"]()
            if i >= la:
                t = tasks[i - la]
                t["s2"]()
                if t.get("post"):
                    t["post"]()

    def run_pipeline_b(tasks, bs):
        n = len(tasks)
        nb = (n + bs - 1) // bs
        for b in range(nb + 1):
            if b < nb:
                for i in range(b * bs, min(n, (b + 1) * bs)):
                    t = tasks[i]
                    t["slot"] = i
                    if t.get("pre"):
                        t["pre"]()
                for i in range(b * bs, min(n, (b + 1) * bs)):
                    tasks[i]["s1a"]()
                for i in range(b * bs, min(n, (b + 1) * bs)):
                    tasks[i]["s1b"]()
            if b >= 1:
                for i in range((b - 1) * bs, min(n, b * bs)):
                    t = tasks[i]
                    t["s2"]()
                    if t.get("post"):
                        t["post"]()

    def gate_fm(k, kg2, g2, ktt, tt):
        for cb in range(2):
            def ev_g(tq, bi):
                t = tt[tq % 2]
                kb.op("act", lambda a: a.activation(out=t, in_=pp[bi][:, :], func=AF.Tanh, scale=0.5),
                      reads=[("pp", bi)], writes=ktt[tq % 2])
                kb.op("dve", lambda v: v.scalar_tensor_tensor(
                    out=g2[:, cb, tq * 512:(tq + 1) * 512], in0=t, scalar=1.0, in1=pp[bi][:, :], op0=ALU.add, op1=ALU.mult),
                    reads=ktt[tq % 2] + [("pp", bi)], writes=kg2)
            proj_fm(k, cb, ev_g)

    def psl(p0, n, d, n_sub):
        st = (p0 % n_sub) * d + p0 // n_sub
        return slice(st, st + (n - 1) * d + 1, d)

    def attn_finalize(h, acc, kacc, g2, kg2, rc, krc, tmp, ktmp, chunk0):
        nr = slice((h % 2) * 64, (h % 2) * 64 + 64)
        dr = slice(((h + 1) % 2) * 64, ((h + 1) % 2) * 64 + 64)
        for tq in range(8):
            ts_ = slice(tq * 256, (tq + 1) * 256)
            kb.op("dve", lambda v: v.reciprocal(out=rc[tq % 2][nr, :], in_=acc[dr, ts_]),
                  reads=kacc, writes=krc[tq % 2])
            kb.op("dve", lambda v: v.scalar_tensor_tensor(out=tmp[tq % 2][nr, :], in0=acc[nr, ts_], scalar=0.5,
                                                           in1=rc[tq % 2][nr, :], op0=ALU.mult, op1=ALU.mult),
                  reads=kacc + krc[tq % 2], writes=ktmp[tq % 2])
            kb.op("pool", lambda g: g.tensor_tensor(out=yT[nr, chunk0 + h // 2, ts_], in0=tmp[tq % 2][nr, :],
                                                    in1=g2[nr, h // 2, ts_], op=ALU.mult),
                  reads=ktmp[tq % 2] + kg2, writes=ky(chunk0 + h // 2, h=h % 2))

    def gate_apply(l, blk, chunk0, tt, ktt, gq, kgq):
        k = load_wblock(l, blk)
        for cb in range(2):
            def ev_g(tq, bi):
                t = tt[tq % 2]
                g_ = gq[tq % 2]
                ts_ = slice(tq * 512, (tq + 1) * 512)
                kb.op("act", lambda a: a.activation(out=t, in_=pp[bi][:, :], func=AF.Tanh, scale=0.5),
                      reads=[("pp", bi)], writes=ktt[tq % 2])
                kb.op("dve", lambda v: v.scalar_tensor_tensor(out=g_, in0=t, scalar=1.0, in1=pp[bi][:, :],
                                                               op0=ALU.add, op1=ALU.mult),
                      reads=ktt[tq % 2] + [("pp", bi)], writes=kgq[tq % 2])
                kb.op("pool", lambda g: g.tensor_tensor(out=yT[:, chunk0 + cb, ts_], in0=g_, in1=yT[:, chunk0 + cb, ts_],
                                                        op=ALU.mult),
                      reads=kgq[tq % 2] + ky(chunk0 + cb), writes=ky(chunk0 + cb))
            proj_fm(k, cb, ev_g)

    def mixer_b(l):
        o_q, o_kz, o_v, o_acc, o_pt, o_rc = 0, 8192, 24576, 32768, 49152, 51200
        qT = av(o_q, [2, S], BF16)
        kqT = ak(o_q, 8192)
        kTz = [av(o_kz + h * 4096, [S], BF16) for h in range(4)]
        kkTz = [ak(o_kz + h * 4096, 4096) for h in range(4)]
        Va = [av(o_v + j * 4096, [16, 128], BF16) for j in range(2)]
        kVa = [ak(o_v + j * 4096, 4096) for j in range(2)]
        accs = [av(o_acc + j * 8192, [S], F32) for j in range(2)]
        kaccs = [ak(o_acc + j * 8192, 8192) for j in range(2)]
        pt = [av(o_pt + j * 512, [256], BF16) for j in range(4)]
        kpt = [ak(o_pt + j * 512, 512) for j in range(4)]
        rcq = [av(o_rc + j * 1024, [256], F32) for j in range(2)]
        krcq = [ak(o_rc + j * 1024, 1024) for j in range(2)]
        for h in range(4):
            oh = slice(((h + 1) % 2) * 64, ((h + 1) % 2) * 64 + 64)
            kb.op("pool", lambda g, h=h, oh=oh: g.memset(kTz[h][oh, :], 0.0), writes=kkTz[h])
        rts = [[av(o_acc + sl * 2560 + j * 512, [128], F32) for j in range(4)] for sl in range(3)]
        krts = [[ak(o_acc + sl * 2560 + j * 512, 512) for j in range(4)] for sl in range(3)]
        qrs = [av(o_acc + sl * 2560 + 2048, [256], BF16) for sl in range(3)]
        kqrs = [ak(o_acc + sl * 2560 + 2048, 512) for sl in range(3)]
        rtasks = []
        for blk in (2, 3):
            k = load_wblock(l, blk)
            for i in range(NT):
                t = {"pre": None, "post": None}

                def s1(t=t, i=i, k=k):
                    sl = t["slot"] % 3
                    bi = (0, 1, 2)[sl]
                    for c in range(8):
                        kb.op("pe", lambda pe, c=c: pe.matmul(pp[bi][:, 0:256], hT[:, c, i * 128:(i + 1) * 128],
                                                              wb[k][:, c, 0:256], start=(c == 0), stop=(c == 7)),
                              reads=[("wb", k), "hT"], writes=[("pp", bi)])
                    z4 = pp[bi][:, 0:256].rearrange("p (h t f) -> p h t f", h=4, t=2)
                    x1, x2 = z4[:, :, 0, :], z4[:, :, 1, :]
                    cs = ropet[:, 0, i, :].unsqueeze(1).broadcast_to([128, 4, 32])
                    sn = ropet[:, 1, i, :].unsqueeze(1).broadcast_to([128, 4, 32])
                    r4 = [r.rearrange("p (h f) -> p h f", h=4) for r in rts[sl]]
                    q4 = qrs[sl].rearrange("p (h t f) -> p h t f", h=4, t=2)
                    for j, (a_, b_) in enumerate(((x1, cs), (x2, sn), (x2, cs), (x1, sn))):
                        kb.op("dve", lambda v, a_=a_, b_=b_, j=j: v.tensor_tensor(out=r4[j], in0=a_, in1=b_, op=ALU.mult),
                              reads=[("pp", bi), "ropet"], writes=krts[sl][j])
                    kb.op("pool", lambda g: g.tensor_tensor(out=q4[:, :, 0, :], in0=r4[0], in1=r4[1], op=ALU.subtract),
                          reads=krts[sl][0] + krts[sl][1], writes=kqrs[sl])
                    kb.op("pool", lambda g: g.tensor_tensor(out=q4[:, :, 1, :], in0=r4[2], in1=r4[3], op=ALU.add),
                          reads=krts[sl][2] + krts[sl][3], writes=kqrs[sl])

                def s2(t=t, i=i, blk=blk):
                    sl = t["slot"] % 3
                    b2 = next_bank(6, 8)
                    ptb = pp[b2][:].bitcast(BF16)
                    for pr in range(2):
                        kb.op("pe", lambda pe, pr=pr: pe.transpose(ptb[:, pr * 128:(pr + 1) * 128],
                                                                   qrs[sl][:, pr * 128:(pr + 1) * 128], ident[:]),
                              reads=kqrs[sl] + ["ident"], writes=[("pp", b2)])
                    if blk == 2:
                        kb.op("act", lambda a: a.copy(out=qT[:, :, i * 128:(i + 1) * 128],
                                                      in_=ptb[:, 0:256].rearrange("p (c n) -> p c n", c=2)),
                              reads=[("pp", b2)], writes=kqT)
                    else:
                        for h in range(4):
                            hp = slice((h % 2) * 64, (h % 2) * 64 + 64)
                            pr = h // 2
                            kb.op("act" if h % 2 == 0 else "dve",
                                  lambda e, h=h, hp=hp, pr=pr: (e.copy if h % 2 == 0 else e.tensor_copy)(
                                      out=kTz[h][hp, i * 128:(i + 1) * 128], in_=ptb[hp, pr * 128:(pr + 1) * 128]),
                                  reads=[("pp", b2)], writes=kkTz[h])
                t["s1"], t["s2"] = s1, s2
                rtasks.append(t)
        run_pipeline(rtasks, 2)
        kv = load_wblock(l, 4)
        for cb in range(2):
            def ev_vt(tq, bi):
                kb.op("act", lambda a: a.copy(out=yT[:, 2 + cb, tq * 512:(tq + 1) * 512], in_=pp[bi][:, :]),
                      reads=[("pp", bi)], writes=ky(2 + cb))
            proj_fm(kv, cb, ev_vt)
        tasks = []
        nva = 0
        for h in range(4):
            acc, kacc = accs[h % 2], kaccs[h % 2]
            voff = 0 if h % 2 == 0 else 64
            for pi, (d, n_sub) in enumerate(((1, 2048), (4, 512), (16, 128))):
                V, kV = Va[nva % 2], kVa[nva % 2]
                nva += 1

                def pre_v(V=V, kV=kV, h=h, voff=voff, d=d, n_sub=n_sub, pi=pi):
                    if pi < 2:
                        kb.op("pool", lambda g: g.memset(V[:, :, 64 - voff:128 - voff], 1.0), writes=kV)
                    for half in range(2):
                        b2 = next_bank(0, 2)
                        ptb = pp[b2][:].bitcast(BF16)
                        for q in range(8):
                            j = half * 8 + q
                            kb.op("pe", lambda pe, q=q, j=j: pe.transpose(
                                ptb[:, q * 128:(q + 1) * 128], yT[:, 2 + h // 2, psl(128 * j, 128, d, n_sub)], ident[:, :]),
                                reads=ky(2 + h // 2) + ["ident"], writes=[("pp", b2)])
                        kb.op("act", lambda a: a.copy(
                            out=V[:, half * 8:half * 8 + 8, voff:voff + 64],
                            in_=ptb.rearrange("p (q e) -> p q e", q=8)[:, :, voff:voff + 64]),
                            reads=[("pp", b2)], writes=kV)
                first_of_pattern = True
                for qb in range(4):
                    ob = next_bank(4, 6)
                    js = []
                    for j in range(max(0, 4 * qb - 1), min(16, 4 * qb + 5)):
                        slo = (128 * j // n_sub) * n_sub
                        qlo = max(128 * j - 64, slo, 512 * qb)
                        qhi = min(128 * j + 192, slo + n_sub, 512 * qb + 512)
                        if qlo < qhi:
                            js.append((j, qlo, qhi))
                    for ji, (j, qlo, qhi) in enumerate(js):
                        t = {}
                        t["pre"] = pre_v if first_of_pattern else None
                        first_of_pattern = False

                        def s1(t=t, j=j, qlo=qlo, qhi=qhi, h=h, d=d, n_sub=n_sub):
                            n = qhi - qlo
                            ns = t["slot"] % 4
                            sbk = (2, 3, 6, 7)[ns]
                            p_, kp_ = pt[ns], kpt[ns]
                            mo = qlo - (128 * j - 64)
                            kb.op("pe", lambda pe: pe.matmul(pp[sbk][:, 0:n], kTz[h][:, psl(128 * j, 128, d, n_sub)],
                                                             qT[:, h // 2, psl(qlo, n, d, n_sub)], start=True, stop=False),
                                  reads=kkTz[h] + kqT, writes=[("pp", sbk)])
                            kb.op("pe", lambda pe: pe.matmul(pp[sbk][:, 0:n], ident[:, :], band[:, mo:mo + n],
                                                             start=False, stop=True),
                                  reads=["ident", "band"], writes=[("pp", sbk)])
                            kb.op("act", lambda a: a.activation(out=p_[:, 0:n], in_=pp[sbk][:, 0:n], func=AF.Exp, scale=0.125),
                                  reads=[("pp", sbk)], writes=kp_)

                        def s2(t=t, j=j, qlo=qlo, qhi=qhi, qb=qb, ob=ob, V=V, kV=kV, first=(ji == 0)):
                            n = qhi - qlo
                            ns = t["slot"] % 4
                            p_, kp_ = pt[ns], kpt[ns]
                            kb.op("pe", lambda pe: pe.matmul(pp[ob][:, qlo - 512 * qb:qhi - 512 * qb], V[:, j, :], p_[:, 0:n],
                                                             start=first, stop=False, skip_group_check=True),
                                  reads=kV + kp_, writes=[("pp", ob)])
                        t["s1"], t["s2"], t["post"] = s1, s2, None
                        if ji == len(js) - 1:
                            def post(qb=qb, ob=ob, d=d, pi=pi, acc=acc, kacc=kacc, h=h):
                                if d == 1:
                                    dst = acc[:, 512 * qb:512 * qb + 512]
                                    src = pp[ob][:, :]
                                elif d == 4:
                                    dst = acc.rearrange("p (l x) -> p x l", x=4)[:, qb, :]
                                    src = pp[ob][:, :]
                                else:
                                    dst = acc.rearrange("p (l x) -> p x l", x=16)[:, 4 * qb:4 * qb + 4, :]
                                    src = pp[ob][:, :].rearrange("p (r l) -> p r l", r=4)
                                if pi == 0:
                                    kb.op("dve", lambda v: v.tensor_copy(out=dst, in_=src), reads=[("pp", ob)], writes=kacc)
                                else:
                                    kb.op("dve", lambda v: v.tensor_tensor(out=dst, in0=src, in1=dst, op=ALU.add),
                                          reads=[("pp", ob)] + kacc, writes=kacc)
                                if pi == 2 and qb == 3:
                                    nr = slice((h % 2) * 64, (h % 2) * 64 + 64)
                                    dr = slice(((h + 1) % 2) * 64, ((h + 1) % 2) * 64 + 64)
                                    for tq in range(8):
                                        ts_ = slice(tq * 256, (tq + 1) * 256)
                                        rc, krc = rcq[tq % 2], krcq[tq % 2]
                                        kb.op("act", lambda a: a.activation(out=rc[nr, :], in_=acc[dr, ts_], func=AF.Ln),
                                              reads=kacc, writes=krc)
                                        kb.op("act", lambda a: a.activation(out=rc[nr, :], in_=rc[nr, :], func=AF.Exp, scale=-1.0,
                                                                            bias=nlh[nr, :]),
                                              reads=krc + ["nlh"], writes=krc)
                                        kb.op("pool", lambda g: g.tensor_tensor(out=yT[nr, 2 + h // 2, ts_], in0=acc[nr, ts_],
                                                                                in1=rc[nr, :], op=ALU.mult),
                                              reads=kacc + krc, writes=ky(2 + h // 2, h=h % 2))
                            t["post"] = post
                        tasks.append(t)
        run_pipeline(tasks, 3)
        tt = [av(o_acc + j * 2048, [512], F32) for j in range(2)]
        ktt = [ak(o_acc + j * 2048, 2048) for j in range(2)]
        gq = [av(o_acc + 4096 + j * 2048, [512], F32) for j in range(2)]
        kgq = [ak(o_acc + 4096 + j * 2048, 2048) for j in range(2)]
        gate_apply(l, 5, 2, tt, ktt, gq, kgq)

    def na_rows(kt):
        rows = []
        for r in range(32):
            rs = min(max(r - 4, 0), 24)
            if any(rs <= 2 * kt + krl < rs + 8 for krl in range(2)):
                rows.append(r)
        return rows[0], rows[-1]

    def mixer_d(l):
        o_q, o_kz, o_v, o_e, o_tt, o_pt = 0, 8192, 24576, 32768, 42240, 46336
        qT = av(o_q, [2, S], BF16)
        kqT = ak(o_q, 8192)
        kTz = [av(o_kz + h * 4096, [S], BF16) for h in range(4)]
        kkTz = [ak(o_kz + h * 4096, 4096) for h in range(4)]
        Va = [av(o_v + j * 4096, [16, 128], BF16) for j in range(2)]
        kVa = [ak(o_v + j * 4096, 4096) for j in range(2)]
        Eb = [av(o_e + j * 4736, [2368], BF16) for j in range(2)]
        kEb = [ak(o_e + j * 4736, 4736) for j in range(2)]
        tt = [av(o_tt + j * 2048, [512], F32) for j in range(2)]
        ktt = [ak(o_tt + j * 2048, 2048) for j in range(2)]
        pt = [av(o_pt + j * 1024, [512], BF16) for j in range(4)]
        kpt = [ak(o_pt + j * 1024, 1024) for j in range(4)]
        for h in range(4):
            oh = slice(((h + 1) % 2) * 64, ((h + 1) % 2) * 64 + 64)
            kb.op("pool", lambda g, h=h, oh=oh: g.memset(kTz[h][oh, :], 0.0), writes=kkTz[h])
        k = load_wblock(l, 8)
        for cb in range(2):
            def ev_q(tq, bi):
                kb.op("act", lambda a: a.copy(out=qT[:, cb, tq * 512:(tq + 1) * 512], in_=pp[bi][:, :]),
                      reads=[("pp", bi)], writes=kqT)
            proj_fm(k, cb, ev_q)
        k = load_wblock(l, 9)
        for cb in range(2):
            def ev_k(tq, bi):
                for hh in range(2):
                    h = 2 * cb + hh
                    hp = slice(hh * 64, hh * 64 + 64)
                    kb.op("act" if hh == 0 else "dve",
                          lambda e, h=h, hp=hp, hh=hh: (e.copy if hh == 0 else e.tensor_copy)(
                              out=kTz[h][hp, tq * 512:(tq + 1) * 512], in_=pp[bi][hp, :]),
                          reads=[("pp", bi)], writes=kkTz[h])
            proj_fm(k, cb, ev_k)
        kv = load_wblock(l, 10)
        for cb in range(2):
            def ev_vt(tq, bi):
                kb.op("act", lambda a: a.copy(out=yT[:, 6 + cb, tq * 512:(tq + 1) * 512], in_=pp[bi][:, :]),
                      reads=[("pp", bi)], writes=ky(6 + cb))
            proj_fm(kv, cb, ev_vt)
        tasks = []
        for h in range(4):
            nr = slice((h % 2) * 64, (h % 2) * 64 + 64)
            dr = slice(((h + 1) % 2) * 64, ((h + 1) % 2) * 64 + 64)
            voff = 0 if h % 2 == 0 else 64
            V, kV = Va[h % 2], kVa[h % 2]
            E, kE = Eb[h % 2], kEb[h % 2]

            def pre_h(h=h, voff=voff, V=V, kV=kV, E=E, kE=kE):
                kb.dma(E, et_d[l, h], reads=["et"], writes=kE)
                if h < 2:
                    kb.op("pool", lambda g: g.memset(V[:, :, 64 - voff:128 - voff], 1.0), writes=kV)
                for half in range(2):
                    b2 = next_bank(0, 2)
                    ptb = pp[b2][:].bitcast(BF16)
                    for q in range(8):
                        j = half * 8 + q
                        kb.op("pe", lambda pe, q=q, j=j: pe.transpose(
                            ptb[:, q * 128:(q + 1) * 128], yT[:, 6 + h // 2, 128 * j:128 * j + 128], ident[:, :]),
                            reads=ky(6 + h // 2) + ["ident"], writes=[("pp", b2)])
                    kb.op("act", lambda a: a.copy(out=V[:, half * 8:half * 8 + 8, voff:voff + 64],
                                                  in_=ptb.rearrange("p (q e) -> p q e", q=8)[:, :, voff:voff + 64]),
                          reads=[("pp", b2)], writes=kV)
            first_of_head = True
            for qb in range(4):
                ob = next_bank(4, 6)
                kts = []
                for kt in range(16):
                    ra, rb = na_rows(kt)
                    ra, rb = max(ra, 8 * qb), min(rb, 8 * qb + 7)
                    if ra <= rb:
                        kts.append((kt, ra, rb))
                for ki, (kt, ra, rb) in enumerate(kts):
                    t = {"pre": pre_h if first_of_head else None, "post": None}
                    first_of_head = False

                    def s1(t=t, kt=kt, ra=ra, rb=rb, h=h, E=E, kE=kE):
                        n = 64 * (rb - ra + 1)
                        ns = t["slot"]
                        sbk = (2, 3, 6, 7)[ns % 4]
                        p_, kp_ = pt[ns % 4], kpt[ns % 4]
                        kb.op("pe", lambda pe: pe.matmul(pp[sbk][:, 0:n], kTz[h][:, 128 * kt:128 * kt + 128],
                                                         qT[:, h // 2, 64 * ra:64 * (rb + 1)], start=True, stop=True,
                                                         skip_group_check=True),
                              reads=kkTz[h] + kqT, writes=[("pp", sbk)])
                        segs = []
                        if ra <= 3:
                            r1 = min(rb, 3)
                            segs.append((ra, r1, 576 + (3 - kt) * 256 + ra * 64))
                        if max(ra, 4) <= min(rb, 28):
                            r0, r1 = max(ra, 4), min(rb, 28)
                            segs.append((r0, r1, (r0 - 2 * kt + 3) * 64))
                        if rb >= 29:
                            r0 = max(ra, 29)
                            segs.append((r0, rb, 1600 + (15 - kt) * 192 + (r0 - 29) * 64))
                        for si, (r0, r1, eoff) in enumerate(segs):
                            c0, c1 = 64 * (r0 - ra), 64 * (r1 - ra + 1)
                            kb.op("pe", lambda pe, c0=c0, c1=c1, eoff=eoff, si=si: pe.matmul(
                                pp[sbk][:, c0:c1], ident[:, :], E[:, eoff:eoff + c1 - c0], start=False,
                                stop=True, skip_group_check=True),
                                reads=["ident"] + kE, writes=[("pp", sbk)])
                        kb.op("act", lambda a: a.activation(out=p_[:, 0:n], in_=pp[sbk][:, 0:n], func=AF.Exp, scale=0.125),
                              reads=[("pp", sbk)], writes=kp_)

                    def s2(t=t, kt=kt, ra=ra, rb=rb, qb=qb, ob=ob, V=V, kV=kV, first=(ki == 0)):
                        n = 64 * (rb - ra + 1)
                        ns = t["slot"]
                        p_, kp_ = pt[ns % 4], kpt[ns % 4]
                        kb.op("pe", lambda pe: pe.matmul(pp[ob][:, 64 * ra - 512 * qb:64 * (rb + 1) - 512 * qb], V[:, kt, :],
                                                         p_[:, 0:n], start=first, stop=False, skip_group_check=True),
                              reads=kV + kp_, writes=[("pp", ob)])
                    t["s1"], t["s2"] = s1, s2
                    if ki == len(kts) - 1:
                        def post(qb=qb, ob=ob, h=h, nr=nr, dr=dr):
                            ts_ = slice(qb * 512, (qb + 1) * 512)
                            rc, krc = tt[qb % 2], ktt[qb % 2]
                            kb.op("act", lambda a: a.activation(out=rc[nr, :], in_=pp[ob][dr, :], func=AF.Ln),
                                  reads=[("pp", ob)], writes=krc)
                            kb.op("act", lambda a: a.activation(out=rc[nr, :], in_=rc[nr, :], func=AF.Exp, scale=-1.0,
                                                                bias=nlh[nr, :]),
                                  reads=krc + ["nlh"], writes=krc)
                            kb.op("dve", lambda v: v.tensor_tensor(out=yT[nr, 6 + h // 2, ts_], in0=pp[ob][nr, :],
                                                                   in1=rc[nr, :], op=ALU.mult),
                                  reads=[("pp", ob)] + krc, writes=ky(6 + h // 2, h=h % 2))
                        t["post"] = post
                    tasks.append(t)
        run_pipeline(tasks, 3)
        ttg = [av(o_v + j * 2048, [512], F32) for j in range(2)]
        kttg = [ak(o_v + j * 2048, 2048) for j in range(2)]
        gq = [av(o_v + 4096 + j * 2048, [512], F32) for j in range(2)]
        kgq = [ak(o_v + 4096 + j * 2048, 2048) for j in range(2)]
        gate_apply(l, 11, 6, ttg, kttg, gq, kgq)

    GC1 = math.sqrt(2.0 / math.pi)
    GC2 = 0.044715

    def mixer_c(l):
        o_u, o_G, o_V, o_P, o_MC, o_et, o_w, o_sin, o_y = 0, 8192, 16384, 20480, 22528, 26112, 30208, 40448, 44544
        ucm = [av(o_u + t * 4096, [16, 8, 16], BF16) for t in range(2)]
        kucm = [ak(o_u + t * 4096, 4096) for t in range(2)]
        Gcm = av(o_G, [16, 256], BF16)
        kG = ak(o_G, 8192)
        gT = av(o_u, [2, S], BF16)
        kgT = ak(o_u, 8192)
        Vs = [av(o_V + j * 512, [256], BF16) for j in range(8)]
        kVs = [ak(o_V + j * 512, 512) for j in range(8)]
        Pb = [av(o_P + j * 1024, [4, 128], BF16) for j in range(2)]
        kPb = [ak(o_P + j * 1024, 1024) for j in range(2)]
        MC = [av(o_MC + j * 1792, [7, 128], BF16) for j in range(2)]
        kMC = [ak(o_MC + j * 1792, 1792) for j in range(2)]
        et = av(o_et, [4, 2, 128], F32)
        ket = ak(o_et, 4096)
        wk_ = [av(o_w + j * 2048, [4, 128], F32) for j in range(5)]
        kwk = [ak(o_w + j * 2048, 2048) for j in range(5)]
        A_, B_, T1, T2, T3 = wk_
        kA, kB_, kT1, kT2, kT3 = kwk
        sres = [av(o_sin + j * 2048, [4, 128], BF16) for j in range(2)]
        sims = [av(o_sin + j * 2048 + 1024, [4, 128], BF16) for j in range(2)]
        ksres = [ak(o_sin + j * 2048, 1024) for j in range(2)]
        ksims = [ak(o_sin + j * 2048 + 1024, 1024) for j in range(2)]
        ytmp = [[av(o_y + s_ * 4096 + j * 1024, [256], F32) for j in range(4)] for s_ in range(2)]
        kytmp = [[ak(o_y + s_ * 4096 + j * 1024, 1024) for j in range(4)] for s_ in range(2)]
        F_, Bh = slice(0, 64), slice(64, 128)
        tt2 = lambda e, o, a, b, op, rd, wr: kb.op(e, lambda v: v.tensor_tensor(out=o, in0=a, in1=b, op=op), reads=rd, writes=wr)
        kcu = load_wblock(l, 6)
        for j in range(8):
            for mt in range(2):
                bi = next_bank(0, 2)
                for c in range(8):
                    kb.op("pe", lambda pe, c=c: pe.matmul(
                        pp[bi][:, 0:256], hT[:, c, slice(1024 * mt + j, 1024 * mt + j + 8 * 127 + 1, 8)],
                        wb[kcu][:, c, :], start=(c == 0), stop=(c == 7)),
                        reads=[("wb", kcu), "hT"], writes=[("pp", bi)])
                kb.op("act" if (j + mt) % 2 == 0 else "dve",
                      lambda e: (e.copy if (j + mt) % 2 == 0 else e.tensor_copy)(
                          out=ucm[mt][:, :, 7 - j, :], in_=pp[bi][:, 0:256].rearrange("p (g c) -> p g c", g=16)),
                      reads=[("pp", bi)], writes=kucm[mt])
        for j in range(2):
            kb.op("pool", lambda g, j=j: g.memset(sres[j][F_, :, 0:1], 0.0), writes=ksres[j])
            kb.op("pool", lambda g, j=j: g.memset(sres[j][Bh, :, 127:128], 0.0), writes=ksres[j])
            kb.op("pool", lambda g, j=j: g.memset(sims[j][F_, :, 0:1], 0.0), writes=ksims[j])
            kb.op("pool", lambda g, j=j: g.memset(sims[j][Bh, :, 127:128], 0.0), writes=ksims[j])
        banks = {}

        def x_front(gb):
            kb.dma(et, etab_d[l, gb], reads=["etab"], writes=ket)
            s0r, s0i = 2, 3
            banks[gb] = (s0r, s0i)
            for gi in range(4):
                g = 4 * gb + gi
                V, kV = Vs[(gb % 2) * 4 + gi], kVs[(gb % 2) * 4 + gi]
                P_, kP_ = Pb[g % 2], kPb[g % 2]
                kb.dma(P_, sblk_d[l, g, :, 3:7, :], reads=["sblk"], writes=kP_)
                bt = next_bank(6, 8)
                ptb = pp[bt][:].bitcast(BF16)
                for mt in range(2):
                    kb.op("pe", lambda pe, mt=mt: pe.transpose(
                        ptb[:, mt * 128:(mt + 1) * 128], ucm[mt][:, g, :, :].rearrange("p j c -> p (j c)"), ident[:]),
                        reads=kucm[mt] + ["ident"], writes=[("pp", bt)])
                kb.op("act", lambda a: a.copy(out=V, in_=ptb[:, 0:256]), reads=[("pp", bt)], writes=kV)
                for (bank, b0) in ((s0r, 0), (s0i, 2)):
                    for sub in range(2):
                        kb.op("pe", lambda pe, sub=sub, bank=bank, b0=b0: pe.matmul(
                            pp[bank][:, gi * 128:(gi + 1) * 128], P_[:, b0 + sub, :], V[:, sub:256:2],
                            start=(sub == 0), stop=(sub == 1), skip_group_check=True),
                            reads=kP_ + kV, writes=[("pp", bank)])
            for (bank, dst, kd) in ((s0r, A_, kA), (s0i, B_, kB_)):
                src = pp[bank][:, :].rearrange("p (g m) -> p g m", g=4)
                kb.op("act", lambda a, src=src, dst=dst: a.copy(out=dst[F_], in_=src[F_]), reads=[("pp", bank)], writes=kd)
                kb.op("act", lambda a, src=src, dst=dst: a.copy(out=dst[Bh], in_=src[Bh, :, ::-1]), reads=[("pp", bank)], writes=kd)

        def x_back(gb, part):
            cs_, sn_ = et[:, :, 0, :], et[:, :, 1, :]
            sre, sim, ksre, ksim = sres[gb % 2], sims[gb % 2], ksres[gb % 2], ksims[gb % 2]
            if part == 0:
                tt2("dve", T1, A_, cs_, ALU.mult, kA + ket, kT1)
                tt2("pool", T2, B_, sn_, ALU.mult, kB_ + ket, kT2)
                tt2("dve", T1, T1, T2, ALU.add, kT1 + kT2, kT1)
                tt2("pool", T3, B_, cs_, ALU.mult, kB_ + ket, kT3)
                tt2("dve", T2, A_, sn_, ALU.mult, kA + ket, kT2)
                tt2("pool", T3, T3, T2, ALU.subtract, kT3 + kT2, kT3)
            elif part == 1:
                for gi in range(4):
                    g = 4 * gb + gi
                    rb_ = rho_sb[:, l, g:g + 1].broadcast_to([128, 128])
                    kb.op("dve", lambda v, gi=gi, rb_=rb_: v.tensor_tensor_scan(
                        out=A_[:, gi, :], data0=rb_, data1=T1[:, gi, :], initial=0.0, op0=ALU.mult, op1=ALU.add),
                        reads=kT1 + ["rho"], writes=kA)
                    kb.op("dve", lambda v, gi=gi, rb_=rb_: v.tensor_tensor_scan(
                        out=B_[:, gi, :], data0=rb_, data1=T3[:, gi, :], initial=0.0, op0=ALU.mult, op1=ALU.add),
                        reads=kT3 + ["rho"], writes=kB_)
            elif part == 2:
                tt2("dve", T1, A_, cs_, ALU.mult, kA + ket, kT1)
                tt2("pool", T2, B_, sn_, ALU.mult, kB_ + ket, kT2)
                tt2("dve", T1, T1, T2, ALU.subtract, kT1 + kT2, kT1)
                tt2("pool", T3, B_, cs_, ALU.mult, kB_ + ket, kT3)
                tt2("dve", T2, A_, sn_, ALU.mult, kA + ket, kT2)
                tt2("pool", T3, T3, T2, ALU.add, kT3 + kT2, kT3)
            else:
                for (src, ksrc, dst, kd) in ((T1, kT1, sre, ksre), (T3, kT3, sim, ksim)):
                    kb.op("act", lambda a, src=src, dst=dst: a.copy(out=dst[F_, :, 1:128], in_=src[F_, :, 0:127]),
                          reads=ksrc, writes=kd)
                    kb.op("pool", lambda g_, src=src, dst=dst: g_.tensor_copy(out=dst[Bh, :, 0:127], in_=src[Bh, :, 126::-1]),
                          reads=ksrc, writes=kd)

        def y_group(gb, gi):
            g = 4 * gb + gi
            V, kV = Vs[(gb % 2) * 4 + gi], kVs[(gb % 2) * 4 + gi]
            sre, sim, ksre, ksim = sres[gb % 2], sims[gb % 2], ksres[gb % 2], ksims[gb % 2]
            M_, kM_ = MC[g % 2], kMC[g % 2]
            Ysb, sq, u_, th_ = ytmp[g % 2]
            kYsb, ksq, ku_, kth_ = kytmp[g % 2]
            kb.dma(M_[:, 0:3, :], sblk_d[l, g, :, 0:3, :], reads=["sblk"], writes=kM_)
            kb.dma(M_[:, 3:7, :], sblk_d[l, g, :, 7:11, :], reads=["sblk"], writes=kM_)
            yb = next_bank(4, 6)
            V0, V1 = V[:, 0:256:2], V[:, 1:256:2]
            plan = ((0, [(0, V0, kV), (2, V1, kV), (3, sre[:, gi, :], ksre), (4, sim[:, gi, :], ksim)]),
                    (1, [(0, V1, kV), (1, V0, kV), (5, sre[:, gi, :], ksre), (6, sim[:, gi, :], ksim)]))
            for so, terms in plan:
                for ti, (bidx, rhs, krhs) in enumerate(terms):
                    kb.op("pe", lambda pe, so=so, ti=ti, bidx=bidx, rhs=rhs: pe.matmul(
                        pp[yb][:, so * 128:(so + 1) * 128], M_[:, bidx, :], rhs, start=(ti == 0), stop=(ti == 3),
                        skip_group_check=True), reads=kM_ + krhs, writes=[("pp", yb)])
            kb.op("act", lambda a: a.copy(out=Ysb, in_=pp[yb][:, 0:256]), reads=[("pp", yb)], writes=kYsb)
            tb = next_bank(6, 8)
            for so in range(2):
                kb.op("pe", lambda pe, so=so: pe.transpose(pp[tb][:, so * 128:(so + 1) * 128],
                                                           Ysb[:, so * 128:(so + 1) * 128], identf[:]),
                      reads=kYsb + ["identf"], writes=[("pp", tb)])
            yy = pp[tb][:, 0:256]
            kb.op("act", lambda a: a.activation(out=sq, in_=yy, func=AF.Square), reads=[("pp", tb)], writes=ksq)
            kb.op("pool", lambda g_: g_.tensor_scalar(out=sq, in0=sq, scalar1=GC2, scalar2=1.0, op0=ALU.mult, op1=ALU.add),
                  reads=ksq, writes=ksq)
            kb.op("dve", lambda v: v.tensor_tensor(out=u_, in0=sq, in1=yy, op=ALU.mult), reads=ksq + [("pp", tb)], writes=ku_)
            kb.op("act", lambda a: a.activation(out=th_, in_=u_, func=AF.Tanh, scale=GC1), reads=ku_, writes=kth_)
            kb.op("dve", lambda v: v.scalar_tensor_tensor(
                out=Gcm[:, :, g * 16:(g + 1) * 16], in0=th_.rearrange("p (s c) -> p s c", c=16), scalar=1.0,
                in1=yy.rearrange("p (s c) -> p s c", c=16), op0=ALU.add, op1=ALU.mult),
                reads=kth_ + [("pp", tb)], writes=kG)

        x_front(0)
        for part in range(4):
            x_back(0, part)
        for gb in range(4):
            if gb + 1 < 4:
                x_front(gb + 1)
            for gi in range(4):
                y_group(gb, gi)
                if gb + 1 < 4:
                    x_back(gb + 1, gi)
        dump("Gcm", Gcm, kG)
        for chc in range(2):
            for half in range(2):
                bt = next_bank(6, 8)
                ptb = pp[bt][:].bitcast(BF16)
                for q in range(8):
                    si = half * 8 + q
                    kb.op("pe", lambda pe, q=q, si=si: pe.transpose(ptb[:, q * 128:(q + 1) * 128],
                                                                    Gcm[:, si, chc * 128:(chc + 1) * 128], ident[:]),
                          reads=kG + ["ident"], writes=[("pp", bt)])
                kb.op("act" if half == 0 else "dve", lambda e: (e.copy if half == 0 else e.tensor_copy)(
                    out=gT[:, chc, :].rearrange("p (m s) -> p s m", s=16)[:, half * 8:half * 8 + 8, :],
                    in_=ptb.rearrange("p (q m) -> p q m", q=8)), reads=[("pp", bt)], writes=kgT)
        dump("gT", gT, kgT)
        gwl = av(o_et, [2, 256], BF16)
        kgwl = ak(o_et, 1024)
        kb.dma(gwl, gwb_d[l], reads=["gwb"], writes=kgwl)
        ttg = [av(o_w + j * 2048, [512], F32) for j in range(2)]
        t2s = [av(o_w + (2 + j) * 2048, [512], F32) for j in range(2)]
        n_ = 0
        for ec in range(2):
            for tq in range(4):
                ts_ = slice(tq * 512, (tq + 1) * 512)
                bi = next_bank(0, 2)
                for cc in range(2):
                    kb.op("pe", lambda pe, cc=cc: pe.matmul(pp[bi][:, :], gwl[:, cc, ec * 128:(ec + 1) * 128], gT[:, cc, ts_],
                                                            start=(cc == 0), stop=(cc == 1)),
                          reads=kgwl + kgT, writes=[("pp", bi)])
                th2, kth2 = ttg[n_ % 2], kwk[n_ % 2]
                t_, kt_ = t2s[n_ % 2], kwk[2 + n_ % 2]
                n_ += 1
                kb.op("act", lambda a: a.activation(out=th2, in_=pp[bi][:, :], func=AF.Tanh, scale=0.25,
                                                    bias=glb[:, l, ec:ec + 1]),
                      reads=[("pp", bi), "glb"], writes=kth2)
                kb.op("dve", lambda v: v.scalar_tensor_tensor(out=t_, in0=th2, scalar=1.0, in1=gT[:, ec, ts_],
                                                               op0=ALU.add, op1=ALU.mult),
                      reads=kth2 + kgT, writes=kt_)
                kb.op("pool", lambda g_: g_.tensor_scalar(out=yT[:, 4 + ec, ts_], in0=t_, scalar1=0.125, scalar2=1.0,
                                                          op0=ALU.mult, op1=ALU.mult),
                      reads=kt_, writes=ky(4 + ec))
        gtt = [av(o_G + j * 2048, [512], F32) for j in range(2)]
        kgtt = [ak(o_G + j * 2048, 2048) for j in range(2)]
        gq = [av(o_G + 4096 + j * 2048, [512], F32) for j in range(2)]
        kgq = [ak(o_G + 4096 + j * 2048, 2048) for j in range(2)]
        gate_apply(l, 7, 4, gtt, kgtt, gq, kgq)

    kb.same_depth = 2
    stg_keys = [("hTs", 0), ("hTs", 1)] + [("yTs", j) for j in range(6)]
    stg_keys += [("xs", "m"), ("xs", "n"), ("xs", "s", 0), ("xs", "s", 1), ("xs", "e", 0), ("xs", "e", 1)]
    for e_ in ("act", "dve", "pool"):
        kb.op(e_, (lambda a: a.copy(out=small[:, 50:51], in_=small[:, 49:50])) if e_ == "act" else
              (lambda v: v.tensor_copy(out=small[:, 51 + (0 if e_ == "dve" else 1):52 + (0 if e_ == "dve" else 1)], in_=small[:, 49:50])),
              reads=stg_keys + ["nlh"], writes=["hT"] + ky(0, 8) + [("x", i_) for i_ in range(NT)])
    for s in range(nseq):
        for i in range(NT):
            kb.dma(x_res[:, i, :], x_d[s, i * 128:(i + 1) * 128, :], writes=[("x", i)])
        for l in range(depth):
            if l == 0:
                for i in range(NT):
                    rms_sq(i)
            rms_fin()
            for i in range(NT + 2):
                if i < NT:
                    kb.op("act", lambda a, i=i: a.activation(out=hs[i % 4], in_=x_res[:, i, :], func=AF.Copy,
                                                             scale=small[:, i:i + 1]),
                          reads=[("x", i), "rstd"], writes=khs[i % 4])
                    bi = 4 + i % 4
                    pt = pp[bi][:].bitcast(BF16)
                    for c in range(8):
                        kb.op("pe", lambda pe, c=c, i=i, pt=pt: pe.transpose(
                            pt[:, c * 128:(c + 1) * 128], hs[i % 4][:, c * 128:(c + 1) * 128], ident[:]),
                            reads=khs[i % 4] + ["ident"], writes=[("pp", bi)])
                if i >= 2:
                    j = i - 2
                    bj = 4 + j % 4
                    ptj = pp[bj][:].bitcast(BF16)
                    if j % 2 == 0:
                        kb.op("act", lambda a, j=j, ptj=ptj: a.copy(
                            out=hT[:, :, j * 128:(j + 1) * 128], in_=ptj.rearrange("p (c n) -> p c n", c=8)),
                            reads=[("pp", bj)], writes=["hT"])
                    else:
                        kb.op("dve", lambda v, j=j, ptj=ptj: v.tensor_copy(
                            out=hT[:, :, j * 128:(j + 1) * 128], in_=ptj.rearrange("p (c n) -> p c n", c=8)),
                            reads=[("pp", bj)], writes=["hT"])
            if "a" in mixers:
                mixer_a(l)
            if "b" in mixers:
                mixer_b(l)
            if "c" in mixers:
                mixer_c(l)
            if "d" in mixers:
                mixer_d(l)
            for mi, m in enumerate("abcd"):
                if m not in mixers:
                    kb.op("pool", lambda g, mi=mi: g.memset(yT[:, 2 * mi:2 * mi + 2, :], 0.0), writes=ky(2 * mi, 2 * mi + 2))
            if dbg and s == 0 and l == 0:
                kb.dma(dbg_d, yT[:], reads=ky(0, 8), writes=["dbgout"])
            wo, kwo = [], []
            for h in range(4):
                wo.append(av(28672 + h * 4096, [8, 256], BF16))
                kwo.append(ak(28672 + h * 4096, 4096))
                kb.dma(wo[h], wob_d[l, h], reads=["wob"], writes=kwo[h])
            for i in range(NT):
                for h in range(4):
                    bi = next_bank(0, 2)
                    for c in range(8):
                        kb.op("pe", lambda pe, c=c, i=i, bi=bi, h=h: pe.matmul(
                            pp[bi][:, 0:256], yT[:, c, i * 128:(i + 1) * 128], wo[h][:, c, :],
                            start=(c == 0), stop=(c == 7)),
                            reads=kwo[h] + ky(c), writes=[("pp", bi)])
                    kb.op("dve", lambda v, i=i, bi=bi, h=h: v.tensor_tensor(
                        out=x_res[:, i, h * 256:(h + 1) * 256], in0=pp[bi][:, 0:256],
                        in1=x_res[:, i, h * 256:(h + 1) * 256], op=ALU.add),
                        reads=[("pp", bi)], writes=[("x", i)])
                rms_sq(i)
        fg = av(8192, [D], F32)
        kb.dma(fg, fg_d, writes=ak(8192, 4096))
        rms_fin()
        for i in range(NT):
            oo = 28672 + (i % 4) * 4096
            ot = av(oo, [1024], F32)
            kb.op("dve", lambda v, i=i, ot=ot: v.scalar_tensor_tensor(
                out=ot, in0=x_res[:, i, :], scalar=small[:, i:i + 1], in1=fg, op0=ALU.mult, op1=ALU.mult),
                reads=[("x", i), "rstd"] + ak(8192, 4096), writes=ak(oo, 4096))
            kb.dma(y_d[s, i * 128:(i + 1) * 128, :], ot, reads=ak(oo, 4096), writes=["y"])
    kb.finish()
    return nc


def host_prep(inputs):
    f = np.float32
    ng = np.ascontiguousarray(np.asarray(inputs["norm_g"], f).reshape(4, 8, 128).transpose(2, 0, 1))
    fgb = np.ascontiguousarray(np.broadcast_to(np.asarray(inputs["final_g"], f)[None, :], (128, D)))
    pw = np.asarray(inputs["pool_w"], f)
    pwb = np.zeros((128, 4, 2, 128), f)
    for g in range(4):
        cb, h = g // 2, g % 2
        pwb[h * 64:(h + 1) * 64, :, cb, h * 64:(h + 1) * 64] = pw[:, g].transpose(1, 0, 2)
    psc = np.ascontiguousarray(np.asarray(inputs["pool_scale"], f).reshape(4, 2, 128).transpose(2, 0, 1))
    pcn = np.zeros((128, 2, 17), f)
    for g, w in enumerate((2, 4, 8, 16)):
        cb, h = g // 2, g % 2
        t = np.arange(S)
        cnt = np.minimum(t + w // 2, S) - np.maximum(t - w // 2, 0)
        pcn[h * 64:(h + 1) * 64, cb, 0:8] = (w / cnt[0:8])[None, :]
        pcn[h * 64:(h + 1) * 64, cb, 8:16] = (w / cnt[S - 8:S])[None, :]
        pcn[h * 64:(h + 1) * 64, cb, 16] = 1.0 / w
    inv = 10000.0 ** (-np.arange(0, 64, 2, dtype=np.float32) / 64)
    ang = np.arange(S, dtype=np.float32)[:, None] * inv[None, :]
    rope = np.stack([np.cos(ang), np.sin(ang)], 0).astype(f)
    rope_t = np.ascontiguousarray(rope.reshape(2, NT, 128, 32).transpose(2, 0, 1, 3))
    kk = np.arange(128)[:, None]
    cc = np.arange(256)[None, :]
    band = np.where(((cc - kk) >= 0) & ((cc - kk) <= 128), 0.0, -240000.0).astype(ml_dtypes.bfloat16)
    rpb = np.asarray(inputs["na_rpb"], f)
    rpbpad = np.zeros((4, 4, 15, 128), f)
    rpbpad[:, :, :, 48:79] = rpb[:, :, ::-1, :]
    kc = np.arange(64)
    c = 63 - np.arange(64)
    cs = np.clip(c - 8, 0, 48)
    colok = ((kc[:, None] >= cs[None, :]) & (kc[:, None] < cs[None, :] + 16)).astype(f)
    nam = np.zeros((128, 2368), f)
    for krl in range(2):
        for ri in range(9):
            dlt = krl - ri + 3
            if -4 <= dlt <= 3:
                nam[krl * 64:(krl + 1) * 64, ri * 64:(ri + 1) * 64] = colok
        for blk in range(28):
            nam[krl * 64:(krl + 1) * 64, 576 + blk * 64:576 + (blk + 1) * 64] = colok
    are = np.asarray(inputs["ssm_a_re"], f)
    aim = np.asarray(inputs["ssm_a_im"], f)
    ldt = np.asarray(inputs["ssm_log_dt"], f)
    lam = np.stack([are.transpose(0, 1, 3, 2), aim.transpose(0, 1, 3, 2),
                    np.broadcast_to(ldt[:, :, None, :], (4, 2, 64, 16))], axis=-1)
    lam = np.ascontiguousarray(lam.reshape(4, 128, 16, 3))
    bre = np.asarray(inputs["ssm_b_re"], f)
    bim = np.asarray(inputs["ssm_b_im"], f)
    bp1 = np.stack([bre.transpose(0, 2, 1, 3), bim.transpose(0, 2, 1, 3)], axis=-1)
    bp = np.ascontiguousarray(np.concatenate([bp1, bp1], axis=1))
    cre = np.asarray(inputs["ssm_c_re"], f)
    cim = np.asarray(inputs["ssm_c_im"], f)
    cp1 = np.stack([cre.transpose(0, 1, 4, 2, 3), cim.transpose(0, 1, 4, 2, 3)], axis=-1)
    cp = np.ascontiguousarray(cp1.reshape(4, 128, 16, 16, 2))
    sd = np.asarray(inputs["ssm_d"], f).reshape(4, 16, 16)
    sdt = np.ascontiguousarray(sd.transpose(2, 0, 1))
    glw = np.ascontiguousarray(np.asarray(inputs["glu_w"], f).reshape(4, 2, 128, 256).transpose(0, 2, 1, 3))
    glbt = np.ascontiguousarray(np.asarray(inputs["glu_b"], f).reshape(4, 2, 128).transpose(2, 0, 1))
    return {
        "ssm_lam": lam, "ssm_bp": bp, "ssm_cp": cp, "ssm_dt": sdt, "glu_w_t": glw, "glu_b_t": glbt,
        "rpbpad": rpbpad, "na_mask": nam.astype(ml_dtypes.bfloat16),
        "rope_t": rope_t, "band": band,
        "pool_w_blk": pwb, "pool_scale_t": psc, "pool_const": pcn,
        "w_in": np.ascontiguousarray(inputs["w_in"], dtype=f),
        "w_out": np.ascontiguousarray(inputs["w_out"], dtype=f),
        "norm_g_t": ng,
        "final_g_b": fgb,
    }


def kernel(**inputs):
    xp = np.asarray(inputs["x_prompt"], np.float32)
    xs = np.asarray(inputs["x_sample"], np.float32)
    shared = host_prep(inputs)
    nc = build()
    in_maps = []
    for c in range(8):
        xc = np.concatenate([xp[4 * c:4 * c + 4], xs[c:c + 1]], axis=0)
        m = dict(shared)
        m["x"] = np.ascontiguousarray(xc)
        in_maps.append(m)
    res = run_bass_kernel_spmd(nc, in_maps, core_ids=list(range(8)))
    yp = np.empty_like(xp)
    ys = np.empty_like(xs)
    for c in range(8):
        y = res.results[c]["y"]
        yp[4 * c:4 * c + 4] = y[0:4]
        ys[c] = y[4]
    return (yp, ys)
```
